# Optimizing a Trainium2 kernel written in Bass

```python
import math
import jax, jax.numpy as jnp
from jax import lax
import numpy as np

D_MODEL = 1024
BATCH = 8
SEQ = 2048
DEPTH = 1
DEC_BATCH = 128
DEC_SEQ = 4
PAST_LEN = 16384
PAGE_SIZE = 128

SSD_EXPAND = 2
SSD_INNER = SSD_EXPAND * D_MODEL
SSD_HEAD_DIM = 64
SSD_HEADS = SSD_INNER // SSD_HEAD_DIM
SSD_GROUPS = 4
SSD_STATE = 128
SSD_CONV = 4
SSD_CHUNK = 128
SSD_CONV_DIM = SSD_INNER + 2 * SSD_GROUPS * SSD_STATE
HG_EXPAND = 128
HG_HEADS = D_MODEL // HG_EXPAND
HG_KDIM = HG_HEADS * HG_EXPAND
HG_VDIM = D_MODEL
HG_HEAD_V = HG_VDIM // HG_HEADS
HG_CHUNK = 32
MEM_LEN = 256
XA_HEADS = 4
XA_HEAD_DIM = D_MODEL // XA_HEADS
FFN_DIM = ((-(-8 * D_MODEL // 3)) + 255) // 256 * 256
EPS = 1e-6
F32 = jnp.float32
IN_SIZES = (SSD_INNER, SSD_CONV_DIM, SSD_HEADS, HG_KDIM, HG_KDIM, HG_VDIM, HG_VDIM, D_MODEL, D_MODEL)
IN_DIM = sum(IN_SIZES)
IN_SPLITS = tuple(int(v) for v in np.cumsum(IN_SIZES)[:-1])

kernel_name = 'hybrid_ssd_hgrn2_gated_decoder_step'


def group_rmsnorm(x, w, groups):
    shp = x.shape
    xg = x.astype(F32).reshape(shp[:-1] + (groups, shp[-1] // groups))
    xg = xg * lax.rsqrt(jnp.mean(xg * xg, axis=-1, keepdims=True) + EPS)
    return (xg.reshape(shp) * w.astype(F32)).astype(x.dtype)


def rmsnorm(x, w):
    return group_rmsnorm(x, w, 1)


def causal_dwconv(full, w, b):
    y = lax.conv_general_dilated(full, w[:, None, :], window_strides=(1,), padding='VALID',
                                 dimension_numbers=('NWC', 'WIO', 'NWC'),
                                 feature_group_count=full.shape[-1])
    return y + b


def ssd_scan(x, dt, a, bm, cm, h0):
    bsz, L = x.shape[:2]
    q = SSD_CHUNK if L % SSD_CHUNK == 0 else L
    nc = L // q
    r = SSD_HEADS // SSD_GROUPS

    def chunks(t, tail):
        return jnp.moveaxis(t.astype(F32).reshape((bsz, nc, q) + tail), 1, 0)

    xs = chunks(x, (SSD_GROUPS, r, SSD_HEAD_DIM))
    dts = chunks(dt, (SSD_GROUPS, r))
    bs = chunks(bm, (SSD_GROUPS, SSD_STATE))
    cs = chunks(cm, (SSD_GROUPS, SSD_STATE))
    ag = a.astype(F32).reshape(SSD_GROUPS, r)
    causal = jnp.tril(jnp.ones((q, q), bool))[None, :, :, None, None]

    def step(h, inp):
        xc, dtc, bc, cc = inp
        cum = jnp.cumsum(dtc * ag, axis=1)
        decay = jnp.exp(jnp.where(causal, cum[:, :, None] - cum[:, None, :], -jnp.inf))
        cb = jnp.einsum('btgn,bsgn->btsg', cc, bc)
        y = jnp.einsum('btsg,btsgr,bsgrp->btgrp', cb, decay, xc * dtc[..., None])
        y = y + jnp.einsum('btgn,bgrpn->btgrp', cc, h) * jnp.exp(cum)[..., None]
        w_end = jnp.exp(cum[:, -1:] - cum) * dtc
        h = h * jnp.exp(cum[:, -1])[..., None, None] + jnp.einsum('bsgr,bsgn,bsgrp->bgrpn', w_end, bc, xc)
        return h, y

    h0g = h0.astype(F32).reshape(bsz, SSD_GROUPS, r, SSD_HEAD_DIM, SSD_STATE)
    hT, ys = lax.scan(step, h0g, (xs, dts, bs, cs))
    y = jnp.moveaxis(ys, 0, 1).reshape(bsz, L, SSD_HEADS, SSD_HEAD_DIM)
    return y.astype(x.dtype), hT.reshape(bsz, SSD_HEADS, SSD_HEAD_DIM, SSD_STATE).astype(h0.dtype)


def hgrn2_scan(q, k, v, logf, s0):
    bsz, L = q.shape[:2]
    c = HG_CHUNK if L % HG_CHUNK == 0 else L
    nc = L // c

    def chunks(t):
        return jnp.moveaxis(t.astype(F32).reshape((bsz, nc, c) + t.shape[2:]), 1, 0)

    causal = jnp.tril(jnp.ones((c, c), bool))[None, :, :, None, None]

    def step(s, inp):
        qc, kc, vc, lc = inp
        g = jnp.cumsum(lc, axis=1)
        decay = jnp.exp(jnp.where(causal, g[:, :, None] - g[:, None, :], -jnp.inf))
        att = jnp.einsum('bthk,btshk,bshk->btsh', qc, decay, kc)
        o = jnp.einsum('btsh,bshv->bthv', att, vc) + jnp.einsum('bthk,bhkv->bthv', qc * jnp.exp(g), s)
        s = s * jnp.exp(g[:, -1])[..., None] + jnp.einsum('bshk,bshv->bhkv', kc * jnp.exp(g[:, -1:] - g), vc)
        return s, o

    sT, os_ = lax.scan(step, s0.astype(F32), (chunks(q), chunks(k), chunks(v), chunks(logf)))
    o = jnp.moveaxis(os_, 0, 1).reshape(bsz, L, HG_HEADS, HG_HEAD_V)
    return o.astype(q.dtype), sT.astype(s0.dtype)


def token_mixers(hn, conv_buf, ssm_h, hg_s, p, lb):
    bsz, L, _ = hn.shape
    proj = hn @ p['w_in']
    z, xbc, dt_raw, hq, hf, hi, hgate, g_ssd, g_hg = jnp.split(proj, IN_SPLITS, axis=-1)
    full = jnp.concatenate([conv_buf, xbc], axis=1)
    conv_new = full[:, -(SSD_CONV - 1):]
    xbc = jax.nn.silu(causal_dwconv(full, p['conv_w'], p['conv_b']))
    xs, bm, cm = jnp.split(xbc, [SSD_INNER, SSD_INNER + SSD_GROUPS * SSD_STATE], axis=-1)
    xs = xs.reshape(bsz, L, SSD_HEADS, SSD_HEAD_DIM)
    bm = bm.reshape(bsz, L, SSD_GROUPS, SSD_STATE)
    cm = cm.reshape(bsz, L, SSD_GROUPS, SSD_STATE)
    dt = jax.nn.softplus(dt_raw.astype(F32) + p['dt_bias'].astype(F32))
    a = -jnp.exp(p['a_log'].astype(F32))
    y, ssm_new = ssd_scan(xs, dt, a, bm, cm, ssm_h)
    y = (y + xs * p['d_skip'][:, None]).reshape(bsz, L, SSD_INNER) * jax.nn.silu(z)
    ssd_out = group_rmsnorm(y, p['ssd_norm'], SSD_GROUPS) @ p['w_ssd_out']
    u = hf.astype(F32)
    logf = jnp.log(lb + (1.0 - lb) * jax.nn.sigmoid(u))
    k = (1.0 - lb) * jax.nn.sigmoid(-u)
    heads = lambda t: t.reshape(bsz, L, HG_HEADS, -1)
    o, hg_new = hgrn2_scan(heads(jax.nn.silu(hq)), heads(k), heads(hi), heads(logf), hg_s)
    o = group_rmsnorm(o.reshape(bsz, L, HG_VDIM), p['hgrn_norm'], HG_HEADS) * jax.nn.silu(hgate)
    hg_out = o @ p['w_hgrn_out']
    merged = jax.nn.sigmoid(g_ssd) * ssd_out + jax.nn.sigmoid(g_hg) * hg_out
    return merged @ p['w_out'], conv_new, ssm_new, hg_new


def memory_kv(mem, p):
    bsz = mem.shape[0]
    mn = rmsnorm(mem, p['norm_mem'])
    mk = (mn @ p['w_ck']).reshape(bsz, MEM_LEN, XA_HEADS, XA_HEAD_DIM)
    mv = (mn @ p['w_cv']).reshape(bsz, MEM_LEN, XA_HEADS, XA_HEAD_DIM)
    return mk, mv


def cross_attn(hn, mk, mv, p):
    bsz, L, _ = hn.shape
    q = (hn @ p['w_cq']).reshape(bsz, L, XA_HEADS, XA_HEAD_DIM)
    s = jnp.einsum('blhd,bmhd->bhlm', q, mk).astype(F32) * (XA_HEAD_DIM ** -0.5)
    pr = jax.nn.softmax(s, axis=-1).astype(mv.dtype)
    o = jnp.einsum('bhlm,bmhd->blhd', pr, mv).reshape(bsz, L, XA_HEADS * XA_HEAD_DIM)
    return o @ p['w_co']


def swiglu(hn, p):
    return (jax.nn.silu(hn @ p['w_gate']) * (hn @ p['w_up'])) @ p['w_down']


def decoder_layer(x, conv_buf, ssm_h, hg_s, mk, mv, p, lb):
    mix, conv_new, ssm_new, hg_new = token_mixers(rmsnorm(x, p['norm_mix']), conv_buf, ssm_h, hg_s, p, lb)
    x = x + mix
    x = x + cross_attn(rmsnorm(x, p['norm_cross']), mk, mv, p)
    x = x + swiglu(rmsnorm(x, p['norm_ffn']), p)
    return x, conv_new, ssm_new, hg_new


def setup_inputs(seed: int = 0) -> dict:
    key = jax.random.key(seed)
    ks = jax.random.split(key, 32)
    nrm = lambda k, shape, scale: jax.random.normal(k, shape, jnp.float32) * scale
    gain = lambda k, shape: 1.0 + 0.02 * jax.random.normal(k, shape, jnp.float32)
    dt0 = jnp.exp(jax.random.uniform(ks[12], (DEPTH, SSD_HEADS), jnp.float32, math.log(1e-3), math.log(1e-1)))
    dt_bias = dt0 + jnp.log(-jnp.expm1(-dt0))
    a_log = jnp.log(jax.random.uniform(ks[13], (DEPTH, SSD_HEADS), jnp.float32, 1.0, 16.0))
    xa = XA_HEADS * XA_HEAD_DIM
    return {
        'x_prompt': nrm(ks[0], (BATCH, SEQ, D_MODEL), 1.0),
        'x_sample': nrm(ks[1], (DEC_BATCH, DEC_SEQ, D_MODEL), 1.0),
        'mem_prompt': nrm(ks[2], (BATCH, MEM_LEN, D_MODEL), 1.0),
        'state_ssm': nrm(ks[3], (DEPTH, DEC_BATCH, SSD_HEADS, SSD_HEAD_DIM, SSD_STATE), 0.5),
        'state_conv': nrm(ks[4], (DEPTH, DEC_BATCH, SSD_CONV - 1, SSD_CONV_DIM), 1.0),
        'state_hgrn': nrm(ks[5], (DEPTH, DEC_BATCH, HG_HEADS, HG_EXPAND, HG_HEAD_V), 0.5),
        'cache_mem_k': nrm(ks[6], (DEPTH, DEC_BATCH, MEM_LEN, XA_HEADS, XA_HEAD_DIM), 1.0),
        'cache_mem_v': nrm(ks[7], (DEPTH, DEC_BATCH, MEM_LEN, XA_HEADS, XA_HEAD_DIM), 1.0),
        'norm_mix': gain(ks[8], (DEPTH, D_MODEL)),
        'w_in': nrm(ks[9], (DEPTH, D_MODEL, IN_DIM), D_MODEL ** -0.5),
        'conv_w': nrm(ks[10], (DEPTH, SSD_CONV, SSD_CONV_DIM), SSD_CONV ** -0.5),
        'conv_b': nrm(ks[11], (DEPTH, SSD_CONV_DIM), 0.02),
        'dt_bias': dt_bias,
        'a_log': a_log,
        'd_skip': gain(ks[14], (DEPTH, SSD_HEADS)),
        'ssd_norm': gain(ks[15], (DEPTH, SSD_INNER)),
        'w_ssd_out': nrm(ks[16], (DEPTH, SSD_INNER, D_MODEL), SSD_INNER ** -0.5),
        'hgrn_lb': nrm(ks[17], (DEPTH + 1, HG_KDIM), 0.5),
        'hgrn_norm': gain(ks[18], (DEPTH, HG_VDIM)),
        'w_hgrn_out': nrm(ks[19], (DEPTH, HG_VDIM, D_MODEL), HG_VDIM ** -0.5),
        'w_out': nrm(ks[20], (DEPTH, D_MODEL, D_MODEL), D_MODEL ** -0.5),
        'norm_cross': gain(ks[21], (DEPTH, D_MODEL)),
        'norm_mem': gain(ks[22], (DEPTH, D_MODEL)),
        'w_cq': nrm(ks[23], (DEPTH, D_MODEL, xa), D_MODEL ** -0.5),
        'w_ck': nrm(ks[24], (DEPTH, D_MODEL, xa), D_MODEL ** -0.5),
        'w_cv': nrm(ks[25], (DEPTH, D_MODEL, xa), D_MODEL ** -0.5),
        'w_co': nrm(ks[26], (DEPTH, xa, D_MODEL), xa ** -0.5),
        'norm_ffn': gain(ks[27], (DEPTH, D_MODEL)),
        'w_gate': nrm(ks[28], (DEPTH, D_MODEL, FFN_DIM), D_MODEL ** -0.5),
        'w_up': nrm(ks[29], (DEPTH, D_MODEL, FFN_DIM), D_MODEL ** -0.5),
        'w_down': nrm(ks[30], (DEPTH, FFN_DIM, D_MODEL), FFN_DIM ** -0.5),
        'norm_final': gain(ks[31], (D_MODEL,)),
    }


def reference(x_prompt, x_sample, mem_prompt, state_ssm, state_conv, state_hgrn, cache_mem_k, cache_mem_v,
              norm_mix, w_in, conv_w, conv_b, dt_bias, a_log, d_skip, ssd_norm, w_ssd_out,
              hgrn_lb, hgrn_norm, w_hgrn_out, w_out, norm_cross, norm_mem, w_cq, w_ck, w_cv, w_co,
              norm_ffn, w_gate, w_up, w_down, norm_final):
    lbs = jnp.cumsum(jax.nn.softmax(hgrn_lb.astype(F32), axis=0), axis=0)
    bp = x_prompt.shape[0]
    dtype = x_prompt.dtype
    conv0 = jnp.zeros((bp, SSD_CONV - 1, SSD_CONV_DIM), dtype)
    ssm0 = jnp.zeros((bp, SSD_HEADS, SSD_HEAD_DIM, SSD_STATE), dtype)
    hg0 = jnp.zeros((bp, HG_HEADS, HG_EXPAND, HG_HEAD_V), dtype)
    hp, hs = x_prompt, x_sample
    ssm_p, conv_p, hg_p, mk_p, mv_p = [], [], [], [], []
    ssm_s, conv_s, hg_s = [], [], []
    for l in range(DEPTH):
        p = dict(norm_mix=norm_mix[l], w_in=w_in[l], conv_w=conv_w[l], conv_b=conv_b[l], dt_bias=dt_bias[l],
                 a_log=a_log[l], d_skip=d_skip[l], ssd_norm=ssd_norm[l], w_ssd_out=w_ssd_out[l],
                 hgrn_norm=hgrn_norm[l], w_hgrn_out=w_hgrn_out[l], w_out=w_out[l], norm_cross=norm_cross[l],
                 norm_mem=norm_mem[l], w_cq=w_cq[l], w_ck=w_ck[l], w_cv=w_cv[l], w_co=w_co[l],
                 norm_ffn=norm_ffn[l], w_gate=w_gate[l], w_up=w_up[l], w_down=w_down[l])
        lb = lbs[l]
        mk, mv = memory_kv(mem_prompt, p)
        hp, c_new, s_new, g_new = decoder_layer(hp, conv0, ssm0, hg0, mk, mv, p, lb)
        ssm_p.append(s_new); conv_p.append(c_new); hg_p.append(g_new); mk_p.append(mk); mv_p.append(mv)
        hs, c_new, s_new, g_new = decoder_layer(hs, state_conv[l], state_ssm[l], state_hgrn[l],
                                                cache_mem_k[l], cache_mem_v[l], p, lb)
        ssm_s.append(s_new); conv_s.append(c_new); hg_s.append(g_new)
    y_prompt = rmsnorm(hp, norm_final)
    y_sample = rmsnorm(hs, norm_final)
    return (y_prompt, y_sample,
            jnp.stack(ssm_p), jnp.stack(conv_p), jnp.stack(hg_p), jnp.stack(mk_p), jnp.stack(mv_p),
            jnp.stack(ssm_s), jnp.stack(conv_s), jnp.stack(hg_s))
```

```python
import numpy as np
from contextlib import ExitStack
import concourse.bass as bass
import concourse.mybir as mybir
from concourse.bass_utils import run_bass_kernel_spmd

F32 = mybir.dt.float32
BF16 = mybir.dt.bfloat16
ALU = mybir.AluOpType
AF = mybir.ActivationFunctionType
AX = mybir.AxisListType

ND = 8
EPS = 1e-6
D = 1024
IN_DIM = 11296
FFN = 2816


class Region:
    __slots__ = ("name", "w", "r")

    def __init__(self, name=""):
        self.name = name
        self.w = None
        self.r = {}


class T:
    __slots__ = ("t", "r")

    def __init__(self, t, r=None):
        self.t = t
        self.r = r if r is not None else Region()


class Rot:
    def __init__(self, items):
        self.items = items
        self.i = 0

    def next(self):
        x = self.items[self.i % len(self.items)]
        self.i += 1
        return x


def _reg(x):
    return x.r if isinstance(x, T) else x


class KB:
    def __init__(self, nc, stack):
        self.nc = nc
        self.stack = stack
        self.engs = {"pe": nc.tensor, "act": nc.scalar, "dve": nc.vector, "pool": nc.gpsimd, "sp": nc.sync}
        self.semh = {}
        self.cnt = {}
        self.known = {e: {} for e in self.engs}
        for e in ("pe", "act", "dve", "pool"):
            self.semh[e] = stack.enter_context(nc.semaphore("s_" + e))
            self.cnt[e] = 0
        self.dcount = {}
        for q in ("sp", "pool"):
            self.dcount[q] = 0
            for i in range(ND):
                self.semh[("d", q, i)] = stack.enter_context(nc.semaphore(f"d_{q}_{i}"))
        self.psall = stack.enter_context(nc.psum_tensor("psall", [128, 4096], F32))
        self.banks = [T(self.psall[:, i * 512:(i + 1) * 512], Region(f"bank{i}")) for i in range(8)]
        self.free = list(range(8))
        self.bi = 0
        self.uid = 0
        self.rec = None

    def sb(self, name, shape, dt=F32, stack=None):
        self.uid += 1
        st = stack if stack is not None else self.stack
        return T(st.enter_context(self.nc.sbuf_tensor(f"{name}_{self.uid}", list(shape), dt)), Region(name))

    def rot(self, name, shape, dt=F32, n=2, stack=None):
        return Rot([self.sb(f"{name}{i}", shape, dt, stack) for i in range(n)])

    def bank(self):
        b = self.free[self.bi % len(self.free)]
        self.bi += 1
        return self.banks[b]

    def reserve(self, n):
        out = [self.free.pop() for _ in range(n)]
        return [self.banks[b] for b in out], out

    def release(self, ids):
        self.free.extend(ids)
        self.free.sort()

    def _wait(self, e, deps):
        kn = self.known[e]
        best = {}
        for d in deps:
            if d is None:
                continue
            k, v = d
            if e == "pe" and k == "pe":
                continue
            if kn.get(k, 0) < v and best.get(k, 0) < v:
                best[k] = v
        for k, v in best.items():
            self.engs[e].wait_ge(self.semh[k], v)
            kn[k] = v

    def _deps(self, reads, writes):
        deps = []
        for r in reads:
            deps.append(_reg(r).w)
        for w in writes:
            w = _reg(w)
            deps.append(w.w)
            deps.extend(w.r.items())
        return deps

    def _mark(self, tok, reads, writes):
        k, v = tok
        for r in reads:
            r = _reg(r)
            if r.r.get(k, 0) < v:
                r.r[k] = v
        for w in writes:
            w = _reg(w)
            w.w = tok
            w.r = {}

    def set_pool(self, ids):
        self.free = list(ids)
        self.bi = 0

    def begin(self, pool):
        self.set_pool(pool)
        self.rec = []

    def end(self):
        r, self.rec = self.rec, None
        return r

    def play(self, lst):
        assert self.rec is None
        for it in lst:
            if it[0] == "op":
                self.op(*it[1:5])
            else:
                self.dma(it[1], it[2], it[3], it[4], it[5], **it[6])

    def op(self, e, emit, reads=(), writes=(), cost=None):
        if self.rec is not None:
            if cost is None:
                if e == "pe":
                    px = _PEProxy()
                    emit(px)
                    cost = px.cost
                else:
                    cost = 0.5
            self.rec.append(("op", e, emit, tuple(reads), tuple(writes), cost))
            return None
        self._wait(e, self._deps(reads, writes))
        inst = emit(self.engs[e])
        self.cnt[e] += 1
        inst.then_inc(self.semh[e], 1)
        self._mark((e, self.cnt[e]), reads, writes)
        return inst

    def dma(self, q, out, in_, reads=(), writes=(), **kw):
        if self.rec is not None:
            self.rec.append(("dma", q, out, in_, tuple(reads), tuple(writes), kw))
            return None
        i = self.dcount[q]
        self.dcount[q] += 1
        key = ("d", q, i % ND)
        prev = 16 * (i // ND)
        deps = self._deps(reads, writes)
        if prev:
            deps.append((key, prev))
        self._wait(q, deps)
        self.engs[q].dma_start(out=out, in_=in_, **kw).then_inc(self.semh[key], 16)
        self._mark((key, prev + 16), reads, writes)

    def _all_tokens(self):
        deps = []
        for q, n in self.dcount.items():
            for s in range(ND):
                cnt = (n - s + ND - 1) // ND if n > s else 0
                if cnt:
                    deps.append((("d", q, s), 16 * cnt))
        for e, c in self.cnt.items():
            if c:
                deps.append((e, c))
        return deps

    def barrier(self):
        deps = self._all_tokens()
        for e in self.engs:
            self._wait(e, deps)

    def finish(self):
        self._wait("sp", self._all_tokens())

    def act(self, out, in_, func, reads, writes, **kw):
        return self.op("act", lambda e: e.activation(out=out, in_=in_, func=func, **kw), reads, writes, cost=_est("act", out))

    def tt(self, eng, out, in0, in1, op, reads, writes):
        return self.op(eng, lambda e: e.tensor_tensor(out=out, in0=in0, in1=in1, op=op), reads, writes, cost=_est(eng, out))

    def ts(self, eng, out, in0, s1, s2, op0, op1, reads, writes):
        if s2 is None:
            return self.op(eng, lambda e: e.tensor_scalar(out=out, in0=in0, scalar1=s1, scalar2=None, op0=op0), reads, writes, cost=_est(eng, out))
        return self.op(eng, lambda e: e.tensor_scalar(out=out, in0=in0, scalar1=s1, scalar2=s2, op0=op0, op1=op1), reads, writes, cost=_est(eng, out))

    def stt(self, out, in0, scalar, in1, op0, op1, reads, writes):
        return self.op("dve", lambda e: e.scalar_tensor_tensor(out=out, in0=in0, scalar=scalar, in1=in1, op0=op0, op1=op1), reads, writes, cost=_est("dve", out))

    def copy(self, eng, out, in_, reads, writes):
        if eng == "act":
            return self.op("act", lambda e: e.copy(out=out, in_=in_), reads, writes, cost=_est("act", out))
        return self.op(eng, lambda e: e.tensor_copy(out=out, in_=in_), reads, writes, cost=_est(eng, out))

    def memset(self, eng, ap, val, writes):
        return self.op(eng, lambda e: e.memset(ap, val), (), writes)


def _free(ap):
    n = 1
    for d in ap.shape[1:]:
        n *= int(d)
    return n


def _est(eng, out):
    n = _free(out)
    if eng == "act":
        return 0.2 + n / 1200.0
    if eng == "dve":
        return 0.2 + n / 960.0
    return 0.25 + n / 450.0


class _PEProxy:
    def __init__(self):
        self.cost = 0.0

    def matmul(self, out, lhsT=None, rhs=None, **kw):
        p = 4 if lhsT.dtype == F32 else 1
        self.cost += 0.03 + max(_free(out), 32) * p / 2400.0
        return self

    def transpose(self, out, in_, ident, **kw):
        p = 4 if in_.dtype == F32 else 1
        self.cost += 0.03 + max(_free(out), 32) * p / 2400.0
        return self


def schedule(*streams):
    ops = [it for st_ in streams for it in st_]
    n = len(ops)
    engs, R, W, cost = [], [], [], []
    for it in ops:
        if it[0] == "op":
            engs.append(it[1]); R.append([_reg(x) for x in it[3]]); W.append([_reg(x) for x in it[4]]); cost.append(it[5])
        else:
            engs.append(it[1]); R.append([_reg(x) for x in it[4]]); W.append([_reg(x) for x in it[5]]); cost.append(2.0)
    preds = [set() for _ in range(n)]
    lastw, readers = {}, {}
    laste = {}
    k = 0
    for si, st_ in enumerate(streams):
        for _ in st_:
            i = k
            k += 1
            for r in R[i]:
                if id(r) in lastw:
                    preds[i].add(lastw[id(r)])
            for w in W[i]:
                if id(w) in lastw:
                    preds[i].add(lastw[id(w)])
                preds[i].update(readers.get(id(w), ()))
            for r in R[i]:
                readers.setdefault(id(r), []).append(i)
            for w in W[i]:
                lastw[id(w)] = i
                readers[id(w)] = []
            key = (si, engs[i])
            if key in laste:
                preds[i].add(laste[key])
            laste[key] = i
            preds[i].discard(i)
    succs = [[] for _ in range(n)]
    indeg = [len(p) for p in preds]
    for i, p in enumerate(preds):
        for j in p:
            succs[j].append(i)
    ready = [i for i in range(n) if indeg[i] == 0]
    efree = {}
    fin = [0.0] * n
    out = []
    LAT = 0.3
    while ready:
        best, bi = None, None
        for i in ready:
            e = engs[i]
            t = efree.get(e, 0.0)
            for j in preds[i]:
                tj = fin[j] + (0.0 if engs[j] == e else LAT)
                if tj > t:
                    t = tj
            if best is None or (t, i) < best:
                best, bi = (t, i), i
        ready.remove(bi)
        e = engs[bi]
        t0 = best[0]
        if ops[bi][0] == "dma":
            efree[e] = t0 + 0.1
            fin[bi] = t0 + 2.0
        else:
            fin[bi] = t0 + cost[bi]
            efree[e] = fin[bi]
        out.append(ops[bi])
        for j in succs[bi]:
            indeg[j] -= 1
            if indeg[j] == 0:
                ready.append(j)
    assert len(out) == n
    return out


def interleave(*lists):
    lists = [l for l in lists if l]
    idx = [0] * len(lists)
    out = []
    total = sum(len(l) for l in lists)
    while len(out) < total:
        best, bk = None, None
        for k, l in enumerate(lists):
            if idx[k] < len(l):
                key = (idx[k] + 0.5) / len(l)
                if best is None or key < best:
                    best, bk = key, k
        out.append(lists[bk][idx[bk]])
        idx[bk] += 1
    return out


def bc(ap, shape, axis):
    return ap.unsqueeze(axis).to_broadcast(list(shape))


C_ID, C_TPI, C_TPA, C_ONE, C_HPI, C_HPA, C_TSI, C_TSA, C_SSM, C_SEGP, C_SEGS = 0, 128, 256, 384, 512, 640, 768, 832, 896, 960, 964


def make_consts():
    c = np.zeros((128, 1024), np.float32)
    i = np.arange(128)
    c[:, C_ID:C_ID + 128] = np.eye(128)
    c[:, C_TPI:C_TPI + 128] = (i[:, None] <= i[None, :])
    c[:, C_TPA:C_TPA + 128] = (i[:, None] > i[None, :])
    c[:, C_ONE:C_ONE + 128] = 1.0
    s32 = i // 32
    same = s32[:, None] == s32[None, :]
    c[:, C_HPI:C_HPI + 128] = same & (i[:, None] <= i[None, :])
    c[:, C_HPA:C_HPA + 128] = same & (i[:, None] > i[None, :])
    j = np.arange(64)
    s4 = j // 4
    same4 = s4[:, None] == s4[None, :]
    c[:64, C_TSI:C_TSI + 64] = same4 & (j[:, None] <= j[None, :])
    c[:64, C_TSA:C_TSA + 64] = same4 & (j[:, None] > j[None, :])
    c[:64, C_SSM:C_SSM + 64] = same4
    c[:, C_SEGP:C_SEGP + 4] = (s32[:, None] == np.arange(4)[None, :])
    c[:64, C_SEGS:C_SEGS + 16] = (s4[:, None] == np.arange(16)[None, :])
    return c


V_NMIX, V_NCROSS, V_NMEM, V_NFFN, V_SNORM, V_HNORM, V_CONVB, V_CONVW = 0, 8, 16, 24, 32, 48, 56, 80
RV_DTB, RV_ALOG, RV_DSKIP, RV_LB0, RV_LB1, RV_NFIN = 0, 32, 64, 96, 1120, 2144
RV_N = 3168


class Ctx:
    pass


def load_w(C, name, src, ncols, nk=8, stack=None):
    kb = C.kb
    w = kb.sb(name, [128, nk, ncols], BF16, stack)
    for k in range(nk):
        kb.dma("pool", w.t[:, k, :], src[k * 128:(k + 1) * 128, :], writes=[w])
    return w


def rms_rstd(C, xt, q, ss, junk=None):
    kb = C.kb
    junk = junk if junk is not None else C.junk.next()
    kb.act(junk.t[:q], xt.t[:q], AF.Square, [xt], [junk, ss], accum_out=ss.t[:q, 1:2])
    kb.ts("dve", ss.t[:q, 2:3], ss.t[:q, 1:2], 1.0 / D, EPS, ALU.mult, ALU.add, [ss], [ss])
    kb.act(ss.t[:q, 1:2], ss.t[:q, 2:3], AF.Sqrt, [ss], [ss])
    kb.op("dve", lambda e: e.reciprocal(out=ss.t[:q, 0:1], in_=ss.t[:q, 1:2]), [ss], [ss])


def rms_to_T(C, xt, q, wcol, dst_ap, dst, scratch=None):
    kb = C.kb
    ss = scratch["ss"] if scratch else C.ss.next()
    rms_rstd(C, xt, q, ss, scratch["junk"] if scratch else None)
    xn = scratch["xn"] if scratch else C.xn.next()
    kb.act(xn.t[:q], xt.t[:q], AF.Copy, [xt, ss], [xn], scale=ss.t[:q, 0:1])
    to_T(C, xn, q, 8, wcol, dst_ap, dst)


def to_T(C, src, q, nk, wcol, dst_ap, dst, eng="dve"):
    kb = C.kb
    for k0 in range(0, nk, 8):
        n = min(8, nk - k0)
        b = kb.bank()
        pb = b.t.bitcast(BF16)

        def emit(e, k0=k0, n=n, pb=pb):
            for k in range(n):
                last = e.transpose(pb[:, k * 128:k * 128 + q], src.t[:q, (k0 + k) * 128:(k0 + k + 1) * 128], C.identb.t[:q, :q])
            return last
        kb.op("pe", emit, [src, C.identb], [b])
        pv = pb.rearrange("p (k t) -> p k t", t=128)[:, 0:n, 0:q]
        if wcol is None:
            kb.copy("act", dst_ap[:, k0:k0 + n, :], pv, [b], [dst])
        else:
            kb.tt(eng, dst_ap[:, k0:k0 + n, :], pv, bc(wcol[:, k0:k0 + n], [128, n, q], 2), ALU.mult, [b, C.colv], [dst])


def pass0(C):
    kb = C.kb
    with ExitStack() as st:
        xin = kb.rot("xin", [128, D], F32, 2, st)
        nts = kb.rot("nts", [128, 8, 128], BF16, 2, st)
        z = kb.sb("zero3", [128, 8, 3], BF16, st)
        kb.memset("pool", z.t[:], 0.0, [z])
        kb.dma("sp", C.nTd[:, :, 0:3], z.t[:], reads=[z], writes=[C.nTd_r[0]])
        for ti in range(C.NT):
            q = 128 if ti < C.NCH else 64
            src = C.xp[ti * 128:(ti + 1) * 128, :] if ti < C.NCH else C.xs[:, :]
            xt = xin.next()
            kb.dma("sp", xt.t[:q], src, writes=[xt])
            nt = nts.next()
            rms_to_T(C, xt, q, C.colv.t[:, V_NMIX:V_NMIX + 8], nt.t[:, :, :q], nt)
            kb.dma("sp", C.nTd[:, :, 3 + ti * 128:3 + ti * 128 + q], nt.t[:, :, :q], reads=[nt], writes=[C.nTd_r[ti + 1]])
        kb.barrier()


def ssd_small(C, q, sm, nTc, tcol, Wdt, tri_inc, tri_after, same):
    kb = C.kb
    b = kb.bank()

    def emit(e):
        for k in range(8):
            last = e.matmul(b.t[:q, 0:32], lhsT=nTc.t[:, k, tcol:tcol + q], rhs=Wdt.t[:, k, :], start=(k == 0), stop=(k == 7))
        return last
    kb.op("pe", emit, [nTc, Wdt], [b])
    kb.tt("dve", sm.t[:q, 0:32], b.t[:q, 0:32], C.rowb.t[:q, RV_DTB:RV_DTB + 32], ALU.add, [b, C.rowb], [sm])
    kb.act(sm.t[:q, 32:64], sm.t[:q, 0:32], AF.Exp, [sm], [sm])
    kb.act(sm.t[:q, 64:96], sm.t[:q, 32:64], AF.Ln, [sm], [sm], bias=1.0)
    kb.tt("dve", sm.t[:q, 96:128], sm.t[:q, 64:96], C.a_bc.t[:q, :], ALU.mult, [sm, C.a_bc], [sm])
    b2 = kb.bank()

    def emit2(e):
        e.matmul(b2.t[:q, 0:32], lhsT=tri_inc, rhs=sm.t[:q, 96:128], start=True, stop=True)
        e.matmul(b2.t[:q, 32:64], lhsT=tri_after, rhs=sm.t[:q, 96:128], start=True, stop=True)
        return e.matmul(b2.t[:q, 64:96], lhsT=same, rhs=sm.t[:q, 96:128], start=True, stop=True)
    kb.op("pe", emit2, [sm, C.cst], [b2])
    kb.act(sm.t[:q, 128:224], b2.t[:q, 0:96], AF.Exp, [b2], [sm])
    kb.tt("dve", sm.t[:q, 224:256], sm.t[:q, 160:192], sm.t[:q, 64:96], ALU.mult, [sm], [sm])


def ssd_conv(C, q, pre_view, acc, BCT, eng_split=3):
    kb = C.kb
    cw = C.colvh.t
    WB, WW = 0, V_CONVW - V_CONVB
    for cc in range(24):
        o = acc["view"](cc)
        kb.act(o, pre_view(cc, 0), AF.Identity, [acc["pre"], C.colvh], [acc["T"]], scale=cw[:, WW + cc * 4:WW + cc * 4 + 1], bias=cw[:, WB + cc:WB + cc + 1])
    for cc in range(24):
        o = acc["view"](cc)
        for j in range(1, 4):
            kb.stt(o, pre_view(cc, j), cw[:, WW + cc * 4 + j:WW + cc * 4 + j + 1], o, ALU.mult, ALU.add, [acc["pre"], C.colvh, acc["T"]], [acc["T"]])
    a = acc["T"]
    th = acc["tanh"]
    kb.act(th, a.t[:, :, :q], AF.Tanh, [a], [acc["pre"]])
    kb.stt(a.t[:, :, :q], th, 1.0, a.t[:, :, :q], ALU.add, ALU.mult, [acc["pre"], a], [a])
    kb.copy(acc.get("cast_eng", "pool"), BCT.t[:, :, :q], a.t[:, 16:24, :q], [a], [BCT])


def ssd_group_front(C, q, g, a, sm, W):
    kb = C.kb
    bx = kb.bank()

    def emit(e):
        for j in range(4):
            last = e.transpose(bx.t[:q, j * 128:(j + 1) * 128], a.t[:, g * 4 + j, :q], C.cst.t[:, C_ID:C_ID + 128])
        return last
    kb.op("pe", emit, [a, C.cst], [bx])
    bx3 = bx.t[:q, :].rearrange("p (h d) -> p h d", d=64)
    v3 = lambda ap: ap.rearrange("p (h d) -> p h d", d=64)
    kb.tt("dve", v3(W["xdt"].t[:q, g * 512:(g + 1) * 512]), bx3, bc(sm.t[:q, 64 + g * 8:72 + g * 8], [q, 8, 64], 2), ALU.mult, [bx, sm], [W["xdt"]])
    kb.tt("dve", v3(W["xw"].t[:q, g * 512:(g + 1) * 512]), bx3, bc(sm.t[:q, 224 + g * 8:232 + g * 8], [q, 8, 64], 2), ALU.mult, [bx, sm], [W["xw"]])
    kb.tt("dve", v3(W["xsd"].t[:q, g * 512:(g + 1) * 512]), bx3, bc(C.rowb.t[:q, RV_DSKIP + g * 8:RV_DSKIP + g * 8 + 8], [q, 8, 64], 2), ALU.mult, [bx, C.rowb], [W["xsd"]])


def ssd_group_y(C, q, g, sm, W, BCT, CBm, tri_inc, tri_after, byi, nTc, tcol, Wz, tok0, ci):
    kb = C.kb
    v3 = lambda ap: ap.rearrange("p (h d) -> p h d", d=64)
    tmp = W["tmp"].next()
    kb.tt("dve", v3(tmp.t[:q, :]), v3(byi.t[:q, :]), bc(sm.t[:q, 128 + g * 8:136 + g * 8], [q, 8, 64], 2), ALU.mult, [byi, sm], [tmp])
    by = kb.bank()
    kb.op("pe", lambda e: e.matmul(by.t[:q, :], lhsT=C.identb.t[:q, :q], rhs=W["xsd"].t[:q, g * 512:(g + 1) * 512], start=True, stop=False), [C.identb, W["xsd"]], [by])
    for hh in range(2):
        Rt = W["R"].next()
        kb.tt("dve", Rt.t[:q, :, :q], bc(tri_inc, [q, 4, q], 1), bc(sm.t[:q, 96 + g * 8 + hh * 4:100 + g * 8 + hh * 4], [q, 4, q], 2), ALU.mult, [C.cst, sm], [Rt])
        bL = kb.bank()
        kb.op("pe", lambda e, Rt=Rt, bL=bL: [e.matmul(bL.t[:q, h * q:(h + 1) * q], lhsT=tri_after, rhs=Rt.t[:q, h, :q], start=True, stop=True) for h in range(4)][-1], [Rt, C.cst], [bL])
        Lt = W["L"].next()
        kb.act(Lt.t[:q, 0:4 * q], bL.t[:q, 0:4 * q], AF.Exp, [bL], [Lt])
        MT = W["MT"].next()
        kb.tt("dve", MT.t[:q, :, :q], Lt.t[:q, 0:4 * q].rearrange("p (h t) -> p h t", t=q), bc(CBm.t[:q, g, :q], [q, 4, q], 1), ALU.mult, [Lt, CBm], [MT])

        def emit(e, MT=MT, hh=hh):
            for h in range(4):
                c0 = (hh * 4 + h) * 64
                last = e.matmul(by.t[:q, c0:c0 + 64], lhsT=MT.t[:q, h, :q], rhs=W["xdt"].t[:q, g * 512 + c0:g * 512 + c0 + 64], start=False, stop=False)
            return last
        kb.op("pe", emit, [MT, W["xdt"]], [by])
    kb.op("pe", lambda e: e.matmul(by.t[:q, :], lhsT=C.identb.t[:q, :q], rhs=tmp.t[:q, :], start=False, stop=True), [C.identb, tmp], [by])
    bz = kb.bank()
    kb.op("pe", lambda e: [e.matmul(bz.t[:q, :], lhsT=nTc.t[:, k, tcol:tcol + q], rhs=Wz.t[:, k, g * 512:(g + 1) * 512], start=(k == 0), stop=(k == 7)) for k in range(8)][-1], [nTc, Wz], [bz])
    sz = W["sz"].next()
    kb.act(sz.t[:q, :], bz.t[:q, :], AF.Tanh, [bz], [sz], scale=0.5)
    kb.stt(sz.t[:q, :], sz.t[:q, :], 1.0, bz.t[:q, :], ALU.add, ALU.mult, [sz, bz], [sz])
    y = W["y"].next()
    kb.tt("dve", y.t[:q, :], by.t[:q, :], sz.t[:q, :], ALU.mult, [by, sz], [y])
    ms = C.ss.next()
    kb.act(sz.t[:q, :], y.t[:q, :], AF.Square, [y], [sz, ms], accum_out=ms.t[:q, 1:2])
    kb.ts("dve", ms.t[:q, 2:3], ms.t[:q, 1:2], 1.0 / 512, 4.0 * EPS, ALU.mult, ALU.add, [ms], [ms])
    kb.act(ms.t[:q, 1:2], ms.t[:q, 2:3], AF.Sqrt, [ms], [ms])
    kb.op("dve", lambda e: e.reciprocal(out=ms.t[:q, 0:1], in_=ms.t[:q, 1:2]), [ms], [ms])
    yn = W["yn"].next()
    kb.act(yn.t[:q, :], y.t[:q, :], AF.Copy, [y, ms], [yn], scale=ms.t[:q, 0:1])
    ynT = W["ynT"].next()
    to_T(C, yn, q, 4, C.colv.t[:, V_SNORM + g * 4:V_SNORM + g * 4 + 4], ynT.t[:, :, :q], ynT)
    kb.dma("sp", C.YNT[:, g * 4:(g + 1) * 4, tok0:tok0 + q], ynT.t[:, :, :q], reads=[ynT], writes=[C.YNT_r[ci][g]])


def ssd_pass(C):
    kb = C.kb
    NCH = C.NCH
    cst = C.cst.t
    with ExitStack() as stw:
        Wz = load_w(C, "Wz", C.w_in[:, 0:2048], 2048, stack=stw)
        Wx = load_w(C, "Wx", C.w_in[:, 2048:5120], 3072, stack=stw)
        Wdt = load_w(C, "Wdt", C.w_in[:, 5120:5152], 32, stack=stw)
        with ExitStack() as st:
            q = 128
            nTr = kb.rot("nTc", [128, 8, 131], BF16, 2, st)
            pre = kb.sb("pre", [128, 24, 131], F32, st)
            accr = kb.rot("acc", [128, 24, 128], F32, 1, st)
            BCTr = kb.rot("BCT", [128, 8, 128], BF16, 2, st)
            CBmr = kb.rot("CBm", [128, 4, 128], F32, 2, st)
            Btokr = kb.rot("Btok", [128, 512], BF16, 2, st)
            smr = kb.rot("sm", [128, 256], F32, 2, st)
            MTs = [[kb.sb(f"MT{i}_{g}", [128, 8, 128], BF16, st) for g in range(4)] for i in range(2)]
            szs = [[kb.sb(f"sz{i}_{g}", [128, 512], BF16, st) for g in range(4)] for i in range(2)]
            Rr = kb.rot("R", [128, 4, 128], F32, 2, st)
            Lr = kb.rot("L", [128, 512], F32, 2, st)
            xdt = [kb.sb(f"xdt{g}", [128, 512], BF16, st) for g in range(4)]
            xw = [kb.sb(f"xw{g}", [128, 512], BF16, st) for g in range(4)]
            xsd = [kb.sb(f"xsd{g}", [128, 512], BF16, st) for g in range(4)]
            hT = [kb.sb(f"hT{g}", [128, 512], F32, st) for g in range(4)]
            hTb = [kb.sb(f"hTb{g}", [128, 512], BF16, st) for g in range(4)]
            SW = [dict(y=kb.rot(f"y{i}", [128, 512], F32, 2, st), yn=kb.rot(f"yn{i}", [128, 512], BF16, 2, st), ynT=kb.rot(f"ynT{i}", [128, 4, 128], BF16, 2, st),
                       tmp=kb.rot(f"ytmp{i}", [128, 512], BF16, 2, st)) for i in range(2)]
            for g in range(4):
                kb.memset("pool", hT[g].t[:], 0.0, [hT[g]])
                kb.memset("pool", hTb[g].t[:], 0.0, [hTb[g]])
            tri_inc, tri_after, ones = cst[:, C_TPI:C_TPI + 128], cst[:, C_TPA:C_TPA + 128], cst[:, C_ONE:C_ONE + 128]
            v3 = lambda ap: ap.rearrange("p (h d) -> p h d", d=64)

            def Fa(c, nTc, sm, a, BCT, CBm, Btok, MTa, sza):
                kb.dma("sp", nTc.t[:, :, :], C.nTd[:, :, c * 128:c * 128 + 131], reads=[C.nTd_r[c], C.nTd_r[c + 1]], writes=[nTc])
                ssd_small(C, q, sm, nTc, 3, Wdt, tri_inc, tri_after, ones)
                for g in range(4):
                    for hh in range(2):
                        h0 = g * 8 + hh * 4
                        Rt = Rr.next()
                        kb.tt("dve", Rt.t[:, :, :], bc(tri_inc, [128, 4, 128], 1), bc(sm.t[:, 96 + h0:100 + h0], [128, 4, 128], 2), ALU.mult, [C.cst, sm], [Rt])
                        bL = kb.bank()
                        kb.op("pe", lambda e, Rt=Rt, bL=bL: [e.matmul(bL.t[:, h * 128:(h + 1) * 128], lhsT=tri_after, rhs=Rt.t[:, h, :], start=True, stop=True) for h in range(4)][-1], [Rt, C.cst], [bL])
                        kb.act(MTa[g].t[:, hh * 4:hh * 4 + 4, :], bL.t[:, :].rearrange("p (h t) -> p h t", t=128), AF.Exp, [bL], [MTa[g]])

            def Fc(c, nTc, sm, a, BCT, CBm, Btok, MTa, sza):
                for g in range(4):
                    bz = kb.bank()
                    kb.op("pe", lambda e, bz=bz, g=g: [e.matmul(bz.t[:, :], lhsT=nTc.t[:, k, 3:131], rhs=Wz.t[:, k, g * 512:(g + 1) * 512], start=(k == 0), stop=(k == 7)) for k in range(8)][-1], [nTc, Wz], [bz])
                    zt_ = Lr.next()
                    kb.act(zt_.t[:, :], bz.t[:, :], AF.Tanh, [bz], [zt_], scale=0.5)
                    kb.stt(sza[g].t[:, :], zt_.t[:, :], 1.0, bz.t[:, :], ALU.add, ALU.mult, [zt_, bz], [sza[g]])

            def Fb(c, nTc, sm, a, BCT, CBm, Btok, MTa, sza):
                for grp in range(8):
                    b = kb.bank()

                    def emit(e, grp=grp, b=b):
                        for j in range(3):
                            cc = grp * 3 + j
                            for k in range(8):
                                last = e.matmul(b.t[:, j * 131:(j + 1) * 131], lhsT=Wx.t[:, k, cc * 128:(cc + 1) * 128], rhs=nTc.t[:, k, 0:131], start=(k == 0), stop=(k == 7))
                        return last
                    kb.op("pe", emit, [Wx, nTc], [b])
                    kb.act(pre.t[:, grp * 3:grp * 3 + 3, :], b.t[:, 0:393].rearrange("p (j t) -> p j t", t=131), AF.Copy, [b], [pre])
                if c == NCH - 1:
                    for s_ in range(6):
                        b = kb.bank()
                        kb.op("pe", lambda e, b=b, s_=s_: [e.matmul(b.t[:3, :], lhsT=nTc.t[:, k, 128:131], rhs=Wx.t[:, k, s_ * 512:(s_ + 1) * 512], start=(k == 0), stop=(k == 7)) for k in range(8)][-1], [Wx, nTc], [b])
                        cv_ = Lr.next()
                        kb.copy("act", cv_.t[:3, :], b.t[:3, :], [b], [cv_])
                        kb.dma("sp", C.o_conv_p[:, s_ * 512:(s_ + 1) * 512], cv_.t[:3, :], reads=[cv_])
                accd = {"T": a, "pre": pre, "view": lambda cc: a.t[:, cc, :], "tanh": pre.t[:, :, 0:128], "cast_eng": "act"}
                ssd_conv(C, q, lambda cc, j: pre.t[:, cc, j:j + 128], accd, BCT)
                b = kb.bank()
                kb.op("pe", lambda e, b=b: [e.matmul(b.t[:, g * 128:(g + 1) * 128], lhsT=BCT.t[:, g, :], rhs=BCT.t[:, 4 + g, :], start=True, stop=True) for g in range(4)][-1], [BCT], [b])
                kb.tt("dve", CBm.t[:, :, :], b.t[:, :].rearrange("p (g t) -> p g t", t=128), bc(tri_inc, [128, 4, 128], 1), ALU.mult, [b, C.cst], [CBm])
                b2 = kb.bank()
                kb.op("pe", lambda e, b2=b2: [e.transpose(b2.t[:, g * 128:(g + 1) * 128], a.t[:, 16 + g, :], cst[:, C_ID:C_ID + 128]) for g in range(4)][-1], [a, C.cst], [b2])
                kb.copy("act", Btok.t[:, :], b2.t[:, :], [b2], [Btok])
                for g in range(4):
                    kb.tt("dve", MTa[g].t[:, :, :], MTa[g].t[:, :, :], bc(CBm.t[:, g, :], [128, 8, 128], 1), ALU.mult, [MTa[g], CBm], [MTa[g]])

            def Gfront(c, g, Wk, nTc, sm, a, BCT, CBm, Btok, MTa, sza):
                bx = kb.bank()
                kb.op("pe", lambda e: [e.transpose(bx.t[:, j * 128:(j + 1) * 128], a.t[:, g * 4 + j, :], cst[:, C_ID:C_ID + 128]) for j in range(4)][-1], [a, C.cst], [bx])
                bx3 = v3(bx.t[:, :])
                kb.tt("dve", v3(xdt[g].t[:, :]), bx3, bc(sm.t[:, 64 + g * 8:72 + g * 8], [128, 8, 64], 2), ALU.mult, [bx, sm], [xdt[g]])
                kb.tt("dve", v3(xw[g].t[:, :]), bx3, bc(sm.t[:, 224 + g * 8:232 + g * 8], [128, 8, 64], 2), ALU.mult, [bx, sm], [xw[g]])
                kb.tt("dve", v3(xsd[g].t[:, :]), bx3, bc(C.rowb.t[:, RV_DSKIP + g * 8:RV_DSKIP + g * 8 + 8], [128, 8, 64], 2), ALU.mult, [bx, C.rowb], [xsd[g]])

            def Ggroup(c, g, Wk, nTc, sm, a, BCT, CBm, Btok, MTa, sza):
                byi = kb.bank()
                kb.op("pe", lambda e: e.matmul(byi.t[:, :], lhsT=BCT.t[:, 4 + g, :], rhs=hTb[g].t[:, :], start=True, stop=True), [BCT, hTb[g]], [byi])
                bd = kb.bank()
                kb.op("pe", lambda e: e.matmul(bd.t[:, :], lhsT=Btok.t[:, g * 128:(g + 1) * 128], rhs=xw[g].t[:, :], start=True, stop=True), [Btok, xw[g]], [bd])
                tmp = Wk["tmp"].next()
                kb.tt("dve", v3(tmp.t[:, :]), v3(byi.t[:, :]), bc(sm.t[:, 128 + g * 8:136 + g * 8], [128, 8, 64], 2), ALU.mult, [byi, sm], [tmp])
                hv = v3(hT[g].t[:, :])
                kb.tt("dve", hv, hv, bc(sm.t[:, 192 + g * 8:200 + g * 8], [128, 8, 64], 2), ALU.mult, [hT[g], sm], [hT[g]])
                kb.tt("dve", hT[g].t[:, :], hT[g].t[:, :], bd.t[:, :], ALU.add, [hT[g], bd], [hT[g]])
                by = kb.bank()

                def emit(e):
                    e.matmul(by.t[:, :], lhsT=C.identb.t[:, :], rhs=xsd[g].t[:, :], start=True, stop=False)
                    for h in range(8):
                        e.matmul(by.t[:, h * 64:(h + 1) * 64], lhsT=MTa[g].t[:, h, :], rhs=xdt[g].t[:, h * 64:(h + 1) * 64], start=False, stop=False)
                    return e.matmul(by.t[:, :], lhsT=C.identb.t[:, :], rhs=tmp.t[:, :], start=False, stop=True)
                kb.op("pe", emit, [C.identb, xsd[g], MTa[g], xdt[g], tmp], [by])
                y = Wk["y"].next()
                kb.tt("dve", y.t[:, :], by.t[:, :], sza[g].t[:, :], ALU.mult, [by, sza[g]], [y])
                ms = C.ss.next()
                yn = Wk["yn"].next()
                kb.act(yn.t[:, :], y.t[:, :], AF.Square, [y], [yn, ms], accum_out=ms.t[:, 1:2])
                kb.ts("dve", ms.t[:, 2:3], ms.t[:, 1:2], 1.0 / 512, 4.0 * EPS, ALU.mult, ALU.add, [ms], [ms])
                kb.act(ms.t[:, 1:2], ms.t[:, 2:3], AF.Sqrt, [ms], [ms])
                kb.op("dve", lambda e: e.reciprocal(out=ms.t[:, 0:1], in_=ms.t[:, 1:2]), [ms], [ms])
                kb.act(yn.t[:, :], y.t[:, :], AF.Copy, [y, ms], [yn], scale=ms.t[:, 0:1])
                ynT = Wk["ynT"].next()
                to_T(C, yn, q, 4, C.colv.t[:, V_SNORM + g * 4:V_SNORM + g * 4 + 4], ynT.t[:, :, :], ynT)
                kb.dma("sp", C.YNT[:, g * 4:(g + 1) * 4, c * 128:c * 128 + 128], ynT.t[:, :, :], reads=[ynT], writes=[C.YNT_r[c][g]])
                kb.copy("act", hTb[g].t[:, :], hT[g].t[:, :], [hT[g]], [hTb[g]])

            sets = [(nTr.next(), smr.next(), accr.next(), BCTr.next(), CBmr.next(), Btokr.next(), MTs[i], szs[i]) for i in range(2)]
            GA, GB = [4, 5], [6, 7]

            def Flists(c, S_):
                kb.begin([0]); Fa(c, *S_); la = kb.end()
                kb.begin([3]); Fc(c, *S_); lc = kb.end()
                kb.begin([1, 2]); Fb(c, *S_); lb2 = kb.end()
                return [la, lc, lb2]
            kb.play(schedule(*Flists(0, sets[0])))
            for c in range(NCH):
                S_ = sets[c % 2]
                kb.begin(GA); Gfront(c, 0, SW[0], *S_); Gfront(c, 1, SW[0], *S_); Ggroup(c, 0, SW[0], *S_); Ggroup(c, 1, SW[0], *S_); ga = kb.end()
                kb.begin(GB); Gfront(c, 2, SW[1], *S_); Gfront(c, 3, SW[1], *S_); Ggroup(c, 2, SW[1], *S_); Ggroup(c, 3, SW[1], *S_); gb = kb.end()
                fl = Flists(c + 1, sets[(c + 1) % 2]) if c + 1 < NCH else []
                kb.play(schedule(ga, gb, *fl))
            kb.set_pool(range(8))
            so = T(pre.t[:].rearrange("p c t -> p (c t)")[:, 0:2048].rearrange("p (j n) -> p j n", n=128), pre.r)
            for j4 in range(4):
                b = kb.bank()
                kb.op("pe", lambda e, b=b, j4=j4: [e.transpose(b.t[:, jj * 128:(jj + 1) * 128], hT[j4].t[:, jj * 128:(jj + 1) * 128], cst[:, C_ID:C_ID + 128]) for jj in range(4)][-1], [hT[j4], C.cst], [b])
                kb.copy("act", so.t[:, j4 * 4:(j4 + 1) * 4, :], b.t[:, :].rearrange("p (j n) -> p j n", n=128), [b], [so])
            kb.dma("sp", C.o_ssm_p.rearrange("(j q) n -> q j n", q=128), so.t[:, :, :], reads=[so])
            kb.barrier()
        with ExitStack() as st:
            q = 64
            tri_inc, tri_after, same = cst[:64, C_TSI:C_TSI + 64], cst[:64, C_TSA:C_TSA + 64], cst[:64, C_SSM:C_SSM + 64]
            nTc = kb.sb("nTs", [128, 8, 64], BF16, st)
            kb.dma("sp", nTc.t[:, :, :], C.nTd[:, :, 3 + C.TP:3 + C.TP + 64], reads=[C.nTd_r[C.NT]], writes=[nTc])
            pre = kb.sb("pres", [128, 24, 16, 7], F32, st)
            a = kb.sb("accs", [128, 24, 64], F32, st)
            BCT = kb.sb("BCTs", [128, 8, 64], BF16, st)
            CBm = kb.sb("CBms", [64, 4, 64], F32, st)
            Btok = kb.sb("Btoks", [64, 512], BF16, st)
            sm = kb.sb("sms", [64, 256], F32, st)
            W = {
                "xdt": kb.sb("xdts", [64, 2048], BF16, st), "xw": kb.sb("xws", [64, 2048], BF16, st), "xsd": kb.sb("xsds", [64, 2048], BF16, st),
                "R": kb.rot("Rs", [64, 4, 64], F32, 2, st), "L": kb.rot("Ls", [64, 256], F32, 2, st), "MT": kb.rot("MTs", [64, 4, 64], BF16, 2, st),
                "y": kb.rot("ys", [64, 512], F32, 2, st), "sz": kb.rot("szs", [64, 512], F32, 2, st), "yn": kb.rot("yns", [64, 512], BF16, 2, st),
                "ynT": kb.rot("ynTs", [128, 4, 64], BF16, 2, st), "tmp": kb.rot("ytmps", [64, 512], BF16, 2, st),
            }
            cnvr = kb.rot("cnvs", [64, 512], F32, 2, st)
            ocs = C.o_conv_s.rearrange("(b j) c -> b j c", j=3)
            scr = kb.rot("stconv", [48, 512], F32, 2, st)
            for g6 in range(6):
                sc = scr.next()
                kb.dma("sp", sc.t[:, :], C.st_conv[:, g6 * 512:(g6 + 1) * 512], writes=[sc])
                b = kb.bank()
                kb.op("pe", lambda e, b=b, sc=sc: [e.transpose(b.t[:, jj * 48:(jj + 1) * 48], sc.t[:48, jj * 128:(jj + 1) * 128], cst[:48, C_ID:C_ID + 48]) for jj in range(4)][-1], [sc, C.cst], [b])
                kb.act(pre.t[:, g6 * 4:(g6 + 1) * 4, :, 0:3], b.t[:, 0:192].rearrange("p (c b j) -> p c b j", c=4, j=3), AF.Copy, [b], [pre])
            ssd_small(C, q, sm, nTc, 0, Wdt, tri_inc, tri_after, same)
            for grp in range(8):
                b = kb.bank()

                def emit(e, grp=grp, b=b):
                    for j in range(3):
                        cc = grp * 3 + j
                        for k in range(8):
                            last = e.matmul(b.t[:, j * 64:(j + 1) * 64], lhsT=Wx.t[:, k, cc * 128:(cc + 1) * 128], rhs=nTc.t[:, k, 0:64], start=(k == 0), stop=(k == 7))
                    return last
                kb.op("pe", emit, [Wx, nTc], [b])
                kb.act(pre.t[:, grp * 3:grp * 3 + 3, :, 3:7], b.t[:, 0:192].rearrange("p (c b j) -> p c b j", c=3, j=4), AF.Copy, [b], [pre])
            for s in range(6):
                b = kb.bank()
                kb.op("pe", lambda e, b=b, s=s: [e.matmul(b.t[:64, :], lhsT=nTc.t[:, k, 0:64], rhs=Wx.t[:, k, s * 512:(s + 1) * 512], start=(k == 0), stop=(k == 7)) for k in range(8)][-1], [Wx, nTc], [b])
                cv_ = cnvr.next()
                kb.copy("act", cv_.t[:64, :], b.t[:64, :], [b], [cv_])
                for t in range(1, 4):
                    kb.dma("sp", ocs[:, t - 1, s * 512:(s + 1) * 512], cv_.t[t:64:4, :], reads=[cv_])
            accd = {"T": a, "pre": pre, "view": lambda cc: a.t[:, cc, :].rearrange("p (b t) -> p b t", t=4), "tanh": pre.t[:].rearrange("p c b j -> p c (b j)")[:, :, 0:64]}
            ssd_conv(C, q, lambda cc, j: pre.t[:, cc, :, j:j + 4], accd, BCT)
            b = kb.bank()
            kb.op("pe", lambda e, b=b: [e.matmul(b.t[:64, g * 64:(g + 1) * 64], lhsT=BCT.t[:, g, :], rhs=BCT.t[:, 4 + g, :], start=True, stop=True) for g in range(4)][-1], [BCT], [b])
            kb.tt("dve", CBm.t[:, :, :], b.t[:64, 0:256].rearrange("p (g t) -> p g t", t=64), bc(tri_inc, [64, 4, 64], 1), ALU.mult, [b, C.cst], [CBm])
            b = kb.bank()
            kb.op("pe", lambda e, b=b: [e.transpose(b.t[:64, g * 128:(g + 1) * 128], a.t[:, 16 + g, :], cst[:, C_ID:C_ID + 128]) for g in range(4)][-1], [a, C.cst], [b])
            kb.copy("act", Btok.t[:, :], b.t[:64, :], [b], [Btok])
            CTm = kb.sb("CTm", [128, 4, 16 * 68], BF16, st)
            kb.memset("pool", CTm.t[:], 0.0, [CTm])
            for g in range(4):
                kb.copy("pool", CTm.t[:, g, :].rearrange("p (b x) -> p b x", x=68)[:, :, 0:4], BCT.t[:, 4 + g, :].rearrange("p (b t) -> p b t", t=4), [BCT], [CTm])
            for g in range(4):
                ssd_group_front(C, q, g, a, sm, W)
            decN = kb.sb("decN", [128, 256], F32, st)
            b = kb.bank()
            def emit_dec(e, b=b):
                for j in range(16):
                    for hl in range(2):
                        last = e.matmul(b.t[hl * 64:(hl + 1) * 64, j * 16:(j + 1) * 16], lhsT=sm.t[:64, 96 + 2 * j + hl:97 + 2 * j + hl].to_broadcast([64, 64]),
                                        rhs=cst[:64, C_SEGS:C_SEGS + 16], start=True, stop=True, tile_position=(0, hl * 64))
                return last
            kb.op("pe", emit_dec, [sm, C.cst], [b])
            kb.act(decN.t[:, :], b.t[:, 0:256], AF.Exp, [b], [decN])
            byis, ids = kb.reserve(4)
            h0r = kb.rot("h0nat", [128, 16, 128], F32, 2, st)
            nsr = kb.rot("newst", [128, 16, 128], F32, 1, st)
            hTbr = kb.rot("hTbb", [128, 2048], BF16, 2, st)
            xw4r = kb.rot("xw4", [4, 2048], BF16, 2, st)
            B4r = kb.rot("B4", [4, 512], BF16, 2, st)
            stin = C.st_ssm.rearrange("(b j q) n -> b q j n", j=16, q=128)
            stout = C.o_ssm_s.rearrange("(b j q) n -> b q j n", j=16, q=128)
            for bb in range(16):
                h0 = h0r.next()
                kb.dma("sp", h0.t[:, :, :], stin[bb], writes=[h0])
                xw4 = xw4r.next()
                B4 = B4r.next()
                kb.dma("sp", xw4.t[:, :], W["xw"].t[4 * bb:4 * bb + 4, :], reads=[W["xw"]], writes=[xw4])
                kb.dma("sp", B4.t[:, :], Btok.t[4 * bb:4 * bb + 4, :], reads=[Btok], writes=[B4])
                hb = hTbr.next()
                for j4 in range(4):
                    b = kb.bank()
                    kb.op("pe", lambda e, b=b, j4=j4, h0=h0: [e.transpose(b.t[:, jj * 128:(jj + 1) * 128], h0.t[:, j4 * 4 + jj, :], cst[:, C_ID:C_ID + 128]) for jj in range(4)][-1], [h0, C.cst], [b])
                    kb.copy("act", hb.t[:, j4 * 512:(j4 + 1) * 512], b.t[:, :], [b], [hb])
                for g in range(4):
                    kb.op("pe", lambda e, g=g, hb=hb, bb=bb: e.matmul(byis[g].t[:64, :], lhsT=CTm.t[:, g, bb * 64:(bb + 1) * 64], rhs=hb.t[:, g * 512:(g + 1) * 512], start=(bb == 0), stop=(bb == 15)), [CTm, hb], [byis[g]])
                ns = nsr.next()
                for j4 in range(4):
                    b = kb.bank()
                    kb.op("pe", lambda e, b=b, j4=j4, xw4=xw4, B4=B4: [e.matmul(b.t[:, jj * 128:(jj + 1) * 128], lhsT=xw4.t[0:4, (j4 * 4 + jj) * 128:(j4 * 4 + jj + 1) * 128], rhs=B4.t[0:4, j4 * 128:(j4 + 1) * 128], start=True, stop=True) for jj in range(4)][-1], [xw4, B4], [b])
                    for jj in range(4):
                        j = j4 * 4 + jj
                        kb.stt(ns.t[:, j, :], h0.t[:, j, :], decN.t[:, j * 16 + bb:j * 16 + bb + 1], b.t[:, jj * 128:(jj + 1) * 128], ALU.mult, ALU.add, [h0, decN, b], [ns])
                kb.dma("sp", stout[bb], ns.t[:, :, :], reads=[ns])
            for g in range(4):
                ssd_group_y(C, q, g, sm, W, BCT, CBm, tri_inc, tri_after, byis[g], nTc, 0, Wz, C.TP, C.NCH)
            kb.release(ids)
            kb.barrier()


def hg_pass(C):
    kb = C.kb
    cst = C.cst.t
    with ExitStack() as stw:
        Wq = load_w(C, "Wq", C.w_in[:, 5152:6176], 1024, stack=stw)
        Wf = load_w(C, "Wf", C.w_in[:, 6176:7200], 1024, stack=stw)
        Wi = load_w(C, "Wi", C.w_in[:, 7200:8224], 1024, stack=stw)
        Wg = load_w(C, "Wg", C.w_in[:, 8224:9248], 1024, stack=stw)
        lb = kb.sb("lb", [128, 1024], F32, stw)
        oml = kb.sb("oml", [128, 1024], F32, stw)
        kb.dma("sp", lb.t[:], C.rowvecs[:, RV_LB0:RV_LB0 + 1024].partition_broadcast(128), writes=[lb])
        kb.dma("sp", oml.t[:], C.rowvecs[:, RV_LB1:RV_LB1 + 1024].partition_broadcast(128), writes=[oml])
        kb.tt("dve", lb.t[:], lb.t[:], oml.t[:], ALU.subtract, [lb, oml], [lb])
        kb.act(lb.t[:], lb.t[:], AF.Sigmoid, [lb], [lb])
        kb.ts("dve", oml.t[:], lb.t[:], -1.0, 1.0, ALU.mult, ALU.add, [lb], [oml])
        omlh = kb.sb("omlh", [128, 1024], F32, stw)
        kb.ts("dve", omlh.t[:], oml.t[:], 0.5, None, ALU.mult, None, [oml], [omlh])

        def proj(q, nTc, tcol, Wt, half):
            b = kb.bank()
            kb.op("pe", lambda e: [e.matmul(b.t[:q, :], lhsT=nTc.t[:, k, tcol:tcol + q], rhs=Wt.t[:, k, half * 512:(half + 1) * 512], start=(k == 0), stop=(k == 7)) for k in range(8)][-1], [nTc, Wt], [b])
            return b

        def front(q, nseg, seglen, nTc, tcol, B, hinc, hafter, segones):
            hs = slice(0, q)
            for half in range(2):
                cs = slice(half * 512, (half + 1) * 512)
                b = proj(q, nTc, tcol, Wf, half)
                kb.act(B["w"].t[hs, cs], b.t[hs, :], AF.Tanh, [b], [B["w"]], scale=0.5)
            kb.stt(B["w"].t[hs, :], B["w"].t[hs, :], 1.0, omlh.t[hs, :], ALU.add, ALU.mult, [B["w"], omlh], [B["w"]])
            kb.tt("dve", B["logf"].t[hs, :], B["w"].t[hs, :], lb.t[hs, :], ALU.add, [B["w"], lb], [B["logf"]])
            kb.act(B["logf"].t[hs, :], B["logf"].t[hs, :], AF.Ln, [B["logf"]], [B["logf"]])
            kb.tt("dve", B["kk"].t[hs, :], oml.t[hs, :], B["w"].t[hs, :], ALU.subtract, [B["w"], oml], [B["kk"]])
            for half in range(2):
                cs = slice(half * 512, (half + 1) * 512)
                b = kb.bank()
                kb.op("pe", lambda e, b=b, cs=cs: e.matmul(b.t[hs, :], lhsT=hinc, rhs=B["logf"].t[hs, cs], start=True, stop=True), [B["logf"], C.cst], [b])
                kb.act(B["E1"].t[hs, cs], b.t[hs, :], AF.Exp, [b], [B["E1"]])
                kb.act(B["E1n"].t[hs, cs], b.t[hs, :], AF.Exp, [b], [B["E1n"]], scale=-1.0)
                b2 = kb.bank()
                kb.op("pe", lambda e, b2=b2, cs=cs: e.matmul(b2.t[hs, :], lhsT=hafter, rhs=B["logf"].t[hs, cs], start=True, stop=True), [B["logf"], C.cst], [b2])
                kb.act(B["E2"].t[hs, cs], b2.t[hs, :], AF.Exp, [b2], [B["E2"]])
            b = kb.bank()
            kb.op("pe", lambda e, b=b: [e.matmul(b.t[:, h * nseg:(h + 1) * nseg], lhsT=B["logf"].t[hs, h * 128:(h + 1) * 128], rhs=segones, start=True, stop=True) for h in range(8)][-1], [B["logf"], C.cst], [b])
            kb.act(B["dS"].t[:, 0:8 * nseg], b.t[:, 0:8 * nseg], AF.Exp, [b], [B["dS"]])
            for half in range(2):
                cs = slice(half * 512, (half + 1) * 512)
                b = proj(q, nTc, tcol, Wq, half)
                kb.act(B["sq"].t[hs, cs], b.t[hs, :], AF.Tanh, [b], [B["sq"]], scale=0.5)
                kb.stt(B["sq"].t[hs, cs], B["sq"].t[hs, cs], 1.0, b.t[hs, :], ALU.add, ALU.mult, [B["sq"], b], [B["sq"]])
            kb.tt("dve", B["qg"].t[hs, :], B["sq"].t[hs, :], B["E1"].t[hs, :], ALU.mult, [B["sq"], B["E1"]], [B["qg"]])
            kb.tt("dve", B["kg"].t[hs, :], B["kk"].t[hs, :], B["E1n"].t[hs, :], ALU.mult, [B["kk"], B["E1n"]], [B["kg"]])
            kb.tt("dve", B["kdec"].t[hs, :], B["kk"].t[hs, :], B["E2"].t[hs, :], ALU.mult, [B["kk"], B["E2"]], [B["kdec"]])
            for half in range(2):
                cs = slice(half * 512, (half + 1) * 512)
                b = proj(q, nTc, tcol, Wi, half)
                kb.copy("act", B["v"].t[hs, cs], b.t[hs, :], [b], [B["v"]])
            for half in range(2):
                cs = slice(half * 512, (half + 1) * 512)
                b = proj(q, nTc, tcol, Wg, half)
                kb.act(B["sg"].t[hs, cs], b.t[hs, :], AF.Tanh, [b], [B["sg"]], scale=0.5)
                kb.stt(B["sg"].t[hs, cs], B["sg"].t[hs, cs], 1.0, b.t[hs, :], ALU.add, ALU.mult, [B["sg"], b], [B["sg"]])
            to_T(C, B["qg"], q, 8, None, B["qgT"].t[:, :, :q], B["qgT"])
            to_T(C, B["kg"], q, 8, None, B["kgT"].t[:, :, :q], B["kgT"])
            x = q + seglen
            kb.copy("pool", B["QM"].t[:, :, :].rearrange("p h (c x) -> p h c x", x=x)[:, :, :, 0:seglen],
                    B["qgT"].t[:, :, :q].rearrange("p h (c j) -> p h c j", j=seglen), [B["qgT"]], [B["QM"]])
            for hh in range(2):
                b = kb.bank()
                kb.op("pe", lambda e, b=b, hh=hh: [e.matmul(b.t[hs, h4 * q:(h4 + 1) * q], lhsT=B["kgT"].t[:, hh * 4 + h4, :q], rhs=B["qgT"].t[:, hh * 4 + h4, :q], start=True, stop=True) for h4 in range(4)][-1], [B["kgT"], B["qgT"]], [b])
                kb.tt("dve", B["att"].t[hs, hh * 4:(hh + 1) * 4, :q], b.t[hs, 0:4 * q].rearrange("p (h t) -> p h t", t=q), bc(hinc, [q, 4, q], 1), ALU.mult, [b, C.cst], [B["att"]])

        def back(q, bo, B, tok0, ci):
            hs = slice(0, q)
            for half in range(2):
                kb.copy("act", B["osb"].t[hs, half * 512:(half + 1) * 512], bo[half].t[hs, :], [bo[half]], [B["osb"]])
            o = B["osb"]
            kb.tt("pool", B["osq"].t[hs, :], o.t[hs, :], o.t[hs, :], ALU.mult, [o], [B["osq"]])
            hsm = B["hsm"]
            kb.op("dve", lambda e: e.tensor_reduce(out=hsm.t[hs, 0:8], in_=B["osq"].t[hs, :].rearrange("p (h v) -> p h v", v=128), axis=AX.X, op=ALU.add), [B["osq"]], [hsm])
            kb.ts("dve", hsm.t[hs, 8:16], hsm.t[hs, 0:8], 4.0 / 128, 16.0 * EPS, ALU.mult, ALU.add, [hsm], [hsm])
            kb.act(hsm.t[hs, 16:24], hsm.t[hs, 8:16], AF.Sqrt, [hsm], [hsm])
            kb.op("dve", lambda e: e.reciprocal(out=hsm.t[hs, 24:32], in_=hsm.t[hs, 16:24]), [hsm], [hsm])
            o3 = o.t[hs, :].rearrange("p (h v) -> p h v", v=128)
            kb.tt("pool", o3, o3, bc(hsm.t[hs, 24:32], [q, 8, 128], 2), ALU.mult, [o, hsm], [o])
            kb.tt("dve", B["on"].t[hs, :], o.t[hs, :], B["sg"].t[hs, :], ALU.mult, [o, B["sg"]], [B["on"]])
            to_T(C, B["on"], q, 8, C.colv.t[:, V_HNORM:V_HNORM + 8], B["onT"].t[:, :, :q], B["onT"])
            kb.dma("sp", C.ONT[:, :, tok0:tok0 + q], B["onT"].t[:, :, :q], reads=[B["onT"]], writes=[C.ONT_r[ci]])

        def bufs(q, nseg, seglen, st, nslots=1):
            shared = {}
            for n in ("w", "logf", "kk", "E1", "E1n", "E2", "sq", "osb", "osq"):
                shared[n] = kb.sb("hg_" + n, [q, 1024], F32, st)
            for n in ("qg", "kg", "on"):
                shared[n] = kb.sb("hg_" + n, [q, 1024], BF16, st)
            shared["qgT"] = kb.sb("hg_qgT", [128, 8, q], BF16, st)
            shared["kgT"] = kb.sb("hg_kgT", [128, 8, q], BF16, st)
            shared["onT"] = kb.sb("hg_onT", [128, 8, q], BF16, st)
            shared["hsm"] = kb.sb("hg_hsm", [q, 32], F32, st)
            out = []
            for i in range(nslots):
                B = dict(shared)
                B["sg"] = kb.sb(f"hg_sg{i}", [q, 1024], F32, st)
                for n in ("kdec", "v"):
                    B[n] = kb.sb(f"hg_{n}{i}", [q, 1024], BF16, st)
                B["QM"] = kb.sb(f"hg_QM{i}", [128, 8, nseg * (q + seglen)], BF16, st)
                B["att"] = kb.sb(f"hg_att{i}", [q, 8, q], BF16, st)
                B["dS"] = kb.sb(f"hg_dS{i}", [128, 8 * nseg], F32, st)
                kb.memset("pool", B["QM"].t[:], 0.0, [B["QM"]])
                out.append(B)
            return out

        with ExitStack() as st:
            q, nseg, seglen = 128, 4, 32
            Bs = bufs(q, nseg, seglen, st, 2)
            nTr = kb.rot("hnTc", [128, 8, 128], BF16, 2, st)
            S = kb.sb("hg_S", [128, 8, 128], F32, st)
            Sb = [kb.sb(f"hg_Sb{c}", [128, 8, 128], BF16, st) for c in range(4)]
            kb.memset("pool", S.t[:], 0.0, [S])
            hinc, hafter, segones = cst[:, C_HPI:C_HPI + 128], cst[:, C_HPA:C_HPA + 128], cst[:, C_SEGP:C_SEGP + 4]

            def F(c, nTc, B):
                kb.dma("sp", nTc.t[:, :, :], C.nTd[:, :, 3 + c * 128:3 + c * 128 + 128], reads=[C.nTd_r[c + 1]], writes=[nTc])
                front(q, nseg, seglen, nTc, 0, B, hinc, hafter, segones)

            def G(c, B):
                for sc in range(4):
                    kb.copy("act", Sb[sc].t[:, :, :], S.t[:, :, :], [S], [Sb[sc]])
                    bd = [kb.bank(), kb.bank()]
                    for hh in range(2):
                        kb.op("pe", lambda e, hh=hh, sc=sc, bd=bd: [e.matmul(bd[hh].t[:, h4 * 128:(h4 + 1) * 128], lhsT=B["kdec"].t[32 * sc:32 * sc + 32, (hh * 4 + h4) * 128:(hh * 4 + h4 + 1) * 128],
                                                                      rhs=B["v"].t[32 * sc:32 * sc + 32, (hh * 4 + h4) * 128:(hh * 4 + h4 + 1) * 128], start=True, stop=True, tile_position=(32 * sc, 0)) for h4 in range(4)][-1], [B["kdec"], B["v"]], [bd[hh]])
                    kb.tt("dve", S.t[:, :, :], S.t[:, :, :], bc(B["dS"].t[:, :].rearrange("p (h c) -> p h c", c=nseg)[:, :, sc], [128, 8, 128], 2), ALU.mult, [S, B["dS"]], [S])
                    for hh in range(2):
                        Sv = S.t[:, hh * 4:(hh + 1) * 4, :]
                        kb.tt("dve", Sv, Sv, bd[hh].t[:, :].rearrange("p (h v) -> p h v", v=128), ALU.add, [S, bd[hh]], [S])
                bo = [kb.bank(), kb.bank()]
                for hh in range(2):
                    def emit(e, hh=hh):
                        for h4 in range(4):
                            h = hh * 4 + h4
                            e.matmul(bo[hh].t[:, h4 * 128:(h4 + 1) * 128], lhsT=B["att"].t[:, h, :], rhs=B["v"].t[:, h * 128:(h + 1) * 128], start=True, stop=False)
                            for sc in range(4):
                                last = e.matmul(bo[hh].t[:, h4 * 128:(h4 + 1) * 128], lhsT=B["QM"].t[:, h, sc * 128:(sc + 1) * 128], rhs=Sb[sc].t[:, h, :], start=False, stop=(sc == 3))
                        return last
                    kb.op("pe", emit, [B["att"], B["v"], B["QM"]] + Sb, [bo[hh]])
                back(q, bo, B, c * 128, c)

            nts = [nTr.next(), nTr.next()]
            FP, GP = [0, 1, 2, 3], [4, 5, 6, 7]
            kb.begin(FP); F(0, nts[0], Bs[0]); kb.play(kb.end())
            for c in range(C.NCH):
                kb.begin(GP); G(c, Bs[c % 2]); gl = kb.end()
                fl = []
                if c + 1 < C.NCH:
                    kb.begin(FP); F(c + 1, nts[(c + 1) % 2], Bs[(c + 1) % 2]); fl = kb.end()
                kb.play(schedule(gl, fl))
            kb.set_pool(range(8))
            kb.dma("sp", C.o_hg_p.rearrange("(h k) v -> k h v", k=128), S.t[:, :, :], reads=[S])
            kb.barrier()
        with ExitStack() as st:
            q, nseg, seglen = 64, 16, 4
            B = bufs(q, nseg, seglen, st)[0]
            nTc = kb.sb("hnTs", [128, 8, 64], BF16, st)
            kb.dma("sp", nTc.t[:, :, :], C.nTd[:, :, 3 + C.TP:3 + C.TP + 64], reads=[C.nTd_r[C.NT]], writes=[nTc])
            hinc, hafter, segones = cst[:64, C_TSI:C_TSI + 64], cst[:64, C_TSA:C_TSA + 64], cst[:64, C_SEGS:C_SEGS + 16]
            front(q, nseg, seglen, nTc, 0, B, hinc, hafter, segones)
            bo, ids = kb.reserve(2)
            zt = kb.sb("hg_zero", [64, 64], BF16, st)
            kb.memset("pool", zt.t[:], 0.0, [zt])
            for hh in range(2):
                def emit(e, hh=hh):
                    e.matmul(bo[hh].t[:64, :], lhsT=zt.t[:, :], rhs=B["v"].t[:, hh * 512:(hh + 1) * 512], start=True, stop=False)
                    for h4 in range(4):
                        last = e.matmul(bo[hh].t[:64, h4 * 128:(h4 + 1) * 128], lhsT=B["att"].t[:, hh * 4 + h4, :], rhs=B["v"].t[:, (hh * 4 + h4) * 128:(hh * 4 + h4 + 1) * 128], start=False, stop=False)
                    return last
                kb.op("pe", emit, [B["att"], B["v"], zt], [bo[hh]])
            Sir = kb.rot("hg_Sin", [128, 8, 128], F32, 2, st)
            Sor = kb.rot("hg_Sout", [128, 8, 128], F32, 2, st)
            Sbr = kb.rot("hg_Sbb", [128, 8, 128], BF16, 2, st)
            k4r = kb.rot("hg_k4", [4, 1024], BF16, 2, st)
            v4r = kb.rot("hg_v4", [4, 1024], BF16, 2, st)
            sin = C.st_hg.rearrange("(b h k) v -> b k h v", h=8, k=128)
            sout = C.o_hg_s.rearrange("(b h k) v -> b k h v", h=8, k=128)
            for bb in range(16):
                Si = Sir.next()
                kb.dma("sp", Si.t[:, :, :], sin[bb], writes=[Si])
                k4 = k4r.next(); v4 = v4r.next()
                kb.dma("sp", k4.t[:, :], B["kdec"].t[4 * bb:4 * bb + 4, :], reads=[B["kdec"]], writes=[k4])
                kb.dma("sp", v4.t[:, :], B["v"].t[4 * bb:4 * bb + 4, :], reads=[B["v"]], writes=[v4])
                Sbb = Sbr.next()
                kb.copy("act", Sbb.t[:, :, :], Si.t[:, :, :], [Si], [Sbb])
                for hh in range(2):
                    kb.op("pe", lambda e, hh=hh, Sbb=Sbb, bb=bb: [e.matmul(bo[hh].t[:64, h4 * 128:(h4 + 1) * 128], lhsT=B["QM"].t[:, hh * 4 + h4, bb * 64:(bb + 1) * 64], rhs=Sbb.t[:, hh * 4 + h4, :], start=False, stop=(bb == 15 and h4 == 3)) for h4 in range(4)][-1], [B["QM"], Sbb], [bo[hh]])
                bd = [kb.bank(), kb.bank()]
                for hh in range(2):
                    kb.op("pe", lambda e, hh=hh, bd=bd, k4=k4, v4=v4: [e.matmul(bd[hh].t[:, h4 * 128:(h4 + 1) * 128], lhsT=k4.t[0:4, (hh * 4 + h4) * 128:(hh * 4 + h4 + 1) * 128], rhs=v4.t[0:4, (hh * 4 + h4) * 128:(hh * 4 + h4 + 1) * 128], start=True, stop=True) for h4 in range(4)][-1], [k4, v4], [bd[hh]])
                So = Sor.next()
                kb.tt("pool", So.t[:, :, :], Si.t[:, :, :], bc(B["dS"].t[:, :].rearrange("p (h c) -> p h c", c=nseg)[:, :, bb], [128, 8, 128], 2), ALU.mult, [Si, B["dS"]], [So])
                for hh in range(2):
                    Sv = So.t[:, hh * 4:(hh + 1) * 4, :]
                    kb.tt("dve", Sv, Sv, bd[hh].t[:, :].rearrange("p (h v) -> p h v", v=128), ALU.add, [So, bd[hh]], [So])
                kb.dma("sp", sout[bb], So.t[:, :, :], reads=[So])
            back(q, bo, B, C.TP, C.NCH)
            kb.release(ids)
            kb.barrier()


def chunk_info(C, ci):
    if ci < C.NCH:
        return 128, ci * 128
    return 64, C.TP


def merge_pass(C):
    kb = C.kb
    with ExitStack() as st:
        Wg1 = load_w(C, "Wg1", C.w_in[:, 9248:10272], 1024, stack=st)
        Wg2 = load_w(C, "Wg2", C.w_in[:, 10272:11296], 1024, stack=st)
        Wso = load_w(C, "Wso", C.w_ssd_out, 1024, nk=16, stack=st)
        Who = load_w(C, "Who", C.w_hgrn_out, 1024, stack=st)
        Wo = load_w(C, "Wo", C.w_out, 1024, stack=st)
        nTr = kb.rot("m_nT", [128, 8, 128], BF16, 2, st)
        yTr = kb.rot("m_yT", [128, 16, 128], BF16, 2, st)
        oTr = kb.rot("m_oT", [128, 8, 128], BF16, 2, st)
        xr = kb.rot("m_x", [128, D], F32, 2, st)
        s1 = kb.sb("m_s1", [128, D], F32, st); s2 = kb.sb("m_s2", [128, D], F32, st)
        m1 = kb.sb("m_m1", [128, D], F32, st); m2 = kb.sb("m_m2", [128, D], F32, st)
        mg = kb.sb("m_mg", [128, D], BF16, st); mT = kb.sb("m_mT", [128, 8, 128], BF16, st)
        x1r = kb.rot("m_x1", [128, D], F32, 2, st)

        def loads(ci):
            q, tok0 = chunk_info(C, ci)
            a, b_, c_, d_ = nTr.next(), yTr.next(), oTr.next(), xr.next()
            kb.dma("sp", a.t[:, :, :q], C.nTd[:, :, 3 + tok0:3 + tok0 + q], reads=[C.nTd_r[ci + 1]], writes=[a])
            kb.dma("sp", b_.t[:, :, :q], C.YNT[:, :, tok0:tok0 + q], reads=C.YNT_r[ci], writes=[b_])
            kb.dma("sp", c_.t[:, :, :q], C.ONT[:, :, tok0:tok0 + q], reads=[C.ONT_r[ci]], writes=[c_])
            src = C.xp[tok0:tok0 + q, :] if ci < C.NCH else C.xs[:, :]
            kb.dma("sp", d_.t[:q, :], src, writes=[d_])
            return a, b_, c_, d_
        nxt = loads(0)
        for ci in range(C.NT):
            q, tok0 = chunk_info(C, ci)
            nTc, ynT, onT, xt = nxt
            if ci + 1 < C.NT:
                nxt = loads(ci + 1)
            hs = slice(0, q)
            for half in range(2):
                cs = slice(half * 512, (half + 1) * 512)
                for (Wt, dst) in ((Wg1, s1), (Wg2, s2)):
                    b = kb.bank()
                    kb.op("pe", lambda e, b=b, Wt=Wt: [e.matmul(b.t[hs, :], lhsT=nTc.t[:, k, :q], rhs=Wt.t[:, k, cs], start=(k == 0), stop=(k == 7)) for k in range(8)][-1], [nTc, Wt], [b])
                    kb.act(dst.t[hs, cs], b.t[hs, :], AF.Tanh, [b], [dst], scale=0.5)
                b = kb.bank()
                kb.op("pe", lambda e, b=b: [e.matmul(b.t[hs, :], lhsT=ynT.t[:, k, :q], rhs=Wso.t[:, k, cs], start=(k == 0), stop=(k == 15)) for k in range(16)][-1], [ynT, Wso], [b])
                kb.stt(m1.t[hs, cs], s1.t[hs, cs], 1.0, b.t[hs, :], ALU.add, ALU.mult, [b, s1], [m1])
                b = kb.bank()
                kb.op("pe", lambda e, b=b: [e.matmul(b.t[hs, :], lhsT=onT.t[:, k, :q], rhs=Who.t[:, k, cs], start=(k == 0), stop=(k == 7)) for k in range(8)][-1], [onT, Who], [b])
                kb.stt(m2.t[hs, cs], s2.t[hs, cs], 1.0, b.t[hs, :], ALU.add, ALU.mult, [b, s2], [m2])
            kb.tt("dve", mg.t[hs, :], m1.t[hs, :], m2.t[hs, :], ALU.add, [m1, m2], [mg])
            to_T(C, mg, q, 8, None, mT.t[:, :, :q], mT)
            x1 = x1r.next()
            for half in range(2):
                cs = slice(half * 512, (half + 1) * 512)
                b = kb.bank()
                kb.op("pe", lambda e, b=b: [e.matmul(b.t[hs, :], lhsT=mT.t[:, k, :q], rhs=Wo.t[:, k, cs], start=(k == 0), stop=(k == 7)) for k in range(8)][-1], [mT, Wo], [b])
                kb.stt(x1.t[hs, cs], b.t[hs, :], 0.5, xt.t[hs, cs], ALU.mult, ALU.add, [b, xt], [x1])
            kb.dma("sp", C.X1[tok0:tok0 + q, :], x1.t[hs, :], reads=[x1], writes=[C.X1_r[ci]])
        kb.barrier()


def attn_pass(C):
    kb = C.kb
    with ExitStack() as st:
        Wcq = load_w(C, "Wcq", C.w_cq, 1024, stack=st)
        Wck = load_w(C, "Wck", C.w_ck, 1024, stack=st)
        Wcv = load_w(C, "Wcv", C.w_cv, 1024, stack=st)
        Wco = load_w(C, "Wco", C.w_co, 1024, stack=st)
        memT = kb.sb("a_memT", [128, 8, 256], BF16, st)
        KT = kb.sb("a_KT", [128, 8, 256], BF16, st)
        V = kb.sb("a_V", [128, 2, D], BF16, st)
        xr = kb.rot("a_x1", [128, D], F32, 2, st)
        kvo = kb.rot("a_kvo", [128, 512], F32, 2, st)
        for mt in range(2):
            xt = xr.next()
            kb.dma("sp", xt.t[:, :], C.mem[mt * 128:(mt + 1) * 128, :], writes=[xt])
            rms_to_T(C, xt, 128, C.colv.t[:, V_NMEM:V_NMEM + 8], memT.t[:, :, mt * 128:(mt + 1) * 128], memT)
        for mt in range(2):
            for half in range(2):
                cs = slice(half * 512, (half + 1) * 512)
                for (Wt, dst, isv) in ((Wck, C.o_mk, False), (Wcv, C.o_mv, True)):
                    b = kb.bank()
                    kb.op("pe", lambda e, b=b, Wt=Wt: [e.matmul(b.t[:, :], lhsT=memT.t[:, k, mt * 128:(mt + 1) * 128], rhs=Wt.t[:, k, cs], start=(k == 0), stop=(k == 7)) for k in range(8)][-1], [memT, Wt], [b])
                    o = kvo.next()
                    kb.copy("act", o.t[:, :], b.t[:, :], [b], [o])
                    kb.dma("sp", dst[mt * 128:(mt + 1) * 128, cs], o.t[:, :], reads=[o])
                    if isv:
                        kb.copy("pool", V.t[:, mt, cs], o.t[:, :], [o], [V])
        for c in range(8):
            b = kb.bank()
            kb.op("pe", lambda e, b=b, c=c: [e.matmul(b.t[:, 0:256], lhsT=Wck.t[:, k, c * 128:(c + 1) * 128], rhs=memT.t[:, k, :], start=(k == 0), stop=(k == 7)) for k in range(8)][-1], [memT, Wck], [b])
            kb.copy("act", KT.t[:, c, :], b.t[:, 0:256], [b], [KT])

        def mkset(i):
            return dict(hnT=kb.sb(f"a_hnT{i}", [128, 8, 128], BF16, st), Qs=kb.sb(f"a_Qs{i}", [128, D], BF16, st), QT=kb.sb(f"a_QT{i}", [128, 8, 128], BF16, st),
                        P=kb.sb(f"a_P{i}", [128, D], BF16, st), PT=kb.sb(f"a_PT{i}", [128, 8, 128], BF16, st), On=kb.sb(f"a_On{i}", [128, D], BF16, st),
                        OT=kb.sb(f"a_OT{i}", [128, 8, 128], BF16, st), sm=kb.sb(f"a_sm{i}", [128, 16], F32, st), x2=kb.sb(f"a_x2{i}", [128, D], F32, st),
                        x1=kb.sb(f"a_x1{i}", [128, D], F32, st), xn=kb.sb(f"a_xn{i}", [128, D], BF16, st), junk=kb.sb(f"a_junk{i}", [128, D], BF16, st),
                        ss=kb.sb(f"a_ss{i}", [128, 4], F32, st))
        sets = [mkset(0), mkset(1)]
        zt = kb.sb("a_zero", [128, 64], BF16, st)
        kb.memset("pool", zt.t[:], 0.0, [zt])

        def load_x1(ci, S):
            q, tok0 = chunk_info(C, ci)
            kb.dma("sp", S["x1"].t[:q, :], C.X1[tok0:tok0 + q, :], reads=[C.X1_r[ci]], writes=[S["x1"]])

        def q_front(q, S):
            hs = slice(0, q)
            hnT, Qs, QT = S["hnT"], S["Qs"], S["QT"]
            rms_to_T(C, S["x1"], q, C.colv.t[:, V_NCROSS:V_NCROSS + 8], hnT.t[:, :, :q], hnT, scratch=S)
            for half in range(2):
                cs = slice(half * 512, (half + 1) * 512)
                b = kb.bank()
                kb.op("pe", lambda e, b=b, cs=cs: [e.matmul(b.t[hs, :], lhsT=hnT.t[:, k, :q], rhs=Wcq.t[:, k, cs], start=(k == 0), stop=(k == 7)) for k in range(8)][-1], [hnT, Wcq], [b])
                kb.act(Qs.t[hs, cs], b.t[hs, :], AF.Copy, [b], [Qs], scale=1.0 / 16.0)
            to_T(C, Qs, q, 8, None, QT.t[:, :, :q], QT)

        def softmax(q, bs, S):
            hs = slice(0, q)
            sm, P, PT = S["sm"], S["P"], S["PT"]
            for hh in range(2):
                kb.op("dve", lambda e, hh=hh: e.tensor_reduce(out=sm.t[hs, hh * 2:hh * 2 + 2], in_=bs[hh].t[hs, :].rearrange("p (h m) -> p h m", m=256), axis=AX.X, op=ALU.max), [bs[hh]], [sm])
            kb.ts("dve", sm.t[hs, 4:8], sm.t[hs, 0:4], -1.0, None, ALU.mult, None, [sm], [sm])
            for h in range(4):
                kb.act(P.t[hs, h * 256:(h + 1) * 256], bs[h // 2].t[hs, (h % 2) * 256:(h % 2 + 1) * 256], AF.Exp, [bs[h // 2], sm], [P, sm], bias=sm.t[hs, 4 + h:5 + h], accum_out=sm.t[hs, 8 + h:9 + h])
            kb.op("dve", lambda e: e.reciprocal(out=sm.t[hs, 12:16], in_=sm.t[hs, 8:12]), [sm], [sm])
            to_T(C, P, q, 8, None, PT.t[:, :, :q], PT)

        def o_back(q, bo, S, ci, tok0):
            hs = slice(0, q)
            sm, On, OT, x2, xt = S["sm"], S["On"], S["OT"], S["x2"], S["x1"]
            for h in range(4):
                kb.act(On.t[hs, h * 256:(h + 1) * 256], bo[h // 2].t[hs, (h % 2) * 256:(h % 2 + 1) * 256], AF.Copy, [bo[h // 2], sm], [On], scale=sm.t[hs, 12 + h:13 + h])
            to_T(C, On, q, 8, None, OT.t[:, :, :q], OT)
            for half in range(2):
                cs = slice(half * 512, (half + 1) * 512)
                b = kb.bank()
                kb.op("pe", lambda e, b=b, cs=cs: [e.matmul(b.t[hs, :], lhsT=OT.t[:, k, :q], rhs=Wco.t[:, k, cs], start=(k == 0), stop=(k == 7)) for k in range(8)][-1], [OT, Wco], [b])
                kb.tt("dve", x2.t[hs, cs], b.t[hs, :], xt.t[hs, cs], ALU.add, [b, xt], [x2])
            kb.dma("sp", C.X2[tok0:tok0 + q, :], x2.t[hs, :], reads=[x2], writes=[C.X2_r[ci]])

        def prompt_chunk(ci, S):
            q, tok0 = 128, ci * 128
            load_x1(ci, S)
            q_front(q, S)
            QT, PT = S["QT"], S["PT"]
            bs = [kb.bank(), kb.bank()]
            for hh in range(2):
                def emit(e, hh=hh):
                    for h2 in range(2):
                        h = hh * 2 + h2
                        for dc in range(2):
                            last = e.matmul(bs[hh].t[:, h2 * 256:(h2 + 1) * 256], lhsT=QT.t[:, h * 2 + dc, :], rhs=KT.t[:, h * 2 + dc, :], start=(dc == 0), stop=(dc == 1))
                    return last
                kb.op("pe", emit, [QT, KT], [bs[hh]])
            softmax(q, bs, S)
            bo = [kb.bank(), kb.bank()]
            for hh in range(2):
                def emit(e, hh=hh):
                    for h2 in range(2):
                        h = hh * 2 + h2
                        for mt in range(2):
                            last = e.matmul(bo[hh].t[:, h2 * 256:(h2 + 1) * 256], lhsT=PT.t[:, h * 2 + mt, :], rhs=V.t[:, mt, h * 256:(h + 1) * 256], start=(mt == 0), stop=(mt == 1))
                    return last
                kb.op("pe", emit, [PT, V], [bo[hh]])
            o_back(q, bo, S, ci, tok0)

        for c0 in range(0, C.NCH, 2):
            kb.begin([0, 1, 2, 3]); prompt_chunk(c0, sets[0]); la = kb.end()
            lb_ = []
            if c0 + 1 < C.NCH:
                kb.begin([4, 5, 6, 7]); prompt_chunk(c0 + 1, sets[1]); lb_ = kb.end()
            kb.play(schedule(la, lb_))
        kb.set_pool(range(8))
        q, tok0, ci = 64, C.TP, C.NCH
        S = sets[0]
        QT, PT = S["QT"], S["PT"]
        load_x1(ci, S)
        q_front(q, S)
        QTm = kb.sb("a_QTm", [128, 8, 16 * 68], BF16, st)
        PTm = kb.sb("a_PTm", [128, 8, 16 * 68], BF16, st)
        kb.memset("pool", QTm.t[:], 0.0, [QTm])
        kb.memset("pool", PTm.t[:], 0.0, [PTm])
        kb.copy("pool", QTm.t[:, :, :].rearrange("p h (c x) -> p h c x", x=68)[:, :, :, 0:4], QT.t[:, :, :64].rearrange("p h (c j) -> p h c j", j=4), [QT], [QTm])
        Kr = kb.rot("a_Kb", [128, 2, D], BF16, 2, st)
        KTr = kb.rot("a_KTb", [128, 8, 256], BF16, 2, st)
        bs, ids = kb.reserve(2)
        for hh in range(2):
            kb.op("pe", lambda e, hh=hh: e.matmul(bs[hh].t[:64, :], lhsT=zt.t[:64, :], rhs=Wcq.t[:64, 0, 0:512], start=True, stop=False), [zt, Wcq], [bs[hh]])
        ckv = C.ck.rearrange("(b mt p) c -> b p mt c", mt=2, p=128)
        cvv = C.cv.rearrange("(b mt p) c -> b p mt c", mt=2, p=128)
        for bb in range(16):
            Kb = Kr.next()
            kb.dma("pool", Kb.t[:, :, :], ckv[bb], writes=[Kb])
            KTb = KTr.next()
            for mt in range(2):
                b = kb.bank()
                pb = b.t.bitcast(BF16)
                kb.op("pe", lambda e, pb=pb, mt=mt, Kb=Kb: [e.transpose(pb[:, c * 128:(c + 1) * 128], Kb.t[:, mt, c * 128:(c + 1) * 128], C.identb.t[:, :]) for c in range(8)][-1], [Kb, C.identb], [b])
                kb.copy("act", KTb.t[:, :, mt * 128:(mt + 1) * 128], pb.rearrange("p (c m) -> p c m", m=128), [b], [KTb])
            for hh in range(2):
                def emit(e, hh=hh, KTb=KTb, bb=bb):
                    for h2 in range(2):
                        h = hh * 2 + h2
                        for dc in range(2):
                            last = e.matmul(bs[hh].t[:64, h2 * 256:(h2 + 1) * 256], lhsT=QTm.t[:, h * 2 + dc, bb * 64:(bb + 1) * 64], rhs=KTb.t[:, h * 2 + dc, :], start=False, stop=(bb == 15 and dc == 1 and h2 == 1))
                    return last
                kb.op("pe", emit, [QTm, KTb], [bs[hh]])
        softmax(q, bs, S)
        kb.copy("pool", PTm.t[:, :, :].rearrange("p h (c x) -> p h c x", x=68)[:, :, :, 0:4], PT.t[:, :, :64].rearrange("p h (c j) -> p h c j", j=4), [PT], [PTm])
        kb.release(ids)
        bo, ids = kb.reserve(2)
        for hh in range(2):
            kb.op("pe", lambda e, hh=hh: e.matmul(bo[hh].t[:64, :], lhsT=zt.t[:64, :], rhs=Wcq.t[:64, 0, 0:512], start=True, stop=False), [zt, Wcq], [bo[hh]])
        for bb in range(16):
            Vb = Kr.next()
            kb.dma("pool", Vb.t[:, :, :], cvv[bb], writes=[Vb])
            for hh in range(2):
                def emit(e, hh=hh, Vb=Vb, bb=bb):
                    for h2 in range(2):
                        h = hh * 2 + h2
                        for mt in range(2):
                            last = e.matmul(bo[hh].t[:64, h2 * 256:(h2 + 1) * 256], lhsT=PTm.t[:, h * 2 + mt, bb * 64:(bb + 1) * 64], rhs=Vb.t[:, mt, h * 256:(h + 1) * 256], start=False, stop=(bb == 15 and mt == 1 and h2 == 1))
                    return last
                kb.op("pe", emit, [PTm, Vb], [bo[hh]])
        o_back(q, bo, S, ci, tok0)
        kb.release(ids)
        kb.barrier()


def ffn_pass(C):
    kb = C.kb
    with ExitStack() as st:
        Wga = load_w(C, "Wga", C.w_gate, FFN, stack=st)
        Wup = load_w(C, "Wup", C.w_up, FFN, stack=st)
        Wdn = load_w(C, "Wdn", C.w_down, 1024, nk=22, stack=st)
        nfin = kb.sb("f_nfin", [128, D], F32, st)
        kb.dma("sp", nfin.t[:], C.rowvecs[:, RV_NFIN:RV_NFIN + D].partition_broadcast(128), writes=[nfin])
        x2r = kb.rot("f_x2", [128, 2, D], F32, 2, st)
        hnT = kb.sb("f_hnT", [128, 8, 256], BF16, st)
        hT = kb.sb("f_hT", [128, 22, 256], BF16, st)
        sgr = kb.rot("f_sg", [128, 256], F32, 2, st)
        yr = kb.rot("f_y", [128, D], F32, 2, st)
        ntile = C.NCH // 2 + 1

        def tinfo(ti):
            if ti < C.NCH // 2:
                return [(128, ti * 256, 2 * ti), (128, ti * 256 + 128, 2 * ti + 1)]
            return [(64, C.TP, C.NCH)]

        def loads(ti):
            t = x2r.next()
            for sub, (q, tok0, ci) in enumerate(tinfo(ti)):
                kb.dma("sp", t.t[:q, sub, :], C.X2[tok0:tok0 + q, :], reads=[C.X2_r[ci]], writes=[t])
            return t
        nxt = loads(0)
        for ti in range(ntile):
            xt = nxt
            if ti + 1 < ntile:
                nxt = loads(ti + 1)
            subs = tinfo(ti)
            Wd = sum(q for q, _, _ in subs)
            for sub, (q, tok0, ci) in enumerate(subs):
                xv = T(xt.t[:, sub, :], xt.r)
                rms_to_T(C, xv, q, C.colv.t[:, V_NFFN:V_NFFN + 8], hnT.t[:, :, sub * 128:sub * 128 + q], hnT)
            for f in range(22):
                b = kb.bank()

                def emit(e, b=b, f=f):
                    for k in range(8):
                        e.matmul(b.t[:, 0:Wd], lhsT=Wga.t[:, k, f * 128:(f + 1) * 128], rhs=hnT.t[:, k, 0:Wd], start=(k == 0), stop=(k == 7))
                    for k in range(8):
                        last = e.matmul(b.t[:, 256:256 + Wd], lhsT=Wup.t[:, k, f * 128:(f + 1) * 128], rhs=hnT.t[:, k, 0:Wd], start=(k == 0), stop=(k == 7))
                    return last
                kb.op("pe", emit, [Wga, Wup, hnT], [b])
                sg = sgr.next()
                kb.act(sg.t[:, 0:Wd], b.t[:, 0:Wd], AF.Tanh, [b], [sg], scale=0.5)
                kb.stt(sg.t[:, 0:Wd], sg.t[:, 0:Wd], 1.0, b.t[:, 0:Wd], ALU.add, ALU.mult, [sg, b], [sg])
                kb.tt("dve", hT.t[:, f, 0:Wd], sg.t[:, 0:Wd], b.t[:, 256:256 + Wd], ALU.mult, [sg, b], [hT])
            for sub, (q, tok0, ci) in enumerate(subs):
                hs = slice(0, q)
                for half in range(2):
                    cs = slice(half * 512, (half + 1) * 512)
                    b = kb.bank()
                    kb.op("pe", lambda e, b=b, sub=sub, q=q, cs=cs: [e.matmul(b.t[:q, :], lhsT=hT.t[:, f, sub * 128:sub * 128 + q], rhs=Wdn.t[:, f, cs], start=(f == 0), stop=(f == 21)) for f in range(22)][-1], [hT, Wdn], [b])
                    kb.stt(xt.t[hs, sub, cs], b.t[hs, :], 0.5, xt.t[hs, sub, cs], ALU.mult, ALU.add, [b, xt], [xt])
                xv = T(xt.t[:, sub, :], xt.r)
                ss = C.ss.next()
                rms_rstd(C, xv, q, ss)
                y = yr.next()
                kb.act(y.t[hs, :], xv.t[hs, :], AF.Copy, [xt, ss], [y], scale=ss.t[hs, 0:1])
                kb.tt("dve", y.t[hs, :], y.t[hs, :], nfin.t[hs, :], ALU.mult, [y, nfin], [y])
                dst = C.o_y_p[tok0:tok0 + q, :] if ci < C.NCH else C.o_y_s[:, :]
                kb.dma("sp", dst, y.t[hs, :], reads=[y])
        kb.barrier()

def build(NCH=16, debug=False, upto=99, skip_ssd=False):
    TP = NCH * 128
    TS = 64
    TT = TP + TS
    nc = bass.Bass("TRN2", target_bir_lowering=False)
    C = Ctx()
    C.nc, C.NCH, C.TP, C.TT, C.NT = nc, NCH, TP, TT, NCH + 1
    C.skip_ssd = skip_ssd

    def din(name, shape):
        return nc.dram_tensor(name, list(shape), F32, kind="ExternalInput").ap()

    def dout(name, shape):
        return nc.dram_tensor(name, list(shape), F32, kind="ExternalOutput").ap()

    def dscr(name, shape, dt):
        if debug:
            return nc.dram_tensor(name, list(shape), dt, kind="ExternalOutput").ap()
        return nc.dram_tensor(name, list(shape), dt).ap()

    C.xp = din("xp", [TP, D]); C.xs = din("xs", [TS, D]); C.mem = din("mem", [256, D])
    C.st_ssm = din("st_ssm", [16 * 2048, 128]); C.st_conv = din("st_conv", [48, 3072]); C.st_hg = din("st_hg", [16 * 1024, 128])
    C.ck = din("ck", [16 * 256, D]); C.cv = din("cv", [16 * 256, D])
    C.w_in = din("w_in", [D, IN_DIM]); C.w_ssd_out = din("w_ssd_out", [2048, D]); C.w_hgrn_out = din("w_hgrn_out", [D, D]); C.w_out = din("w_out", [D, D])
    C.w_cq = din("w_cq", [D, D]); C.w_ck = din("w_ck", [D, D]); C.w_cv = din("w_cv", [D, D]); C.w_co = din("w_co", [D, D])
    C.w_gate = din("w_gate", [D, FFN]); C.w_up = din("w_up", [D, FFN]); C.w_down = din("w_down", [FFN, D])
    C.consts = din("consts", [128, 1024]); C.colvecs = din("colvecs", [128, 176]); C.rowvecs = din("rowvecs", [1, RV_N])
    C.o_y_p = dout("o_y_p", [TP, D]); C.o_y_s = dout("o_y_s", [TS, D]); C.o_ssm_p = dout("o_ssm_p", [2048, 128]); C.o_conv_p = dout("o_conv_p", [3, 3072])
    C.o_hg_p = dout("o_hg_p", [1024, 128]); C.o_mk = dout("o_mk", [256, D]); C.o_mv = dout("o_mv", [256, D])
    C.o_ssm_s = dout("o_ssm_s", [16 * 2048, 128]); C.o_conv_s = dout("o_conv_s", [48, 3072]); C.o_hg_s = dout("o_hg_s", [16 * 1024, 128])
    C.nTd = dscr("nTd", [128, 8, 3 + TT], BF16); C.nTd_r = [Region(f"nTd{i}") for i in range(C.NT + 1)]
    C.YNT = dscr("YNT", [128, 16, TT], BF16); C.YNT_r = [[Region(f"YNT{i}_{g}") for g in range(4)] for i in range(C.NT)]
    C.ONT = dscr("ONT", [128, 8, TT], BF16); C.ONT_r = [Region(f"ONT{i}") for i in range(C.NT)]
    C.X1 = dscr("X1", [TT, D], F32); C.X1_r = [Region(f"X1{i}") for i in range(C.NT)]
    C.X2 = dscr("X2", [TT, D], F32); C.X2_r = [Region(f"X2{i}") for i in range(C.NT)]

    with ExitStack() as st:
        kb = KB(nc, st)
        C.kb = kb
        C.cst = kb.sb("cst", [128, 1024]); kb.dma("sp", C.cst.t[:], C.consts[:, :], writes=[C.cst])
        C.colv = kb.sb("colv", [128, 176]); kb.dma("sp", C.colv.t[:], C.colvecs[:, :], writes=[C.colv])
        C.rowb = kb.sb("rowb", [128, 96]); kb.dma("sp", C.rowb.t[:], C.rowvecs[:, 0:96].partition_broadcast(128), writes=[C.rowb])
        C.identb = kb.sb("identb", [128, 128], BF16); kb.copy("dve", C.identb.t[:], C.cst.t[:, C_ID:C_ID + 128], [C.cst], [C.identb])
        C.a_bc = kb.sb("a_bc", [128, 32])
        kb.act(C.a_bc.t[:], C.rowb.t[:, RV_ALOG:RV_ALOG + 32], AF.Exp, [C.rowb], [C.a_bc])
        kb.ts("dve", C.a_bc.t[:], C.a_bc.t[:], -1.0, None, ALU.mult, None, [C.a_bc], [C.a_bc])
        C.colvh = kb.sb("colvh", [128, 120]); kb.ts("dve", C.colvh.t[:], C.colv.t[:, V_CONVB:V_CONVB + 120], 0.5, None, ALU.mult, None, [C.colv], [C.colvh])
        C.junk = kb.rot("junk", [128, D], BF16, 1)
        C.ss = kb.rot("ss", [128, 4], F32, 4)
        C.xn = kb.rot("xn", [128, D], BF16, 1)
        if upto >= 0:
            pass0(C)
        if upto >= 1 and not C.skip_ssd:
            ssd_pass(C)
        if upto >= 2 and not C.skip_ssd:
            hg_pass(C)
        if upto >= 3:
            merge_pass(C)
        if upto >= 4:
            attn_pass(C)
        if upto >= 5:
            ffn_pass(C)
        kb.finish()
    return nc


def host_inputs(inp, NCH=16):
    f = lambda a: np.ascontiguousarray(np.asarray(a, dtype=np.float32))
    TP = NCH * 128
    cv = np.zeros((128, 176), np.float32)
    col8 = lambda v: f(v).reshape(-1, 128).T
    cv[:, V_NMIX:V_NMIX + 8] = col8(inp["norm_mix"][0]); cv[:, V_NCROSS:V_NCROSS + 8] = col8(inp["norm_cross"][0])
    cv[:, V_NMEM:V_NMEM + 8] = col8(inp["norm_mem"][0]); cv[:, V_NFFN:V_NFFN + 8] = col8(inp["norm_ffn"][0])
    cv[:, V_SNORM:V_SNORM + 16] = col8(inp["ssd_norm"][0]); cv[:, V_HNORM:V_HNORM + 8] = col8(inp["hgrn_norm"][0])
    cv[:, V_CONVB:V_CONVB + 24] = col8(inp["conv_b"][0])
    cw = f(inp["conv_w"][0])
    cv[:, V_CONVW:V_CONVW + 96] = cw.T.reshape(24, 128, 4).transpose(1, 0, 2).reshape(128, 96)
    rv = np.concatenate([f(inp["dt_bias"][0]), f(inp["a_log"][0]), f(inp["d_skip"][0]), f(inp["hgrn_lb"][0]), f(inp["hgrn_lb"][1]), f(inp["norm_final"])])[None, :]
    consts = make_consts()
    shared = dict(w_in=f(inp["w_in"][0]), w_ssd_out=f(inp["w_ssd_out"][0]), w_hgrn_out=f(inp["w_hgrn_out"][0]), w_out=f(inp["w_out"][0]),
                  w_cq=f(inp["w_cq"][0]), w_ck=f(inp["w_ck"][0]), w_cv=f(inp["w_cv"][0]), w_co=f(inp["w_co"][0]),
                  w_gate=f(inp["w_gate"][0]), w_up=f(inp["w_up"][0]), w_down=f(inp["w_down"][0]),
                  consts=consts, colvecs=cv, rowvecs=f(rv))
    maps = []
    for c in range(8):
        sl = slice(16 * c, 16 * c + 16)
        m = dict(shared)
        m["xp"] = f(inp["x_prompt"][c][:TP]); m["xs"] = f(inp["x_sample"][sl]).reshape(64, D); m["mem"] = f(inp["mem_prompt"][c])
        m["st_ssm"] = f(inp["state_ssm"][0, sl]).reshape(16 * 2048, 128); m["st_conv"] = f(inp["state_conv"][0, sl]).reshape(48, 3072)
        m["st_hg"] = f(inp["state_hgrn"][0, sl]).reshape(16 * 1024, 128)
        m["ck"] = f(inp["cache_mem_k"][0, sl]).reshape(16 * 256, D); m["cv"] = f(inp["cache_mem_v"][0, sl]).reshape(16 * 256, D)
        maps.append(m)
    return maps


def kernel(**inputs):
    NCH = 16
    nc = build(NCH)
    maps = host_inputs(inputs, NCH)
    res = run_bass_kernel_spmd(nc, maps, core_ids=list(range(8))).results
    g = lambda k: [np.asarray(r[k], dtype=np.float32) for r in res]
    y_p = np.stack(g("o_y_p")).reshape(8, 2048, D)
    y_s = np.stack(g("o_y_s")).reshape(128, 4, D)
    ssm_p = np.stack(g("o_ssm_p")).reshape(1, 8, 32, 64, 128)
    conv_p = np.stack(g("o_conv_p")).reshape(1, 8, 3, 3072)
    hg_p = np.stack(g("o_hg_p")).reshape(1, 8, 8, 128, 128)
    mk = np.stack(g("o_mk")).reshape(1, 8, 256, 4, 256)
    mv = np.stack(g("o_mv")).reshape(1, 8, 256, 4, 256)
    ssm_s = np.stack(g("o_ssm_s")).reshape(1, 128, 32, 64, 128)
    conv_s = np.stack(g("o_conv_s")).reshape(1, 128, 3, 3072)
    hg_s = np.stack(g("o_hg_s")).reshape(1, 128, 8, 128, 128)
    return (y_p, y_s, ssm_p, conv_p, hg_p, mk, mv, ssm_s, conv_s, hg_s)
```

```python
import numpy as np
from contextlib import ExitStack
import concourse.bass as bass
import concourse.mybir as mybir
from concourse.bass_utils import run_bass_kernel_spmd

F32 = mybir.dt.float32
BF16 = mybir.dt.bfloat16
ALU = mybir.AluOpType
AF = mybir.ActivationFunctionType
AX = mybir.AxisListType

ND = 8
EPS = 1e-6
D = 1024
IN_DIM = 11296
FFN = 2816


class Region:
    __slots__ = ("name", "w", "r")

    def __init__(self, name=""):
        self.name = name
        self.w = None
        self.r = {}


class T:
    __slots__ = ("t", "r")

    def __init__(self, t, r=None):
        self.t = t
        self.r = r if r is not None else Region()


class Rot:
    def __init__(self, items):
        self.items = items
        self.i = 0

    def next(self):
        x = self.items[self.i % len(self.items)]
        self.i += 1
        return x


def _reg(x):
    return x.r if isinstance(x, T) else x


class KB:
    def __init__(self, nc, stack):
        self.nc = nc
        self.stack = stack
        self.engs = {"pe": nc.tensor, "act": nc.scalar, "dve": nc.vector, "pool": nc.gpsimd, "sp": nc.sync}
        self.semh = {}
        self.cnt = {}
        self.known = {e: {} for e in self.engs}
        for e in ("pe", "act", "dve", "pool"):
            self.semh[e] = stack.enter_context(nc.semaphore("s_" + e))
            self.cnt[e] = 0
        self.dcount = {}
        for q in ("sp", "pool"):
            self.dcount[q] = 0
            for i in range(ND):
                self.semh[("d", q, i)] = stack.enter_context(nc.semaphore(f"d_{q}_{i}"))
        self.psall = stack.enter_context(nc.psum_tensor("psall", [128, 4096], F32))
        self.banks = [T(self.psall[:, i * 512:(i + 1) * 512], Region(f"bank{i}")) for i in range(8)]
        self.free = list(range(8))
        self.bi = 0
        self.uid = 0
        self.rec = None

    def sb(self, name, shape, dt=F32, stack=None):
        self.uid += 1
        st = stack if stack is not None else self.stack
        return T(st.enter_context(self.nc.sbuf_tensor(f"{name}_{self.uid}", list(shape), dt)), Region(name))

    def rot(self, name, shape, dt=F32, n=2, stack=None):
        return Rot([self.sb(f"{name}{i}", shape, dt, stack) for i in range(n)])

    def bank(self):
        b = self.free[self.bi % len(self.free)]
        self.bi += 1
        return self.banks[b]

    def reserve(self, n):
        out = [self.free.pop() for _ in range(n)]
        return [self.banks[b] for b in out], out

    def release(self, ids):
        self.free.extend(ids)
        self.free.sort()

    def _wait(self, e, deps):
        kn = self.known[e]
        best = {}
        for d in deps:
            if d is None:
                continue
            k, v = d
            if e == "pe" and k == "pe":
                continue
            if kn.get(k, 0) < v and best.get(k, 0) < v:
                best[k] = v
        for k, v in best.items():
            self.engs[e].wait_ge(self.semh[k], v)
            kn[k] = v

    def _deps(self, reads, writes):
        deps = []
        for r in reads:
            deps.append(_reg(r).w)
        for w in writes:
            w = _reg(w)
            deps.append(w.w)
            deps.extend(w.r.items())
        return deps

    def _mark(self, tok, reads, writes):
        k, v = tok
        for r in reads:
            r = _reg(r)
            if r.r.get(k, 0) < v:
                r.r[k] = v
        for w in writes:
            w = _reg(w)
            w.w = tok
            w.r = {}

    def set_pool(self, ids):
        self.free = list(ids)
        self.bi = 0

    def begin(self, pool):
        self.set_pool(pool)
        self.rec = []

    def end(self):
        r, self.rec = self.rec, None
        return r

    def play(self, lst):
        assert self.rec is None
        for it in lst:
            if it[0] == "op":
                self.op(*it[1:5])
            else:
                self.dma(it[1], it[2], it[3], it[4], it[5], **it[6])

    def op(self, e, emit, reads=(), writes=(), cost=None):
        if self.rec is not None:
            if cost is None:
                if e == "pe":
                    px = _PEProxy()
                    emit(px)
                    cost = px.cost
                else:
                    cost = 0.5
            self.rec.append(("op", e, emit, tuple(reads), tuple(writes), cost))
            return None
        self._wait(e, self._deps(reads, writes))
        inst = emit(self.engs[e])
        self.cnt[e] += 1
        inst.then_inc(self.semh[e], 1)
        self._mark((e, self.cnt[e]), reads, writes)
        return inst

    def dma(self, q, out, in_, reads=(), writes=(), **kw):
        if self.rec is not None:
            self.rec.append(("dma", q, out, in_, tuple(reads), tuple(writes), kw))
            return None
        i = self.dcount[q]
        self.dcount[q] += 1
        key = ("d", q, i % ND)
        prev = 16 * (i // ND)
        deps = self._deps(reads, writes)
        if prev:
            deps.append((key, prev))
        self._wait(q, deps)
        self.engs[q].dma_start(out=out, in_=in_, **kw).then_inc(self.semh[key], 16)
        self._mark((key, prev + 16), reads, writes)

    def _all_tokens(self):
        deps = []
        for q, n in self.dcount.items():
            for s in range(ND):
                cnt = (n - s + ND - 1) // ND if n > s else 0
                if cnt:
                    deps.append((("d", q, s), 16 * cnt))
        for e, c in self.cnt.items():
            if c:
                deps.append((e, c))
        return deps

    def barrier(self):
        deps = self._all_tokens()
        for e in self.engs:
            self._wait(e, deps)

    def finish(self):
        self._wait("sp", self._all_tokens())

    def act(self, out, in_, func, reads, writes, **kw):
        return self.op("act", lambda e: e.activation(out=out, in_=in_, func=func, **kw), reads, writes, cost=_est("act", out))

    def tt(self, eng, out, in0, in1, op, reads, writes):
        return self.op(eng, lambda e: e.tensor_tensor(out=out, in0=in0, in1=in1, op=op), reads, writes, cost=_est(eng, out))

    def ts(self, eng, out, in0, s1, s2, op0, op1, reads, writes):
        if s2 is None:
            return self.op(eng, lambda e: e.tensor_scalar(out=out, in0=in0, scalar1=s1, scalar2=None, op0=op0), reads, writes, cost=_est(eng, out))
        return self.op(eng, lambda e: e.tensor_scalar(out=out, in0=in0, scalar1=s1, scalar2=s2, op0=op0, op1=op1), reads, writes, cost=_est(eng, out))

    def stt(self, out, in0, scalar, in1, op0, op1, reads, writes):
        return self.op("dve", lambda e: e.scalar_tensor_tensor(out=out, in0=in0, scalar=scalar, in1=in1, op0=op0, op1=op1), reads, writes, cost=_est("dve", out))

    def copy(self, eng, out, in_, reads, writes):
        if eng == "act":
            return self.op("act", lambda e: e.copy(out=out, in_=in_), reads, writes, cost=_est("act", out))
        return self.op(eng, lambda e: e.tensor_copy(out=out, in_=in_), reads, writes, cost=_est(eng, out))

    def memset(self, eng, ap, val, writes):
        return self.op(eng, lambda e: e.memset(ap, val), (), writes)


def _free(ap):
    n = 1
    for d in ap.shape[1:]:
        n *= int(d)
    return n


def _est(eng, out):
    n = _free(out)
    if eng == "act":
        return 0.2 + n / 1200.0
    if eng == "dve":
        return 0.2 + n / 960.0
    return 0.25 + n / 450.0


class _PEProxy:
    def __init__(self):
        self.cost = 0.0

    def matmul(self, out, lhsT=None, rhs=None, **kw):
        p = 4 if lhsT.dtype == F32 else 1
        self.cost += 0.03 + max(_free(out), 32) * p / 2400.0
        return self

    def transpose(self, out, in_, ident, **kw):
        p = 4 if in_.dtype == F32 else 1
        self.cost += 0.03 + max(_free(out), 32) * p / 2400.0
        return self


def schedule(*streams):
    ops = [it for st_ in streams for it in st_]
    n = len(ops)
    engs, R, W, cost = [], [], [], []
    for it in ops:
        if it[0] == "op":
            engs.append(it[1]); R.append([_reg(x) for x in it[3]]); W.append([_reg(x) for x in it[4]]); cost.append(it[5])
        else:
            engs.append(it[1]); R.append([_reg(x) for x in it[4]]); W.append([_reg(x) for x in it[5]]); cost.append(2.0)
    preds = [set() for _ in range(n)]
    lastw, readers = {}, {}
    laste = {}
    k = 0
    for si, st_ in enumerate(streams):
        for _ in st_:
            i = k
            k += 1
            for r in R[i]:
                if id(r) in lastw:
                    preds[i].add(lastw[id(r)])
            for w in W[i]:
                if id(w) in lastw:
                    preds[i].add(lastw[id(w)])
                preds[i].update(readers.get(id(w), ()))
            for r in R[i]:
                readers.setdefault(id(r), []).append(i)
            for w in W[i]:
                lastw[id(w)] = i
                readers[id(w)] = []
            key = (si, engs[i])
            if key in laste:
                preds[i].add(laste[key])
            laste[key] = i
            preds[i].discard(i)
    succs = [[] for _ in range(n)]
    indeg = [len(p) for p in preds]
    for i, p in enumerate(preds):
        for j in p:
            succs[j].append(i)
    ready = [i for i in range(n) if indeg[i] == 0]
    efree = {}
    fin = [0.0] * n
    out = []
    LAT = 0.3
    while ready:
        best, bi = None, None
        for i in ready:
            e = engs[i]
            t = efree.get(e, 0.0)
            for j in preds[i]:
                tj = fin[j] + (0.0 if engs[j] == e else LAT)
                if tj > t:
                    t = tj
            if best is None or (t, i) < best:
                best, bi = (t, i), i
        ready.remove(bi)
        e = engs[bi]
        t0 = best[0]
        if ops[bi][0] == "dma":
            efree[e] = t0 + 0.1
            fin[bi] = t0 + 2.0
        else:
            fin[bi] = t0 + cost[bi]
            efree[e] = fin[bi]
        out.append(ops[bi])
        for j in succs[bi]:
            indeg[j] -= 1
            if indeg[j] == 0:
                ready.append(j)
    assert len(out) == n
    return out


def interleave(*lists):
    lists = [l for l in lists if l]
    idx = [0] * len(lists)
    out = []
    total = sum(len(l) for l in lists)
    while len(out) < total:
        best, bk = None, None
        for k, l in enumerate(lists):
            if idx[k] < len(l):
                key = (idx[k] + 0.5) / len(l)
                if best is None or key < best:
                    best, bk = key, k
        out.append(lists[bk][idx[bk]])
        idx[bk] += 1
    return out


def bc(ap, shape, axis):
    return ap.unsqueeze(axis).to_broadcast(list(shape))


C_ID, C_TPI, C_TPA, C_ONE, C_HPI, C_HPA, C_TSI, C_TSA, C_SSM, C_SEGP, C_SEGS = 0, 128, 256, 384, 512, 640, 768, 832, 896, 960, 964


def make_consts():
    c = np.zeros((128, 1024), np.float32)
    i = np.arange(128)
    c[:, C_ID:C_ID + 128] = np.eye(128)
    c[:, C_TPI:C_TPI + 128] = (i[:, None] <= i[None, :])
    c[:, C_TPA:C_TPA + 128] = (i[:, None] > i[None, :])
    c[:, C_ONE:C_ONE + 128] = 1.0
    s32 = i // 32
    same = s32[:, None] == s32[None, :]
    c[:, C_HPI:C_HPI + 128] = same & (i[:, None] <= i[None, :])
    c[:, C_HPA:C_HPA + 128] = same & (i[:, None] > i[None, :])
    j = np.arange(64)
    s4 = j // 4
    same4 = s4[:, None] == s4[None, :]
    c[:64, C_TSI:C_TSI + 64] = same4 & (j[:, None] <= j[None, :])
    c[:64, C_TSA:C_TSA + 64] = same4 & (j[:, None] > j[None, :])
    c[:64, C_SSM:C_SSM + 64] = same4
    c[:, C_SEGP:C_SEGP + 4] = (s32[:, None] == np.arange(4)[None, :])
    c[:64, C_SEGS:C_SEGS + 16] = (s4[:, None] == np.arange(16)[None, :])
    return c


V_NMIX, V_NCROSS, V_NMEM, V_NFFN, V_SNORM, V_HNORM, V_CONVB, V_CONVW = 0, 8, 16, 24, 32, 48, 56, 80
RV_DTB, RV_ALOG, RV_DSKIP, RV_LB0, RV_LB1, RV_NFIN = 0, 32, 64, 96, 1120, 2144
RV_N = 3168


class Ctx:
    pass


def load_w(C, name, src, ncols, nk=8, stack=None):
    kb = C.kb
    w = kb.sb(name, [128, nk, ncols], BF16, stack)
    for k in range(nk):
        kb.dma("pool", w.t[:, k, :], src[k * 128:(k + 1) * 128, :], writes=[w])
    return w


def rms_rstd(C, xt, q, ss, junk=None):
    kb = C.kb
    junk = junk if junk is not None else C.junk.next()
    kb.act(junk.t[:q], xt.t[:q], AF.Square, [xt], [junk, ss], accum_out=ss.t[:q, 1:2])
    kb.ts("dve", ss.t[:q, 2:3], ss.t[:q, 1:2], 1.0 / D, EPS, ALU.mult, ALU.add, [ss], [ss])
    kb.act(ss.t[:q, 1:2], ss.t[:q, 2:3], AF.Sqrt, [ss], [ss])
    kb.op("dve", lambda e: e.reciprocal(out=ss.t[:q, 0:1], in_=ss.t[:q, 1:2]), [ss], [ss])


def rms_to_T(C, xt, q, wcol, dst_ap, dst, scratch=None):
    kb = C.kb
    ss = scratch["ss"] if scratch else C.ss.next()
    rms_rstd(C, xt, q, ss, scratch["junk"] if scratch else None)
    xn = scratch["xn"] if scratch else C.xn.next()
    kb.act(xn.t[:q], xt.t[:q], AF.Copy, [xt, ss], [xn], scale=ss.t[:q, 0:1])
    to_T(C, xn, q, 8, wcol, dst_ap, dst)


def to_T(C, src, q, nk, wcol, dst_ap, dst, eng="dve"):
    kb = C.kb
    for k0 in range(0, nk, 8):
        n = min(8, nk - k0)
        b = kb.bank()
        pb = b.t.bitcast(BF16)

        def emit(e, k0=k0, n=n, pb=pb):
            for k in range(n):
                last = e.transpose(pb[:, k * 128:k * 128 + q], src.t[:q, (k0 + k) * 128:(k0 + k + 1) * 128], C.identb.t[:q, :q])
            return last
        kb.op("pe", emit, [src, C.identb], [b])
        pv = pb.rearrange("p (k t) -> p k t", t=128)[:, 0:n, 0:q]
        if wcol is None:
            kb.copy("act", dst_ap[:, k0:k0 + n, :], pv, [b], [dst])
        else:
            kb.tt(eng, dst_ap[:, k0:k0 + n, :], pv, bc(wcol[:, k0:k0 + n], [128, n, q], 2), ALU.mult, [b, C.colv], [dst])


def pass0(C):
    kb = C.kb
    with ExitStack() as st:
        xin = kb.rot("xin", [128, D], F32, 2, st)
        nts = kb.rot("nts", [128, 8, 128], BF16, 2, st)
        z = kb.sb("zero3", [128, 8, 3], BF16, st)
        kb.memset("pool", z.t[:], 0.0, [z])
        kb.dma("sp", C.nTd[:, :, 0:3], z.t[:], reads=[z], writes=[C.nTd_r[0]])
        for ti in range(C.NT):
            q = 128 if ti < C.NCH else 64
            src = C.xp[ti * 128:(ti + 1) * 128, :] if ti < C.NCH else C.xs[:, :]
            xt = xin.next()
            kb.dma("sp", xt.t[:q], src, writes=[xt])
            nt = nts.next()
            rms_to_T(C, xt, q, C.colv.t[:, V_NMIX:V_NMIX + 8], nt.t[:, :, :q], nt)
            kb.dma("sp", C.nTd[:, :, 3 + ti * 128:3 + ti * 128 + q], nt.t[:, :, :q], reads=[nt], writes=[C.nTd_r[ti + 1]])
        kb.barrier()


def ssd_small(C, q, sm, nTc, tcol, Wdt, tri_inc, tri_after, same):
    kb = C.kb
    b = kb.bank()

    def emit(e):
        for k in range(8):
            last = e.matmul(b.t[:q, 0:32], lhsT=nTc.t[:, k, tcol:tcol + q], rhs=Wdt.t[:, k, :], start=(k == 0), stop=(k == 7))
        return last
    kb.op("pe", emit, [nTc, Wdt], [b])
    kb.tt("dve", sm.t[:q, 0:32], b.t[:q, 0:32], C.rowb.t[:q, RV_DTB:RV_DTB + 32], ALU.add, [b, C.rowb], [sm])
    kb.act(sm.t[:q, 32:64], sm.t[:q, 0:32], AF.Exp, [sm], [sm])
    kb.act(sm.t[:q, 64:96], sm.t[:q, 32:64], AF.Ln, [sm], [sm], bias=1.0)
    kb.tt("dve", sm.t[:q, 96:128], sm.t[:q, 64:96], C.a_bc.t[:q, :], ALU.mult, [sm, C.a_bc], [sm])
    b2 = kb.bank()

    def emit2(e):
        e.matmul(b2.t[:q, 0:32], lhsT=tri_inc, rhs=sm.t[:q, 96:128], start=True, stop=True)
        e.matmul(b2.t[:q, 32:64], lhsT=tri_after, rhs=sm.t[:q, 96:128], start=True, stop=True)
        return e.matmul(b2.t[:q, 64:96], lhsT=same, rhs=sm.t[:q, 96:128], start=True, stop=True)
    kb.op("pe", emit2, [sm, C.cst], [b2])
    kb.act(sm.t[:q, 128:224], b2.t[:q, 0:96], AF.Exp, [b2], [sm])
    kb.tt("dve", sm.t[:q, 224:256], sm.t[:q, 160:192], sm.t[:q, 64:96], ALU.mult, [sm], [sm])


def ssd_conv(C, q, pre_view, acc, BCT, eng_split=3):
    kb = C.kb
    cw = C.colvh.t
    WB, WW = 0, V_CONVW - V_CONVB
    for cc in range(24):
        o = acc["view"](cc)
        kb.act(o, pre_view(cc, 0), AF.Identity, [acc["pre"], C.colvh], [acc["T"]], scale=cw[:, WW + cc * 4:WW + cc * 4 + 1], bias=cw[:, WB + cc:WB + cc + 1])
    for cc in range(24):
        o = acc["view"](cc)
        for j in range(1, 4):
            kb.stt(o, pre_view(cc, j), cw[:, WW + cc * 4 + j:WW + cc * 4 + j + 1], o, ALU.mult, ALU.add, [acc["pre"], C.colvh, acc["T"]], [acc["T"]])
    a = acc["T"]
    th = acc["tanh"]
    kb.act(th, a.t[:, :, :q], AF.Tanh, [a], [acc["pre"]])
    kb.stt(a.t[:, :, :q], th, 1.0, a.t[:, :, :q], ALU.add, ALU.mult, [acc["pre"], a], [a])
    kb.copy(acc.get("cast_eng", "pool"), BCT.t[:, :, :q], a.t[:, 16:24, :q], [a], [BCT])


def ssd_group_front(C, q, g, a, sm, W):
    kb = C.kb
    bx = kb.bank()

    def emit(e):
        for j in range(4):
            last = e.transpose(bx.t[:q, j * 128:(j + 1) * 128], a.t[:, g * 4 + j, :q], C.cst.t[:, C_ID:C_ID + 128])
        return last
    kb.op("pe", emit, [a, C.cst], [bx])
    bx3 = bx.t[:q, :].rearrange("p (h d) -> p h d", d=64)
    v3 = lambda ap: ap.rearrange("p (h d) -> p h d", d=64)
    kb.tt("dve", v3(W["xdt"].t[:q, g * 512:(g + 1) * 512]), bx3, bc(sm.t[:q, 64 + g * 8:72 + g * 8], [q, 8, 64], 2), ALU.mult, [bx, sm], [W["xdt"]])
    kb.tt("dve", v3(W["xw"].t[:q, g * 512:(g + 1) * 512]), bx3, bc(sm.t[:q, 224 + g * 8:232 + g * 8], [q, 8, 64], 2), ALU.mult, [bx, sm], [W["xw"]])
    kb.tt("dve", v3(W["xsd"].t[:q, g * 512:(g + 1) * 512]), bx3, bc(C.rowb.t[:q, RV_DSKIP + g * 8:RV_DSKIP + g * 8 + 8], [q, 8, 64], 2), ALU.mult, [bx, C.rowb], [W["xsd"]])


def ssd_group_y(C, q, g, sm, W, BCT, CBm, tri_inc, tri_after, byi, nTc, tcol, Wz, tok0, ci):
    kb = C.kb
    v3 = lambda ap: ap.rearrange("p (h d) -> p h d", d=64)
    tmp = W["tmp"].next()
    kb.tt("dve", v3(tmp.t[:q, :]), v3(byi.t[:q, :]), bc(sm.t[:q, 128 + g * 8:136 + g * 8], [q, 8, 64], 2), ALU.mult, [byi, sm], [tmp])
    by = kb.bank()
    kb.op("pe", lambda e: e.matmul(by.t[:q, :], lhsT=C.identb.t[:q, :q], rhs=W["xsd"].t[:q, g * 512:(g + 1) * 512], start=True, stop=False), [C.identb, W["xsd"]], [by])
    for hh in range(2):
        Rt = W["R"].next()
        kb.tt("dve", Rt.t[:q, :, :q], bc(tri_inc, [q, 4, q], 1), bc(sm.t[:q, 96 + g * 8 + hh * 4:100 + g * 8 + hh * 4], [q, 4, q], 2), ALU.mult, [C.cst, sm], [Rt])
        bL = kb.bank()
        kb.op("pe", lambda e, Rt=Rt, bL=bL: [e.matmul(bL.t[:q, h * q:(h + 1) * q], lhsT=tri_after, rhs=Rt.t[:q, h, :q], start=True, stop=True) for h in range(4)][-1], [Rt, C.cst], [bL])
        Lt = W["L"].next()
        kb.act(Lt.t[:q, 0:4 * q], bL.t[:q, 0:4 * q], AF.Exp, [bL], [Lt])
        MT = W["MT"].next()
        kb.tt("dve", MT.t[:q, :, :q], Lt.t[:q, 0:4 * q].rearrange("p (h t) -> p h t", t=q), bc(CBm.t[:q, g, :q], [q, 4, q], 1), ALU.mult, [Lt, CBm], [MT])

        def emit(e, MT=MT, hh=hh):
            for h in range(4):
                c0 = (hh * 4 + h) * 64
                last = e.matmul(by.t[:q, c0:c0 + 64], lhsT=MT.t[:q, h, :q], rhs=W["xdt"].t[:q, g * 512 + c0:g * 512 + c0 + 64], start=False, stop=False)
            return last
        kb.op("pe", emit, [MT, W["xdt"]], [by])
    kb.op("pe", lambda e: e.matmul(by.t[:q, :], lhsT=C.identb.t[:q, :q], rhs=tmp.t[:q, :], start=False, stop=True), [C.identb, tmp], [by])
    bz = kb.bank()
    kb.op("pe", lambda e: [e.matmul(bz.t[:q, :], lhsT=nTc.t[:, k, tcol:tcol + q], rhs=Wz.t[:, k, g * 512:(g + 1) * 512], start=(k == 0), stop=(k == 7)) for k in range(8)][-1], [nTc, Wz], [bz])
    sz = W["sz"].next()
    kb.act(sz.t[:q, :], bz.t[:q, :], AF.Tanh, [bz], [sz], scale=0.5)
    kb.stt(sz.t[:q, :], sz.t[:q, :], 1.0, bz.t[:q, :], ALU.add, ALU.mult, [sz, bz], [sz])
    y = W["y"].next()
    kb.tt("dve", y.t[:q, :], by.t[:q, :], sz.t[:q, :], ALU.mult, [by, sz], [y])
    ms = C.ss.next()
    kb.act(sz.t[:q, :], y.t[:q, :], AF.Square, [y], [sz, ms], accum_out=ms.t[:q, 1:2])
    kb.ts("dve", ms.t[:q, 2:3], ms.t[:q, 1:2], 1.0 / 512, 4.0 * EPS, ALU.mult, ALU.add, [ms], [ms])
    kb.act(ms.t[:q, 1:2], ms.t[:q, 2:3], AF.Sqrt, [ms], [ms])
    kb.op("dve", lambda e: e.reciprocal(out=ms.t[:q, 0:1], in_=ms.t[:q, 1:2]), [ms], [ms])
    yn = W["yn"].next()
    kb.act(yn.t[:q, :], y.t[:q, :], AF.Copy, [y, ms], [yn], scale=ms.t[:q, 0:1])
    ynT = W["ynT"].next()
    to_T(C, yn, q, 4, C.colv.t[:, V_SNORM + g * 4:V_SNORM + g * 4 + 4], ynT.t[:, :, :q], ynT)
    kb.dma("sp", C.YNT[:, g * 4:(g + 1) * 4, tok0:tok0 + q], ynT.t[:, :, :q], reads=[ynT], writes=[C.YNT_r[ci][g]])


def ssd_pass(C):
    kb = C.kb
    NCH = C.NCH
    cst = C.cst.t
    with ExitStack() as stw:
        Wz = load_w(C, "Wz", C.w_in[:, 0:2048], 2048, stack=stw)
        Wx = load_w(C, "Wx", C.w_in[:, 2048:5120], 3072, stack=stw)
        Wdt = load_w(C, "Wdt", C.w_in[:, 5120:5152], 32, stack=stw)
        with ExitStack() as st:
            q = 128
            nTr = kb.rot("nTc", [128, 8, 131], BF16, 2, st)
            pre = kb.sb("pre", [128, 24, 131], F32, st)
            accr = kb.rot("acc", [128, 24, 128], F32, 1, st)
            BCTr = kb.rot("BCT", [128, 8, 128], BF16, 2, st)
            CBmr = kb.rot("CBm", [128, 4, 128], F32, 2, st)
            Btokr = kb.rot("Btok", [128, 512], BF16, 2, st)
            smr = kb.rot("sm", [128, 256], F32, 2, st)
            MTs = [[kb.sb(f"MT{i}_{g}", [128, 8, 128], BF16, st) for g in range(4)] for i in range(2)]
            szs = [[kb.sb(f"sz{i}_{g}", [128, 512], BF16, st) for g in range(4)] for i in range(2)]
            Rr = kb.rot("R", [128, 4, 128], F32, 2, st)
            Lr = kb.rot("L", [128, 512], F32, 2, st)
            xdt = [kb.sb(f"xdt{g}", [128, 512], BF16, st) for g in range(4)]
            xw = [kb.sb(f"xw{g}", [128, 512], BF16, st) for g in range(4)]
            xsd = [kb.sb(f"xsd{g}", [128, 512], BF16, st) for g in range(4)]
            hT = [kb.sb(f"hT{g}", [128, 512], F32, st) for g in range(4)]
            hTb = [kb.sb(f"hTb{g}", [128, 512], BF16, st) for g in range(4)]
            SW = [dict(y=kb.rot(f"y{i}", [128, 512], F32, 2, st), yn=kb.rot(f"yn{i}", [128, 512], BF16, 2, st), ynT=kb.rot(f"ynT{i}", [128, 4, 128], BF16, 2, st),
                       tmp=kb.rot(f"ytmp{i}", [128, 512], BF16, 2, st)) for i in range(2)]
            for g in range(4):
                kb.memset("pool", hT[g].t[:], 0.0, [hT[g]])
                kb.memset("pool", hTb[g].t[:], 0.0, [hTb[g]])
            tri_inc, tri_after, ones = cst[:, C_TPI:C_TPI + 128], cst[:, C_TPA:C_TPA + 128], cst[:, C_ONE:C_ONE + 128]
            v3 = lambda ap: ap.rearrange("p (h d) -> p h d", d=64)

            def Fa(c, nTc, sm, a, BCT, CBm, Btok, MTa, sza):
                kb.dma("sp", nTc.t[:, :, :], C.nTd[:, :, c * 128:c * 128 + 131], reads=[C.nTd_r[c], C.nTd_r[c + 1]], writes=[nTc])
                ssd_small(C, q, sm, nTc, 3, Wdt, tri_inc, tri_after, ones)
                for g in range(4):
                    for hh in range(2):
                        h0 = g * 8 + hh * 4
                        Rt = Rr.next()
                        kb.tt("dve", Rt.t[:, :, :], bc(tri_inc, [128, 4, 128], 1), bc(sm.t[:, 96 + h0:100 + h0], [128, 4, 128], 2), ALU.mult, [C.cst, sm], [Rt])
                        bL = kb.bank()
                        kb.op("pe", lambda e, Rt=Rt, bL=bL: [e.matmul(bL.t[:, h * 128:(h + 1) * 128], lhsT=tri_after, rhs=Rt.t[:, h, :], start=True, stop=True) for h in range(4)][-1], [Rt, C.cst], [bL])
                        kb.act(MTa[g].t[:, hh * 4:hh * 4 + 4, :], bL.t[:, :].rearrange("p (h t) -> p h t", t=128), AF.Exp, [bL], [MTa[g]])

            def Fc(c, nTc, sm, a, BCT, CBm, Btok, MTa, sza):
                for g in range(4):
                    bz = kb.bank()
                    kb.op("pe", lambda e, bz=bz, g=g: [e.matmul(bz.t[:, :], lhsT=nTc.t[:, k, 3:131], rhs=Wz.t[:, k, g * 512:(g + 1) * 512], start=(k == 0), stop=(k == 7)) for k in range(8)][-1], [nTc, Wz], [bz])
                    zt_ = Lr.next()
                    kb.act(zt_.t[:, :], bz.t[:, :], AF.Tanh, [bz], [zt_], scale=0.5)
                    kb.stt(sza[g].t[:, :], zt_.t[:, :], 1.0, bz.t[:, :], ALU.add, ALU.mult, [zt_, bz], [sza[g]])

            def Fb(c, nTc, sm, a, BCT, CBm, Btok, MTa, sza):
                for grp in range(8):
                    b = kb.bank()

                    def emit(e, grp=grp, b=b):
                        for j in range(3):
                            cc = grp * 3 + j
                            for k in range(8):
                                last = e.matmul(b.t[:, j * 131:(j + 1) * 131], lhsT=Wx.t[:, k, cc * 128:(cc + 1) * 128], rhs=nTc.t[:, k, 0:131], start=(k == 0), stop=(k == 7))
                        return last
                    kb.op("pe", emit, [Wx, nTc], [b])
                    kb.act(pre.t[:, grp * 3:grp * 3 + 3, :], b.t[:, 0:393].rearrange("p (j t) -> p j t", t=131), AF.Copy, [b], [pre])
                if c == NCH - 1:
                    for s_ in range(6):
                        b = kb.bank()
                        kb.op("pe", lambda e, b=b, s_=s_: [e.matmul(b.t[:3, :], lhsT=nTc.t[:, k, 128:131], rhs=Wx.t[:, k, s_ * 512:(s_ + 1) * 512], start=(k == 0), stop=(k == 7)) for k in range(8)][-1], [Wx, nTc], [b])
                        cv_ = Lr.next()
                        kb.copy("act", cv_.t[:3, :], b.t[:3, :], [b], [cv_])
                        kb.dma("sp", C.o_conv_p[:, s_ * 512:(s_ + 1) * 512], cv_.t[:3, :], reads=[cv_])
                accd = {"T": a, "pre": pre, "view": lambda cc: a.t[:, cc, :], "tanh": pre.t[:, :, 0:128], "cast_eng": "act"}
                ssd_conv(C, q, lambda cc, j: pre.t[:, cc, j:j + 128], accd, BCT)
                b = kb.bank()
                kb.op("pe", lambda e, b=b: [e.matmul(b.t[:, g * 128:(g + 1) * 128], lhsT=BCT.t[:, g, :], rhs=BCT.t[:, 4 + g, :], start=True, stop=True) for g in range(4)][-1], [BCT], [b])
                kb.tt("dve", CBm.t[:, :, :], b.t[:, :].rearrange("p (g t) -> p g t", t=128), bc(tri_inc, [128, 4, 128], 1), ALU.mult, [b, C.cst], [CBm])
                b2 = kb.bank()
                kb.op("pe", lambda e, b2=b2: [e.transpose(b2.t[:, g * 128:(g + 1) * 128], a.t[:, 16 + g, :], cst[:, C_ID:C_ID + 128]) for g in range(4)][-1], [a, C.cst], [b2])
                kb.copy("act", Btok.t[:, :], b2.t[:, :], [b2], [Btok])
                for g in range(4):
                    kb.tt("dve", MTa[g].t[:, :, :], MTa[g].t[:, :, :], bc(CBm.t[:, g, :], [128, 8, 128], 1), ALU.mult, [MTa[g], CBm], [MTa[g]])

            def Gfront(c, g, Wk, nTc, sm, a, BCT, CBm, Btok, MTa, sza):
                bx = kb.bank()
                kb.op("pe", lambda e: [e.transpose(bx.t[:, j * 128:(j + 1) * 128], a.t[:, g * 4 + j, :], cst[:, C_ID:C_ID + 128]) for j in range(4)][-1], [a, C.cst], [bx])
                bx3 = v3(bx.t[:, :])
                kb.tt("dve", v3(xdt[g].t[:, :]), bx3, bc(sm.t[:, 64 + g * 8:72 + g * 8], [128, 8, 64], 2), ALU.mult, [bx, sm], [xdt[g]])
                kb.tt("dve", v3(xw[g].t[:, :]), bx3, bc(sm.t[:, 224 + g * 8:232 + g * 8], [128, 8, 64], 2), ALU.mult, [bx, sm], [xw[g]])
                kb.tt("dve", v3(xsd[g].t[:, :]), bx3, bc(C.rowb.t[:, RV_DSKIP + g * 8:RV_DSKIP + g * 8 + 8], [128, 8, 64], 2), ALU.mult, [bx, C.rowb], [xsd[g]])

            def Ggroup(c, g, Wk, nTc, sm, a, BCT, CBm, Btok, MTa, sza):
                byi = kb.bank()
                kb.op("pe", lambda e: e.matmul(byi.t[:, :], lhsT=BCT.t[:, 4 + g, :], rhs=hTb[g].t[:, :], start=True, stop=True), [BCT, hTb[g]], [byi])
                bd = kb.bank()
                kb.op("pe", lambda e: e.matmul(bd.t[:, :], lhsT=Btok.t[:, g * 128:(g + 1) * 128], rhs=xw[g].t[:, :], start=True, stop=True), [Btok, xw[g]], [bd])
                tmp = Wk["tmp"].next()
                kb.tt("dve", v3(tmp.t[:, :]), v3(byi.t[:, :]), bc(sm.t[:, 128 + g * 8:136 + g * 8], [128, 8, 64], 2), ALU.mult, [byi, sm], [tmp])
                hv = v3(hT[g].t[:, :])
                kb.tt("dve", hv, hv, bc(sm.t[:, 192 + g * 8:200 + g * 8], [128, 8, 64], 2), ALU.mult, [hT[g], sm], [hT[g]])
                kb.tt("dve", hT[g].t[:, :], hT[g].t[:, :], bd.t[:, :], ALU.add, [hT[g], bd], [hT[g]])
                by = kb.bank()

                def emit(e):
                    e.matmul(by.t[:, :], lhsT=C.identb.t[:, :], rhs=xsd[g].t[:, :], start=True, stop=False)
                    for h in range(8):
                        e.matmul(by.t[:, h * 64:(h + 1) * 64], lhsT=MTa[g].t[:, h, :], rhs=xdt[g].t[:, h * 64:(h + 1) * 64], start=False, stop=False)
                    return e.matmul(by.t[:, :], lhsT=C.identb.t[:, :], rhs=tmp.t[:, :], start=False, stop=True)
                kb.op("pe", emit, [C.identb, xsd[g], MTa[g], xdt[g], tmp], [by])
                y = Wk["y"].next()
                kb.tt("dve", y.t[:, :], by.t[:, :], sza[g].t[:, :], ALU.mult, [by, sza[g]], [y])
                ms = C.ss.next()
                yn = Wk["yn"].next()
                kb.act(yn.t[:, :], y.t[:, :], AF.Square, [y], [yn, ms], accum_out=ms.t[:, 1:2])
                kb.ts("dve", ms.t[:, 2:3], ms.t[:, 1:2], 1.0 / 512, 4.0 * EPS, ALU.mult, ALU.add, [ms], [ms])
                kb.act(ms.t[:, 1:2], ms.t[:, 2:3], AF.Sqrt, [ms], [ms])
                kb.op("dve", lambda e: e.reciprocal(out=ms.t[:, 0:1], in_=ms.t[:, 1:2]), [ms], [ms])
                kb.act(yn.t[:, :], y.t[:, :], AF.Copy, [y, ms], [yn], scale=ms.t[:, 0:1])
                ynT = Wk["ynT"].next()
                to_T(C, yn, q, 4, C.colv.t[:, V_SNORM + g * 4:V_SNORM + g * 4 + 4], ynT.t[:, :, :], ynT)
                kb.dma("sp", C.YNT[:, g * 4:(g + 1) * 4, c * 128:c * 128 + 128], ynT.t[:, :, :], reads=[ynT], writes=[C.YNT_r[c][g]])
                kb.copy("act", hTb[g].t[:, :], hT[g].t[:, :], [hT[g]], [hTb[g]])

            sets = [(nTr.next(), smr.next(), accr.next(), BCTr.next(), CBmr.next(), Btokr.next(), MTs[i], szs[i]) for i in range(2)]
            GA, GB = [4, 5], [6, 7]

            def Flists(c, S_):
                kb.begin([0]); Fa(c, *S_); la = kb.end()
                kb.begin([3]); Fc(c, *S_); lc = kb.end()
                kb.begin([1, 2]); Fb(c, *S_); lb2 = kb.end()
                return [la, lc, lb2]
            kb.play(schedule(*Flists(0, sets[0])))
            for c in range(NCH):
                S_ = sets[c % 2]
                kb.begin(GA); Gfront(c, 0, SW[0], *S_); Gfront(c, 1, SW[0], *S_); Ggroup(c, 0, SW[0], *S_); Ggroup(c, 1, SW[0], *S_); ga = kb.end()
                kb.begin(GB); Gfront(c, 2, SW[1], *S_); Gfront(c, 3, SW[1], *S_); Ggroup(c, 2, SW[1], *S_); Ggroup(c, 3, SW[1], *S_); gb = kb.end()
                fl = Flists(c + 1, sets[(c + 1) % 2]) if c + 1 < NCH else []
                kb.play(schedule(ga, gb, *fl))
            kb.set_pool(range(8))
            so = T(pre.t[:].rearrange("p c t -> p (c t)")[:, 0:2048].rearrange("p (j n) -> p j n", n=128), pre.r)
            for j4 in range(4):
                b = kb.bank()
                kb.op("pe", lambda e, b=b, j4=j4: [e.transpose(b.t[:, jj * 128:(jj + 1) * 128], hT[j4].t[:, jj * 128:(jj + 1) * 128], cst[:, C_ID:C_ID + 128]) for jj in range(4)][-1], [hT[j4], C.cst], [b])
                kb.copy("act", so.t[:, j4 * 4:(j4 + 1) * 4, :], b.t[:, :].rearrange("p (j n) -> p j n", n=128), [b], [so])
            kb.dma("sp", C.o_ssm_p.rearrange("(j q) n -> q j n", q=128), so.t[:, :, :], reads=[so])
            kb.barrier()
        with ExitStack() as st:
            q = 64
            tri_inc, tri_after, same = cst[:64, C_TSI:C_TSI + 64], cst[:64, C_TSA:C_TSA + 64], cst[:64, C_SSM:C_SSM + 64]
            nTc = kb.sb("nTs", [128, 8, 64], BF16, st)
            kb.dma("sp", nTc.t[:, :, :], C.nTd[:, :, 3 + C.TP:3 + C.TP + 64], reads=[C.nTd_r[C.NT]], writes=[nTc])
            pre = kb.sb("pres", [128, 24, 16, 7], F32, st)
            a = kb.sb("accs", [128, 24, 64], F32, st)
            BCT = kb.sb("BCTs", [128, 8, 64], BF16, st)
            CBm = kb.sb("CBms", [64, 4, 64], F32, st)
            Btok = kb.sb("Btoks", [64, 512], BF16, st)
            sm = kb.sb("sms", [64, 256], F32, st)
            W = {
                "xdt": kb.sb("xdts", [64, 2048], BF16, st), "xw": kb.sb("xws", [64, 2048], BF16, st), "xsd": kb.sb("xsds", [64, 2048], BF16, st),
                "R": kb.rot("Rs", [64, 4, 64], F32, 2, st), "L": kb.rot("Ls", [64, 256], F32, 2, st), "MT": kb.rot("MTs", [64, 4, 64], BF16, 2, st),
                "y": kb.rot("ys", [64, 512], F32, 2, st), "sz": kb.rot("szs", [64, 512], F32, 2, st), "yn": kb.rot("yns", [64, 512], BF16, 2, st),
                "ynT": kb.rot("ynTs", [128, 4, 64], BF16, 2, st), "tmp": kb.rot("ytmps", [64, 512], BF16, 2, st),
            }
            cnvr = kb.rot("cnvs", [64, 512], F32, 2, st)
            ocs = C.o_conv_s.rearrange("(b j) c -> b j c", j=3)
            scr = kb.rot("stconv", [48, 512], F32, 2, st)
            for g6 in range(6):
                sc = scr.next()
                kb.dma("sp", sc.t[:, :], C.st_conv[:, g6 * 512:(g6 + 1) * 512], writes=[sc])
                b = kb.bank()
                kb.op("pe", lambda e, b=b, sc=sc: [e.transpose(b.t[:, jj * 48:(jj + 1) * 48], sc.t[:48, jj * 128:(jj + 1) * 128], cst[:48, C_ID:C_ID + 48]) for jj in range(4)][-1], [sc, C.cst], [b])
                kb.act(pre.t[:, g6 * 4:(g6 + 1) * 4, :, 0:3], b.t[:, 0:192].rearrange("p (c b j) -> p c b j", c=4, j=3), AF.Copy, [b], [pre])
            ssd_small(C, q, sm, nTc, 0, Wdt, tri_inc, tri_after, same)
            for grp in range(8):
                b = kb.bank()

                def emit(e, grp=grp, b=b):
                    for j in range(3):
                        cc = grp * 3 + j
                        for k in range(8):
                            last = e.matmul(b.t[:, j * 64:(j + 1) * 64], lhsT=Wx.t[:, k, cc * 128:(cc + 1) * 128], rhs=nTc.t[:, k, 0:64], start=(k == 0), stop=(k == 7))
                    return last
                kb.op("pe", emit, [Wx, nTc], [b])
                kb.act(pre.t[:, grp * 3:grp * 3 + 3, :, 3:7], b.t[:, 0:192].rearrange("p (c b j) -> p c b j", c=3, j=4), AF.Copy, [b], [pre])
            for s in range(6):
                b = kb.bank()
                kb.op("pe", lambda e, b=b, s=s: [e.matmul(b.t[:64, :], lhsT=nTc.t[:, k, 0:64], rhs=Wx.t[:, k, s * 512:(s + 1) * 512], start=(k == 0), stop=(k == 7)) for k in range(8)][-1], [Wx, nTc], [b])
                cv_ = cnvr.next()
                kb.copy("act", cv_.t[:64, :], b.t[:64, :], [b], [cv_])
                for t in range(1, 4):
                    kb.dma("sp", ocs[:, t - 1, s * 512:(s + 1) * 512], cv_.t[t:64:4, :], reads=[cv_])
            accd = {"T": a, "pre": pre, "view": lambda cc: a.t[:, cc, :].rearrange("p (b t) -> p b t", t=4), "tanh": pre.t[:].rearrange("p c b j -> p c (b j)")[:, :, 0:64]}
            ssd_conv(C, q, lambda cc, j: pre.t[:, cc, :, j:j + 4], accd, BCT)
            b = kb.bank()
            kb.op("pe", lambda e, b=b: [e.matmul(b.t[:64, g * 64:(g + 1) * 64], lhsT=BCT.t[:, g, :], rhs=BCT.t[:, 4 + g, :], start=True, stop=True) for g in range(4)][-1], [BCT], [b])
            kb.tt("dve", CBm.t[:, :, :], b.t[:64, 0:256].rearrange("p (g t) -> p g t", t=64), bc(tri_inc, [64, 4, 64], 1), ALU.mult, [b, C.cst], [CBm])
            b = kb.bank()
            kb.op("pe", lambda e, b=b: [e.transpose(b.t[:64, g * 128:(g + 1) * 128], a.t[:, 16 + g, :], cst[:, C_ID:C_ID + 128]) for g in range(4)][-1], [a, C.cst], [b])
            kb.copy("act", Btok.t[:, :], b.t[:64, :], [b], [Btok])
            CTm = kb.sb("CTm", [128, 4, 16 * 68], BF16, st)
            kb.memset("pool", CTm.t[:], 0.0, [CTm])
            for g in range(4):
                kb.copy("pool", CTm.t[:, g, :].rearrange("p (b x) -> p b x", x=68)[:, :, 0:4], BCT.t[:, 4 + g, :].rearrange("p (b t) -> p b t", t=4), [BCT], [CTm])
            for g in range(4):
                ssd_group_front(C, q, g, a, sm, W)
            decN = kb.sb("decN", [128, 256], F32, st)
            b = kb.bank()
            def emit_dec(e, b=b):
                for j in range(16):
                    for hl in range(2):
                        last = e.matmul(b.t[hl * 64:(hl + 1) * 64, j * 16:(j + 1) * 16], lhsT=sm.t[:64, 96 + 2 * j + hl:97 + 2 * j + hl].to_broadcast([64, 64]),
                                        rhs=cst[:64, C_SEGS:C_SEGS + 16], start=True, stop=True, tile_position=(0, hl * 64))
                return last
            kb.op("pe", emit_dec, [sm, C.cst], [b])
            kb.act(decN.t[:, :], b.t[:, 0:256], AF.Exp, [b], [decN])
            byis, ids = kb.reserve(4)
            h0r = kb.rot("h0nat", [128, 16, 128], F32, 2, st)
            nsr = kb.rot("newst", [128, 16, 128], F32, 1, st)
            hTbr = kb.rot("hTbb", [128, 2048], BF16, 2, st)
            xw4r = kb.rot("xw4", [4, 2048], BF16, 2, st)
            B4r = kb.rot("B4", [4, 512], BF16, 2, st)
            stin = C.st_ssm.rearrange("(b j q) n -> b q j n", j=16, q=128)
            stout = C.o_ssm_s.rearrange("(b j q) n -> b q j n", j=16, q=128)
            for bb in range(16):
                h0 = h0r.next()
                kb.dma("sp", h0.t[:, :, :], stin[bb], writes=[h0])
                xw4 = xw4r.next()
                B4 = B4r.next()
                kb.dma("sp", xw4.t[:, :], W["xw"].t[4 * bb:4 * bb + 4, :], reads=[W["xw"]], writes=[xw4])
                kb.dma("sp", B4.t[:, :], Btok.t[4 * bb:4 * bb + 4, :], reads=[Btok], writes=[B4])
                hb = hTbr.next()
                for j4 in range(4):
                    b = kb.bank()
                    kb.op("pe", lambda e, b=b, j4=j4, h0=h0: [e.transpose(b.t[:, jj * 128:(jj + 1) * 128], h0.t[:, j4 * 4 + jj, :], cst[:, C_ID:C_ID + 128]) for jj in range(4)][-1], [h0, C.cst], [b])
                    kb.copy("act", hb.t[:, j4 * 512:(j4 + 1) * 512], b.t[:, :], [b], [hb])
                for g in range(4):
                    kb.op("pe", lambda e, g=g, hb=hb, bb=bb: e.matmul(byis[g].t[:64, :], lhsT=CTm.t[:, g, bb * 64:(bb + 1) * 64], rhs=hb.t[:, g * 512:(g + 1) * 512], start=(bb == 0), stop=(bb == 15)), [CTm, hb], [byis[g]])
                ns = nsr.next()
                for j4 in range(4):
                    b = kb.bank()
                    kb.op("pe", lambda e, b=b, j4=j4, xw4=xw4, B4=B4: [e.matmul(b.t[:, jj * 128:(jj + 1) * 128], lhsT=xw4.t[0:4, (j4 * 4 + jj) * 128:(j4 * 4 + jj + 1) * 128], rhs=B4.t[0:4, j4 * 128:(j4 + 1) * 128], start=True, stop=True) for jj in range(4)][-1], [xw4, B4], [b])
                    for jj in range(4):
                        j = j4 * 4 + jj
                        kb.stt(ns.t[:, j, :], h0.t[:, j, :], decN.t[:, j * 16 + bb:j * 16 + bb + 1], b.t[:, jj * 128:(jj + 1) * 128], ALU.mult, ALU.add, [h0, decN, b], [ns])
                kb.dma("sp", stout[bb], ns.t[:, :, :], reads=[ns])
            for g in range(4):
                ssd_group_y(C, q, g, sm, W, BCT, CBm, tri_inc, tri_after, byis[g], nTc, 0, Wz, C.TP, C.NCH)
            kb.release(ids)
            kb.barrier()


def hg_pass(C):
    kb = C.kb
    cst = C.cst.t
    with ExitStack() as stw:
        Wq = load_w(C, "Wq", C.w_in[:, 5152:6176], 1024, stack=stw)
        Wf = load_w(C, "Wf", C.w_in[:, 6176:7200], 1024, stack=stw)
        Wi = load_w(C, "Wi", C.w_in[:, 7200:8224], 1024, stack=stw)
        Wg = load_w(C, "Wg", C.w_in[:, 8224:9248], 1024, stack=stw)
        lb = kb.sb("lb", [128, 1024], F32, stw)
        oml = kb.sb("oml", [128, 1024], F32, stw)
        kb.dma("sp", lb.t[:], C.rowvecs[:, RV_LB0:RV_LB0 + 1024].partition_broadcast(128), writes=[lb])
        kb.dma("sp", oml.t[:], C.rowvecs[:, RV_LB1:RV_LB1 + 1024].partition_broadcast(128), writes=[oml])
        kb.tt("dve", lb.t[:], lb.t[:], oml.t[:], ALU.subtract, [lb, oml], [lb])
        kb.act(lb.t[:], lb.t[:], AF.Sigmoid, [lb], [lb])
        kb.ts("dve", oml.t[:], lb.t[:], -1.0, 1.0, ALU.mult, ALU.add, [lb], [oml])
        omlh = kb.sb("omlh", [128, 1024], F32, stw)
        kb.ts("dve", omlh.t[:], oml.t[:], 0.5, None, ALU.mult, None, [oml], [omlh])

        def proj(q, nTc, tcol, Wt, half):
            b = kb.bank()
            kb.op("pe", lambda e: [e.matmul(b.t[:q, :], lhsT=nTc.t[:, k, tcol:tcol + q], rhs=Wt.t[:, k, half * 512:(half + 1) * 512], start=(k == 0), stop=(k == 7)) for k in range(8)][-1], [nTc, Wt], [b])
            return b

        def front(q, nseg, seglen, nTc, tcol, B, hinc, hafter, segones):
            hs = slice(0, q)
            for half in range(2):
                cs = slice(half * 512, (half + 1) * 512)
                b = proj(q, nTc, tcol, Wf, half)
                kb.act(B["w"].t[hs, cs], b.t[hs, :], AF.Tanh, [b], [B["w"]], scale=0.5)
            kb.stt(B["w"].t[hs, :], B["w"].t[hs, :], 1.0, omlh.t[hs, :], ALU.add, ALU.mult, [B["w"], omlh], [B["w"]])
            kb.tt("dve", B["logf"].t[hs, :], B["w"].t[hs, :], lb.t[hs, :], ALU.add, [B["w"], lb], [B["logf"]])
            kb.act(B["logf"].t[hs, :], B["logf"].t[hs, :], AF.Ln, [B["logf"]], [B["logf"]])
            kb.tt("dve", B["kk"].t[hs, :], oml.t[hs, :], B["w"].t[hs, :], ALU.subtract, [B["w"], oml], [B["kk"]])
            for half in range(2):
                cs = slice(half * 512, (half + 1) * 512)
                b = kb.bank()
                kb.op("pe", lambda e, b=b, cs=cs: e.matmul(b.t[hs, :], lhsT=hinc, rhs=B["logf"].t[hs, cs], start=True, stop=True), [B["logf"], C.cst], [b])
                kb.act(B["E1"].t[hs, cs], b.t[hs, :], AF.Exp, [b], [B["E1"]])
                kb.act(B["E1n"].t[hs, cs], b.t[hs, :], AF.Exp, [b], [B["E1n"]], scale=-1.0)
                b2 = kb.bank()
                kb.op("pe", lambda e, b2=b2, cs=cs: e.matmul(b2.t[hs, :], lhsT=hafter, rhs=B["logf"].t[hs, cs], start=True, stop=True), [B["logf"], C.cst], [b2])
                kb.act(B["E2"].t[hs, cs], b2.t[hs, :], AF.Exp, [b2], [B["E2"]])
            b = kb.bank()
            kb.op("pe", lambda e, b=b: [e.matmul(b.t[:, h * nseg:(h + 1) * nseg], lhsT=B["logf"].t[hs, h * 128:(h + 1) * 128], rhs=segones, start=True, stop=True) for h in range(8)][-1], [B["logf"], C.cst], [b])
            kb.act(B["dS"].t[:, 0:8 * nseg], b.t[:, 0:8 * nseg], AF.Exp, [b], [B["dS"]])
            for half in range(2):
                cs = slice(half * 512, (half + 1) * 512)
                b = proj(q, nTc, tcol, Wq, half)
                kb.act(B["sq"].t[hs, cs], b.t[hs, :], AF.Tanh, [b], [B["sq"]], scale=0.5)
                kb.stt(B["sq"].t[hs, cs], B["sq"].t[hs, cs], 1.0, b.t[hs, :], ALU.add, ALU.mult, [B["sq"], b], [B["sq"]])
            kb.tt("dve", B["qg"].t[hs, :], B["sq"].t[hs, :], B["E1"].t[hs, :], ALU.mult, [B["sq"], B["E1"]], [B["qg"]])
            kb.tt("dve", B["kg"].t[hs, :], B["kk"].t[hs, :], B["E1n"].t[hs, :], ALU.mult, [B["kk"], B["E1n"]], [B["kg"]])
            kb.tt("dve", B["kdec"].t[hs, :], B["kk"].t[hs, :], B["E2"].t[hs, :], ALU.mult, [B["kk"], B["E2"]], [B["kdec"]])
            for half in range(2):
                cs = slice(half * 512, (half + 1) * 512)
                b = proj(q, nTc, tcol, Wi, half)
                kb.copy("act", B["v"].t[hs, cs], b.t[hs, :], [b], [B["v"]])
            for half in range(2):
                cs = slice(half * 512, (half + 1) * 512)
                b = proj(q, nTc, tcol, Wg, half)
                kb.act(B["sg"].t[hs, cs], b.t[hs, :], AF.Tanh, [b], [B["sg"]], scale=0.5)
                kb.stt(B["sg"].t[hs, cs], B["sg"].t[hs, cs], 1.0, b.t[hs, :], ALU.add, ALU.mult, [B["sg"], b], [B["sg"]])
            to_T(C, B["qg"], q, 8, None, B["qgT"].t[:, :, :q], B["qgT"])
            to_T(C, B["kg"], q, 8, None, B["kgT"].t[:, :, :q], B["kgT"])
            x = q + seglen
            kb.copy("pool", B["QM"].t[:, :, :].rearrange("p h (c x) -> p h c x", x=x)[:, :, :, 0:seglen],
                    B["qgT"].t[:, :, :q].rearrange("p h (c j) -> p h c j", j=seglen), [B["qgT"]], [B["QM"]])
            for hh in range(2):
                b = kb.bank()
                kb.op("pe", lambda e, b=b, hh=hh: [e.matmul(b.t[hs, h4 * q:(h4 + 1) * q], lhsT=B["kgT"].t[:, hh * 4 + h4, :q], rhs=B["qgT"].t[:, hh * 4 + h4, :q], start=True, stop=True) for h4 in range(4)][-1], [B["kgT"], B["qgT"]], [b])
                kb.tt("dve", B["att"].t[hs, hh * 4:(hh + 1) * 4, :q], b.t[hs, 0:4 * q].rearrange("p (h t) -> p h t", t=q), bc(hinc, [q, 4, q], 1), ALU.mult, [b, C.cst], [B["att"]])

        def back(q, bo, B, tok0, ci):
            hs = slice(0, q)
            for half in range(2):
                kb.copy("act", B["osb"].t[hs, half * 512:(half + 1) * 512], bo[half].t[hs, :], [bo[half]], [B["osb"]])
            o = B["osb"]
            kb.tt("pool", B["osq"].t[hs, :], o.t[hs, :], o.t[hs, :], ALU.mult, [o], [B["osq"]])
            hsm = B["hsm"]
            kb.op("dve", lambda e: e.tensor_reduce(out=hsm.t[hs, 0:8], in_=B["osq"].t[hs, :].rearrange("p (h v) -> p h v", v=128), axis=AX.X, op=ALU.add), [B["osq"]], [hsm])
            kb.ts("dve", hsm.t[hs, 8:16], hsm.t[hs, 0:8], 4.0 / 128, 16.0 * EPS, ALU.mult, ALU.add, [hsm], [hsm])
            kb.act(hsm.t[hs, 16:24], hsm.t[hs, 8:16], AF.Sqrt, [hsm], [hsm])
            kb.op("dve", lambda e: e.reciprocal(out=hsm.t[hs, 24:32], in_=hsm.t[hs, 16:24]), [hsm], [hsm])
            o3 = o.t[hs, :].rearrange("p (h v) -> p h v", v=128)
            kb.tt("pool", o3, o3, bc(hsm.t[hs, 24:32], [q, 8, 128], 2), ALU.mult, [o, hsm], [o])
            kb.tt("dve", B["on"].t[hs, :], o.t[hs, :], B["sg"].t[hs, :], ALU.mult, [o, B["sg"]], [B["on"]])
            to_T(C, B["on"], q, 8, C.colv.t[:, V_HNORM:V_HNORM + 8], B["onT"].t[:, :, :q], B["onT"])
            kb.dma("sp", C.ONT[:, :, tok0:tok0 + q], B["onT"].t[:, :, :q], reads=[B["onT"]], writes=[C.ONT_r[ci]])

        def bufs(q, nseg, seglen, st, nslots=1):
            shared = {}
            for n in ("w", "logf", "kk", "E1", "E1n", "E2", "sq", "osb", "osq"):
                shared[n] = kb.sb("hg_" + n, [q, 1024], F32, st)
            for n in ("qg", "kg", "on"):
                shared[n] = kb.sb("hg_" + n, [q, 1024], BF16, st)
            shared["qgT"] = kb.sb("hg_qgT", [128, 8, q], BF16, st)
            shared["kgT"] = kb.sb("hg_kgT", [128, 8, q], BF16, st)
            shared["onT"] = kb.sb("hg_onT", [128, 8, q], BF16, st)
            shared["hsm"] = kb.sb("hg_hsm", [q, 32], F32, st)
            out = []
            for i in range(nslots):
                B = dict(shared)
                B["sg"] = kb.sb(f"hg_sg{i}", [q, 1024], F32, st)
                for n in ("kdec", "v"):
                    B[n] = kb.sb(f"hg_{n}{i}", [q, 1024], BF16, st)
                B["QM"] = kb.sb(f"hg_QM{i}", [128, 8, nseg * (q + seglen)], BF16, st)
                B["att"] = kb.sb(f"hg_att{i}", [q, 8, q], BF16, st)
                B["dS"] = kb.sb(f"hg_dS{i}", [128, 8 * nseg], F32, st)
                kb.memset("pool", B["QM"].t[:], 0.0, [B["QM"]])
                out.append(B)
            return out

        with ExitStack() as st:
            q, nseg, seglen = 128, 4, 32
            Bs = bufs(q, nseg, seglen, st, 2)
            nTr = kb.rot("hnTc", [128, 8, 128], BF16, 2, st)
            S = kb.sb("hg_S", [128, 8, 128], F32, st)
            Sb = [kb.sb(f"hg_Sb{c}", [128, 8, 128], BF16, st) for c in range(4)]
            kb.memset("pool", S.t[:], 0.0, [S])
            hinc, hafter, segones = cst[:, C_HPI:C_HPI + 128], cst[:, C_HPA:C_HPA + 128], cst[:, C_SEGP:C_SEGP + 4]

            def F(c, nTc, B):
                kb.dma("sp", nTc.t[:, :, :], C.nTd[:, :, 3 + c * 128:3 + c * 128 + 128], reads=[C.nTd_r[c + 1]], writes=[nTc])
                front(q, nseg, seglen, nTc, 0, B, hinc, hafter, segones)

            def G(c, B):
                for sc in range(4):
                    kb.copy("act", Sb[sc].t[:, :, :], S.t[:, :, :], [S], [Sb[sc]])
                    bd = [kb.bank(), kb.bank()]
                    for hh in range(2):
                        kb.op("pe", lambda e, hh=hh, sc=sc, bd=bd: [e.matmul(bd[hh].t[:, h4 * 128:(h4 + 1) * 128], lhsT=B["kdec"].t[32 * sc:32 * sc + 32, (hh * 4 + h4) * 128:(hh * 4 + h4 + 1) * 128],
                                                                      rhs=B["v"].t[32 * sc:32 * sc + 32, (hh * 4 + h4) * 128:(hh * 4 + h4 + 1) * 128], start=True, stop=True, tile_position=(32 * sc, 0)) for h4 in range(4)][-1], [B["kdec"], B["v"]], [bd[hh]])
                    kb.tt("dve", S.t[:, :, :], S.t[:, :, :], bc(B["dS"].t[:, :].rearrange("p (h c) -> p h c", c=nseg)[:, :, sc], [128, 8, 128], 2), ALU.mult, [S, B["dS"]], [S])
                    for hh in range(2):
                        Sv = S.t[:, hh * 4:(hh + 1) * 4, :]
                        kb.tt("dve", Sv, Sv, bd[hh].t[:, :].rearrange("p (h v) -> p h v", v=128), ALU.add, [S, bd[hh]], [S])
                bo = [kb.bank(), kb.bank()]
                for hh in range(2):
                    def emit(e, hh=hh):
                        for h4 in range(4):
                            h = hh * 4 + h4
                            e.matmul(bo[hh].t[:, h4 * 128:(h4 + 1) * 128], lhsT=B["att"].t[:, h, :], rhs=B["v"].t[:, h * 128:(h + 1) * 128], start=True, stop=False)
                            for sc in range(4):
                                last = e.matmul(bo[hh].t[:, h4 * 128:(h4 + 1) * 128], lhsT=B["QM"].t[:, h, sc * 128:(sc + 1) * 128], rhs=Sb[sc].t[:, h, :], start=False, stop=(sc == 3))
                        return last
                    kb.op("pe", emit, [B["att"], B["v"], B["QM"]] + Sb, [bo[hh]])
                back(q, bo, B, c * 128, c)

            nts = [nTr.next(), nTr.next()]
            FP, GP = [0, 1, 2, 3], [4, 5, 6, 7]
            kb.begin(FP); F(0, nts[0], Bs[0]); kb.play(kb.end())
            for c in range(C.NCH):
                kb.begin(GP); G(c, Bs[c % 2]); gl = kb.end()
                fl = []
                if c + 1 < C.NCH:
                    kb.begin(FP); F(c + 1, nts[(c + 1) % 2], Bs[(c + 1) % 2]); fl = kb.end()
                kb.play(schedule(gl, fl))
            kb.set_pool(range(8))
            kb.dma("sp", C.o_hg_p.rearrange("(h k) v -> k h v", k=128), S.t[:, :, :], reads=[S])
            kb.barrier()
        with ExitStack() as st:
            q, nseg, seglen = 64, 16, 4
            B = bufs(q, nseg, seglen, st)[0]
            nTc = kb.sb("hnTs", [128, 8, 64], BF16, st)
            kb.dma("sp", nTc.t[:, :, :], C.nTd[:, :, 3 + C.TP:3 + C.TP + 64], reads=[C.nTd_r[C.NT]], writes=[nTc])
            hinc, hafter, segones = cst[:64, C_TSI:C_TSI + 64], cst[:64, C_TSA:C_TSA + 64], cst[:64, C_SEGS:C_SEGS + 16]
            front(q, nseg, seglen, nTc, 0, B, hinc, hafter, segones)
            bo, ids = kb.reserve(2)
            zt = kb.sb("hg_zero", [64, 64], BF16, st)
            kb.memset("pool", zt.t[:], 0.0, [zt])
            for hh in range(2):
                def emit(e, hh=hh):
                    e.matmul(bo[hh].t[:64, :], lhsT=zt.t[:, :], rhs=B["v"].t[:, hh * 512:(hh + 1) * 512], start=True, stop=False)
                    for h4 in range(4):
                        last = e.matmul(bo[hh].t[:64, h4 * 128:(h4 + 1) * 128], lhsT=B["att"].t[:, hh * 4 + h4, :], rhs=B["v"].t[:, (hh * 4 + h4) * 128:(hh * 4 + h4 + 1) * 128], start=False, stop=False)
                    return last
                kb.op("pe", emit, [B["att"], B["v"], zt], [bo[hh]])
            Sir = kb.rot("hg_Sin", [128, 8, 128], F32, 2, st)
            Sor = kb.rot("hg_Sout", [128, 8, 128], F32, 2, st)
            Sbr = kb.rot("hg_Sbb", [128, 8, 128], BF16, 2, st)
            k4r = kb.rot("hg_k4", [4, 1024], BF16, 2, st)
            v4r = kb.rot("hg_v4", [4, 1024], BF16, 2, st)
            sin = C.st_hg.rearrange("(b h k) v -> b k h v", h=8, k=128)
            sout = C.o_hg_s.rearrange("(b h k) v -> b k h v", h=8, k=128)
            for bb in range(16):
                Si = Sir.next()
                kb.dma("sp", Si.t[:, :, :], sin[bb], writes=[Si])
                k4 = k4r.next(); v4 = v4r.next()
                kb.dma("sp", k4.t[:, :], B["kdec"].t[4 * bb:4 * bb + 4, :], reads=[B["kdec"]], writes=[k4])
                kb.dma("sp", v4.t[:, :], B["v"].t[4 * bb:4 * bb + 4, :], reads=[B["v"]], writes=[v4])
                Sbb = Sbr.next()
                kb.copy("act", Sbb.t[:, :, :], Si.t[:, :, :], [Si], [Sbb])
                for hh in range(2):
                    kb.op("pe", lambda e, hh=hh, Sbb=Sbb, bb=bb: [e.matmul(bo[hh].t[:64, h4 * 128:(h4 + 1) * 128], lhsT=B["QM"].t[:, hh * 4 + h4, bb * 64:(bb + 1) * 64], rhs=Sbb.t[:, hh * 4 + h4, :], start=False, stop=(bb == 15 and h4 == 3)) for h4 in range(4)][-1], [B["QM"], Sbb], [bo[hh]])
                bd = [kb.bank(), kb.bank()]
                for hh in range(2):
                    kb.op("pe", lambda e, hh=hh, bd=bd, k4=k4, v4=v4: [e.matmul(bd[hh].t[:, h4 * 128:(h4 + 1) * 128], lhsT=k4.t[0:4, (hh * 4 + h4) * 128:(hh * 4 + h4 + 1) * 128], rhs=v4.t[0:4, (hh * 4 + h4) * 128:(hh * 4 + h4 + 1) * 128], start=True, stop=True) for h4 in range(4)][-1], [k4, v4], [bd[hh]])
                So = Sor.next()
                kb.tt("pool", So.t[:, :, :], Si.t[:, :, :], bc(B["dS"].t[:, :].rearrange("p (h c) -> p h c", c=nseg)[:, :, bb], [128, 8, 128], 2), ALU.mult, [Si, B["dS"]], [So])
                for hh in range(2):
                    Sv = So.t[:, hh * 4:(hh + 1) * 4, :]
                    kb.tt("dve", Sv, Sv, bd[hh].t[:, :].rearrange("p (h v) -> p h v", v=128), ALU.add, [So, bd[hh]], [So])
                kb.dma("sp", sout[bb], So.t[:, :, :], reads=[So])
            back(q, bo, B, C.TP, C.NCH)
            kb.release(ids)
            kb.barrier()


def chunk_info(C, ci):
    if ci < C.NCH:
        return 128, ci * 128
    return 64, C.TP


def merge_pass(C):
    kb = C.kb
    with ExitStack() as st:
        Wg1 = load_w(C, "Wg1", C.w_in[:, 9248:10272], 1024, stack=st)
        Wg2 = load_w(C, "Wg2", C.w_in[:, 10272:11296], 1024, stack=st)
        Wso = load_w(C, "Wso", C.w_ssd_out, 1024, nk=16, stack=st)
        Who = load_w(C, "Who", C.w_hgrn_out, 1024, stack=st)
        Wo = load_w(C, "Wo", C.w_out, 1024, stack=st)
        nTr = kb.rot("m_nT", [128, 8, 128], BF16, 2, st)
        yTr = kb.rot("m_yT", [128, 16, 128], BF16, 2, st)
        oTr = kb.rot("m_oT", [128, 8, 128], BF16, 2, st)
        xr = kb.rot("m_x", [128, D], F32, 2, st)
        s1 = kb.sb("m_s1", [128, D], F32, st); s2 = kb.sb("m_s2", [128, D], F32, st)
        m1 = kb.sb("m_m1", [128, D], F32, st); m2 = kb.sb("m_m2", [128, D], F32, st)
        mg = kb.sb("m_mg", [128, D], BF16, st); mT = kb.sb("m_mT", [128, 8, 128], BF16, st)
        x1r = kb.rot("m_x1", [128, D], F32, 2, st)

        def loads(ci):
            q, tok0 = chunk_info(C, ci)
            a, b_, c_, d_ = nTr.next(), yTr.next(), oTr.next(), xr.next()
            kb.dma("sp", a.t[:, :, :q], C.nTd[:, :, 3 + tok0:3 + tok0 + q], reads=[C.nTd_r[ci + 1]], writes=[a])
            kb.dma("sp", b_.t[:, :, :q], C.YNT[:, :, tok0:tok0 + q], reads=C.YNT_r[ci], writes=[b_])
            kb.dma("sp", c_.t[:, :, :q], C.ONT[:, :, tok0:tok0 + q], reads=[C.ONT_r[ci]], writes=[c_])
            src = C.xp[tok0:tok0 + q, :] if ci < C.NCH else C.xs[:, :]
            kb.dma("sp", d_.t[:q, :], src, writes=[d_])
            return a, b_, c_, d_
        nxt = loads(0)
        for ci in range(C.NT):
            q, tok0 = chunk_info(C, ci)
            nTc, ynT, onT, xt = nxt
            if ci + 1 < C.NT:
                nxt = loads(ci + 1)
            hs = slice(0, q)
            for half in range(2):
                cs = slice(half * 512, (half + 1) * 512)
                for (Wt, dst) in ((Wg1, s1), (Wg2, s2)):
                    b = kb.bank()
                    kb.op("pe", lambda e, b=b, Wt=Wt: [e.matmul(b.t[hs, :], lhsT=nTc.t[:, k, :q], rhs=Wt.t[:, k, cs], start=(k == 0), stop=(k == 7)) for k in range(8)][-1], [nTc, Wt], [b])
                    kb.act(dst.t[hs, cs], b.t[hs, :], AF.Tanh, [b], [dst], scale=0.5)
                b = kb.bank()
                kb.op("pe", lambda e, b=b: [e.matmul(b.t[hs, :], lhsT=ynT.t[:, k, :q], rhs=Wso.t[:, k, cs], start=(k == 0), stop=(k == 15)) for k in range(16)][-1], [ynT, Wso], [b])
                kb.stt(m1.t[hs, cs], s1.t[hs, cs], 1.0, b.t[hs, :], ALU.add, ALU.mult, [b, s1], [m1])
                b = kb.bank()
                kb.op("pe", lambda e, b=b: [e.matmul(b.t[hs, :], lhsT=onT.t[:, k, :q], rhs=Who.t[:, k, cs], start=(k == 0), stop=(k == 7)) for k in range(8)][-1], [onT, Who], [b])
                kb.stt(m2.t[hs, cs], s2.t[hs, cs], 1.0, b.t[hs, :], ALU.add, ALU.mult, [b, s2], [m2])
            kb.tt("dve", mg.t[hs, :], m1.t[hs, :], m2.t[hs, :], ALU.add, [m1, m2], [mg])
            to_T(C, mg, q, 8, None, mT.t[:, :, :q], mT)
            x1 = x1r.next()
            for half in range(2):
                cs = slice(half * 512, (half + 1) * 512)
                b = kb.bank()
                kb.op("pe", lambda e, b=b: [e.matmul(b.t[hs, :], lhsT=mT.t[:, k, :q], rhs=Wo.t[:, k, cs], start=(k == 0), stop=(k == 7)) for k in range(8)][-1], [mT, Wo], [b])
                kb.stt(x1.t[hs, cs], b.t[hs, :], 0.5, xt.t[hs, cs], ALU.mult, ALU.add, [b, xt], [x1])
            kb.dma("sp", C.X1[tok0:tok0 + q, :], x1.t[hs, :], reads=[x1], writes=[C.X1_r[ci]])
        kb.barrier()


def attn_pass(C):
    kb = C.kb
    with ExitStack() as st:
        Wcq = load_w(C, "Wcq", C.w_cq, 1024, stack=st)
        Wck = load_w(C, "Wck", C.w_ck, 1024, stack=st)
        Wcv = load_w(C, "Wcv", C.w_cv, 1024, stack=st)
        Wco = load_w(C, "Wco", C.w_co, 1024, stack=st)
        memT = kb.sb("a_memT", [128, 8, 256], BF16, st)
        KT = kb.sb("a_KT", [128, 8, 256], BF16, st)
        V = kb.sb("a_V", [128, 2, D], BF16, st)
        xr = kb.rot("a_x1", [128, D], F32, 2, st)
        kvo = kb.rot("a_kvo", [128, 512], F32, 2, st)
        for mt in range(2):
            xt = xr.next()
            kb.dma("sp", xt.t[:, :], C.mem[mt * 128:(mt + 1) * 128, :], writes=[xt])
            rms_to_T(C, xt, 128, C.colv.t[:, V_NMEM:V_NMEM + 8], memT.t[:, :, mt * 128:(mt + 1) * 128], memT)
        for mt in range(2):
            for half in range(2):
                cs = slice(half * 512, (half + 1) * 512)
                for (Wt, dst, isv) in ((Wck, C.o_mk, False), (Wcv, C.o_mv, True)):
                    b = kb.bank()
                    kb.op("pe", lambda e, b=b, Wt=Wt: [e.matmul(b.t[:, :], lhsT=memT.t[:, k, mt * 128:(mt + 1) * 128], rhs=Wt.t[:, k, cs], start=(k == 0), stop=(k == 7)) for k in range(8)][-1], [memT, Wt], [b])
                    o = kvo.next()
                    kb.copy("act", o.t[:, :], b.t[:, :], [b], [o])
                    kb.dma("sp", dst[mt * 128:(mt + 1) * 128, cs], o.t[:, :], reads=[o])
                    if isv:
                        kb.copy("pool", V.t[:, mt, cs], o.t[:, :], [o], [V])
        for c in range(8):
            b = kb.bank()
            kb.op("pe", lambda e, b=b, c=c: [e.matmul(b.t[:, 0:256], lhsT=Wck.t[:, k, c * 128:(c + 1) * 128], rhs=memT.t[:, k, :], start=(k == 0), stop=(k == 7)) for k in range(8)][-1], [memT, Wck], [b])
            kb.copy("act", KT.t[:, c, :], b.t[:, 0:256], [b], [KT])

        def mkset(i, st=st):
            return dict(hnT=kb.sb(f"a_hnT{i}", [128, 8, 128], BF16, st), Qs=kb.sb(f"a_Qs{i}", [128, D], BF16, st), QT=kb.sb(f"a_QT{i}", [128, 8, 128], BF16, st),
                        P=kb.sb(f"a_P{i}", [128, D], BF16, st), PT=kb.sb(f"a_PT{i}", [128, 8, 128], BF16, st), On=kb.sb(f"a_On{i}", [128, D], BF16, st),
                        OT=kb.sb(f"a_OT{i}", [128, 8, 128], BF16, st), sm=kb.sb(f"a_sm{i}", [128, 16], F32, st), x2=kb.sb(f"a_x2{i}", [128, D], F32, st),
                        x1=kb.sb(f"a_x1{i}", [128, D], F32, st), xn=kb.sb(f"a_xn{i}", [128, D], BF16, st), junk=kb.sb(f"a_junk{i}", [128, D], BF16, st),
                        ss=kb.sb(f"a_ss{i}", [128, 4], F32, st))
        sets = [mkset(0), mkset(1)]
        zt = kb.sb("a_zero", [128, 64], BF16, st)
        kb.memset("pool", zt.t[:], 0.0, [zt])

        def load_x1(ci, S):
            q, tok0 = chunk_info(C, ci)
            kb.dma("sp", S["x1"].t[:q, :], C.X1[tok0:tok0 + q, :], reads=[C.X1_r[ci]], writes=[S["x1"]])

        def q_front(q, S):
            hs = slice(0, q)
            hnT, Qs, QT = S["hnT"], S["Qs"], S["QT"]
            rms_to_T(C, S["x1"], q, C.colv.t[:, V_NCROSS:V_NCROSS + 8], hnT.t[:, :, :q], hnT, scratch=S)
            for half in range(2):
                cs = slice(half * 512, (half + 1) * 512)
                b = kb.bank()
                kb.op("pe", lambda e, b=b, cs=cs: [e.matmul(b.t[hs, :], lhsT=hnT.t[:, k, :q], rhs=Wcq.t[:, k, cs], start=(k == 0), stop=(k == 7)) for k in range(8)][-1], [hnT, Wcq], [b])
                kb.act(Qs.t[hs, cs], b.t[hs, :], AF.Copy, [b], [Qs], scale=1.0 / 16.0)
            to_T(C, Qs, q, 8, None, QT.t[:, :, :q], QT)

        def softmax(q, bs, S):
            hs = slice(0, q)
            sm, P, PT = S["sm"], S["P"], S["PT"]
            for hh in range(2):
                kb.op("dve", lambda e, hh=hh: e.tensor_reduce(out=sm.t[hs, hh * 2:hh * 2 + 2], in_=bs[hh].t[hs, :].rearrange("p (h m) -> p h m", m=256), axis=AX.X, op=ALU.max), [bs[hh]], [sm])
            kb.ts("dve", sm.t[hs, 4:8], sm.t[hs, 0:4], -1.0, None, ALU.mult, None, [sm], [sm])
            for h in range(4):
                kb.act(P.t[hs, h * 256:(h + 1) * 256], bs[h // 2].t[hs, (h % 2) * 256:(h % 2 + 1) * 256], AF.Exp, [bs[h // 2], sm], [P, sm], bias=sm.t[hs, 4 + h:5 + h], accum_out=sm.t[hs, 8 + h:9 + h])
            kb.op("dve", lambda e: e.reciprocal(out=sm.t[hs, 12:16], in_=sm.t[hs, 8:12]), [sm], [sm])
            to_T(C, P, q, 8, None, PT.t[:, :, :q], PT)

        def o_back(q, bo, S, ci, tok0):
            hs = slice(0, q)
            sm, On, OT, x2, xt = S["sm"], S["On"], S["OT"], S["x2"], S["x1"]
            for h in range(4):
                kb.act(On.t[hs, h * 256:(h + 1) * 256], bo[h // 2].t[hs, (h % 2) * 256:(h % 2 + 1) * 256], AF.Copy, [bo[h // 2], sm], [On], scale=sm.t[hs, 12 + h:13 + h])
            to_T(C, On, q, 8, None, OT.t[:, :, :q], OT)
            for half in range(2):
                cs = slice(half * 512, (half + 1) * 512)
                b = kb.bank()
                kb.op("pe", lambda e, b=b, cs=cs: [e.matmul(b.t[hs, :], lhsT=OT.t[:, k, :q], rhs=Wco.t[:, k, cs], start=(k == 0), stop=(k == 7)) for k in range(8)][-1], [OT, Wco], [b])
                kb.tt("dve", x2.t[hs, cs], b.t[hs, :], xt.t[hs, cs], ALU.add, [b, xt], [x2])
            kb.dma("sp", C.X2[tok0:tok0 + q, :], x2.t[hs, :], reads=[x2], writes=[C.X2_r[ci]])

        def prompt_chunk(ci, S):
            q, tok0 = 128, ci * 128
            load_x1(ci, S)
            q_front(q, S)
            QT, PT = S["QT"], S["PT"]
            bs = [kb.bank(), kb.bank()]
            for hh in range(2):
                def emit(e, hh=hh):
                    for h2 in range(2):
                        h = hh * 2 + h2
                        for dc in range(2):
                            last = e.matmul(bs[hh].t[:, h2 * 256:(h2 + 1) * 256], lhsT=QT.t[:, h * 2 + dc, :], rhs=KT.t[:, h * 2 + dc, :], start=(dc == 0), stop=(dc == 1))
                    return last
                kb.op("pe", emit, [QT, KT], [bs[hh]])
            softmax(q, bs, S)
            bo = [kb.bank(), kb.bank()]
            for hh in range(2):
                def emit(e, hh=hh):
                    for h2 in range(2):
                        h = hh * 2 + h2
                        for mt in range(2):
                            last = e.matmul(bo[hh].t[:, h2 * 256:(h2 + 1) * 256], lhsT=PT.t[:, h * 2 + mt, :], rhs=V.t[:, mt, h * 256:(h + 1) * 256], start=(mt == 0), stop=(mt == 1))
                    return last
                kb.op("pe", emit, [PT, V], [bo[hh]])
            o_back(q, bo, S, ci, tok0)

        with ExitStack() as st2:
            psets = sets + [mkset(2, st2), mkset(3, st2)]
            for c0 in range(0, C.NCH, 4):
                lists = []
                for i in range(4):
                    if c0 + i < C.NCH:
                        kb.begin([2 * i, 2 * i + 1]); prompt_chunk(c0 + i, psets[i]); lists.append(kb.end())
                kb.play(schedule(*lists))
            kb.set_pool(range(8))
            kb.barrier()
        q, tok0, ci = 64, C.TP, C.NCH
        S = sets[0]
        QT, PT = S["QT"], S["PT"]
        load_x1(ci, S)
        q_front(q, S)
        QTm = kb.sb("a_QTm", [128, 8, 16 * 68], BF16, st)
        PTm = kb.sb("a_PTm", [128, 8, 16 * 68], BF16, st)
        kb.memset("pool", QTm.t[:], 0.0, [QTm])
        kb.memset("pool", PTm.t[:], 0.0, [PTm])
        kb.copy("pool", QTm.t[:, :, :].rearrange("p h (c x) -> p h c x", x=68)[:, :, :, 0:4], QT.t[:, :, :64].rearrange("p h (c j) -> p h c j", j=4), [QT], [QTm])
        Kr = kb.rot("a_Kb", [128, 2, D], BF16, 2, st)
        KTr = kb.rot("a_KTb", [128, 8, 256], BF16, 2, st)
        bs, ids = kb.reserve(2)
        for hh in range(2):
            kb.op("pe", lambda e, hh=hh: e.matmul(bs[hh].t[:64, :], lhsT=zt.t[:64, :], rhs=Wcq.t[:64, 0, 0:512], start=True, stop=False), [zt, Wcq], [bs[hh]])
        ckv = C.ck.rearrange("(b mt p) c -> b p mt c", mt=2, p=128)
        cvv = C.cv.rearrange("(b mt p) c -> b p mt c", mt=2, p=128)
        for bb in range(16):
            Kb = Kr.next()
            kb.dma("pool", Kb.t[:, :, :], ckv[bb], writes=[Kb])
            KTb = KTr.next()
            for mt in range(2):
                b = kb.bank()
                pb = b.t.bitcast(BF16)
                kb.op("pe", lambda e, pb=pb, mt=mt, Kb=Kb: [e.transpose(pb[:, c * 128:(c + 1) * 128], Kb.t[:, mt, c * 128:(c + 1) * 128], C.identb.t[:, :]) for c in range(8)][-1], [Kb, C.identb], [b])
                kb.copy("act", KTb.t[:, :, mt * 128:(mt + 1) * 128], pb.rearrange("p (c m) -> p c m", m=128), [b], [KTb])
            for hh in range(2):
                def emit(e, hh=hh, KTb=KTb, bb=bb):
                    for h2 in range(2):
                        h = hh * 2 + h2
                        for dc in range(2):
                            last = e.matmul(bs[hh].t[:64, h2 * 256:(h2 + 1) * 256], lhsT=QTm.t[:, h * 2 + dc, bb * 64:(bb + 1) * 64], rhs=KTb.t[:, h * 2 + dc, :], start=False, stop=(bb == 15 and dc == 1 and h2 == 1))
                    return last
                kb.op("pe", emit, [QTm, KTb], [bs[hh]])
        softmax(q, bs, S)
        kb.copy("pool", PTm.t[:, :, :].rearrange("p h (c x) -> p h c x", x=68)[:, :, :, 0:4], PT.t[:, :, :64].rearrange("p h (c j) -> p h c j", j=4), [PT], [PTm])
        kb.release(ids)
        bo, ids = kb.reserve(2)
        for hh in range(2):
            kb.op("pe", lambda e, hh=hh: e.matmul(bo[hh].t[:64, :], lhsT=zt.t[:64, :], rhs=Wcq.t[:64, 0, 0:512], start=True, stop=False), [zt, Wcq], [bo[hh]])
        for bb in range(16):
            Vb = Kr.next()
            kb.dma("pool", Vb.t[:, :, :], cvv[bb], writes=[Vb])
            for hh in range(2):
                def emit(e, hh=hh, Vb=Vb, bb=bb):
                    for h2 in range(2):
                        h = hh * 2 + h2
                        for mt in range(2):
                            last = e.matmul(bo[hh].t[:64, h2 * 256:(h2 + 1) * 256], lhsT=PTm.t[:, h * 2 + mt, bb * 64:(bb + 1) * 64], rhs=Vb.t[:, mt, h * 256:(h + 1) * 256], start=False, stop=(bb == 15 and mt == 1 and h2 == 1))
                    return last
                kb.op("pe", emit, [PTm, Vb], [bo[hh]])
        o_back(q, bo, S, ci, tok0)
        kb.release(ids)
        kb.barrier()


def ffn_pass(C):
    kb = C.kb
    with ExitStack() as st:
        Wga = load_w(C, "Wga", C.w_gate, FFN, stack=st)
        Wup = load_w(C, "Wup", C.w_up, FFN, stack=st)
        Wdn = load_w(C, "Wdn", C.w_down, 1024, nk=22, stack=st)
        nfin = kb.sb("f_nfin", [128, D], F32, st)
        kb.dma("sp", nfin.t[:], C.rowvecs[:, RV_NFIN:RV_NFIN + D].partition_broadcast(128), writes=[nfin])
        x2r = kb.rot("f_x2", [128, 2, D], F32, 2, st)
        hnT = kb.sb("f_hnT", [128, 8, 256], BF16, st)
        hT = kb.sb("f_hT", [128, 22, 256], BF16, st)
        sgr = kb.rot("f_sg", [128, 256], F32, 2, st)
        yr = kb.rot("f_y", [128, D], F32, 2, st)
        ntile = C.NCH // 2 + 1

        def tinfo(ti):
            if ti < C.NCH // 2:
                return [(128, ti * 256, 2 * ti), (128, ti * 256 + 128, 2 * ti + 1)]
            return [(64, C.TP, C.NCH)]

        def loads(ti):
            t = x2r.next()
            for sub, (q, tok0, ci) in enumerate(tinfo(ti)):
                kb.dma("sp", t.t[:q, sub, :], C.X2[tok0:tok0 + q, :], reads=[C.X2_r[ci]], writes=[t])
            return t
        nxt = loads(0)
        for ti in range(ntile):
            xt = nxt
            if ti + 1 < ntile:
                nxt = loads(ti + 1)
            subs = tinfo(ti)
            Wd = sum(q for q, _, _ in subs)
            for sub, (q, tok0, ci) in enumerate(subs):
                xv = T(xt.t[:, sub, :], xt.r)
                rms_to_T(C, xv, q, C.colv.t[:, V_NFFN:V_NFFN + 8], hnT.t[:, :, sub * 128:sub * 128 + q], hnT)
            for f in range(22):
                b = kb.bank()

                def emit(e, b=b, f=f):
                    for k in range(8):
                        e.matmul(b.t[:, 0:Wd], lhsT=Wga.t[:, k, f * 128:(f + 1) * 128], rhs=hnT.t[:, k, 0:Wd], start=(k == 0), stop=(k == 7))
                    for k in range(8):
                        last = e.matmul(b.t[:, 256:256 + Wd], lhsT=Wup.t[:, k, f * 128:(f + 1) * 128], rhs=hnT.t[:, k, 0:Wd], start=(k == 0), stop=(k == 7))
                    return last
                kb.op("pe", emit, [Wga, Wup, hnT], [b])
                sg = sgr.next()
                kb.act(sg.t[:, 0:Wd], b.t[:, 0:Wd], AF.Tanh, [b], [sg], scale=0.5)
                kb.stt(sg.t[:, 0:Wd], sg.t[:, 0:Wd], 1.0, b.t[:, 0:Wd], ALU.add, ALU.mult, [sg, b], [sg])
                kb.tt("dve", hT.t[:, f, 0:Wd], sg.t[:, 0:Wd], b.t[:, 256:256 + Wd], ALU.mult, [sg, b], [hT])
            for sub, (q, tok0, ci) in enumerate(subs):
                hs = slice(0, q)
                for half in range(2):
                    cs = slice(half * 512, (half + 1) * 512)
                    b = kb.bank()
                    kb.op("pe", lambda e, b=b, sub=sub, q=q, cs=cs: [e.matmul(b.t[:q, :], lhsT=hT.t[:, f, sub * 128:sub * 128 + q], rhs=Wdn.t[:, f, cs], start=(f == 0), stop=(f == 21)) for f in range(22)][-1], [hT, Wdn], [b])
                    kb.stt(xt.t[hs, sub, cs], b.t[hs, :], 0.5, xt.t[hs, sub, cs], ALU.mult, ALU.add, [b, xt], [xt])
                xv = T(xt.t[:, sub, :], xt.r)
                ss = C.ss.next()
                rms_rstd(C, xv, q, ss)
                y = yr.next()
                kb.act(y.t[hs, :], xv.t[hs, :], AF.Copy, [xt, ss], [y], scale=ss.t[hs, 0:1])
                kb.tt("dve", y.t[hs, :], y.t[hs, :], nfin.t[hs, :], ALU.mult, [y, nfin], [y])
                dst = C.o_y_p[tok0:tok0 + q, :] if ci < C.NCH else C.o_y_s[:, :]
                kb.dma("sp", dst, y.t[hs, :], reads=[y])
        kb.barrier()

def build(NCH=16, debug=False, upto=99, skip_ssd=False):
    TP = NCH * 128
    TS = 64
    TT = TP + TS
    nc = bass.Bass("TRN2", target_bir_lowering=False)
    C = Ctx()
    C.nc, C.NCH, C.TP, C.TT, C.NT = nc, NCH, TP, TT, NCH + 1
    C.skip_ssd = skip_ssd

    def din(name, shape):
        return nc.dram_tensor(name, list(shape), F32, kind="ExternalInput").ap()

    def dout(name, shape):
        return nc.dram_tensor(name, list(shape), F32, kind="ExternalOutput").ap()

    def dscr(name, shape, dt):
        if debug:
            return nc.dram_tensor(name, list(shape), dt, kind="ExternalOutput").ap()
        return nc.dram_tensor(name, list(shape), dt).ap()

    C.xp = din("xp", [TP, D]); C.xs = din("xs", [TS, D]); C.mem = din("mem", [256, D])
    C.st_ssm = din("st_ssm", [16 * 2048, 128]); C.st_conv = din("st_conv", [48, 3072]); C.st_hg = din("st_hg", [16 * 1024, 128])
    C.ck = din("ck", [16 * 256, D]); C.cv = din("cv", [16 * 256, D])
    C.w_in = din("w_in", [D, IN_DIM]); C.w_ssd_out = din("w_ssd_out", [2048, D]); C.w_hgrn_out = din("w_hgrn_out", [D, D]); C.w_out = din("w_out", [D, D])
    C.w_cq = din("w_cq", [D, D]); C.w_ck = din("w_ck", [D, D]); C.w_cv = din("w_cv", [D, D]); C.w_co = din("w_co", [D, D])
    C.w_gate = din("w_gate", [D, FFN]); C.w_up = din("w_up", [D, FFN]); C.w_down = din("w_down", [FFN, D])
    C.consts = din("consts", [128, 1024]); C.colvecs = din("colvecs", [128, 176]); C.rowvecs = din("rowvecs", [1, RV_N])
    C.o_y_p = dout("o_y_p", [TP, D]); C.o_y_s = dout("o_y_s", [TS, D]); C.o_ssm_p = dout("o_ssm_p", [2048, 128]); C.o_conv_p = dout("o_conv_p", [3, 3072])
    C.o_hg_p = dout("o_hg_p", [1024, 128]); C.o_mk = dout("o_mk", [256, D]); C.o_mv = dout("o_mv", [256, D])
    C.o_ssm_s = dout("o_ssm_s", [16 * 2048, 128]); C.o_conv_s = dout("o_conv_s", [48, 3072]); C.o_hg_s = dout("o_hg_s", [16 * 1024, 128])
    C.nTd = dscr("nTd", [128, 8, 3 + TT], BF16); C.nTd_r = [Region(f"nTd{i}") for i in range(C.NT + 1)]
    C.YNT = dscr("YNT", [128, 16, TT], BF16); C.YNT_r = [[Region(f"YNT{i}_{g}") for g in range(4)] for i in range(C.NT)]
    C.ONT = dscr("ONT", [128, 8, TT], BF16); C.ONT_r = [Region(f"ONT{i}") for i in range(C.NT)]
    C.X1 = dscr("X1", [TT, D], F32); C.X1_r = [Region(f"X1{i}") for i in range(C.NT)]
    C.X2 = dscr("X2", [TT, D], F32); C.X2_r = [Region(f"X2{i}") for i in range(C.NT)]

    with ExitStack() as st:
        kb = KB(nc, st)
        C.kb = kb
        C.cst = kb.sb("cst", [128, 1024]); kb.dma("sp", C.cst.t[:], C.consts[:, :], writes=[C.cst])
        C.colv = kb.sb("colv", [128, 176]); kb.dma("sp", C.colv.t[:], C.colvecs[:, :], writes=[C.colv])
        C.rowb = kb.sb("rowb", [128, 96]); kb.dma("sp", C.rowb.t[:], C.rowvecs[:, 0:96].partition_broadcast(128), writes=[C.rowb])
        C.identb = kb.sb("identb", [128, 128], BF16); kb.copy("dve", C.identb.t[:], C.cst.t[:, C_ID:C_ID + 128], [C.cst], [C.identb])
        C.a_bc = kb.sb("a_bc", [128, 32])
        kb.act(C.a_bc.t[:], C.rowb.t[:, RV_ALOG:RV_ALOG + 32], AF.Exp, [C.rowb], [C.a_bc])
        kb.ts("dve", C.a_bc.t[:], C.a_bc.t[:], -1.0, None, ALU.mult, None, [C.a_bc], [C.a_bc])
        C.colvh = kb.sb("colvh", [128, 120]); kb.ts("dve", C.colvh.t[:], C.colv.t[:, V_CONVB:V_CONVB + 120], 0.5, None, ALU.mult, None, [C.colv], [C.colvh])
        C.junk = kb.rot("junk", [128, D], BF16, 1)
        C.ss = kb.rot("ss", [128, 4], F32, 4)
        C.xn = kb.rot("xn", [128, D], BF16, 1)
        if upto >= 0:
            pass0(C)
        if upto >= 1 and not C.skip_ssd:
            ssd_pass(C)
        if upto >= 2 and not C.skip_ssd:
            hg_pass(C)
        if upto >= 3:
            merge_pass(C)
        if upto >= 4:
            attn_pass(C)
        if upto >= 5:
            ffn_pass(C)
        kb.finish()
    return nc


def host_inputs(inp, NCH=16):
    f = lambda a: np.ascontiguousarray(np.asarray(a, dtype=np.float32))
    TP = NCH * 128
    cv = np.zeros((128, 176), np.float32)
    col8 = lambda v: f(v).reshape(-1, 128).T
    cv[:, V_NMIX:V_NMIX + 8] = col8(inp["norm_mix"][0]); cv[:, V_NCROSS:V_NCROSS + 8] = col8(inp["norm_cross"][0])
    cv[:, V_NMEM:V_NMEM + 8] = col8(inp["norm_mem"][0]); cv[:, V_NFFN:V_NFFN + 8] = col8(inp["norm_ffn"][0])
    cv[:, V_SNORM:V_SNORM + 16] = col8(inp["ssd_norm"][0]); cv[:, V_HNORM:V_HNORM + 8] = col8(inp["hgrn_norm"][0])
    cv[:, V_CONVB:V_CONVB + 24] = col8(inp["conv_b"][0])
    cw = f(inp["conv_w"][0])
    cv[:, V_CONVW:V_CONVW + 96] = cw.T.reshape(24, 128, 4).transpose(1, 0, 2).reshape(128, 96)
    rv = np.concatenate([f(inp["dt_bias"][0]), f(inp["a_log"][0]), f(inp["d_skip"][0]), f(inp["hgrn_lb"][0]), f(inp["hgrn_lb"][1]), f(inp["norm_final"])])[None, :]
    consts = make_consts()
    shared = dict(w_in=f(inp["w_in"][0]), w_ssd_out=f(inp["w_ssd_out"][0]), w_hgrn_out=f(inp["w_hgrn_out"][0]), w_out=f(inp["w_out"][0]),
                  w_cq=f(inp["w_cq"][0]), w_ck=f(inp["w_ck"][0]), w_cv=f(inp["w_cv"][0]), w_co=f(inp["w_co"][0]),
                  w_gate=f(inp["w_gate"][0]), w_up=f(inp["w_up"][0]), w_down=f(inp["w_down"][0]),
                  consts=consts, colvecs=cv, rowvecs=f(rv))
    maps = []
    for c in range(8):
        sl = slice(16 * c, 16 * c + 16)
        m = dict(shared)
        m["xp"] = f(inp["x_prompt"][c][:TP]); m["xs"] = f(inp["x_sample"][sl]).reshape(64, D); m["mem"] = f(inp["mem_prompt"][c])
        m["st_ssm"] = f(inp["state_ssm"][0, sl]).reshape(16 * 2048, 128); m["st_conv"] = f(inp["state_conv"][0, sl]).reshape(48, 3072)
        m["st_hg"] = f(inp["state_hgrn"][0, sl]).reshape(16 * 1024, 128)
        m["ck"] = f(inp["cache_mem_k"][0, sl]).reshape(16 * 256, D); m["cv"] = f(inp["cache_mem_v"][0, sl]).reshape(16 * 256, D)
        maps.append(m)
    return maps


def kernel(**inputs):
    NCH = 16
    nc = build(NCH)
    maps = host_inputs(inputs, NCH)
    res = run_bass_kernel_spmd(nc, maps, core_ids=list(range(8))).results
    g = lambda k: [np.asarray(r[k], dtype=np.float32) for r in res]
    y_p = np.stack(g("o_y_p")).reshape(8, 2048, D)
    y_s = np.stack(g("o_y_s")).reshape(128, 4, D)
    ssm_p = np.stack(g("o_ssm_p")).reshape(1, 8, 32, 64, 128)
    conv_p = np.stack(g("o_conv_p")).reshape(1, 8, 3, 3072)
    hg_p = np.stack(g("o_hg_p")).reshape(1, 8, 8, 128, 128)
    mk = np.stack(g("o_mk")).reshape(1, 8, 256, 4, 256)
    mv = np.stack(g("o_mv")).reshape(1, 8, 256, 4, 256)
    ssm_s = np.stack(g("o_ssm_s")).reshape(1, 128, 32, 64, 128)
    conv_s = np.stack(g("o_conv_s")).reshape(1, 128, 3, 3072)
    hg_s = np.stack(g("o_hg_s")).reshape(1, 128, 8, 128, 128)
    return (y_p, y_s, ssm_p, conv_p, hg_p, mk, mv, ssm_s, conv_s, hg_s)
```

```python
import numpy as np
from contextlib import ExitStack
import concourse.bass as bass
import concourse.mybir as mybir
from concourse.bass_utils import run_bass_kernel_spmd

F32 = mybir.dt.float32
BF16 = mybir.dt.bfloat16
ALU = mybir.AluOpType
AF = mybir.ActivationFunctionType
AX = mybir.AxisListType

ND = 8
EPS = 1e-6
D = 1024
IN_DIM = 11296
FFN = 2816


class Region:
    __slots__ = ("name", "w", "r")

    def __init__(self, name=""):
        self.name = name
        self.w = None
        self.r = {}


class T:
    __slots__ = ("t", "r")

    def __init__(self, t, r=None):
        self.t = t
        self.r = r if r is not None else Region()


class Rot:
    def __init__(self, items):
        self.items = items
        self.i = 0

    def next(self):
        x = self.items[self.i % len(self.items)]
        self.i += 1
        return x


def _reg(x):
    return x.r if isinstance(x, T) else x


class KB:
    def __init__(self, nc, stack):
        self.nc = nc
        self.stack = stack
        self.engs = {"pe": nc.tensor, "act": nc.scalar, "dve": nc.vector, "pool": nc.gpsimd, "sp": nc.sync}
        self.semh = {}
        self.cnt = {}
        self.known = {e: {} for e in self.engs}
        for e in ("pe", "act", "dve", "pool"):
            self.semh[e] = stack.enter_context(nc.semaphore("s_" + e))
            self.cnt[e] = 0
        self.dcount = {}
        for q in ("sp", "pool"):
            self.dcount[q] = 0
            for i in range(ND):
                self.semh[("d", q, i)] = stack.enter_context(nc.semaphore(f"d_{q}_{i}"))
        self.psall = stack.enter_context(nc.psum_tensor("psall", [128, 4096], F32))
        self.banks = [T(self.psall[:, i * 512:(i + 1) * 512], Region(f"bank{i}")) for i in range(8)]
        self.free = list(range(8))
        self.bi = 0
        self.uid = 0
        self.rec = None

    def sb(self, name, shape, dt=F32, stack=None):
        self.uid += 1
        st = stack if stack is not None else self.stack
        return T(st.enter_context(self.nc.sbuf_tensor(f"{name}_{self.uid}", list(shape), dt)), Region(name))

    def rot(self, name, shape, dt=F32, n=2, stack=None):
        return Rot([self.sb(f"{name}{i}", shape, dt, stack) for i in range(n)])

    def bank(self):
        b = self.free[self.bi % len(self.free)]
        self.bi += 1
        return self.banks[b]

    def reserve(self, n):
        out = [self.free.pop() for _ in range(n)]
        return [self.banks[b] for b in out], out

    def release(self, ids):
        self.free.extend(ids)
        self.free.sort()

    def _wait(self, e, deps):
        kn = self.known[e]
        best = {}
        for d in deps:
            if d is None:
                continue
            k, v = d
            if e == "pe" and k == "pe":
                continue
            if kn.get(k, 0) < v and best.get(k, 0) < v:
                best[k] = v
        for k, v in best.items():
            self.engs[e].wait_ge(self.semh[k], v)
            kn[k] = v

    def _deps(self, reads, writes):
        deps = []
        for r in reads:
            deps.append(_reg(r).w)
        for w in writes:
            w = _reg(w)
            deps.append(w.w)
            deps.extend(w.r.items())
        return deps

    def _mark(self, tok, reads, writes):
        k, v = tok
        for r in reads:
            r = _reg(r)
            if r.r.get(k, 0) < v:
                r.r[k] = v
        for w in writes:
            w = _reg(w)
            w.w = tok
            w.r = {}

    def set_pool(self, ids):
        self.free = list(ids)
        self.bi = 0

    def begin(self, pool):
        self.set_pool(pool)
        self.rec = []

    def end(self):
        r, self.rec = self.rec, None
        return r

    def play(self, lst):
        assert self.rec is None
        for it in lst:
            if it[0] == "op":
                self.op(*it[1:5])
            else:
                self.dma(it[1], it[2], it[3], it[4], it[5], **it[6])

    def op(self, e, emit, reads=(), writes=(), cost=None):
        if self.rec is not None:
            if cost is None:
                if e == "pe":
                    px = _PEProxy()
                    emit(px)
                    cost = px.cost
                else:
                    cost = 0.5
            self.rec.append(("op", e, emit, tuple(reads), tuple(writes), cost))
            return None
        self._wait(e, self._deps(reads, writes))
        inst = emit(self.engs[e])
        self.cnt[e] += 1
        inst.then_inc(self.semh[e], 1)
        self._mark((e, self.cnt[e]), reads, writes)
        return inst

    def dma(self, q, out, in_, reads=(), writes=(), **kw):
        if self.rec is not None:
            self.rec.append(("dma", q, out, in_, tuple(reads), tuple(writes), kw))
            return None
        i = self.dcount[q]
        self.dcount[q] += 1
        key = ("d", q, i % ND)
        prev = 16 * (i // ND)
        deps = self._deps(reads, writes)
        if prev:
            deps.append((key, prev))
        self._wait(q, deps)
        self.engs[q].dma_start(out=out, in_=in_, **kw).then_inc(self.semh[key], 16)
        self._mark((key, prev + 16), reads, writes)

    def _all_tokens(self):
        deps = []
        for q, n in self.dcount.items():
            for s in range(ND):
                cnt = (n - s + ND - 1) // ND if n > s else 0
                if cnt:
                    deps.append((("d", q, s), 16 * cnt))
        for e, c in self.cnt.items():
            if c:
                deps.append((e, c))
        return deps

    def barrier(self):
        deps = self._all_tokens()
        for e in self.engs:
            self._wait(e, deps)

    def finish(self):
        self._wait("sp", self._all_tokens())

    def act(self, out, in_, func, reads, writes, **kw):
        return self.op("act", lambda e: e.activation(out=out, in_=in_, func=func, **kw), reads, writes, cost=_est("act", out))

    def tt(self, eng, out, in0, in1, op, reads, writes):
        return self.op(eng, lambda e: e.tensor_tensor(out=out, in0=in0, in1=in1, op=op), reads, writes, cost=_est(eng, out))

    def ts(self, eng, out, in0, s1, s2, op0, op1, reads, writes):
        if s2 is None:
            return self.op(eng, lambda e: e.tensor_scalar(out=out, in0=in0, scalar1=s1, scalar2=None, op0=op0), reads, writes, cost=_est(eng, out))
        return self.op(eng, lambda e: e.tensor_scalar(out=out, in0=in0, scalar1=s1, scalar2=s2, op0=op0, op1=op1), reads, writes, cost=_est(eng, out))

    def stt(self, out, in0, scalar, in1, op0, op1, reads, writes):
        return self.op("dve", lambda e: e.scalar_tensor_tensor(out=out, in0=in0, scalar=scalar, in1=in1, op0=op0, op1=op1), reads, writes, cost=_est("dve", out))

    def copy(self, eng, out, in_, reads, writes):
        if eng == "act":
            return self.op("act", lambda e: e.copy(out=out, in_=in_), reads, writes, cost=_est("act", out))
        return self.op(eng, lambda e: e.tensor_copy(out=out, in_=in_), reads, writes, cost=_est(eng, out))

    def memset(self, eng, ap, val, writes):
        return self.op(eng, lambda e: e.memset(ap, val), (), writes)


def _free(ap):
    n = 1
    for d in ap.shape[1:]:
        n *= int(d)
    return n


def _est(eng, out):
    n = _free(out)
    if eng == "act":
        return 0.2 + n / 1200.0
    if eng == "dve":
        return 0.2 + n / 960.0
    return 0.25 + n / 450.0


class _PEProxy:
    def __init__(self):
        self.cost = 0.0

    def matmul(self, out, lhsT=None, rhs=None, **kw):
        p = 4 if lhsT.dtype == F32 else 1
        self.cost += 0.03 + max(_free(out), 32) * p / 2400.0
        return self

    def transpose(self, out, in_, ident, **kw):
        p = 4 if in_.dtype == F32 else 1
        self.cost += 0.03 + max(_free(out), 32) * p / 2400.0
        return self


def schedule(*streams):
    ops = [it for st_ in streams for it in st_]
    n = len(ops)
    engs, R, W, cost = [], [], [], []
    for it in ops:
        if it[0] == "op":
            engs.append(it[1]); R.append([_reg(x) for x in it[3]]); W.append([_reg(x) for x in it[4]]); cost.append(it[5])
        else:
            engs.append(it[1]); R.append([_reg(x) for x in it[4]]); W.append([_reg(x) for x in it[5]]); cost.append(2.0)
    preds = [set() for _ in range(n)]
    lastw, readers = {}, {}
    laste = {}
    k = 0
    for si, st_ in enumerate(streams):
        for _ in st_:
            i = k
            k += 1
            for r in R[i]:
                if id(r) in lastw:
                    preds[i].add(lastw[id(r)])
            for w in W[i]:
                if id(w) in lastw:
                    preds[i].add(lastw[id(w)])
                preds[i].update(readers.get(id(w), ()))
            for r in R[i]:
                readers.setdefault(id(r), []).append(i)
            for w in W[i]:
                lastw[id(w)] = i
                readers[id(w)] = []
            key = (si, engs[i])
            if key in laste:
                preds[i].add(laste[key])
            laste[key] = i
            preds[i].discard(i)
    succs = [[] for _ in range(n)]
    indeg = [len(p) for p in preds]
    for i, p in enumerate(preds):
        for j in p:
            succs[j].append(i)
    ready = [i for i in range(n) if indeg[i] == 0]
    efree = {}
    fin = [0.0] * n
    out = []
    LAT = 0.3
    while ready:
        best, bi = None, None
        for i in ready:
            e = engs[i]
            t = efree.get(e, 0.0)
            for j in preds[i]:
                tj = fin[j] + (0.0 if engs[j] == e else LAT)
                if tj > t:
                    t = tj
            if best is None or (t, i) < best:
                best, bi = (t, i), i
        ready.remove(bi)
        e = engs[bi]
        t0 = best[0]
        if ops[bi][0] == "dma":
            efree[e] = t0 + 0.1
            fin[bi] = t0 + 2.0
        else:
            fin[bi] = t0 + cost[bi]
            efree[e] = fin[bi]
        out.append(ops[bi])
        for j in succs[bi]:
            indeg[j] -= 1
            if indeg[j] == 0:
                ready.append(j)
    assert len(out) == n
    return out


def interleave(*lists):
    lists = [l for l in lists if l]
    idx = [0] * len(lists)
    out = []
    total = sum(len(l) for l in lists)
    while len(out) < total:
        best, bk = None, None
        for k, l in enumerate(lists):
            if idx[k] < len(l):
                key = (idx[k] + 0.5) / len(l)
                if best is None or key < best:
                    best, bk = key, k
        out.append(lists[bk][idx[bk]])
        idx[bk] += 1
    return out


def bc(ap, shape, axis):
    return ap.unsqueeze(axis).to_broadcast(list(shape))


C_ID, C_TPI, C_TPA, C_ONE, C_HPI, C_HPA, C_TSI, C_TSA, C_SSM, C_SEGP, C_SEGS = 0, 128, 256, 384, 512, 640, 768, 832, 896, 960, 964


def make_consts():
    c = np.zeros((128, 1024), np.float32)
    i = np.arange(128)
    c[:, C_ID:C_ID + 128] = np.eye(128)
    c[:, C_TPI:C_TPI + 128] = (i[:, None] <= i[None, :])
    c[:, C_TPA:C_TPA + 128] = (i[:, None] > i[None, :])
    c[:, C_ONE:C_ONE + 128] = 1.0
    s32 = i // 32
    same = s32[:, None] == s32[None, :]
    c[:, C_HPI:C_HPI + 128] = same & (i[:, None] <= i[None, :])
    c[:, C_HPA:C_HPA + 128] = same & (i[:, None] > i[None, :])
    j = np.arange(64)
    s4 = j // 4
    same4 = s4[:, None] == s4[None, :]
    c[:64, C_TSI:C_TSI + 64] = same4 & (j[:, None] <= j[None, :])
    c[:64, C_TSA:C_TSA + 64] = same4 & (j[:, None] > j[None, :])
    c[:64, C_SSM:C_SSM + 64] = same4
    c[:, C_SEGP:C_SEGP + 4] = (s32[:, None] == np.arange(4)[None, :])
    c[:64, C_SEGS:C_SEGS + 16] = (s4[:, None] == np.arange(16)[None, :])
    return c


V_NMIX, V_NCROSS, V_NMEM, V_NFFN, V_SNORM, V_HNORM, V_CONVB, V_CONVW = 0, 8, 16, 24, 32, 48, 56, 80
RV_DTB, RV_ALOG, RV_DSKIP, RV_LB0, RV_LB1, RV_NFIN = 0, 32, 64, 96, 1120, 2144
RV_N = 3168


class Ctx:
    pass


def load_w(C, name, src, ncols, nk=8, stack=None):
    kb = C.kb
    w = kb.sb(name, [128, nk, ncols], BF16, stack)
    for k in range(nk):
        kb.dma("pool", w.t[:, k, :], src[k * 128:(k + 1) * 128, :], writes=[w])
    return w


def rms_rstd(C, xt, q, ss, junk=None):
    kb = C.kb
    junk = junk if junk is not None else C.junk.next()
    kb.act(junk.t[:q], xt.t[:q], AF.Square, [xt], [junk, ss], accum_out=ss.t[:q, 1:2])
    kb.ts("dve", ss.t[:q, 2:3], ss.t[:q, 1:2], 1.0 / D, EPS, ALU.mult, ALU.add, [ss], [ss])
    kb.act(ss.t[:q, 1:2], ss.t[:q, 2:3], AF.Sqrt, [ss], [ss])
    kb.op("dve", lambda e: e.reciprocal(out=ss.t[:q, 0:1], in_=ss.t[:q, 1:2]), [ss], [ss])


def rms_to_T(C, xt, q, wcol, dst_ap, dst, scratch=None):
    kb = C.kb
    ss = scratch["ss"] if scratch else C.ss.next()
    rms_rstd(C, xt, q, ss, scratch["junk"] if scratch else None)
    xn = scratch["xn"] if scratch else C.xn.next()
    kb.act(xn.t[:q], xt.t[:q], AF.Copy, [xt, ss], [xn], scale=ss.t[:q, 0:1])
    to_T(C, xn, q, 8, wcol, dst_ap, dst)


def to_T(C, src, q, nk, wcol, dst_ap, dst, eng="dve"):
    kb = C.kb
    for k0 in range(0, nk, 8):
        n = min(8, nk - k0)
        b = kb.bank()
        pb = b.t.bitcast(BF16)

        def emit(e, k0=k0, n=n, pb=pb):
            for k in range(n):
                last = e.transpose(pb[:, k * 128:k * 128 + q], src.t[:q, (k0 + k) * 128:(k0 + k + 1) * 128], C.identb.t[:q, :q])
            return last
        kb.op("pe", emit, [src, C.identb], [b])
        pv = pb.rearrange("p (k t) -> p k t", t=128)[:, 0:n, 0:q]
        if wcol is None:
            kb.copy("act", dst_ap[:, k0:k0 + n, :], pv, [b], [dst])
        else:
            kb.tt(eng, dst_ap[:, k0:k0 + n, :], pv, bc(wcol[:, k0:k0 + n], [128, n, q], 2), ALU.mult, [b, C.colv], [dst])


def pass0(C):
    kb = C.kb
    with ExitStack() as st:
        z = kb.sb("zero3", [128, 8, 3], BF16, st)
        kb.memset("pool", z.t[:], 0.0, [z])
        kb.dma("sp", C.nTd[:, :, 0:3], z.t[:], reads=[z], writes=[C.nTd_r[0]])
        sets = [dict(xin=kb.rot(f"xin{i}", [128, D], F32, 2, st), nts=kb.rot(f"nts{i}", [128, 8, 128], BF16, 2, st),
                     sc=[dict(xn=kb.sb(f"p0xn{i}{j}", [128, D], BF16, st), junk=kb.sb(f"p0jk{i}{j}", [128, D], BF16, st), ss=kb.sb(f"p0ss{i}{j}", [128, 4], F32, st)) for j in range(2)]) for i in range(2)]

        def tile(ti, S, j):
            q = 128 if ti < C.NCH else 64
            src = C.xp[ti * 128:(ti + 1) * 128, :] if ti < C.NCH else C.xs[:, :]
            xt = S["xin"].next()
            kb.dma("sp", xt.t[:q], src, writes=[xt])
            nt = S["nts"].next()
            rms_to_T(C, xt, q, C.colv.t[:, V_NMIX:V_NMIX + 8], nt.t[:, :, :q], nt, scratch=S["sc"][j])
            kb.dma("sp", C.nTd[:, :, 3 + ti * 128:3 + ti * 128 + q], nt.t[:, :, :q], reads=[nt], writes=[C.nTd_r[ti + 1]])
        lists = []
        for i in range(2):
            kb.begin([0, 1, 2, 3] if i == 0 else [4, 5, 6, 7])
            for n_, ti in enumerate(range(i, C.NT, 2)):
                tile(ti, sets[i], n_ % 2)
            lists.append(kb.end())
        kb.play(schedule(*lists))
        kb.set_pool(range(8))
        kb.barrier()


def ssd_small(C, q, sm, nTc, tcol, Wdt, tri_inc, tri_after, same):
    kb = C.kb
    b = kb.bank()

    def emit(e):
        for k in range(8):
            last = e.matmul(b.t[:q, 0:32], lhsT=nTc.t[:, k, tcol:tcol + q], rhs=Wdt.t[:, k, :], start=(k == 0), stop=(k == 7))
        return last
    kb.op("pe", emit, [nTc, Wdt], [b])
    kb.tt("dve", sm.t[:q, 0:32], b.t[:q, 0:32], C.rowb.t[:q, RV_DTB:RV_DTB + 32], ALU.add, [b, C.rowb], [sm])
    kb.act(sm.t[:q, 32:64], sm.t[:q, 0:32], AF.Exp, [sm], [sm])
    kb.act(sm.t[:q, 64:96], sm.t[:q, 32:64], AF.Ln, [sm], [sm], bias=1.0)
    kb.tt("dve", sm.t[:q, 96:128], sm.t[:q, 64:96], C.a_bc.t[:q, :], ALU.mult, [sm, C.a_bc], [sm])
    b2 = kb.bank()

    def emit2(e):
        e.matmul(b2.t[:q, 0:32], lhsT=tri_inc, rhs=sm.t[:q, 96:128], start=True, stop=True)
        e.matmul(b2.t[:q, 32:64], lhsT=tri_after, rhs=sm.t[:q, 96:128], start=True, stop=True)
        return e.matmul(b2.t[:q, 64:96], lhsT=same, rhs=sm.t[:q, 96:128], start=True, stop=True)
    kb.op("pe", emit2, [sm, C.cst], [b2])
    kb.act(sm.t[:q, 128:224], b2.t[:q, 0:96], AF.Exp, [b2], [sm])
    kb.tt("dve", sm.t[:q, 224:256], sm.t[:q, 160:192], sm.t[:q, 64:96], ALU.mult, [sm], [sm])


def ssd_conv(C, q, pre_view, acc, BCT, eng_split=3):
    kb = C.kb
    cw = C.colvh.t
    WB, WW = 0, V_CONVW - V_CONVB
    for cc in range(24):
        o = acc["view"](cc)
        kb.act(o, pre_view(cc, 0), AF.Identity, [acc["pre"], C.colvh], [acc["T"]], scale=cw[:, WW + cc * 4:WW + cc * 4 + 1], bias=cw[:, WB + cc:WB + cc + 1])
    for cc in range(24):
        o = acc["view"](cc)
        for j in range(1, 4):
            kb.stt(o, pre_view(cc, j), cw[:, WW + cc * 4 + j:WW + cc * 4 + j + 1], o, ALU.mult, ALU.add, [acc["pre"], C.colvh, acc["T"]], [acc["T"]])
    a = acc["T"]
    th = acc["tanh"]
    kb.act(th, a.t[:, :, :q], AF.Tanh, [a], [acc["pre"]])
    kb.stt(a.t[:, :, :q], th, 1.0, a.t[:, :, :q], ALU.add, ALU.mult, [acc["pre"], a], [a])
    kb.copy(acc.get("cast_eng", "pool"), BCT.t[:, :, :q], a.t[:, 16:24, :q], [a], [BCT])


def ssd_group_front(C, q, g, a, sm, W):
    kb = C.kb
    bx = kb.bank()

    def emit(e):
        for j in range(4):
            last = e.transpose(bx.t[:q, j * 128:(j + 1) * 128], a.t[:, g * 4 + j, :q], C.cst.t[:, C_ID:C_ID + 128])
        return last
    kb.op("pe", emit, [a, C.cst], [bx])
    bx3 = bx.t[:q, :].rearrange("p (h d) -> p h d", d=64)
    v3 = lambda ap: ap.rearrange("p (h d) -> p h d", d=64)
    kb.tt("dve", v3(W["xdt"].t[:q, g * 512:(g + 1) * 512]), bx3, bc(sm.t[:q, 64 + g * 8:72 + g * 8], [q, 8, 64], 2), ALU.mult, [bx, sm], [W["xdt"]])
    kb.tt("dve", v3(W["xw"].t[:q, g * 512:(g + 1) * 512]), bx3, bc(sm.t[:q, 224 + g * 8:232 + g * 8], [q, 8, 64], 2), ALU.mult, [bx, sm], [W["xw"]])
    kb.tt("dve", v3(W["xsd"].t[:q, g * 512:(g + 1) * 512]), bx3, bc(C.rowb.t[:q, RV_DSKIP + g * 8:RV_DSKIP + g * 8 + 8], [q, 8, 64], 2), ALU.mult, [bx, C.rowb], [W["xsd"]])


def ssd_group_y(C, q, g, sm, W, BCT, CBm, tri_inc, tri_after, byi, nTc, tcol, Wz, tok0, ci):
    kb = C.kb
    v3 = lambda ap: ap.rearrange("p (h d) -> p h d", d=64)
    tmp = W["tmp"].next()
    kb.tt("dve", v3(tmp.t[:q, :]), v3(byi.t[:q, :]), bc(sm.t[:q, 128 + g * 8:136 + g * 8], [q, 8, 64], 2), ALU.mult, [byi, sm], [tmp])
    by = kb.bank()
    kb.op("pe", lambda e: e.matmul(by.t[:q, :], lhsT=C.identb.t[:q, :q], rhs=W["xsd"].t[:q, g * 512:(g + 1) * 512], start=True, stop=False), [C.identb, W["xsd"]], [by])
    for hh in range(2):
        Rt = W["R"].next()
        kb.tt("dve", Rt.t[:q, :, :q], bc(tri_inc, [q, 4, q], 1), bc(sm.t[:q, 96 + g * 8 + hh * 4:100 + g * 8 + hh * 4], [q, 4, q], 2), ALU.mult, [C.cst, sm], [Rt])
        bL = kb.bank()
        kb.op("pe", lambda e, Rt=Rt, bL=bL: [e.matmul(bL.t[:q, h * q:(h + 1) * q], lhsT=tri_after, rhs=Rt.t[:q, h, :q], start=True, stop=True) for h in range(4)][-1], [Rt, C.cst], [bL])
        Lt = W["L"].next()
        kb.act(Lt.t[:q, 0:4 * q], bL.t[:q, 0:4 * q], AF.Exp, [bL], [Lt])
        MT = W["MT"].next()
        kb.tt("dve", MT.t[:q, :, :q], Lt.t[:q, 0:4 * q].rearrange("p (h t) -> p h t", t=q), bc(CBm.t[:q, g, :q], [q, 4, q], 1), ALU.mult, [Lt, CBm], [MT])

        def emit(e, MT=MT, hh=hh):
            for h in range(4):
                c0 = (hh * 4 + h) * 64
                last = e.matmul(by.t[:q, c0:c0 + 64], lhsT=MT.t[:q, h, :q], rhs=W["xdt"].t[:q, g * 512 + c0:g * 512 + c0 + 64], start=False, stop=False)
            return last
        kb.op("pe", emit, [MT, W["xdt"]], [by])
    kb.op("pe", lambda e: e.matmul(by.t[:q, :], lhsT=C.identb.t[:q, :q], rhs=tmp.t[:q, :], start=False, stop=True), [C.identb, tmp], [by])
    bz = kb.bank()
    kb.op("pe", lambda e: [e.matmul(bz.t[:q, :], lhsT=nTc.t[:, k, tcol:tcol + q], rhs=Wz.t[:, k, g * 512:(g + 1) * 512], start=(k == 0), stop=(k == 7)) for k in range(8)][-1], [nTc, Wz], [bz])
    sz = W["sz"].next()
    kb.act(sz.t[:q, :], bz.t[:q, :], AF.Tanh, [bz], [sz], scale=0.5)
    kb.stt(sz.t[:q, :], sz.t[:q, :], 1.0, bz.t[:q, :], ALU.add, ALU.mult, [sz, bz], [sz])
    y = W["y"].next()
    kb.tt("dve", y.t[:q, :], by.t[:q, :], sz.t[:q, :], ALU.mult, [by, sz], [y])
    ms = C.ss.next()
    kb.act(sz.t[:q, :], y.t[:q, :], AF.Square, [y], [sz, ms], accum_out=ms.t[:q, 1:2])
    kb.ts("dve", ms.t[:q, 2:3], ms.t[:q, 1:2], 1.0 / 512, 4.0 * EPS, ALU.mult, ALU.add, [ms], [ms])
    kb.act(ms.t[:q, 1:2], ms.t[:q, 2:3], AF.Sqrt, [ms], [ms])
    kb.op("dve", lambda e: e.reciprocal(out=ms.t[:q, 0:1], in_=ms.t[:q, 1:2]), [ms], [ms])
    yn = W["yn"].next()
    kb.act(yn.t[:q, :], y.t[:q, :], AF.Copy, [y, ms], [yn], scale=ms.t[:q, 0:1])
    ynT = W["ynT"].next()
    to_T(C, yn, q, 4, C.colv.t[:, V_SNORM + g * 4:V_SNORM + g * 4 + 4], ynT.t[:, :, :q], ynT)
    kb.dma("sp", C.YNT[:, g * 4:(g + 1) * 4, tok0:tok0 + q], ynT.t[:, :, :q], reads=[ynT], writes=[C.YNT_r[ci][g]])


def ssd_pass(C):
    kb = C.kb
    NCH = C.NCH
    cst = C.cst.t
    with ExitStack() as stw:
        Wz = load_w(C, "Wz", C.w_in[:, 0:2048], 2048, stack=stw)
        Wx = load_w(C, "Wx", C.w_in[:, 2048:5120], 3072, stack=stw)
        Wdt = load_w(C, "Wdt", C.w_in[:, 5120:5152], 32, stack=stw)
        with ExitStack() as st:
            q = 128
            nTr = kb.rot("nTc", [128, 8, 131], BF16, 2, st)
            pre = kb.sb("pre", [128, 24, 131], F32, st)
            accr = kb.rot("acc", [128, 24, 128], F32, 1, st)
            BCTr = kb.rot("BCT", [128, 8, 128], BF16, 2, st)
            CBmr = kb.rot("CBm", [128, 4, 128], F32, 2, st)
            Btokr = kb.rot("Btok", [128, 512], BF16, 2, st)
            smr = kb.rot("sm", [128, 256], F32, 2, st)
            MTs = [[kb.sb(f"MT{i}_{g}", [128, 8, 128], BF16, st) for g in range(4)] for i in range(2)]
            szs = [[kb.sb(f"sz{i}_{g}", [128, 512], BF16, st) for g in range(4)] for i in range(2)]
            Rr = kb.rot("R", [128, 4, 128], F32, 2, st)
            Lr = kb.rot("L", [128, 512], F32, 2, st)
            xdt = [kb.sb(f"xdt{g}", [128, 512], BF16, st) for g in range(4)]
            xw = [kb.sb(f"xw{g}", [128, 512], BF16, st) for g in range(4)]
            xsd = [kb.sb(f"xsd{g}", [128, 512], BF16, st) for g in range(4)]
            hT = [kb.sb(f"hT{g}", [128, 512], F32, st) for g in range(4)]
            hTb = [kb.sb(f"hTb{g}", [128, 512], BF16, st) for g in range(4)]
            SW = [dict(y=kb.rot(f"y{i}", [128, 512], F32, 2, st), yn=kb.rot(f"yn{i}", [128, 512], BF16, 2, st), ynT=kb.rot(f"ynT{i}", [128, 4, 128], BF16, 2, st),
                       tmp=kb.rot(f"ytmp{i}", [128, 512], BF16, 2, st)) for i in range(2)]
            for g in range(4):
                kb.memset("pool", hT[g].t[:], 0.0, [hT[g]])
                kb.memset("pool", hTb[g].t[:], 0.0, [hTb[g]])
            tri_inc, tri_after, ones = cst[:, C_TPI:C_TPI + 128], cst[:, C_TPA:C_TPA + 128], cst[:, C_ONE:C_ONE + 128]
            v3 = lambda ap: ap.rearrange("p (h d) -> p h d", d=64)

            def Fa(c, nTc, sm, a, BCT, CBm, Btok, MTa, sza):
                kb.dma("sp", nTc.t[:, :, :], C.nTd[:, :, c * 128:c * 128 + 131], reads=[C.nTd_r[c], C.nTd_r[c + 1]], writes=[nTc])
                ssd_small(C, q, sm, nTc, 3, Wdt, tri_inc, tri_after, ones)
                for g in range(4):
                    for hh in range(2):
                        h0 = g * 8 + hh * 4
                        Rt = Rr.next()
                        kb.tt("dve", Rt.t[:, :, :], bc(tri_inc, [128, 4, 128], 1), bc(sm.t[:, 96 + h0:100 + h0], [128, 4, 128], 2), ALU.mult, [C.cst, sm], [Rt])
                        bL = kb.bank()
                        kb.op("pe", lambda e, Rt=Rt, bL=bL: [e.matmul(bL.t[:, h * 128:(h + 1) * 128], lhsT=tri_after, rhs=Rt.t[:, h, :], start=True, stop=True) for h in range(4)][-1], [Rt, C.cst], [bL])
                        kb.act(MTa[g].t[:, hh * 4:hh * 4 + 4, :], bL.t[:, :].rearrange("p (h t) -> p h t", t=128), AF.Exp, [bL], [MTa[g]])

            def Fc(c, nTc, sm, a, BCT, CBm, Btok, MTa, sza):
                for g in range(4):
                    bz = kb.bank()
                    kb.op("pe", lambda e, bz=bz, g=g: [e.matmul(bz.t[:, :], lhsT=nTc.t[:, k, 3:131], rhs=Wz.t[:, k, g * 512:(g + 1) * 512], start=(k == 0), stop=(k == 7)) for k in range(8)][-1], [nTc, Wz], [bz])
                    zt_ = Lr.next()
                    kb.act(zt_.t[:, :], bz.t[:, :], AF.Tanh, [bz], [zt_], scale=0.5)
                    kb.stt(sza[g].t[:, :], zt_.t[:, :], 1.0, bz.t[:, :], ALU.add, ALU.mult, [zt_, bz], [sza[g]])

            def Fb(c, nTc, sm, a, BCT, CBm, Btok, MTa, sza):
                for grp in range(8):
                    b = kb.bank()

                    def emit(e, grp=grp, b=b):
                        for j in range(3):
                            cc = grp * 3 + j
                            for k in range(8):
                                last = e.matmul(b.t[:, j * 131:(j + 1) * 131], lhsT=Wx.t[:, k, cc * 128:(cc + 1) * 128], rhs=nTc.t[:, k, 0:131], start=(k == 0), stop=(k == 7))
                        return last
                    kb.op("pe", emit, [Wx, nTc], [b])
                    kb.act(pre.t[:, grp * 3:grp * 3 + 3, :], b.t[:, 0:393].rearrange("p (j t) -> p j t", t=131), AF.Copy, [b], [pre])
                if c == NCH - 1:
                    for s_ in range(6):
                        b = kb.bank()
                        kb.op("pe", lambda e, b=b, s_=s_: [e.matmul(b.t[:3, :], lhsT=nTc.t[:, k, 128:131], rhs=Wx.t[:, k, s_ * 512:(s_ + 1) * 512], start=(k == 0), stop=(k == 7)) for k in range(8)][-1], [Wx, nTc], [b])
                        cv_ = Lr.next()
                        kb.copy("act", cv_.t[:3, :], b.t[:3, :], [b], [cv_])
                        kb.dma("sp", C.o_conv_p[:, s_ * 512:(s_ + 1) * 512], cv_.t[:3, :], reads=[cv_])
                accd = {"T": a, "pre": pre, "view": lambda cc: a.t[:, cc, :], "tanh": pre.t[:, :, 0:128], "cast_eng": "act"}
                ssd_conv(C, q, lambda cc, j: pre.t[:, cc, j:j + 128], accd, BCT)
                b = kb.bank()
                kb.op("pe", lambda e, b=b: [e.matmul(b.t[:, g * 128:(g + 1) * 128], lhsT=BCT.t[:, g, :], rhs=BCT.t[:, 4 + g, :], start=True, stop=True) for g in range(4)][-1], [BCT], [b])
                kb.tt("dve", CBm.t[:, :, :], b.t[:, :].rearrange("p (g t) -> p g t", t=128), bc(tri_inc, [128, 4, 128], 1), ALU.mult, [b, C.cst], [CBm])
                b2 = kb.bank()
                kb.op("pe", lambda e, b2=b2: [e.transpose(b2.t[:, g * 128:(g + 1) * 128], a.t[:, 16 + g, :], cst[:, C_ID:C_ID + 128]) for g in range(4)][-1], [a, C.cst], [b2])
                kb.copy("act", Btok.t[:, :], b2.t[:, :], [b2], [Btok])
                for g in range(4):
                    kb.tt("dve", MTa[g].t[:, :, :], MTa[g].t[:, :, :], bc(CBm.t[:, g, :], [128, 8, 128], 1), ALU.mult, [MTa[g], CBm], [MTa[g]])

            def Gfront(c, g, Wk, nTc, sm, a, BCT, CBm, Btok, MTa, sza):
                bx = kb.bank()
                kb.op("pe", lambda e: [e.transpose(bx.t[:, j * 128:(j + 1) * 128], a.t[:, g * 4 + j, :], cst[:, C_ID:C_ID + 128]) for j in range(4)][-1], [a, C.cst], [bx])
                bx3 = v3(bx.t[:, :])
                kb.tt("dve", v3(xdt[g].t[:, :]), bx3, bc(sm.t[:, 64 + g * 8:72 + g * 8], [128, 8, 64], 2), ALU.mult, [bx, sm], [xdt[g]])
                kb.tt("dve", v3(xw[g].t[:, :]), bx3, bc(sm.t[:, 224 + g * 8:232 + g * 8], [128, 8, 64], 2), ALU.mult, [bx, sm], [xw[g]])
                kb.tt("dve", v3(xsd[g].t[:, :]), bx3, bc(C.rowb.t[:, RV_DSKIP + g * 8:RV_DSKIP + g * 8 + 8], [128, 8, 64], 2), ALU.mult, [bx, C.rowb], [xsd[g]])

            def Ggroup(c, g, Wk, nTc, sm, a, BCT, CBm, Btok, MTa, sza):
                byi = kb.bank()
                kb.op("pe", lambda e: e.matmul(byi.t[:, :], lhsT=BCT.t[:, 4 + g, :], rhs=hTb[g].t[:, :], start=True, stop=True), [BCT, hTb[g]], [byi])
                bd = kb.bank()
                kb.op("pe", lambda e: e.matmul(bd.t[:, :], lhsT=Btok.t[:, g * 128:(g + 1) * 128], rhs=xw[g].t[:, :], start=True, stop=True), [Btok, xw[g]], [bd])
                tmp = Wk["tmp"].next()
                kb.tt("dve", v3(tmp.t[:, :]), v3(byi.t[:, :]), bc(sm.t[:, 128 + g * 8:136 + g * 8], [128, 8, 64], 2), ALU.mult, [byi, sm], [tmp])
                hv = v3(hT[g].t[:, :])
                kb.tt("dve", hv, hv, bc(sm.t[:, 192 + g * 8:200 + g * 8], [128, 8, 64], 2), ALU.mult, [hT[g], sm], [hT[g]])
                kb.tt("dve", hT[g].t[:, :], hT[g].t[:, :], bd.t[:, :], ALU.add, [hT[g], bd], [hT[g]])
                by = kb.bank()

                def emit(e):
                    e.matmul(by.t[:, :], lhsT=C.identb.t[:, :], rhs=xsd[g].t[:, :], start=True, stop=False)
                    for h in range(8):
                        e.matmul(by.t[:, h * 64:(h + 1) * 64], lhsT=MTa[g].t[:, h, :], rhs=xdt[g].t[:, h * 64:(h + 1) * 64], start=False, stop=False)
                    return e.matmul(by.t[:, :], lhsT=C.identb.t[:, :], rhs=tmp.t[:, :], start=False, stop=True)
                kb.op("pe", emit, [C.identb, xsd[g], MTa[g], xdt[g], tmp], [by])
                y = Wk["y"].next()
                kb.tt("dve", y.t[:, :], by.t[:, :], sza[g].t[:, :], ALU.mult, [by, sza[g]], [y])
                ms = C.ss.next()
                yn = Wk["yn"].next()
                kb.act(yn.t[:, :], y.t[:, :], AF.Square, [y], [yn, ms], accum_out=ms.t[:, 1:2])
                kb.ts("dve", ms.t[:, 2:3], ms.t[:, 1:2], 1.0 / 512, 4.0 * EPS, ALU.mult, ALU.add, [ms], [ms])
                kb.act(ms.t[:, 1:2], ms.t[:, 2:3], AF.Sqrt, [ms], [ms])
                kb.op("dve", lambda e: e.reciprocal(out=ms.t[:, 0:1], in_=ms.t[:, 1:2]), [ms], [ms])
                kb.act(yn.t[:, :], y.t[:, :], AF.Copy, [y, ms], [yn], scale=ms.t[:, 0:1])
                ynT = Wk["ynT"].next()
                to_T(C, yn, q, 4, C.colv.t[:, V_SNORM + g * 4:V_SNORM + g * 4 + 4], ynT.t[:, :, :], ynT)
                kb.dma("sp", C.YNT[:, g * 4:(g + 1) * 4, c * 128:c * 128 + 128], ynT.t[:, :, :], reads=[ynT], writes=[C.YNT_r[c][g]])
                kb.copy("act", hTb[g].t[:, :], hT[g].t[:, :], [hT[g]], [hTb[g]])

            sets = [(nTr.next(), smr.next(), accr.next(), BCTr.next(), CBmr.next(), Btokr.next(), MTs[i], szs[i]) for i in range(2)]
            GA, GB = [4, 5], [6, 7]

            def Flists(c, S_):
                kb.begin([0]); Fa(c, *S_); la = kb.end()
                kb.begin([3]); Fc(c, *S_); lc = kb.end()
                kb.begin([1, 2]); Fb(c, *S_); lb2 = kb.end()
                return [la, lc, lb2]
            kb.play(schedule(*Flists(0, sets[0])))
            for c in range(NCH):
                S_ = sets[c % 2]
                kb.begin(GA); Gfront(c, 0, SW[0], *S_); Gfront(c, 1, SW[0], *S_); Ggroup(c, 0, SW[0], *S_); Ggroup(c, 1, SW[0], *S_); ga = kb.end()
                kb.begin(GB); Gfront(c, 2, SW[1], *S_); Gfront(c, 3, SW[1], *S_); Ggroup(c, 2, SW[1], *S_); Ggroup(c, 3, SW[1], *S_); gb = kb.end()
                fl = Flists(c + 1, sets[(c + 1) % 2]) if c + 1 < NCH else []
                kb.play(schedule(ga, gb, *fl))
            kb.set_pool(range(8))
            so = T(pre.t[:].rearrange("p c t -> p (c t)")[:, 0:2048].rearrange("p (j n) -> p j n", n=128), pre.r)
            for j4 in range(4):
                b = kb.bank()
                kb.op("pe", lambda e, b=b, j4=j4: [e.transpose(b.t[:, jj * 128:(jj + 1) * 128], hT[j4].t[:, jj * 128:(jj + 1) * 128], cst[:, C_ID:C_ID + 128]) for jj in range(4)][-1], [hT[j4], C.cst], [b])
                kb.copy("act", so.t[:, j4 * 4:(j4 + 1) * 4, :], b.t[:, :].rearrange("p (j n) -> p j n", n=128), [b], [so])
            kb.dma("sp", C.o_ssm_p.rearrange("(j q) n -> q j n", q=128), so.t[:, :, :], reads=[so])
            kb.barrier()
        with ExitStack() as st:
            q = 64
            tri_inc, tri_after, same = cst[:64, C_TSI:C_TSI + 64], cst[:64, C_TSA:C_TSA + 64], cst[:64, C_SSM:C_SSM + 64]
            nTc = kb.sb("nTs", [128, 8, 64], BF16, st)
            kb.dma("sp", nTc.t[:, :, :], C.nTd[:, :, 3 + C.TP:3 + C.TP + 64], reads=[C.nTd_r[C.NT]], writes=[nTc])
            pre = kb.sb("pres", [128, 24, 16, 7], F32, st)
            a = kb.sb("accs", [128, 24, 64], F32, st)
            BCT = kb.sb("BCTs", [128, 8, 64], BF16, st)
            CBm = kb.sb("CBms", [64, 4, 64], F32, st)
            Btok = kb.sb("Btoks", [64, 512], BF16, st)
            sm = kb.sb("sms", [64, 256], F32, st)
            W = {
                "xdt": kb.sb("xdts", [64, 2048], BF16, st), "xw": kb.sb("xws", [64, 2048], BF16, st), "xsd": kb.sb("xsds", [64, 2048], BF16, st),
                "R": kb.rot("Rs", [64, 4, 64], F32, 2, st), "L": kb.rot("Ls", [64, 256], F32, 2, st), "MT": kb.rot("MTs", [64, 4, 64], BF16, 2, st),
                "y": kb.rot("ys", [64, 512], F32, 2, st), "sz": kb.rot("szs", [64, 512], F32, 2, st), "yn": kb.rot("yns", [64, 512], BF16, 2, st),
                "ynT": kb.rot("ynTs", [128, 4, 64], BF16, 2, st), "tmp": kb.rot("ytmps", [64, 512], BF16, 2, st),
            }
            cnvr = kb.rot("cnvs", [64, 512], F32, 2, st)
            ocs = C.o_conv_s.rearrange("(b j) c -> b j c", j=3)
            scr = kb.rot("stconv", [48, 512], F32, 2, st)
            for g6 in range(6):
                sc = scr.next()
                kb.dma("sp", sc.t[:, :], C.st_conv[:, g6 * 512:(g6 + 1) * 512], writes=[sc])
                b = kb.bank()
                kb.op("pe", lambda e, b=b, sc=sc: [e.transpose(b.t[:, jj * 48:(jj + 1) * 48], sc.t[:48, jj * 128:(jj + 1) * 128], cst[:48, C_ID:C_ID + 48]) for jj in range(4)][-1], [sc, C.cst], [b])
                kb.act(pre.t[:, g6 * 4:(g6 + 1) * 4, :, 0:3], b.t[:, 0:192].rearrange("p (c b j) -> p c b j", c=4, j=3), AF.Copy, [b], [pre])
            ssd_small(C, q, sm, nTc, 0, Wdt, tri_inc, tri_after, same)
            for grp in range(8):
                b = kb.bank()

                def emit(e, grp=grp, b=b):
                    for j in range(3):
                        cc = grp * 3 + j
                        for k in range(8):
                            last = e.matmul(b.t[:, j * 64:(j + 1) * 64], lhsT=Wx.t[:, k, cc * 128:(cc + 1) * 128], rhs=nTc.t[:, k, 0:64], start=(k == 0), stop=(k == 7))
                    return last
                kb.op("pe", emit, [Wx, nTc], [b])
                kb.act(pre.t[:, grp * 3:grp * 3 + 3, :, 3:7], b.t[:, 0:192].rearrange("p (c b j) -> p c b j", c=3, j=4), AF.Copy, [b], [pre])
            for s in range(6):
                b = kb.bank()
                kb.op("pe", lambda e, b=b, s=s: [e.matmul(b.t[:64, :], lhsT=nTc.t[:, k, 0:64], rhs=Wx.t[:, k, s * 512:(s + 1) * 512], start=(k == 0), stop=(k == 7)) for k in range(8)][-1], [Wx, nTc], [b])
                cv_ = cnvr.next()
                kb.copy("act", cv_.t[:64, :], b.t[:64, :], [b], [cv_])
                for t in range(1, 4):
                    kb.dma("sp", ocs[:, t - 1, s * 512:(s + 1) * 512], cv_.t[t:64:4, :], reads=[cv_])
            accd = {"T": a, "pre": pre, "view": lambda cc: a.t[:, cc, :].rearrange("p (b t) -> p b t", t=4), "tanh": pre.t[:].rearrange("p c b j -> p c (b j)")[:, :, 0:64]}
            ssd_conv(C, q, lambda cc, j: pre.t[:, cc, :, j:j + 4], accd, BCT)
            b = kb.bank()
            kb.op("pe", lambda e, b=b: [e.matmul(b.t[:64, g * 64:(g + 1) * 64], lhsT=BCT.t[:, g, :], rhs=BCT.t[:, 4 + g, :], start=True, stop=True) for g in range(4)][-1], [BCT], [b])
            kb.tt("dve", CBm.t[:, :, :], b.t[:64, 0:256].rearrange("p (g t) -> p g t", t=64), bc(tri_inc, [64, 4, 64], 1), ALU.mult, [b, C.cst], [CBm])
            b = kb.bank()
            kb.op("pe", lambda e, b=b: [e.transpose(b.t[:64, g * 128:(g + 1) * 128], a.t[:, 16 + g, :], cst[:, C_ID:C_ID + 128]) for g in range(4)][-1], [a, C.cst], [b])
            kb.copy("act", Btok.t[:, :], b.t[:64, :], [b], [Btok])
            CTm = kb.sb("CTm", [128, 4, 16 * 68], BF16, st)
            kb.memset("pool", CTm.t[:], 0.0, [CTm])
            for g in range(4):
                kb.copy("pool", CTm.t[:, g, :].rearrange("p (b x) -> p b x", x=68)[:, :, 0:4], BCT.t[:, 4 + g, :].rearrange("p (b t) -> p b t", t=4), [BCT], [CTm])
            for g in range(4):
                ssd_group_front(C, q, g, a, sm, W)
            decN = kb.sb("decN", [128, 256], F32, st)
            b = kb.bank()
            def emit_dec(e, b=b):
                for j in range(16):
                    for hl in range(2):
                        last = e.matmul(b.t[hl * 64:(hl + 1) * 64, j * 16:(j + 1) * 16], lhsT=sm.t[:64, 96 + 2 * j + hl:97 + 2 * j + hl].to_broadcast([64, 64]),
                                        rhs=cst[:64, C_SEGS:C_SEGS + 16], start=True, stop=True, tile_position=(0, hl * 64))
                return last
            kb.op("pe", emit_dec, [sm, C.cst], [b])
            kb.act(decN.t[:, :], b.t[:, 0:256], AF.Exp, [b], [decN])
            byis, ids = kb.reserve(4)
            h0r = kb.rot("h0nat", [128, 16, 128], F32, 2, st)
            nsr = kb.rot("newst", [128, 16, 128], F32, 1, st)
            hTbr = kb.rot("hTbb", [128, 2048], BF16, 2, st)
            xw4r = kb.rot("xw4", [4, 2048], BF16, 2, st)
            B4r = kb.rot("B4", [4, 512], BF16, 2, st)
            stin = C.st_ssm.rearrange("(b j q) n -> b q j n", j=16, q=128)
            stout = C.o_ssm_s.rearrange("(b j q) n -> b q j n", j=16, q=128)
            for bb in range(16):
                h0 = h0r.next()
                kb.dma("sp", h0.t[:, :, :], stin[bb], writes=[h0])
                xw4 = xw4r.next()
                B4 = B4r.next()
                kb.dma("sp", xw4.t[:, :], W["xw"].t[4 * bb:4 * bb + 4, :], reads=[W["xw"]], writes=[xw4])
                kb.dma("sp", B4.t[:, :], Btok.t[4 * bb:4 * bb + 4, :], reads=[Btok], writes=[B4])
                hb = hTbr.next()
                for j4 in range(4):
                    b = kb.bank()
                    kb.op("pe", lambda e, b=b, j4=j4, h0=h0: [e.transpose(b.t[:, jj * 128:(jj + 1) * 128], h0.t[:, j4 * 4 + jj, :], cst[:, C_ID:C_ID + 128]) for jj in range(4)][-1], [h0, C.cst], [b])
                    kb.copy("act", hb.t[:, j4 * 512:(j4 + 1) * 512], b.t[:, :], [b], [hb])
                for g in range(4):
                    kb.op("pe", lambda e, g=g, hb=hb, bb=bb: e.matmul(byis[g].t[:64, :], lhsT=CTm.t[:, g, bb * 64:(bb + 1) * 64], rhs=hb.t[:, g * 512:(g + 1) * 512], start=(bb == 0), stop=(bb == 15)), [CTm, hb], [byis[g]])
                ns = nsr.next()
                for j4 in range(4):
                    b = kb.bank()
                    kb.op("pe", lambda e, b=b, j4=j4, xw4=xw4, B4=B4: [e.matmul(b.t[:, jj * 128:(jj + 1) * 128], lhsT=xw4.t[0:4, (j4 * 4 + jj) * 128:(j4 * 4 + jj + 1) * 128], rhs=B4.t[0:4, j4 * 128:(j4 + 1) * 128], start=True, stop=True) for jj in range(4)][-1], [xw4, B4], [b])
                    for jj in range(4):
                        j = j4 * 4 + jj
                        kb.stt(ns.t[:, j, :], h0.t[:, j, :], decN.t[:, j * 16 + bb:j * 16 + bb + 1], b.t[:, jj * 128:(jj + 1) * 128], ALU.mult, ALU.add, [h0, decN, b], [ns])
                kb.dma("sp", stout[bb], ns.t[:, :, :], reads=[ns])
            for g in range(4):
                ssd_group_y(C, q, g, sm, W, BCT, CBm, tri_inc, tri_after, byis[g], nTc, 0, Wz, C.TP, C.NCH)
            kb.release(ids)
            kb.barrier()


def hg_pass(C):
    kb = C.kb
    cst = C.cst.t
    with ExitStack() as stw:
        Wq = load_w(C, "Wq", C.w_in[:, 5152:6176], 1024, stack=stw)
        Wf = load_w(C, "Wf", C.w_in[:, 6176:7200], 1024, stack=stw)
        Wi = load_w(C, "Wi", C.w_in[:, 7200:8224], 1024, stack=stw)
        Wg = load_w(C, "Wg", C.w_in[:, 8224:9248], 1024, stack=stw)
        lb = kb.sb("lb", [128, 1024], F32, stw)
        oml = kb.sb("oml", [128, 1024], F32, stw)
        kb.dma("sp", lb.t[:], C.rowvecs[:, RV_LB0:RV_LB0 + 1024].partition_broadcast(128), writes=[lb])
        kb.dma("sp", oml.t[:], C.rowvecs[:, RV_LB1:RV_LB1 + 1024].partition_broadcast(128), writes=[oml])
        kb.tt("dve", lb.t[:], lb.t[:], oml.t[:], ALU.subtract, [lb, oml], [lb])
        kb.act(lb.t[:], lb.t[:], AF.Sigmoid, [lb], [lb])
        kb.ts("dve", oml.t[:], lb.t[:], -1.0, 1.0, ALU.mult, ALU.add, [lb], [oml])
        omlh = kb.sb("omlh", [128, 1024], F32, stw)
        kb.ts("dve", omlh.t[:], oml.t[:], 0.5, None, ALU.mult, None, [oml], [omlh])

        def proj(q, nTc, tcol, Wt, half):
            b = kb.bank()
            kb.op("pe", lambda e: [e.matmul(b.t[:q, :], lhsT=nTc.t[:, k, tcol:tcol + q], rhs=Wt.t[:, k, half * 512:(half + 1) * 512], start=(k == 0), stop=(k == 7)) for k in range(8)][-1], [nTc, Wt], [b])
            return b

        def front(q, nseg, seglen, nTc, tcol, B, hinc, hafter, segones):
            hs = slice(0, q)
            for half in range(2):
                cs = slice(half * 512, (half + 1) * 512)
                b = proj(q, nTc, tcol, Wf, half)
                kb.act(B["w"].t[hs, cs], b.t[hs, :], AF.Tanh, [b], [B["w"]], scale=0.5)
            kb.stt(B["w"].t[hs, :], B["w"].t[hs, :], 1.0, omlh.t[hs, :], ALU.add, ALU.mult, [B["w"], omlh], [B["w"]])
            kb.tt("dve", B["logf"].t[hs, :], B["w"].t[hs, :], lb.t[hs, :], ALU.add, [B["w"], lb], [B["logf"]])
            kb.act(B["logf"].t[hs, :], B["logf"].t[hs, :], AF.Ln, [B["logf"]], [B["logf"]])
            kb.tt("dve", B["kk"].t[hs, :], oml.t[hs, :], B["w"].t[hs, :], ALU.subtract, [B["w"], oml], [B["kk"]])
            for half in range(2):
                cs = slice(half * 512, (half + 1) * 512)
                b = kb.bank()
                kb.op("pe", lambda e, b=b, cs=cs: e.matmul(b.t[hs, :], lhsT=hinc, rhs=B["logf"].t[hs, cs], start=True, stop=True), [B["logf"], C.cst], [b])
                kb.act(B["E1"].t[hs, cs], b.t[hs, :], AF.Exp, [b], [B["E1"]])
                kb.act(B["E1n"].t[hs, cs], b.t[hs, :], AF.Exp, [b], [B["E1n"]], scale=-1.0)
                b2 = kb.bank()
                kb.op("pe", lambda e, b2=b2, cs=cs: e.matmul(b2.t[hs, :], lhsT=hafter, rhs=B["logf"].t[hs, cs], start=True, stop=True), [B["logf"], C.cst], [b2])
                kb.act(B["E2"].t[hs, cs], b2.t[hs, :], AF.Exp, [b2], [B["E2"]])
            b = kb.bank()
            kb.op("pe", lambda e, b=b: [e.matmul(b.t[:, h * nseg:(h + 1) * nseg], lhsT=B["logf"].t[hs, h * 128:(h + 1) * 128], rhs=segones, start=True, stop=True) for h in range(8)][-1], [B["logf"], C.cst], [b])
            kb.act(B["dS"].t[:, 0:8 * nseg], b.t[:, 0:8 * nseg], AF.Exp, [b], [B["dS"]])
            for half in range(2):
                cs = slice(half * 512, (half + 1) * 512)
                b = proj(q, nTc, tcol, Wq, half)
                kb.act(B["sq"].t[hs, cs], b.t[hs, :], AF.Tanh, [b], [B["sq"]], scale=0.5)
                kb.stt(B["sq"].t[hs, cs], B["sq"].t[hs, cs], 1.0, b.t[hs, :], ALU.add, ALU.mult, [B["sq"], b], [B["sq"]])
            kb.tt("dve", B["qg"].t[hs, :], B["sq"].t[hs, :], B["E1"].t[hs, :], ALU.mult, [B["sq"], B["E1"]], [B["qg"]])
            kb.tt("dve", B["kg"].t[hs, :], B["kk"].t[hs, :], B["E1n"].t[hs, :], ALU.mult, [B["kk"], B["E1n"]], [B["kg"]])
            kb.tt("dve", B["kdec"].t[hs, :], B["kk"].t[hs, :], B["E2"].t[hs, :], ALU.mult, [B["kk"], B["E2"]], [B["kdec"]])
            for half in range(2):
                cs = slice(half * 512, (half + 1) * 512)
                b = proj(q, nTc, tcol, Wi, half)
                kb.copy("act", B["v"].t[hs, cs], b.t[hs, :], [b], [B["v"]])
            for half in range(2):
                cs = slice(half * 512, (half + 1) * 512)
                b = proj(q, nTc, tcol, Wg, half)
                kb.act(B["sg"].t[hs, cs], b.t[hs, :], AF.Tanh, [b], [B["sg"]], scale=0.5)
                kb.stt(B["sg"].t[hs, cs], B["sg"].t[hs, cs], 1.0, b.t[hs, :], ALU.add, ALU.mult, [B["sg"], b], [B["sg"]])
            to_T(C, B["qg"], q, 8, None, B["qgT"].t[:, :, :q], B["qgT"])
            to_T(C, B["kg"], q, 8, None, B["kgT"].t[:, :, :q], B["kgT"])
            x = q + seglen
            kb.copy("pool", B["QM"].t[:, :, :].rearrange("p h (c x) -> p h c x", x=x)[:, :, :, 0:seglen],
                    B["qgT"].t[:, :, :q].rearrange("p h (c j) -> p h c j", j=seglen), [B["qgT"]], [B["QM"]])
            for hh in range(2):
                b = kb.bank()
                kb.op("pe", lambda e, b=b, hh=hh: [e.matmul(b.t[hs, h4 * q:(h4 + 1) * q], lhsT=B["kgT"].t[:, hh * 4 + h4, :q], rhs=B["qgT"].t[:, hh * 4 + h4, :q], start=True, stop=True) for h4 in range(4)][-1], [B["kgT"], B["qgT"]], [b])
                kb.tt("dve", B["att"].t[hs, hh * 4:(hh + 1) * 4, :q], b.t[hs, 0:4 * q].rearrange("p (h t) -> p h t", t=q), bc(hinc, [q, 4, q], 1), ALU.mult, [b, C.cst], [B["att"]])

        def back(q, bo, B, tok0, ci):
            hs = slice(0, q)
            for half in range(2):
                kb.copy("act", B["osb"].t[hs, half * 512:(half + 1) * 512], bo[half].t[hs, :], [bo[half]], [B["osb"]])
            o = B["osb"]
            kb.tt("pool", B["osq"].t[hs, :], o.t[hs, :], o.t[hs, :], ALU.mult, [o], [B["osq"]])
            hsm = B["hsm"]
            kb.op("dve", lambda e: e.tensor_reduce(out=hsm.t[hs, 0:8], in_=B["osq"].t[hs, :].rearrange("p (h v) -> p h v", v=128), axis=AX.X, op=ALU.add), [B["osq"]], [hsm])
            kb.ts("dve", hsm.t[hs, 8:16], hsm.t[hs, 0:8], 4.0 / 128, 16.0 * EPS, ALU.mult, ALU.add, [hsm], [hsm])
            kb.act(hsm.t[hs, 16:24], hsm.t[hs, 8:16], AF.Sqrt, [hsm], [hsm])
            kb.op("dve", lambda e: e.reciprocal(out=hsm.t[hs, 24:32], in_=hsm.t[hs, 16:24]), [hsm], [hsm])
            o3 = o.t[hs, :].rearrange("p (h v) -> p h v", v=128)
            kb.tt("pool", o3, o3, bc(hsm.t[hs, 24:32], [q, 8, 128], 2), ALU.mult, [o, hsm], [o])
            kb.tt("dve", B["on"].t[hs, :], o.t[hs, :], B["sg"].t[hs, :], ALU.mult, [o, B["sg"]], [B["on"]])
            to_T(C, B["on"], q, 8, C.colv.t[:, V_HNORM:V_HNORM + 8], B["onT"].t[:, :, :q], B["onT"])
            kb.dma("sp", C.ONT[:, :, tok0:tok0 + q], B["onT"].t[:, :, :q], reads=[B["onT"]], writes=[C.ONT_r[ci]])

        def bufs(q, nseg, seglen, st, nslots=1):
            shared = {}
            for n in ("w", "logf", "kk", "E1", "E1n", "E2", "sq", "osb", "osq"):
                shared[n] = kb.sb("hg_" + n, [q, 1024], F32, st)
            for n in ("qg", "kg", "on"):
                shared[n] = kb.sb("hg_" + n, [q, 1024], BF16, st)
            shared["qgT"] = kb.sb("hg_qgT", [128, 8, q], BF16, st)
            shared["kgT"] = kb.sb("hg_kgT", [128, 8, q], BF16, st)
            shared["onT"] = kb.sb("hg_onT", [128, 8, q], BF16, st)
            shared["hsm"] = kb.sb("hg_hsm", [q, 32], F32, st)
            out = []
            for i in range(nslots):
                B = dict(shared)
                B["sg"] = kb.sb(f"hg_sg{i}", [q, 1024], F32, st)
                for n in ("kdec", "v"):
                    B[n] = kb.sb(f"hg_{n}{i}", [q, 1024], BF16, st)
                B["QM"] = kb.sb(f"hg_QM{i}", [128, 8, nseg * (q + seglen)], BF16, st)
                B["att"] = kb.sb(f"hg_att{i}", [q, 8, q], BF16, st)
                B["dS"] = kb.sb(f"hg_dS{i}", [128, 8 * nseg], F32, st)
                kb.memset("pool", B["QM"].t[:], 0.0, [B["QM"]])
                out.append(B)
            return out

        with ExitStack() as st:
            q, nseg, seglen = 128, 4, 32
            Bs = bufs(q, nseg, seglen, st, 2)
            nTr = kb.rot("hnTc", [128, 8, 128], BF16, 2, st)
            S = kb.sb("hg_S", [128, 8, 128], F32, st)
            Sb = [kb.sb(f"hg_Sb{c}", [128, 8, 128], BF16, st) for c in range(4)]
            kb.memset("pool", S.t[:], 0.0, [S])
            hinc, hafter, segones = cst[:, C_HPI:C_HPI + 128], cst[:, C_HPA:C_HPA + 128], cst[:, C_SEGP:C_SEGP + 4]

            def F(c, nTc, B):
                kb.dma("sp", nTc.t[:, :, :], C.nTd[:, :, 3 + c * 128:3 + c * 128 + 128], reads=[C.nTd_r[c + 1]], writes=[nTc])
                front(q, nseg, seglen, nTc, 0, B, hinc, hafter, segones)

            def G(c, B):
                for sc in range(4):
                    kb.copy("act", Sb[sc].t[:, :, :], S.t[:, :, :], [S], [Sb[sc]])
                    bd = [kb.bank(), kb.bank()]
                    for hh in range(2):
                        kb.op("pe", lambda e, hh=hh, sc=sc, bd=bd: [e.matmul(bd[hh].t[:, h4 * 128:(h4 + 1) * 128], lhsT=B["kdec"].t[32 * sc:32 * sc + 32, (hh * 4 + h4) * 128:(hh * 4 + h4 + 1) * 128],
                                                                      rhs=B["v"].t[32 * sc:32 * sc + 32, (hh * 4 + h4) * 128:(hh * 4 + h4 + 1) * 128], start=True, stop=True, tile_position=(32 * sc, 0)) for h4 in range(4)][-1], [B["kdec"], B["v"]], [bd[hh]])
                    kb.tt("dve", S.t[:, :, :], S.t[:, :, :], bc(B["dS"].t[:, :].rearrange("p (h c) -> p h c", c=nseg)[:, :, sc], [128, 8, 128], 2), ALU.mult, [S, B["dS"]], [S])
                    for hh in range(2):
                        Sv = S.t[:, hh * 4:(hh + 1) * 4, :]
                        kb.tt("dve", Sv, Sv, bd[hh].t[:, :].rearrange("p (h v) -> p h v", v=128), ALU.add, [S, bd[hh]], [S])
                bo = [kb.bank(), kb.bank()]
                for hh in range(2):
                    def emit(e, hh=hh):
                        for h4 in range(4):
                            h = hh * 4 + h4
                            e.matmul(bo[hh].t[:, h4 * 128:(h4 + 1) * 128], lhsT=B["att"].t[:, h, :], rhs=B["v"].t[:, h * 128:(h + 1) * 128], start=True, stop=False)
                            for sc in range(4):
                                last = e.matmul(bo[hh].t[:, h4 * 128:(h4 + 1) * 128], lhsT=B["QM"].t[:, h, sc * 128:(sc + 1) * 128], rhs=Sb[sc].t[:, h, :], start=False, stop=(sc == 3))
                        return last
                    kb.op("pe", emit, [B["att"], B["v"], B["QM"]] + Sb, [bo[hh]])
                back(q, bo, B, c * 128, c)

            nts = [nTr.next(), nTr.next()]
            FP, GP = [0, 1, 2, 3], [4, 5, 6, 7]
            kb.begin(FP); F(0, nts[0], Bs[0]); kb.play(kb.end())
            for c in range(C.NCH):
                kb.begin(GP); G(c, Bs[c % 2]); gl = kb.end()
                fl = []
                if c + 1 < C.NCH:
                    kb.begin(FP); F(c + 1, nts[(c + 1) % 2], Bs[(c + 1) % 2]); fl = kb.end()
                kb.play(schedule(gl, fl))
            kb.set_pool(range(8))
            kb.dma("sp", C.o_hg_p.rearrange("(h k) v -> k h v", k=128), S.t[:, :, :], reads=[S])
            kb.barrier()
        with ExitStack() as st:
            q, nseg, seglen = 64, 16, 4
            B = bufs(q, nseg, seglen, st)[0]
            nTc = kb.sb("hnTs", [128, 8, 64], BF16, st)
            kb.dma("sp", nTc.t[:, :, :], C.nTd[:, :, 3 + C.TP:3 + C.TP + 64], reads=[C.nTd_r[C.NT]], writes=[nTc])
            hinc, hafter, segones = cst[:64, C_TSI:C_TSI + 64], cst[:64, C_TSA:C_TSA + 64], cst[:64, C_SEGS:C_SEGS + 16]
            front(q, nseg, seglen, nTc, 0, B, hinc, hafter, segones)
            bo, ids = kb.reserve(2)
            zt = kb.sb("hg_zero", [64, 64], BF16, st)
            kb.memset("pool", zt.t[:], 0.0, [zt])
            for hh in range(2):
                def emit(e, hh=hh):
                    e.matmul(bo[hh].t[:64, :], lhsT=zt.t[:, :], rhs=B["v"].t[:, hh * 512:(hh + 1) * 512], start=True, stop=False)
                    for h4 in range(4):
                        last = e.matmul(bo[hh].t[:64, h4 * 128:(h4 + 1) * 128], lhsT=B["att"].t[:, hh * 4 + h4, :], rhs=B["v"].t[:, (hh * 4 + h4) * 128:(hh * 4 + h4 + 1) * 128], start=False, stop=False)
                    return last
                kb.op("pe", emit, [B["att"], B["v"], zt], [bo[hh]])
            Sir = kb.rot("hg_Sin", [128, 8, 128], F32, 2, st)
            Sor = kb.rot("hg_Sout", [128, 8, 128], F32, 2, st)
            Sbr = kb.rot("hg_Sbb", [128, 8, 128], BF16, 2, st)
            k4r = kb.rot("hg_k4", [4, 1024], BF16, 2, st)
            v4r = kb.rot("hg_v4", [4, 1024], BF16, 2, st)
            sin = C.st_hg.rearrange("(b h k) v -> b k h v", h=8, k=128)
            sout = C.o_hg_s.rearrange("(b h k) v -> b k h v", h=8, k=128)
            for bb in range(16):
                Si = Sir.next()
                kb.dma("sp", Si.t[:, :, :], sin[bb], writes=[Si])
                k4 = k4r.next(); v4 = v4r.next()
                kb.dma("sp", k4.t[:, :], B["kdec"].t[4 * bb:4 * bb + 4, :], reads=[B["kdec"]], writes=[k4])
                kb.dma("sp", v4.t[:, :], B["v"].t[4 * bb:4 * bb + 4, :], reads=[B["v"]], writes=[v4])
                Sbb = Sbr.next()
                kb.copy("act", Sbb.t[:, :, :], Si.t[:, :, :], [Si], [Sbb])
                for hh in range(2):
                    kb.op("pe", lambda e, hh=hh, Sbb=Sbb, bb=bb: [e.matmul(bo[hh].t[:64, h4 * 128:(h4 + 1) * 128], lhsT=B["QM"].t[:, hh * 4 + h4, bb * 64:(bb + 1) * 64], rhs=Sbb.t[:, hh * 4 + h4, :], start=False, stop=(bb == 15 and h4 == 3)) for h4 in range(4)][-1], [B["QM"], Sbb], [bo[hh]])
                bd = [kb.bank(), kb.bank()]
                for hh in range(2):
                    kb.op("pe", lambda e, hh=hh, bd=bd, k4=k4, v4=v4: [e.matmul(bd[hh].t[:, h4 * 128:(h4 + 1) * 128], lhsT=k4.t[0:4, (hh * 4 + h4) * 128:(hh * 4 + h4 + 1) * 128], rhs=v4.t[0:4, (hh * 4 + h4) * 128:(hh * 4 + h4 + 1) * 128], start=True, stop=True) for h4 in range(4)][-1], [k4, v4], [bd[hh]])
                So = Sor.next()
                kb.tt("pool", So.t[:, :, :], Si.t[:, :, :], bc(B["dS"].t[:, :].rearrange("p (h c) -> p h c", c=nseg)[:, :, bb], [128, 8, 128], 2), ALU.mult, [Si, B["dS"]], [So])
                for hh in range(2):
                    Sv = So.t[:, hh * 4:(hh + 1) * 4, :]
                    kb.tt("dve", Sv, Sv, bd[hh].t[:, :].rearrange("p (h v) -> p h v", v=128), ALU.add, [So, bd[hh]], [So])
                kb.dma("sp", sout[bb], So.t[:, :, :], reads=[So])
            back(q, bo, B, C.TP, C.NCH)
            kb.release(ids)
            kb.barrier()


def chunk_info(C, ci):
    if ci < C.NCH:
        return 128, ci * 128
    return 64, C.TP


def merge_pass(C):
    kb = C.kb
    with ExitStack() as st:
        Wg1 = load_w(C, "Wg1", C.w_in[:, 9248:10272], 1024, stack=st)
        Wg2 = load_w(C, "Wg2", C.w_in[:, 10272:11296], 1024, stack=st)
        Wso = load_w(C, "Wso", C.w_ssd_out, 1024, nk=16, stack=st)
        Who = load_w(C, "Who", C.w_hgrn_out, 1024, stack=st)
        Wo = load_w(C, "Wo", C.w_out, 1024, stack=st)
        nTr = kb.rot("m_nT", [128, 8, 128], BF16, 2, st)
        yTr = kb.rot("m_yT", [128, 16, 128], BF16, 2, st)
        oTr = kb.rot("m_oT", [128, 8, 128], BF16, 2, st)
        xr = kb.rot("m_x", [128, D], F32, 2, st)
        s1 = kb.sb("m_s1", [128, D], F32, st); s2 = kb.sb("m_s2", [128, D], F32, st)
        m1 = kb.sb("m_m1", [128, D], F32, st); m2 = kb.sb("m_m2", [128, D], F32, st)
        mg = kb.sb("m_mg", [128, D], BF16, st); mT = kb.sb("m_mT", [128, 8, 128], BF16, st)
        x1r = kb.rot("m_x1", [128, D], F32, 2, st)

        def loads(ci):
            q, tok0 = chunk_info(C, ci)
            a, b_, c_, d_ = nTr.next(), yTr.next(), oTr.next(), xr.next()
            kb.dma("sp", a.t[:, :, :q], C.nTd[:, :, 3 + tok0:3 + tok0 + q], reads=[C.nTd_r[ci + 1]], writes=[a])
            kb.dma("sp", b_.t[:, :, :q], C.YNT[:, :, tok0:tok0 + q], reads=C.YNT_r[ci], writes=[b_])
            kb.dma("sp", c_.t[:, :, :q], C.ONT[:, :, tok0:tok0 + q], reads=[C.ONT_r[ci]], writes=[c_])
            src = C.xp[tok0:tok0 + q, :] if ci < C.NCH else C.xs[:, :]
            kb.dma("sp", d_.t[:q, :], src, writes=[d_])
            return a, b_, c_, d_
        nxt = loads(0)
        for ci in range(C.NT):
            q, tok0 = chunk_info(C, ci)
            nTc, ynT, onT, xt = nxt
            if ci + 1 < C.NT:
                nxt = loads(ci + 1)
            hs = slice(0, q)
            for half in range(2):
                cs = slice(half * 512, (half + 1) * 512)
                for (Wt, dst) in ((Wg1, s1), (Wg2, s2)):
                    b = kb.bank()
                    kb.op("pe", lambda e, b=b, Wt=Wt: [e.matmul(b.t[hs, :], lhsT=nTc.t[:, k, :q], rhs=Wt.t[:, k, cs], start=(k == 0), stop=(k == 7)) for k in range(8)][-1], [nTc, Wt], [b])
                    kb.act(dst.t[hs, cs], b.t[hs, :], AF.Tanh, [b], [dst], scale=0.5)
                b = kb.bank()
                kb.op("pe", lambda e, b=b: [e.matmul(b.t[hs, :], lhsT=ynT.t[:, k, :q], rhs=Wso.t[:, k, cs], start=(k == 0), stop=(k == 15)) for k in range(16)][-1], [ynT, Wso], [b])
                kb.stt(m1.t[hs, cs], s1.t[hs, cs], 1.0, b.t[hs, :], ALU.add, ALU.mult, [b, s1], [m1])
                b = kb.bank()
                kb.op("pe", lambda e, b=b: [e.matmul(b.t[hs, :], lhsT=onT.t[:, k, :q], rhs=Who.t[:, k, cs], start=(k == 0), stop=(k == 7)) for k in range(8)][-1], [onT, Who], [b])
                kb.stt(m2.t[hs, cs], s2.t[hs, cs], 1.0, b.t[hs, :], ALU.add, ALU.mult, [b, s2], [m2])
            kb.tt("dve", mg.t[hs, :], m1.t[hs, :], m2.t[hs, :], ALU.add, [m1, m2], [mg])
            to_T(C, mg, q, 8, None, mT.t[:, :, :q], mT)
            x1 = x1r.next()
            for half in range(2):
                cs = slice(half * 512, (half + 1) * 512)
                b = kb.bank()
                kb.op("pe", lambda e, b=b: [e.matmul(b.t[hs, :], lhsT=mT.t[:, k, :q], rhs=Wo.t[:, k, cs], start=(k == 0), stop=(k == 7)) for k in range(8)][-1], [mT, Wo], [b])
                kb.stt(x1.t[hs, cs], b.t[hs, :], 0.5, xt.t[hs, cs], ALU.mult, ALU.add, [b, xt], [x1])
            kb.dma("sp", C.X1[tok0:tok0 + q, :], x1.t[hs, :], reads=[x1], writes=[C.X1_r[ci]])
        kb.barrier()


def attn_pass(C):
    kb = C.kb
    with ExitStack() as st:
        Wcq = load_w(C, "Wcq", C.w_cq, 1024, stack=st)
        Wck = load_w(C, "Wck", C.w_ck, 1024, stack=st)
        Wcv = load_w(C, "Wcv", C.w_cv, 1024, stack=st)
        Wco = load_w(C, "Wco", C.w_co, 1024, stack=st)
        memT = kb.sb("a_memT", [128, 8, 256], BF16, st)
        KT = kb.sb("a_KT", [128, 8, 256], BF16, st)
        V = kb.sb("a_V", [128, 2, D], BF16, st)
        xr = kb.rot("a_x1", [128, D], F32, 2, st)
        kvo = kb.rot("a_kvo", [128, 512], F32, 2, st)
        for mt in range(2):
            xt = xr.next()
            kb.dma("sp", xt.t[:, :], C.mem[mt * 128:(mt + 1) * 128, :], writes=[xt])
            rms_to_T(C, xt, 128, C.colv.t[:, V_NMEM:V_NMEM + 8], memT.t[:, :, mt * 128:(mt + 1) * 128], memT)
        for mt in range(2):
            for half in range(2):
                cs = slice(half * 512, (half + 1) * 512)
                for (Wt, dst, isv) in ((Wck, C.o_mk, False), (Wcv, C.o_mv, True)):
                    b = kb.bank()
                    kb.op("pe", lambda e, b=b, Wt=Wt: [e.matmul(b.t[:, :], lhsT=memT.t[:, k, mt * 128:(mt + 1) * 128], rhs=Wt.t[:, k, cs], start=(k == 0), stop=(k == 7)) for k in range(8)][-1], [memT, Wt], [b])
                    o = kvo.next()
                    kb.copy("act", o.t[:, :], b.t[:, :], [b], [o])
                    kb.dma("sp", dst[mt * 128:(mt + 1) * 128, cs], o.t[:, :], reads=[o])
                    if isv:
                        kb.copy("pool", V.t[:, mt, cs], o.t[:, :], [o], [V])
        for c in range(8):
            b = kb.bank()
            kb.op("pe", lambda e, b=b, c=c: [e.matmul(b.t[:, 0:256], lhsT=Wck.t[:, k, c * 128:(c + 1) * 128], rhs=memT.t[:, k, :], start=(k == 0), stop=(k == 7)) for k in range(8)][-1], [memT, Wck], [b])
            kb.copy("act", KT.t[:, c, :], b.t[:, 0:256], [b], [KT])

        def mkset(i, st=st):
            return dict(hnT=kb.sb(f"a_hnT{i}", [128, 8, 128], BF16, st), Qs=kb.sb(f"a_Qs{i}", [128, D], BF16, st), QT=kb.sb(f"a_QT{i}", [128, 8, 128], BF16, st),
                        P=kb.sb(f"a_P{i}", [128, D], BF16, st), PT=kb.sb(f"a_PT{i}", [128, 8, 128], BF16, st), On=kb.sb(f"a_On{i}", [128, D], BF16, st),
                        OT=kb.sb(f"a_OT{i}", [128, 8, 128], BF16, st), sm=kb.sb(f"a_sm{i}", [128, 16], F32, st), x2=kb.sb(f"a_x2{i}", [128, D], F32, st),
                        x1=kb.sb(f"a_x1{i}", [128, D], F32, st), xn=kb.sb(f"a_xn{i}", [128, D], BF16, st), junk=kb.sb(f"a_junk{i}", [128, D], BF16, st),
                        ss=kb.sb(f"a_ss{i}", [128, 4], F32, st))
        sets = [mkset(0), mkset(1)]
        zt = kb.sb("a_zero", [128, 64], BF16, st)
        kb.memset("pool", zt.t[:], 0.0, [zt])

        def load_x1(ci, S):
            q, tok0 = chunk_info(C, ci)
            kb.dma("sp", S["x1"].t[:q, :], C.X1[tok0:tok0 + q, :], reads=[C.X1_r[ci]], writes=[S["x1"]])

        def q_front(q, S):
            hs = slice(0, q)
            hnT, Qs, QT = S["hnT"], S["Qs"], S["QT"]
            rms_to_T(C, S["x1"], q, C.colv.t[:, V_NCROSS:V_NCROSS + 8], hnT.t[:, :, :q], hnT, scratch=S)
            for half in range(2):
                cs = slice(half * 512, (half + 1) * 512)
                b = kb.bank()
                kb.op("pe", lambda e, b=b, cs=cs: [e.matmul(b.t[hs, :], lhsT=hnT.t[:, k, :q], rhs=Wcq.t[:, k, cs], start=(k == 0), stop=(k == 7)) for k in range(8)][-1], [hnT, Wcq], [b])
                kb.act(Qs.t[hs, cs], b.t[hs, :], AF.Copy, [b], [Qs], scale=1.0 / 16.0)
            to_T(C, Qs, q, 8, None, QT.t[:, :, :q], QT)

        def softmax(q, bs, S):
            hs = slice(0, q)
            sm, P, PT = S["sm"], S["P"], S["PT"]
            for hh in range(2):
                kb.op("dve", lambda e, hh=hh: e.tensor_reduce(out=sm.t[hs, hh * 2:hh * 2 + 2], in_=bs[hh].t[hs, :].rearrange("p (h m) -> p h m", m=256), axis=AX.X, op=ALU.max), [bs[hh]], [sm])
            kb.ts("dve", sm.t[hs, 4:8], sm.t[hs, 0:4], -1.0, None, ALU.mult, None, [sm], [sm])
            for h in range(4):
                kb.act(P.t[hs, h * 256:(h + 1) * 256], bs[h // 2].t[hs, (h % 2) * 256:(h % 2 + 1) * 256], AF.Exp, [bs[h // 2], sm], [P, sm], bias=sm.t[hs, 4 + h:5 + h], accum_out=sm.t[hs, 8 + h:9 + h])
            kb.op("dve", lambda e: e.reciprocal(out=sm.t[hs, 12:16], in_=sm.t[hs, 8:12]), [sm], [sm])
            to_T(C, P, q, 8, None, PT.t[:, :, :q], PT)

        def o_back(q, bo, S, ci, tok0):
            hs = slice(0, q)
            sm, On, OT, x2, xt = S["sm"], S["On"], S["OT"], S["x2"], S["x1"]
            for h in range(4):
                kb.act(On.t[hs, h * 256:(h + 1) * 256], bo[h // 2].t[hs, (h % 2) * 256:(h % 2 + 1) * 256], AF.Copy, [bo[h // 2], sm], [On], scale=sm.t[hs, 12 + h:13 + h])
            to_T(C, On, q, 8, None, OT.t[:, :, :q], OT)
            for half in range(2):
                cs = slice(half * 512, (half + 1) * 512)
                b = kb.bank()
                kb.op("pe", lambda e, b=b, cs=cs: [e.matmul(b.t[hs, :], lhsT=OT.t[:, k, :q], rhs=Wco.t[:, k, cs], start=(k == 0), stop=(k == 7)) for k in range(8)][-1], [OT, Wco], [b])
                kb.tt("dve", x2.t[hs, cs], b.t[hs, :], xt.t[hs, cs], ALU.add, [b, xt], [x2])
            kb.dma("sp", C.X2[tok0:tok0 + q, :], x2.t[hs, :], reads=[x2], writes=[C.X2_r[ci]])

        def prompt_chunk(ci, S):
            q, tok0 = 128, ci * 128
            load_x1(ci, S)
            q_front(q, S)
            QT, PT = S["QT"], S["PT"]
            bs = [kb.bank(), kb.bank()]
            for hh in range(2):
                def emit(e, hh=hh):
                    for h2 in range(2):
                        h = hh * 2 + h2
                        for dc in range(2):
                            last = e.matmul(bs[hh].t[:, h2 * 256:(h2 + 1) * 256], lhsT=QT.t[:, h * 2 + dc, :], rhs=KT.t[:, h * 2 + dc, :], start=(dc == 0), stop=(dc == 1))
                    return last
                kb.op("pe", emit, [QT, KT], [bs[hh]])
            softmax(q, bs, S)
            bo = [kb.bank(), kb.bank()]
            for hh in range(2):
                def emit(e, hh=hh):
                    for h2 in range(2):
                        h = hh * 2 + h2
                        for mt in range(2):
                            last = e.matmul(bo[hh].t[:, h2 * 256:(h2 + 1) * 256], lhsT=PT.t[:, h * 2 + mt, :], rhs=V.t[:, mt, h * 256:(h + 1) * 256], start=(mt == 0), stop=(mt == 1))
                    return last
                kb.op("pe", emit, [PT, V], [bo[hh]])
            o_back(q, bo, S, ci, tok0)

        with ExitStack() as st2:
            psets = sets + [mkset(2, st2), mkset(3, st2)]
            for c0 in range(0, C.NCH, 4):
                lists = []
                for i in range(4):
                    if c0 + i < C.NCH:
                        kb.begin([2 * i, 2 * i + 1]); prompt_chunk(c0 + i, psets[i]); lists.append(kb.end())
                kb.play(schedule(*lists))
            kb.set_pool(range(8))
            kb.barrier()
        q, tok0, ci = 64, C.TP, C.NCH
        S = sets[0]
        QT, PT = S["QT"], S["PT"]
        load_x1(ci, S)
        q_front(q, S)
        QTm = kb.sb("a_QTm", [128, 8, 16 * 68], BF16, st)
        PTm = kb.sb("a_PTm", [128, 8, 16 * 68], BF16, st)
        kb.memset("pool", QTm.t[:], 0.0, [QTm])
        kb.memset("pool", PTm.t[:], 0.0, [PTm])
        kb.copy("pool", QTm.t[:, :, :].rearrange("p h (c x) -> p h c x", x=68)[:, :, :, 0:4], QT.t[:, :, :64].rearrange("p h (c j) -> p h c j", j=4), [QT], [QTm])
        Kr = kb.rot("a_Kb", [128, 2, D], BF16, 2, st)
        KTr = kb.rot("a_KTb", [128, 8, 256], BF16, 2, st)
        bs, ids = kb.reserve(2)
        for hh in range(2):
            kb.op("pe", lambda e, hh=hh: e.matmul(bs[hh].t[:64, :], lhsT=zt.t[:64, :], rhs=Wcq.t[:64, 0, 0:512], start=True, stop=False), [zt, Wcq], [bs[hh]])
        ckv = C.ck.rearrange("(b mt p) c -> b p mt c", mt=2, p=128)
        cvv = C.cv.rearrange("(b mt p) c -> b p mt c", mt=2, p=128)
        for bb in range(16):
            Kb = Kr.next()
            kb.dma("pool", Kb.t[:, :, :], ckv[bb], writes=[Kb])
            KTb = KTr.next()
            for mt in range(2):
                b = kb.bank()
                pb = b.t.bitcast(BF16)
                kb.op("pe", lambda e, pb=pb, mt=mt, Kb=Kb: [e.transpose(pb[:, c * 128:(c + 1) * 128], Kb.t[:, mt, c * 128:(c + 1) * 128], C.identb.t[:, :]) for c in range(8)][-1], [Kb, C.identb], [b])
                kb.copy("act", KTb.t[:, :, mt * 128:(mt + 1) * 128], pb.rearrange("p (c m) -> p c m", m=128), [b], [KTb])
            for hh in range(2):
                def emit(e, hh=hh, KTb=KTb, bb=bb):
                    for h2 in range(2):
                        h = hh * 2 + h2
                        for dc in range(2):
                            last = e.matmul(bs[hh].t[:64, h2 * 256:(h2 + 1) * 256], lhsT=QTm.t[:, h * 2 + dc, bb * 64:(bb + 1) * 64], rhs=KTb.t[:, h * 2 + dc, :], start=False, stop=(bb == 15 and dc == 1 and h2 == 1))
                    return last
                kb.op("pe", emit, [QTm, KTb], [bs[hh]])
        softmax(q, bs, S)
        kb.copy("pool", PTm.t[:, :, :].rearrange("p h (c x) -> p h c x", x=68)[:, :, :, 0:4], PT.t[:, :, :64].rearrange("p h (c j) -> p h c j", j=4), [PT], [PTm])
        kb.release(ids)
        bo, ids = kb.reserve(2)
        for hh in range(2):
            kb.op("pe", lambda e, hh=hh: e.matmul(bo[hh].t[:64, :], lhsT=zt.t[:64, :], rhs=Wcq.t[:64, 0, 0:512], start=True, stop=False), [zt, Wcq], [bo[hh]])
        for bb in range(16):
            Vb = Kr.next()
            kb.dma("pool", Vb.t[:, :, :], cvv[bb], writes=[Vb])
            for hh in range(2):
                def emit(e, hh=hh, Vb=Vb, bb=bb):
                    for h2 in range(2):
                        h = hh * 2 + h2
                        for mt in range(2):
                            last = e.matmul(bo[hh].t[:64, h2 * 256:(h2 + 1) * 256], lhsT=PTm.t[:, h * 2 + mt, bb * 64:(bb + 1) * 64], rhs=Vb.t[:, mt, h * 256:(h + 1) * 256], start=False, stop=(bb == 15 and mt == 1 and h2 == 1))
                    return last
                kb.op("pe", emit, [PTm, Vb], [bo[hh]])
        o_back(q, bo, S, ci, tok0)
        kb.release(ids)
        kb.barrier()


def ffn_pass(C):
    kb = C.kb
    with ExitStack() as st:
        Wga = load_w(C, "Wga", C.w_gate, FFN, stack=st)
        Wup = load_w(C, "Wup", C.w_up, FFN, stack=st)
        Wdn = load_w(C, "Wdn", C.w_down, 1024, nk=22, stack=st)
        nfin = kb.sb("f_nfin", [128, D], F32, st)
        kb.dma("sp", nfin.t[:], C.rowvecs[:, RV_NFIN:RV_NFIN + D].partition_broadcast(128), writes=[nfin])
        x2r = kb.rot("f_x2", [128, 2, D], F32, 2, st)
        hnT = kb.sb("f_hnT", [128, 8, 256], BF16, st)
        hT = kb.sb("f_hT", [128, 22, 256], BF16, st)
        sgr = kb.rot("f_sg", [128, 256], F32, 2, st)
        yr = kb.rot("f_y", [128, D], F32, 2, st)
        ntile = C.NCH // 2 + 1

        def tinfo(ti):
            if ti < C.NCH // 2:
                return [(128, ti * 256, 2 * ti), (128, ti * 256 + 128, 2 * ti + 1)]
            return [(64, C.TP, C.NCH)]

        def loads(ti):
            t = x2r.next()
            for sub, (q, tok0, ci) in enumerate(tinfo(ti)):
                kb.dma("sp", t.t[:q, sub, :], C.X2[tok0:tok0 + q, :], reads=[C.X2_r[ci]], writes=[t])
            return t
        nxt = loads(0)
        for ti in range(ntile):
            xt = nxt
            if ti + 1 < ntile:
                nxt = loads(ti + 1)
            subs = tinfo(ti)
            Wd = sum(q for q, _, _ in subs)
            for sub, (q, tok0, ci) in enumerate(subs):
                xv = T(xt.t[:, sub, :], xt.r)
                rms_to_T(C, xv, q, C.colv.t[:, V_NFFN:V_NFFN + 8], hnT.t[:, :, sub * 128:sub * 128 + q], hnT)
            for f in range(22):
                b = kb.bank()

                def emit(e, b=b, f=f):
                    for k in range(8):
                        e.matmul(b.t[:, 0:Wd], lhsT=Wga.t[:, k, f * 128:(f + 1) * 128], rhs=hnT.t[:, k, 0:Wd], start=(k == 0), stop=(k == 7))
                    for k in range(8):
                        last = e.matmul(b.t[:, 256:256 + Wd], lhsT=Wup.t[:, k, f * 128:(f + 1) * 128], rhs=hnT.t[:, k, 0:Wd], start=(k == 0), stop=(k == 7))
                    return last
                kb.op("pe", emit, [Wga, Wup, hnT], [b])
                sg = sgr.next()
                kb.act(sg.t[:, 0:Wd], b.t[:, 0:Wd], AF.Tanh, [b], [sg], scale=0.5)
                kb.stt(sg.t[:, 0:Wd], sg.t[:, 0:Wd], 1.0, b.t[:, 0:Wd], ALU.add, ALU.mult, [sg, b], [sg])
                kb.tt("dve", hT.t[:, f, 0:Wd], sg.t[:, 0:Wd], b.t[:, 256:256 + Wd], ALU.mult, [sg, b], [hT])
            for sub, (q, tok0, ci) in enumerate(subs):
                hs = slice(0, q)
                for half in range(2):
                    cs = slice(half * 512, (half + 1) * 512)
                    b = kb.bank()
                    kb.op("pe", lambda e, b=b, sub=sub, q=q, cs=cs: [e.matmul(b.t[:q, :], lhsT=hT.t[:, f, sub * 128:sub * 128 + q], rhs=Wdn.t[:, f, cs], start=(f == 0), stop=(f == 21)) for f in range(22)][-1], [hT, Wdn], [b])
                    kb.stt(xt.t[hs, sub, cs], b.t[hs, :], 0.5, xt.t[hs, sub, cs], ALU.mult, ALU.add, [b, xt], [xt])
                xv = T(xt.t[:, sub, :], xt.r)
                ss = C.ss.next()
                rms_rstd(C, xv, q, ss)
                y = yr.next()
                kb.act(y.t[hs, :], xv.t[hs, :], AF.Copy, [xt, ss], [y], scale=ss.t[hs, 0:1])
                kb.tt("dve", y.t[hs, :], y.t[hs, :], nfin.t[hs, :], ALU.mult, [y, nfin], [y])
                dst = C.o_y_p[tok0:tok0 + q, :] if ci < C.NCH else C.o_y_s[:, :]
                kb.dma("sp", dst, y.t[hs, :], reads=[y])
        kb.barrier()

def build(NCH=16, debug=False, upto=99, skip_ssd=False):
    TP = NCH * 128
    TS = 64
    TT = TP + TS
    nc = bass.Bass("TRN2", target_bir_lowering=False)
    C = Ctx()
    C.nc, C.NCH, C.TP, C.TT, C.NT = nc, NCH, TP, TT, NCH + 1
    C.skip_ssd = skip_ssd

    def din(name, shape):
        return nc.dram_tensor(name, list(shape), F32, kind="ExternalInput").ap()

    def dout(name, shape):
        return nc.dram_tensor(name, list(shape), F32, kind="ExternalOutput").ap()

    def dscr(name, shape, dt):
        if debug:
            return nc.dram_tensor(name, list(shape), dt, kind="ExternalOutput").ap()
        return nc.dram_tensor(name, list(shape), dt).ap()

    C.xp = din("xp", [TP, D]); C.xs = din("xs", [TS, D]); C.mem = din("mem", [256, D])
    C.st_ssm = din("st_ssm", [16 * 2048, 128]); C.st_conv = din("st_conv", [48, 3072]); C.st_hg = din("st_hg", [16 * 1024, 128])
    C.ck = din("ck", [16 * 256, D]); C.cv = din("cv", [16 * 256, D])
    C.w_in = din("w_in", [D, IN_DIM]); C.w_ssd_out = din("w_ssd_out", [2048, D]); C.w_hgrn_out = din("w_hgrn_out", [D, D]); C.w_out = din("w_out", [D, D])
    C.w_cq = din("w_cq", [D, D]); C.w_ck = din("w_ck", [D, D]); C.w_cv = din("w_cv", [D, D]); C.w_co = din("w_co", [D, D])
    C.w_gate = din("w_gate", [D, FFN]); C.w_up = din("w_up", [D, FFN]); C.w_down = din("w_down", [FFN, D])
    C.consts = din("consts", [128, 1024]); C.colvecs = din("colvecs", [128, 176]); C.rowvecs = din("rowvecs", [1, RV_N])
    C.o_y_p = dout("o_y_p", [TP, D]); C.o_y_s = dout("o_y_s", [TS, D]); C.o_ssm_p = dout("o_ssm_p", [2048, 128]); C.o_conv_p = dout("o_conv_p", [3, 3072])
    C.o_hg_p = dout("o_hg_p", [1024, 128]); C.o_mk = dout("o_mk", [256, D]); C.o_mv = dout("o_mv", [256, D])
    C.o_ssm_s = dout("o_ssm_s", [16 * 2048, 128]); C.o_conv_s = dout("o_conv_s", [48, 3072]); C.o_hg_s = dout("o_hg_s", [16 * 1024, 128])
    C.nTd = dscr("nTd", [128, 8, 3 + TT], BF16); C.nTd_r = [Region(f"nTd{i}") for i in range(C.NT + 1)]
    C.YNT = dscr("YNT", [128, 16, TT], BF16); C.YNT_r = [[Region(f"YNT{i}_{g}") for g in range(4)] for i in range(C.NT)]
    C.ONT = dscr("ONT", [128, 8, TT], BF16); C.ONT_r = [Region(f"ONT{i}") for i in range(C.NT)]
    C.X1 = dscr("X1", [TT, D], F32); C.X1_r = [Region(f"X1{i}") for i in range(C.NT)]
    C.X2 = dscr("X2", [TT, D], F32); C.X2_r = [Region(f"X2{i}") for i in range(C.NT)]

    with ExitStack() as st:
        kb = KB(nc, st)
        C.kb = kb
        C.cst = kb.sb("cst", [128, 1024]); kb.dma("sp", C.cst.t[:], C.consts[:, :], writes=[C.cst])
        C.colv = kb.sb("colv", [128, 176]); kb.dma("sp", C.colv.t[:], C.colvecs[:, :], writes=[C.colv])
        C.rowb = kb.sb("rowb", [128, 96]); kb.dma("sp", C.rowb.t[:], C.rowvecs[:, 0:96].partition_broadcast(128), writes=[C.rowb])
        C.identb = kb.sb("identb", [128, 128], BF16); kb.copy("dve", C.identb.t[:], C.cst.t[:, C_ID:C_ID + 128], [C.cst], [C.identb])
        C.a_bc = kb.sb("a_bc", [128, 32])
        kb.act(C.a_bc.t[:], C.rowb.t[:, RV_ALOG:RV_ALOG + 32], AF.Exp, [C.rowb], [C.a_bc])
        kb.ts("dve", C.a_bc.t[:], C.a_bc.t[:], -1.0, None, ALU.mult, None, [C.a_bc], [C.a_bc])
        C.colvh = kb.sb("colvh", [128, 120]); kb.ts("dve", C.colvh.t[:], C.colv.t[:, V_CONVB:V_CONVB + 120], 0.5, None, ALU.mult, None, [C.colv], [C.colvh])
        C.junk = kb.rot("junk", [128, D], BF16, 1)
        C.ss = kb.rot("ss", [128, 4], F32, 4)
        C.xn = kb.rot("xn", [128, D], BF16, 1)
        if upto >= 0:
            pass0(C)
        if upto >= 1 and not C.skip_ssd:
            ssd_pass(C)
        if upto >= 2 and not C.skip_ssd:
            hg_pass(C)
        if upto >= 3:
            merge_pass(C)
        if upto >= 4:
            attn_pass(C)
        if upto >= 5:
            ffn_pass(C)
        kb.finish()
    return nc


def host_inputs(inp, NCH=16):
    f = lambda a: np.ascontiguousarray(np.asarray(a, dtype=np.float32))
    TP = NCH * 128
    cv = np.zeros((128, 176), np.float32)
    col8 = lambda v: f(v).reshape(-1, 128).T
    cv[:, V_NMIX:V_NMIX + 8] = col8(inp["norm_mix"][0]); cv[:, V_NCROSS:V_NCROSS + 8] = col8(inp["norm_cross"][0])
    cv[:, V_NMEM:V_NMEM + 8] = col8(inp["norm_mem"][0]); cv[:, V_NFFN:V_NFFN + 8] = col8(inp["norm_ffn"][0])
    cv[:, V_SNORM:V_SNORM + 16] = col8(inp["ssd_norm"][0]); cv[:, V_HNORM:V_HNORM + 8] = col8(inp["hgrn_norm"][0])
    cv[:, V_CONVB:V_CONVB + 24] = col8(inp["conv_b"][0])
    cw = f(inp["conv_w"][0])
    cv[:, V_CONVW:V_CONVW + 96] = cw.T.reshape(24, 128, 4).transpose(1, 0, 2).reshape(128, 96)
    rv = np.concatenate([f(inp["dt_bias"][0]), f(inp["a_log"][0]), f(inp["d_skip"][0]), f(inp["hgrn_lb"][0]), f(inp["hgrn_lb"][1]), f(inp["norm_final"])])[None, :]
    consts = make_consts()
    shared = dict(w_in=f(inp["w_in"][0]), w_ssd_out=f(inp["w_ssd_out"][0]), w_hgrn_out=f(inp["w_hgrn_out"][0]), w_out=f(inp["w_out"][0]),
                  w_cq=f(inp["w_cq"][0]), w_ck=f(inp["w_ck"][0]), w_cv=f(inp["w_cv"][0]), w_co=f(inp["w_co"][0]),
                  w_gate=f(inp["w_gate"][0]), w_up=f(inp["w_up"][0]), w_down=f(inp["w_down"][0]),
                  consts=consts, colvecs=cv, rowvecs=f(rv))
    maps = []
    for c in range(8):
        sl = slice(16 * c, 16 * c + 16)
        m = dict(shared)
        m["xp"] = f(inp["x_prompt"][c][:TP]); m["xs"] = f(inp["x_sample"][sl]).reshape(64, D); m["mem"] = f(inp["mem_prompt"][c])
        m["st_ssm"] = f(inp["state_ssm"][0, sl]).reshape(16 * 2048, 128); m["st_conv"] = f(inp["state_conv"][0, sl]).reshape(48, 3072)
        m["st_hg"] = f(inp["state_hgrn"][0, sl]).reshape(16 * 1024, 128)
        m["ck"] = f(inp["cache_mem_k"][0, sl]).reshape(16 * 256, D); m["cv"] = f(inp["cache_mem_v"][0, sl]).reshape(16 * 256, D)
        maps.append(m)
    return maps


def kernel(**inputs):
    NCH = 16
    nc = build(NCH)
    maps = host_inputs(inputs, NCH)
    res = run_bass_kernel_spmd(nc, maps, core_ids=list(range(8))).results
    g = lambda k: [np.asarray(r[k], dtype=np.float32) for r in res]
    y_p = np.stack(g("o_y_p")).reshape(8, 2048, D)
    y_s = np.stack(g("o_y_s")).reshape(128, 4, D)
    ssm_p = np.stack(g("o_ssm_p")).reshape(1, 8, 32, 64, 128)
    conv_p = np.stack(g("o_conv_p")).reshape(1, 8, 3, 3072)
    hg_p = np.stack(g("o_hg_p")).reshape(1, 8, 8, 128, 128)
    mk = np.stack(g("o_mk")).reshape(1, 8, 256, 4, 256)
    mv = np.stack(g("o_mv")).reshape(1, 8, 256, 4, 256)
    ssm_s = np.stack(g("o_ssm_s")).reshape(1, 128, 32, 64, 128)
    conv_s = np.stack(g("o_conv_s")).reshape(1, 128, 3, 3072)
    hg_s = np.stack(g("o_hg_s")).reshape(1, 128, 8, 128, 128)
    return (y_p, y_s, ssm_p, conv_p, hg_p, mk, mv, ssm_s, conv_s, hg_s)
```

```python
import numpy as np
from contextlib import ExitStack
import concourse.bass as bass
import concourse.mybir as mybir
from concourse.bass_utils import run_bass_kernel_spmd

F32 = mybir.dt.float32
BF16 = mybir.dt.bfloat16
ALU = mybir.AluOpType
AF = mybir.ActivationFunctionType
AX = mybir.AxisListType

ND = 8
EPS = 1e-6
D = 1024
IN_DIM = 11296
FFN = 2816


class Region:
    __slots__ = ("name", "w", "r")

    def __init__(self, name=""):
        self.name = name
        self.w = None
        self.r = {}


class T:
    __slots__ = ("t", "r")

    def __init__(self, t, r=None):
        self.t = t
        self.r = r if r is not None else Region()


class Rot:
    def __init__(self, items):
        self.items = items
        self.i = 0

    def next(self):
        x = self.items[self.i % len(self.items)]
        self.i += 1
        return x


def _reg(x):
    return x.r if isinstance(x, T) else x


class KB:
    def __init__(self, nc, stack):
        self.nc = nc
        self.stack = stack
        self.engs = {"pe": nc.tensor, "act": nc.scalar, "dve": nc.vector, "pool": nc.gpsimd, "sp": nc.sync}
        self.semh = {}
        self.cnt = {}
        self.known = {e: {} for e in self.engs}
        for e in ("pe", "act", "dve", "pool"):
            self.semh[e] = stack.enter_context(nc.semaphore("s_" + e))
            self.cnt[e] = 0
        self.dcount = {}
        for q in ("sp", "pool"):
            self.dcount[q] = 0
            for i in range(ND):
                self.semh[("d", q, i)] = stack.enter_context(nc.semaphore(f"d_{q}_{i}"))
        self.psall = stack.enter_context(nc.psum_tensor("psall", [128, 4096], F32))
        self.banks = [T(self.psall[:, i * 512:(i + 1) * 512], Region(f"bank{i}")) for i in range(8)]
        self.free = list(range(8))
        self.bi = 0
        self.uid = 0
        self.rec = None

    def sb(self, name, shape, dt=F32, stack=None):
        self.uid += 1
        st = stack if stack is not None else self.stack
        return T(st.enter_context(self.nc.sbuf_tensor(f"{name}_{self.uid}", list(shape), dt)), Region(name))

    def rot(self, name, shape, dt=F32, n=2, stack=None):
        return Rot([self.sb(f"{name}{i}", shape, dt, stack) for i in range(n)])

    def bank(self):
        b = self.free[self.bi % len(self.free)]
        self.bi += 1
        return self.banks[b]

    def reserve(self, n):
        out = [self.free.pop() for _ in range(n)]
        return [self.banks[b] for b in out], out

    def release(self, ids):
        self.free.extend(ids)
        self.free.sort()

    def _wait(self, e, deps):
        kn = self.known[e]
        best = {}
        for d in deps:
            if d is None:
                continue
            k, v = d
            if e == "pe" and k == "pe":
                continue
            if kn.get(k, 0) < v and best.get(k, 0) < v:
                best[k] = v
        for k, v in best.items():
            self.engs[e].wait_ge(self.semh[k], v)
            kn[k] = v

    def _deps(self, reads, writes):
        deps = []
        for r in reads:
            deps.append(_reg(r).w)
        for w in writes:
            w = _reg(w)
            deps.append(w.w)
            deps.extend(w.r.items())
        return deps

    def _mark(self, tok, reads, writes):
        k, v = tok
        for r in reads:
            r = _reg(r)
            if r.r.get(k, 0) < v:
                r.r[k] = v
        for w in writes:
            w = _reg(w)
            w.w = tok
            w.r = {}

    def set_pool(self, ids):
        self.free = list(ids)
        self.bi = 0

    def begin(self, pool):
        self.set_pool(pool)
        self.rec = []

    def end(self):
        r, self.rec = self.rec, None
        return r

    def play(self, lst):
        assert self.rec is None
        for it in lst:
            if it[0] == "op":
                self.op(*it[1:5])
            else:
                self.dma(it[1], it[2], it[3], it[4], it[5], **it[6])

    def op(self, e, emit, reads=(), writes=(), cost=None):
        if self.rec is not None:
            if cost is None:
                if e == "pe":
                    px = _PEProxy()
                    emit(px)
                    cost = px.cost
                else:
                    cost = 0.5
            self.rec.append(("op", e, emit, tuple(reads), tuple(writes), cost))
            return None
        self._wait(e, self._deps(reads, writes))
        inst = emit(self.engs[e])
        self.cnt[e] += 1
        inst.then_inc(self.semh[e], 1)
        self._mark((e, self.cnt[e]), reads, writes)
        return inst

    def dma(self, q, out, in_, reads=(), writes=(), **kw):
        if self.rec is not None:
            self.rec.append(("dma", q, out, in_, tuple(reads), tuple(writes), kw))
            return None
        i = self.dcount[q]
        self.dcount[q] += 1
        key = ("d", q, i % ND)
        prev = 16 * (i // ND)
        deps = self._deps(reads, writes)
        if prev:
            deps.append((key, prev))
        self._wait(q, deps)
        self.engs[q].dma_start(out=out, in_=in_, **kw).then_inc(self.semh[key], 16)
        self._mark((key, prev + 16), reads, writes)

    def _all_tokens(self):
        deps = []
        for q, n in self.dcount.items():
            for s in range(ND):
                cnt = (n - s + ND - 1) // ND if n > s else 0
                if cnt:
                    deps.append((("d", q, s), 16 * cnt))
        for e, c in self.cnt.items():
            if c:
                deps.append((e, c))
        return deps

    def barrier(self):
        deps = self._all_tokens()
        for e in self.engs:
            self._wait(e, deps)

    def finish(self):
        self._wait("sp", self._all_tokens())

    def act(self, out, in_, func, reads, writes, **kw):
        return self.op("act", lambda e: e.activation(out=out, in_=in_, func=func, **kw), reads, writes, cost=_est("act", out))

    def tt(self, eng, out, in0, in1, op, reads, writes):
        return self.op(eng, lambda e: e.tensor_tensor(out=out, in0=in0, in1=in1, op=op), reads, writes, cost=_est(eng, out))

    def ts(self, eng, out, in0, s1, s2, op0, op1, reads, writes):
        if s2 is None:
            return self.op(eng, lambda e: e.tensor_scalar(out=out, in0=in0, scalar1=s1, scalar2=None, op0=op0), reads, writes, cost=_est(eng, out))
        return self.op(eng, lambda e: e.tensor_scalar(out=out, in0=in0, scalar1=s1, scalar2=s2, op0=op0, op1=op1), reads, writes, cost=_est(eng, out))

    def stt(self, out, in0, scalar, in1, op0, op1, reads, writes):
        return self.op("dve", lambda e: e.scalar_tensor_tensor(out=out, in0=in0, scalar=scalar, in1=in1, op0=op0, op1=op1), reads, writes, cost=_est("dve", out))

    def copy(self, eng, out, in_, reads, writes):
        if eng == "act":
            return self.op("act", lambda e: e.copy(out=out, in_=in_), reads, writes, cost=_est("act", out))
        return self.op(eng, lambda e: e.tensor_copy(out=out, in_=in_), reads, writes, cost=_est(eng, out))

    def memset(self, eng, ap, val, writes):
        return self.op(eng, lambda e: e.memset(ap, val), (), writes)


def _free(ap):
    n = 1
    for d in ap.shape[1:]:
        n *= int(d)
    return n


def _est(eng, out):
    n = _free(out)
    if eng == "act":
        return 0.2 + n / 1200.0
    if eng == "dve":
        return 0.2 + n / 960.0
    return 0.25 + n / 450.0


class _PEProxy:
    def __init__(self):
        self.cost = 0.0

    def matmul(self, out, lhsT=None, rhs=None, **kw):
        p = 4 if lhsT.dtype == F32 else 1
        self.cost += 0.03 + max(_free(out), 32) * p / 2400.0
        return self

    def transpose(self, out, in_, ident, **kw):
        p = 4 if in_.dtype == F32 else 1
        self.cost += 0.03 + max(_free(out), 32) * p / 2400.0
        return self


def schedule(*streams):
    ops = [it for st_ in streams for it in st_]
    n = len(ops)
    engs, R, W, cost = [], [], [], []
    for it in ops:
        if it[0] == "op":
            engs.append(it[1]); R.append([_reg(x) for x in it[3]]); W.append([_reg(x) for x in it[4]]); cost.append(it[5])
        else:
            engs.append(it[1]); R.append([_reg(x) for x in it[4]]); W.append([_reg(x) for x in it[5]]); cost.append(2.0)
    preds = [set() for _ in range(n)]
    lastw, readers = {}, {}
    laste = {}
    k = 0
    for si, st_ in enumerate(streams):
        for _ in st_:
            i = k
            k += 1
            for r in R[i]:
                if id(r) in lastw:
                    preds[i].add(lastw[id(r)])
            for w in W[i]:
                if id(w) in lastw:
                    preds[i].add(lastw[id(w)])
                preds[i].update(readers.get(id(w), ()))
            for r in R[i]:
                readers.setdefault(id(r), []).append(i)
            for w in W[i]:
                lastw[id(w)] = i
                readers[id(w)] = []
            key = (si, engs[i])
            if key in laste:
                preds[i].add(laste[key])
            laste[key] = i
            preds[i].discard(i)
    succs = [[] for _ in range(n)]
    indeg = [len(p) for p in preds]
    for i, p in enumerate(preds):
        for j in p:
            succs[j].append(i)
    ready = [i for i in range(n) if indeg[i] == 0]
    efree = {}
    fin = [0.0] * n
    out = []
    LAT = 0.3
    while ready:
        best, bi = None, None
        for i in ready:
            e = engs[i]
            t = efree.get(e, 0.0)
            for j in preds[i]:
                tj = fin[j] + (0.0 if engs[j] == e else LAT)
                if tj > t:
                    t = tj
            if best is None or (t, i) < best:
                best, bi = (t, i), i
        ready.remove(bi)
        e = engs[bi]
        t0 = best[0]
        if ops[bi][0] == "dma":
            efree[e] = t0 + 0.1
            fin[bi] = t0 + 2.0
        else:
            fin[bi] = t0 + cost[bi]
            efree[e] = fin[bi]
        out.append(ops[bi])
        for j in succs[bi]:
            indeg[j] -= 1
            if indeg[j] == 0:
                ready.append(j)
    assert len(out) == n
    return out


def interleave(*lists):
    lists = [l for l in lists if l]
    idx = [0] * len(lists)
    out = []
    total = sum(len(l) for l in lists)
    while len(out) < total:
        best, bk = None, None
        for k, l in enumerate(lists):
            if idx[k] < len(l):
                key = (idx[k] + 0.5) / len(l)
                if best is None or key < best:
                    best, bk = key, k
        out.append(lists[bk][idx[bk]])
        idx[bk] += 1
    return out


def bc(ap, shape, axis):
    return ap.unsqueeze(axis).to_broadcast(list(shape))


C_ID, C_TPI, C_TPA, C_ONE, C_HPI, C_HPA, C_TSI, C_TSA, C_SSM, C_SEGP, C_SEGS = 0, 128, 256, 384, 512, 640, 768, 832, 896, 960, 964


def make_consts():
    c = np.zeros((128, 1024), np.float32)
    i = np.arange(128)
    c[:, C_ID:C_ID + 128] = np.eye(128)
    c[:, C_TPI:C_TPI + 128] = (i[:, None] <= i[None, :])
    c[:, C_TPA:C_TPA + 128] = (i[:, None] > i[None, :])
    c[:, C_ONE:C_ONE + 128] = 1.0
    s32 = i // 32
    same = s32[:, None] == s32[None, :]
    c[:, C_HPI:C_HPI + 128] = same & (i[:, None] <= i[None, :])
    c[:, C_HPA:C_HPA + 128] = same & (i[:, None] > i[None, :])
    j = np.arange(64)
    s4 = j // 4
    same4 = s4[:, None] == s4[None, :]
    c[:64, C_TSI:C_TSI + 64] = same4 & (j[:, None] <= j[None, :])
    c[:64, C_TSA:C_TSA + 64] = same4 & (j[:, None] > j[None, :])
    c[:64, C_SSM:C_SSM + 64] = same4
    c[:, C_SEGP:C_SEGP + 4] = (s32[:, None] == np.arange(4)[None, :])
    c[:64, C_SEGS:C_SEGS + 16] = (s4[:, None] == np.arange(16)[None, :])
    return c


V_NMIX, V_NCROSS, V_NMEM, V_NFFN, V_SNORM, V_HNORM, V_CONVB, V_CONVW = 0, 8, 16, 24, 32, 48, 56, 80
RV_DTB, RV_ALOG, RV_DSKIP, RV_LB0, RV_LB1, RV_NFIN = 0, 32, 64, 96, 1120, 2144
RV_N = 3168


class Ctx:
    pass


def load_w(C, name, src, ncols, nk=8, stack=None):
    kb = C.kb
    w = kb.sb(name, [128, nk, ncols], BF16, stack)
    for k in range(nk):
        kb.dma("pool", w.t[:, k, :], src[k * 128:(k + 1) * 128, :], writes=[w])
    return w


def rms_rstd(C, xt, q, ss, junk=None):
    kb = C.kb
    junk = junk if junk is not None else C.junk.next()
    kb.act(junk.t[:q], xt.t[:q], AF.Square, [xt], [junk, ss], accum_out=ss.t[:q, 1:2])
    kb.ts("dve", ss.t[:q, 2:3], ss.t[:q, 1:2], 1.0 / D, EPS, ALU.mult, ALU.add, [ss], [ss])
    kb.act(ss.t[:q, 1:2], ss.t[:q, 2:3], AF.Sqrt, [ss], [ss])
    kb.op("dve", lambda e: e.reciprocal(out=ss.t[:q, 0:1], in_=ss.t[:q, 1:2]), [ss], [ss])


def rms_to_T(C, xt, q, wcol, dst_ap, dst, scratch=None):
    kb = C.kb
    ss = scratch["ss"] if scratch else C.ss.next()
    rms_rstd(C, xt, q, ss, scratch["junk"] if scratch else None)
    xn = scratch["xn"] if scratch else C.xn.next()
    kb.act(xn.t[:q], xt.t[:q], AF.Copy, [xt, ss], [xn], scale=ss.t[:q, 0:1])
    to_T(C, xn, q, 8, wcol, dst_ap, dst)


def to_T(C, src, q, nk, wcol, dst_ap, dst, eng="dve"):
    kb = C.kb
    for k0 in range(0, nk, 8):
        n = min(8, nk - k0)
        b = kb.bank()
        pb = b.t.bitcast(BF16)

        def emit(e, k0=k0, n=n, pb=pb):
            for k in range(n):
                last = e.transpose(pb[:, k * 128:k * 128 + q], src.t[:q, (k0 + k) * 128:(k0 + k + 1) * 128], C.identb.t[:q, :q])
            return last
        kb.op("pe", emit, [src, C.identb], [b])
        pv = pb.rearrange("p (k t) -> p k t", t=128)[:, 0:n, 0:q]
        if wcol is None:
            kb.copy("act", dst_ap[:, k0:k0 + n, :], pv, [b], [dst])
        else:
            kb.tt(eng, dst_ap[:, k0:k0 + n, :], pv, bc(wcol[:, k0:k0 + n], [128, n, q], 2), ALU.mult, [b, C.colv], [dst])


def pass0(C):
    kb = C.kb
    with ExitStack() as st:
        z = kb.sb("zero3", [128, 8, 3], BF16, st)
        kb.memset("pool", z.t[:], 0.0, [z])
        kb.dma("sp", C.nTd[:, :, 0:3], z.t[:], reads=[z], writes=[C.nTd_r[0]])
        sets = [dict(xin=kb.rot(f"xin{i}", [128, D], F32, 2, st), nts=kb.rot(f"nts{i}", [128, 8, 128], BF16, 2, st),
                     sc=[dict(xn=kb.sb(f"p0xn{i}{j}", [128, D], BF16, st), junk=kb.sb(f"p0jk{i}{j}", [128, D], BF16, st), ss=kb.sb(f"p0ss{i}{j}", [128, 4], F32, st)) for j in range(2)]) for i in range(2)]

        def tile(ti, S, j):
            q = 128 if ti < C.NCH else 64
            src = C.xp[ti * 128:(ti + 1) * 128, :] if ti < C.NCH else C.xs[:, :]
            xt = S["xin"].next()
            kb.dma("sp", xt.t[:q], src, writes=[xt])
            nt = S["nts"].next()
            rms_to_T(C, xt, q, C.colv.t[:, V_NMIX:V_NMIX + 8], nt.t[:, :, :q], nt, scratch=S["sc"][j])
            kb.dma("sp", C.nTd[:, :, 3 + ti * 128:3 + ti * 128 + q], nt.t[:, :, :q], reads=[nt], writes=[C.nTd_r[ti + 1]])
        lists = []
        for i in range(2):
            kb.begin([0, 1, 2, 3] if i == 0 else [4, 5, 6, 7])
            for n_, ti in enumerate(range(i, C.NT, 2)):
                tile(ti, sets[i], n_ % 2)
            lists.append(kb.end())
        kb.play(schedule(*lists))
        kb.set_pool(range(8))
        kb.barrier()


def ssd_small(C, q, sm, nTc, tcol, Wdt, tri_inc, tri_after, same):
    kb = C.kb
    b = kb.bank()

    def emit(e):
        for k in range(8):
            last = e.matmul(b.t[:q, 0:32], lhsT=nTc.t[:, k, tcol:tcol + q], rhs=Wdt.t[:, k, :], start=(k == 0), stop=(k == 7))
        return last
    kb.op("pe", emit, [nTc, Wdt], [b])
    kb.tt("dve", sm.t[:q, 0:32], b.t[:q, 0:32], C.rowb.t[:q, RV_DTB:RV_DTB + 32], ALU.add, [b, C.rowb], [sm])
    kb.act(sm.t[:q, 32:64], sm.t[:q, 0:32], AF.Exp, [sm], [sm])
    kb.act(sm.t[:q, 64:96], sm.t[:q, 32:64], AF.Ln, [sm], [sm], bias=1.0)
    kb.tt("dve", sm.t[:q, 96:128], sm.t[:q, 64:96], C.a_bc.t[:q, :], ALU.mult, [sm, C.a_bc], [sm])
    b2 = kb.bank()

    def emit2(e):
        e.matmul(b2.t[:q, 0:32], lhsT=tri_inc, rhs=sm.t[:q, 96:128], start=True, stop=True)
        e.matmul(b2.t[:q, 32:64], lhsT=tri_after, rhs=sm.t[:q, 96:128], start=True, stop=True)
        return e.matmul(b2.t[:q, 64:96], lhsT=same, rhs=sm.t[:q, 96:128], start=True, stop=True)
    kb.op("pe", emit2, [sm, C.cst], [b2])
    kb.act(sm.t[:q, 128:224], b2.t[:q, 0:96], AF.Exp, [b2], [sm])
    kb.tt("dve", sm.t[:q, 224:256], sm.t[:q, 160:192], sm.t[:q, 64:96], ALU.mult, [sm], [sm])


def ssd_conv(C, q, pre_view, acc, BCT, eng_split=3):
    kb = C.kb
    cw = C.colvh.t
    WB, WW = 0, V_CONVW - V_CONVB
    for cc in range(24):
        o = acc["view"](cc)
        kb.act(o, pre_view(cc, 0), AF.Identity, [acc["pre"], C.colvh], [acc["T"]], scale=cw[:, WW + cc * 4:WW + cc * 4 + 1], bias=cw[:, WB + cc:WB + cc + 1])
    for cc in range(24):
        o = acc["view"](cc)
        for j in range(1, 4):
            kb.stt(o, pre_view(cc, j), cw[:, WW + cc * 4 + j:WW + cc * 4 + j + 1], o, ALU.mult, ALU.add, [acc["pre"], C.colvh, acc["T"]], [acc["T"]])
    a = acc["T"]
    th = acc["tanh"]
    kb.act(th, a.t[:, :, :q], AF.Tanh, [a], [acc["pre"]])
    kb.stt(a.t[:, :, :q], th, 1.0, a.t[:, :, :q], ALU.add, ALU.mult, [acc["pre"], a], [a])
    kb.copy(acc.get("cast_eng", "pool"), BCT.t[:, :, :q], a.t[:, 16:24, :q], [a], [BCT])


def ssd_group_front(C, q, g, a, sm, W):
    kb = C.kb
    bx = kb.bank()

    def emit(e):
        for j in range(4):
            last = e.transpose(bx.t[:q, j * 128:(j + 1) * 128], a.t[:, g * 4 + j, :q], C.cst.t[:, C_ID:C_ID + 128])
        return last
    kb.op("pe", emit, [a, C.cst], [bx])
    bx3 = bx.t[:q, :].rearrange("p (h d) -> p h d", d=64)
    v3 = lambda ap: ap.rearrange("p (h d) -> p h d", d=64)
    kb.tt("dve", v3(W["xdt"].t[:q, g * 512:(g + 1) * 512]), bx3, bc(sm.t[:q, 64 + g * 8:72 + g * 8], [q, 8, 64], 2), ALU.mult, [bx, sm], [W["xdt"]])
    kb.tt("dve", v3(W["xw"].t[:q, g * 512:(g + 1) * 512]), bx3, bc(sm.t[:q, 224 + g * 8:232 + g * 8], [q, 8, 64], 2), ALU.mult, [bx, sm], [W["xw"]])
    kb.tt("dve", v3(W["xsd"].t[:q, g * 512:(g + 1) * 512]), bx3, bc(C.rowb.t[:q, RV_DSKIP + g * 8:RV_DSKIP + g * 8 + 8], [q, 8, 64], 2), ALU.mult, [bx, C.rowb], [W["xsd"]])


def ssd_group_y(C, q, g, sm, W, BCT, CBm, tri_inc, tri_after, byi, nTc, tcol, Wz, tok0, ci):
    kb = C.kb
    v3 = lambda ap: ap.rearrange("p (h d) -> p h d", d=64)
    tmp = W["tmp"].next()
    kb.tt("dve", v3(tmp.t[:q, :]), v3(byi.t[:q, :]), bc(sm.t[:q, 128 + g * 8:136 + g * 8], [q, 8, 64], 2), ALU.mult, [byi, sm], [tmp])
    by = kb.bank()
    kb.op("pe", lambda e: e.matmul(by.t[:q, :], lhsT=C.identb.t[:q, :q], rhs=W["xsd"].t[:q, g * 512:(g + 1) * 512], start=True, stop=False), [C.identb, W["xsd"]], [by])
    for hh in range(2):
        Rt = W["R"].next()
        kb.tt("dve", Rt.t[:q, :, :q], bc(tri_inc, [q, 4, q], 1), bc(sm.t[:q, 96 + g * 8 + hh * 4:100 + g * 8 + hh * 4], [q, 4, q], 2), ALU.mult, [C.cst, sm], [Rt])
        bL = kb.bank()
        kb.op("pe", lambda e, Rt=Rt, bL=bL: [e.matmul(bL.t[:q, h * q:(h + 1) * q], lhsT=tri_after, rhs=Rt.t[:q, h, :q], start=True, stop=True) for h in range(4)][-1], [Rt, C.cst], [bL])
        Lt = W["L"].next()
        kb.act(Lt.t[:q, 0:4 * q], bL.t[:q, 0:4 * q], AF.Exp, [bL], [Lt])
        MT = W["MT"].next()
        kb.tt("dve", MT.t[:q, :, :q], Lt.t[:q, 0:4 * q].rearrange("p (h t) -> p h t", t=q), bc(CBm.t[:q, g, :q], [q, 4, q], 1), ALU.mult, [Lt, CBm], [MT])

        def emit(e, MT=MT, hh=hh):
            for h in range(4):
                c0 = (hh * 4 + h) * 64
                last = e.matmul(by.t[:q, c0:c0 + 64], lhsT=MT.t[:q, h, :q], rhs=W["xdt"].t[:q, g * 512 + c0:g * 512 + c0 + 64], start=False, stop=False)
            return last
        kb.op("pe", emit, [MT, W["xdt"]], [by])
    kb.op("pe", lambda e: e.matmul(by.t[:q, :], lhsT=C.identb.t[:q, :q], rhs=tmp.t[:q, :], start=False, stop=True), [C.identb, tmp], [by])
    bz = kb.bank()
    kb.op("pe", lambda e: [e.matmul(bz.t[:q, :], lhsT=nTc.t[:, k, tcol:tcol + q], rhs=Wz.t[:, k, g * 512:(g + 1) * 512], start=(k == 0), stop=(k == 7)) for k in range(8)][-1], [nTc, Wz], [bz])
    sz = W["sz"].next()
    kb.act(sz.t[:q, :], bz.t[:q, :], AF.Tanh, [bz], [sz], scale=0.5)
    kb.stt(sz.t[:q, :], sz.t[:q, :], 1.0, bz.t[:q, :], ALU.add, ALU.mult, [sz, bz], [sz])
    y = W["y"].next()
    kb.tt("dve", y.t[:q, :], by.t[:q, :], sz.t[:q, :], ALU.mult, [by, sz], [y])
    ms = C.ss.next()
    kb.act(sz.t[:q, :], y.t[:q, :], AF.Square, [y], [sz, ms], accum_out=ms.t[:q, 1:2])
    kb.ts("dve", ms.t[:q, 2:3], ms.t[:q, 1:2], 1.0 / 512, 4.0 * EPS, ALU.mult, ALU.add, [ms], [ms])
    kb.act(ms.t[:q, 1:2], ms.t[:q, 2:3], AF.Sqrt, [ms], [ms])
    kb.op("dve", lambda e: e.reciprocal(out=ms.t[:q, 0:1], in_=ms.t[:q, 1:2]), [ms], [ms])
    yn = W["yn"].next()
    kb.act(yn.t[:q, :], y.t[:q, :], AF.Copy, [y, ms], [yn], scale=ms.t[:q, 0:1])
    ynT = W["ynT"].next()
    to_T(C, yn, q, 4, C.colv.t[:, V_SNORM + g * 4:V_SNORM + g * 4 + 4], ynT.t[:, :, :q], ynT)
    kb.dma("sp", C.YNT[:, g * 4:(g + 1) * 4, tok0:tok0 + q], ynT.t[:, :, :q], reads=[ynT], writes=[C.YNT_r[ci][g]])


def ssd_pass(C):
    kb = C.kb
    NCH = C.NCH
    cst = C.cst.t
    with ExitStack() as stw:
        Wz = load_w(C, "Wz", C.w_in[:, 0:2048], 2048, stack=stw)
        Wx = load_w(C, "Wx", C.w_in[:, 2048:5120], 3072, stack=stw)
        Wdt = load_w(C, "Wdt", C.w_in[:, 5120:5152], 32, stack=stw)
        with ExitStack() as st:
            q = 128
            nTr = kb.rot("nTc", [128, 8, 131], BF16, 2, st)
            pre = kb.sb("pre", [128, 24, 131], F32, st)
            accr = kb.rot("acc", [128, 24, 128], F32, 1, st)
            BCTr = kb.rot("BCT", [128, 8, 128], BF16, 2, st)
            CBmr = kb.rot("CBm", [128, 4, 128], F32, 2, st)
            Btokr = kb.rot("Btok", [128, 512], BF16, 2, st)
            smr = kb.rot("sm", [128, 256], F32, 2, st)
            MTs = [[kb.sb(f"MT{i}_{g}", [128, 8, 128], BF16, st) for g in range(4)] for i in range(2)]
            szs = [[kb.sb(f"sz{i}_{g}", [128, 512], BF16, st) for g in range(4)] for i in range(2)]
            Rr = kb.rot("R", [128, 4, 128], F32, 2, st)
            Lr = kb.rot("L", [128, 512], F32, 2, st)
            xdt = [kb.sb(f"xdt{g}", [128, 512], BF16, st) for g in range(4)]
            xw = [kb.sb(f"xw{g}", [128, 512], BF16, st) for g in range(4)]
            xsd = [kb.sb(f"xsd{g}", [128, 512], BF16, st) for g in range(4)]
            hT = [kb.sb(f"hT{g}", [128, 512], F32, st) for g in range(4)]
            hTb = [kb.sb(f"hTb{g}", [128, 512], BF16, st) for g in range(4)]
            SW = [dict(y=kb.rot(f"y{i}", [128, 512], F32, 2, st), yn=kb.rot(f"yn{i}", [128, 512], BF16, 2, st), ynT=kb.rot(f"ynT{i}", [128, 4, 128], BF16, 2, st),
                       tmp=kb.rot(f"ytmp{i}", [128, 512], BF16, 2, st)) for i in range(2)]
            for g in range(4):
                kb.memset("pool", hT[g].t[:], 0.0, [hT[g]])
                kb.memset("pool", hTb[g].t[:], 0.0, [hTb[g]])
            tri_inc, tri_after, ones = cst[:, C_TPI:C_TPI + 128], cst[:, C_TPA:C_TPA + 128], cst[:, C_ONE:C_ONE + 128]
            v3 = lambda ap: ap.rearrange("p (h d) -> p h d", d=64)

            def Fa(c, nTc, sm, a, BCT, CBm, Btok, MTa, sza):
                kb.dma("sp", nTc.t[:, :, :], C.nTd[:, :, c * 128:c * 128 + 131], reads=[C.nTd_r[c], C.nTd_r[c + 1]], writes=[nTc])
                ssd_small(C, q, sm, nTc, 3, Wdt, tri_inc, tri_after, ones)
                for g in range(4):
                    for hh in range(2):
                        h0 = g * 8 + hh * 4
                        Rt = Rr.next()
                        kb.tt("dve", Rt.t[:, :, :], bc(tri_inc, [128, 4, 128], 1), bc(sm.t[:, 96 + h0:100 + h0], [128, 4, 128], 2), ALU.mult, [C.cst, sm], [Rt])
                        bL = kb.bank()
                        kb.op("pe", lambda e, Rt=Rt, bL=bL: [e.matmul(bL.t[:, h * 128:(h + 1) * 128], lhsT=tri_after, rhs=Rt.t[:, h, :], start=True, stop=True) for h in range(4)][-1], [Rt, C.cst], [bL])
                        kb.act(MTa[g].t[:, hh * 4:hh * 4 + 4, :], bL.t[:, :].rearrange("p (h t) -> p h t", t=128), AF.Exp, [bL], [MTa[g]])

            def Fc(c, nTc, sm, a, BCT, CBm, Btok, MTa, sza):
                for g in range(4):
                    bz = kb.bank()
                    kb.op("pe", lambda e, bz=bz, g=g: [e.matmul(bz.t[:, :], lhsT=nTc.t[:, k, 3:131], rhs=Wz.t[:, k, g * 512:(g + 1) * 512], start=(k == 0), stop=(k == 7)) for k in range(8)][-1], [nTc, Wz], [bz])
                    zt_ = Lr.next()
                    kb.act(zt_.t[:, :], bz.t[:, :], AF.Tanh, [bz], [zt_], scale=0.5)
                    kb.stt(sza[g].t[:, :], zt_.t[:, :], 1.0, bz.t[:, :], ALU.add, ALU.mult, [zt_, bz], [sza[g]])

            def Fb(c, nTc, sm, a, BCT, CBm, Btok, MTa, sza):
                for grp in range(8):
                    b = kb.bank()

                    def emit(e, grp=grp, b=b):
                        for j in range(3):
                            cc = grp * 3 + j
                            for k in range(8):
                                last = e.matmul(b.t[:, j * 131:(j + 1) * 131], lhsT=Wx.t[:, k, cc * 128:(cc + 1) * 128], rhs=nTc.t[:, k, 0:131], start=(k == 0), stop=(k == 7))
                        return last
                    kb.op("pe", emit, [Wx, nTc], [b])
                    kb.act(pre.t[:, grp * 3:grp * 3 + 3, :], b.t[:, 0:393].rearrange("p (j t) -> p j t", t=131), AF.Copy, [b], [pre])
                if c == NCH - 1:
                    for s_ in range(6):
                        b = kb.bank()
                        kb.op("pe", lambda e, b=b, s_=s_: [e.matmul(b.t[:3, :], lhsT=nTc.t[:, k, 128:131], rhs=Wx.t[:, k, s_ * 512:(s_ + 1) * 512], start=(k == 0), stop=(k == 7)) for k in range(8)][-1], [Wx, nTc], [b])
                        cv_ = Lr.next()
                        kb.copy("act", cv_.t[:3, :], b.t[:3, :], [b], [cv_])
                        kb.dma("sp", C.o_conv_p[:, s_ * 512:(s_ + 1) * 512], cv_.t[:3, :], reads=[cv_])
                accd = {"T": a, "pre": pre, "view": lambda cc: a.t[:, cc, :], "tanh": pre.t[:, :, 0:128], "cast_eng": "act"}
                ssd_conv(C, q, lambda cc, j: pre.t[:, cc, j:j + 128], accd, BCT)
                b = kb.bank()
                kb.op("pe", lambda e, b=b: [e.matmul(b.t[:, g * 128:(g + 1) * 128], lhsT=BCT.t[:, g, :], rhs=BCT.t[:, 4 + g, :], start=True, stop=True) for g in range(4)][-1], [BCT], [b])
                kb.tt("dve", CBm.t[:, :, :], b.t[:, :].rearrange("p (g t) -> p g t", t=128), bc(tri_inc, [128, 4, 128], 1), ALU.mult, [b, C.cst], [CBm])
                b2 = kb.bank()
                kb.op("pe", lambda e, b2=b2: [e.transpose(b2.t[:, g * 128:(g + 1) * 128], a.t[:, 16 + g, :], cst[:, C_ID:C_ID + 128]) for g in range(4)][-1], [a, C.cst], [b2])
                kb.copy("act", Btok.t[:, :], b2.t[:, :], [b2], [Btok])
                for g in range(4):
                    kb.tt("dve", MTa[g].t[:, :, :], MTa[g].t[:, :, :], bc(CBm.t[:, g, :], [128, 8, 128], 1), ALU.mult, [MTa[g], CBm], [MTa[g]])

            def Gfront(c, g, Wk, nTc, sm, a, BCT, CBm, Btok, MTa, sza):
                bx = kb.bank()
                kb.op("pe", lambda e: [e.transpose(bx.t[:, j * 128:(j + 1) * 128], a.t[:, g * 4 + j, :], cst[:, C_ID:C_ID + 128]) for j in range(4)][-1], [a, C.cst], [bx])
                bx3 = v3(bx.t[:, :])
                kb.tt("dve", v3(xdt[g].t[:, :]), bx3, bc(sm.t[:, 64 + g * 8:72 + g * 8], [128, 8, 64], 2), ALU.mult, [bx, sm], [xdt[g]])
                kb.tt("dve", v3(xw[g].t[:, :]), bx3, bc(sm.t[:, 224 + g * 8:232 + g * 8], [128, 8, 64], 2), ALU.mult, [bx, sm], [xw[g]])
                kb.tt("dve", v3(xsd[g].t[:, :]), bx3, bc(C.rowb.t[:, RV_DSKIP + g * 8:RV_DSKIP + g * 8 + 8], [128, 8, 64], 2), ALU.mult, [bx, C.rowb], [xsd[g]])

            def Ggroup(c, g, Wk, nTc, sm, a, BCT, CBm, Btok, MTa, sza):
                byi = kb.bank()
                kb.op("pe", lambda e: e.matmul(byi.t[:, :], lhsT=BCT.t[:, 4 + g, :], rhs=hTb[g].t[:, :], start=True, stop=True), [BCT, hTb[g]], [byi])
                bd = kb.bank()
                kb.op("pe", lambda e: e.matmul(bd.t[:, :], lhsT=Btok.t[:, g * 128:(g + 1) * 128], rhs=xw[g].t[:, :], start=True, stop=True), [Btok, xw[g]], [bd])
                tmp = Wk["tmp"].next()
                kb.tt("dve", v3(tmp.t[:, :]), v3(byi.t[:, :]), bc(sm.t[:, 128 + g * 8:136 + g * 8], [128, 8, 64], 2), ALU.mult, [byi, sm], [tmp])
                hv = v3(hT[g].t[:, :])
                kb.tt("dve", hv, hv, bc(sm.t[:, 192 + g * 8:200 + g * 8], [128, 8, 64], 2), ALU.mult, [hT[g], sm], [hT[g]])
                kb.tt("dve", hT[g].t[:, :], hT[g].t[:, :], bd.t[:, :], ALU.add, [hT[g], bd], [hT[g]])
                by = kb.bank()

                def emit(e):
                    e.matmul(by.t[:, :], lhsT=C.identb.t[:, :], rhs=xsd[g].t[:, :], start=True, stop=False)
                    for h in range(8):
                        e.matmul(by.t[:, h * 64:(h + 1) * 64], lhsT=MTa[g].t[:, h, :], rhs=xdt[g].t[:, h * 64:(h + 1) * 64], start=False, stop=False)
                    return e.matmul(by.t[:, :], lhsT=C.identb.t[:, :], rhs=tmp.t[:, :], start=False, stop=True)
                kb.op("pe", emit, [C.identb, xsd[g], MTa[g], xdt[g], tmp], [by])
                y = Wk["y"].next()
                kb.tt("dve", y.t[:, :], by.t[:, :], sza[g].t[:, :], ALU.mult, [by, sza[g]], [y])
                ms = C.ss.next()
                yn = Wk["yn"].next()
                kb.act(yn.t[:, :], y.t[:, :], AF.Square, [y], [yn, ms], accum_out=ms.t[:, 1:2])
                kb.ts("dve", ms.t[:, 2:3], ms.t[:, 1:2], 1.0 / 512, 4.0 * EPS, ALU.mult, ALU.add, [ms], [ms])
                kb.act(ms.t[:, 1:2], ms.t[:, 2:3], AF.Sqrt, [ms], [ms])
                kb.op("dve", lambda e: e.reciprocal(out=ms.t[:, 0:1], in_=ms.t[:, 1:2]), [ms], [ms])
                kb.act(yn.t[:, :], y.t[:, :], AF.Copy, [y, ms], [yn], scale=ms.t[:, 0:1])
                ynT = Wk["ynT"].next()
                to_T(C, yn, q, 4, C.colv.t[:, V_SNORM + g * 4:V_SNORM + g * 4 + 4], ynT.t[:, :, :], ynT)
                kb.dma("sp", C.YNT[:, g * 4:(g + 1) * 4, c * 128:c * 128 + 128], ynT.t[:, :, :], reads=[ynT], writes=[C.YNT_r[c][g]])
                kb.copy("act", hTb[g].t[:, :], hT[g].t[:, :], [hT[g]], [hTb[g]])

            sets = [(nTr.next(), smr.next(), accr.next(), BCTr.next(), CBmr.next(), Btokr.next(), MTs[i], szs[i]) for i in range(2)]
            GA, GB = [4, 5], [6, 7]

            def Flists(c, S_):
                kb.begin([0]); Fa(c, *S_); la = kb.end()
                kb.begin([3]); Fc(c, *S_); lc = kb.end()
                kb.begin([1, 2]); Fb(c, *S_); lb2 = kb.end()
                return [la, lc, lb2]
            kb.play(schedule(*Flists(0, sets[0])))
            for c in range(NCH):
                S_ = sets[c % 2]
                kb.begin(GA); Gfront(c, 0, SW[0], *S_); Gfront(c, 1, SW[0], *S_); Ggroup(c, 0, SW[0], *S_); Ggroup(c, 1, SW[0], *S_); ga = kb.end()
                kb.begin(GB); Gfront(c, 2, SW[1], *S_); Gfront(c, 3, SW[1], *S_); Ggroup(c, 2, SW[1], *S_); Ggroup(c, 3, SW[1], *S_); gb = kb.end()
                fl = Flists(c + 1, sets[(c + 1) % 2]) if c + 1 < NCH else []
                kb.play(schedule(ga, gb, *fl))
            kb.set_pool(range(8))
            so = T(pre.t[:].rearrange("p c t -> p (c t)")[:, 0:2048].rearrange("p (j n) -> p j n", n=128), pre.r)
            for j4 in range(4):
                b = kb.bank()
                kb.op("pe", lambda e, b=b, j4=j4: [e.transpose(b.t[:, jj * 128:(jj + 1) * 128], hT[j4].t[:, jj * 128:(jj + 1) * 128], cst[:, C_ID:C_ID + 128]) for jj in range(4)][-1], [hT[j4], C.cst], [b])
                kb.copy("act", so.t[:, j4 * 4:(j4 + 1) * 4, :], b.t[:, :].rearrange("p (j n) -> p j n", n=128), [b], [so])
            kb.dma("sp", C.o_ssm_p.rearrange("(j q) n -> q j n", q=128), so.t[:, :, :], reads=[so])
            kb.barrier()
        with ExitStack() as st:
            q = 64
            tri_inc, tri_after, same = cst[:64, C_TSI:C_TSI + 64], cst[:64, C_TSA:C_TSA + 64], cst[:64, C_SSM:C_SSM + 64]
            nTc = kb.sb("nTs", [128, 8, 64], BF16, st)
            kb.dma("sp", nTc.t[:, :, :], C.nTd[:, :, 3 + C.TP:3 + C.TP + 64], reads=[C.nTd_r[C.NT]], writes=[nTc])
            pre = kb.sb("pres", [128, 24, 16, 7], F32, st)
            a = kb.sb("accs", [128, 24, 64], F32, st)
            BCT = kb.sb("BCTs", [128, 8, 64], BF16, st)
            CBm = kb.sb("CBms", [64, 4, 64], F32, st)
            Btok = kb.sb("Btoks", [64, 512], BF16, st)
            sm = kb.sb("sms", [64, 256], F32, st)
            W = {
                "xdt": kb.sb("xdts", [64, 2048], BF16, st), "xw": kb.sb("xws", [64, 2048], BF16, st), "xsd": kb.sb("xsds", [64, 2048], BF16, st),
                "R": kb.rot("Rs", [64, 4, 64], F32, 2, st), "L": kb.rot("Ls", [64, 256], F32, 2, st), "MT": kb.rot("MTs", [64, 4, 64], BF16, 2, st),
                "y": kb.rot("ys", [64, 512], F32, 2, st), "sz": kb.rot("szs", [64, 512], F32, 2, st), "yn": kb.rot("yns", [64, 512], BF16, 2, st),
                "ynT": kb.rot("ynTs", [128, 4, 64], BF16, 2, st), "tmp": kb.rot("ytmps", [64, 512], BF16, 2, st),
            }
            cnvr = kb.rot("cnvs", [64, 512], F32, 2, st)
            ocs = C.o_conv_s.rearrange("(b j) c -> b j c", j=3)
            scr = kb.rot("stconv", [48, 512], F32, 2, st)
            for g6 in range(6):
                sc = scr.next()
                kb.dma("sp", sc.t[:, :], C.st_conv[:, g6 * 512:(g6 + 1) * 512], writes=[sc])
                b = kb.bank()
                kb.op("pe", lambda e, b=b, sc=sc: [e.transpose(b.t[:, jj * 48:(jj + 1) * 48], sc.t[:48, jj * 128:(jj + 1) * 128], cst[:48, C_ID:C_ID + 48]) for jj in range(4)][-1], [sc, C.cst], [b])
                kb.act(pre.t[:, g6 * 4:(g6 + 1) * 4, :, 0:3], b.t[:, 0:192].rearrange("p (c b j) -> p c b j", c=4, j=3), AF.Copy, [b], [pre])
            ssd_small(C, q, sm, nTc, 0, Wdt, tri_inc, tri_after, same)
            for grp in range(8):
                b = kb.bank()

                def emit(e, grp=grp, b=b):
                    for j in range(3):
                        cc = grp * 3 + j
                        for k in range(8):
                            last = e.matmul(b.t[:, j * 64:(j + 1) * 64], lhsT=Wx.t[:, k, cc * 128:(cc + 1) * 128], rhs=nTc.t[:, k, 0:64], start=(k == 0), stop=(k == 7))
                    return last
                kb.op("pe", emit, [Wx, nTc], [b])
                kb.act(pre.t[:, grp * 3:grp * 3 + 3, :, 3:7], b.t[:, 0:192].rearrange("p (c b j) -> p c b j", c=3, j=4), AF.Copy, [b], [pre])
            for s in range(6):
                b = kb.bank()
                kb.op("pe", lambda e, b=b, s=s: [e.matmul(b.t[:64, :], lhsT=nTc.t[:, k, 0:64], rhs=Wx.t[:, k, s * 512:(s + 1) * 512], start=(k == 0), stop=(k == 7)) for k in range(8)][-1], [Wx, nTc], [b])
                cv_ = cnvr.next()
                kb.copy("act", cv_.t[:64, :], b.t[:64, :], [b], [cv_])
                for t in range(1, 4):
                    kb.dma("sp", ocs[:, t - 1, s * 512:(s + 1) * 512], cv_.t[t:64:4, :], reads=[cv_])
            accd = {"T": a, "pre": pre, "view": lambda cc: a.t[:, cc, :].rearrange("p (b t) -> p b t", t=4), "tanh": pre.t[:].rearrange("p c b j -> p c (b j)")[:, :, 0:64]}
            ssd_conv(C, q, lambda cc, j: pre.t[:, cc, :, j:j + 4], accd, BCT)
            b = kb.bank()
            kb.op("pe", lambda e, b=b: [e.matmul(b.t[:64, g * 64:(g + 1) * 64], lhsT=BCT.t[:, g, :], rhs=BCT.t[:, 4 + g, :], start=True, stop=True) for g in range(4)][-1], [BCT], [b])
            kb.tt("dve", CBm.t[:, :, :], b.t[:64, 0:256].rearrange("p (g t) -> p g t", t=64), bc(tri_inc, [64, 4, 64], 1), ALU.mult, [b, C.cst], [CBm])
            b = kb.bank()
            kb.op("pe", lambda e, b=b: [e.transpose(b.t[:64, g * 128:(g + 1) * 128], a.t[:, 16 + g, :], cst[:, C_ID:C_ID + 128]) for g in range(4)][-1], [a, C.cst], [b])
            kb.copy("act", Btok.t[:, :], b.t[:64, :], [b], [Btok])
            CTm = kb.sb("CTm", [128, 4, 16 * 68], BF16, st)
            kb.memset("pool", CTm.t[:], 0.0, [CTm])
            for g in range(4):
                kb.copy("pool", CTm.t[:, g, :].rearrange("p (b x) -> p b x", x=68)[:, :, 0:4], BCT.t[:, 4 + g, :].rearrange("p (b t) -> p b t", t=4), [BCT], [CTm])
            for g in range(4):
                ssd_group_front(C, q, g, a, sm, W)
            decN = kb.sb("decN", [128, 256], F32, st)
            b = kb.bank()
            def emit_dec(e, b=b):
                for j in range(16):
                    for hl in range(2):
                        last = e.matmul(b.t[hl * 64:(hl + 1) * 64, j * 16:(j + 1) * 16], lhsT=sm.t[:64, 96 + 2 * j + hl:97 + 2 * j + hl].to_broadcast([64, 64]),
                                        rhs=cst[:64, C_SEGS:C_SEGS + 16], start=True, stop=True, tile_position=(0, hl * 64))
                return last
            kb.op("pe", emit_dec, [sm, C.cst], [b])
            kb.act(decN.t[:, :], b.t[:, 0:256], AF.Exp, [b], [decN])
            byis, ids = kb.reserve(4)
            h0r = kb.rot("h0nat", [128, 16, 128], F32, 2, st)
            nsr = kb.rot("newst", [128, 16, 128], F32, 1, st)
            hTbr = kb.rot("hTbb", [128, 2048], BF16, 2, st)
            xw4r = kb.rot("xw4", [4, 2048], BF16, 2, st)
            B4r = kb.rot("B4", [4, 512], BF16, 2, st)
            stin = C.st_ssm.rearrange("(b j q) n -> b q j n", j=16, q=128)
            stout = C.o_ssm_s.rearrange("(b j q) n -> b q j n", j=16, q=128)
            for bb in range(16):
                h0 = h0r.next()
                kb.dma("sp", h0.t[:, :, :], stin[bb], writes=[h0])
                xw4 = xw4r.next()
                B4 = B4r.next()
                kb.dma("sp", xw4.t[:, :], W["xw"].t[4 * bb:4 * bb + 4, :], reads=[W["xw"]], writes=[xw4])
                kb.dma("sp", B4.t[:, :], Btok.t[4 * bb:4 * bb + 4, :], reads=[Btok], writes=[B4])
                hb = hTbr.next()
                for j4 in range(4):
                    b = kb.bank()
                    kb.op("pe", lambda e, b=b, j4=j4, h0=h0: [e.transpose(b.t[:, jj * 128:(jj + 1) * 128], h0.t[:, j4 * 4 + jj, :], cst[:, C_ID:C_ID + 128]) for jj in range(4)][-1], [h0, C.cst], [b])
                    kb.copy("act", hb.t[:, j4 * 512:(j4 + 1) * 512], b.t[:, :], [b], [hb])
                for g in range(4):
                    kb.op("pe", lambda e, g=g, hb=hb, bb=bb: e.matmul(byis[g].t[:64, :], lhsT=CTm.t[:, g, bb * 64:(bb + 1) * 64], rhs=hb.t[:, g * 512:(g + 1) * 512], start=(bb == 0), stop=(bb == 15)), [CTm, hb], [byis[g]])
                ns = nsr.next()
                for j4 in range(4):
                    b = kb.bank()
                    kb.op("pe", lambda e, b=b, j4=j4, xw4=xw4, B4=B4: [e.matmul(b.t[:, jj * 128:(jj + 1) * 128], lhsT=xw4.t[0:4, (j4 * 4 + jj) * 128:(j4 * 4 + jj + 1) * 128], rhs=B4.t[0:4, j4 * 128:(j4 + 1) * 128], start=True, stop=True) for jj in range(4)][-1], [xw4, B4], [b])
                    for jj in range(4):
                        j = j4 * 4 + jj
                        kb.stt(ns.t[:, j, :], h0.t[:, j, :], decN.t[:, j * 16 + bb:j * 16 + bb + 1], b.t[:, jj * 128:(jj + 1) * 128], ALU.mult, ALU.add, [h0, decN, b], [ns])
                kb.dma("sp", stout[bb], ns.t[:, :, :], reads=[ns])
            for g in range(4):
                ssd_group_y(C, q, g, sm, W, BCT, CBm, tri_inc, tri_after, byis[g], nTc, 0, Wz, C.TP, C.NCH)
            kb.release(ids)
            kb.barrier()


def hg_pass(C):
    kb = C.kb
    cst = C.cst.t
    with ExitStack() as stw:
        Wq = load_w(C, "Wq", C.w_in[:, 5152:6176], 1024, stack=stw)
        Wf = load_w(C, "Wf", C.w_in[:, 6176:7200], 1024, stack=stw)
        Wi = load_w(C, "Wi", C.w_in[:, 7200:8224], 1024, stack=stw)
        Wg = load_w(C, "Wg", C.w_in[:, 8224:9248], 1024, stack=stw)
        lb = kb.sb("lb", [128, 1024], F32, stw)
        oml = kb.sb("oml", [128, 1024], F32, stw)
        kb.dma("sp", lb.t[:], C.rowvecs[:, RV_LB0:RV_LB0 + 1024].partition_broadcast(128), writes=[lb])
        kb.dma("sp", oml.t[:], C.rowvecs[:, RV_LB1:RV_LB1 + 1024].partition_broadcast(128), writes=[oml])
        kb.tt("dve", lb.t[:], lb.t[:], oml.t[:], ALU.subtract, [lb, oml], [lb])
        kb.act(lb.t[:], lb.t[:], AF.Sigmoid, [lb], [lb])
        kb.ts("dve", oml.t[:], lb.t[:], -1.0, 1.0, ALU.mult, ALU.add, [lb], [oml])
        omlh = kb.sb("omlh", [128, 1024], F32, stw)
        kb.ts("dve", omlh.t[:], oml.t[:], 0.5, None, ALU.mult, None, [oml], [omlh])

        def proj(q, nTc, tcol, Wt, half):
            b = kb.bank()
            kb.op("pe", lambda e: [e.matmul(b.t[:q, :], lhsT=nTc.t[:, k, tcol:tcol + q], rhs=Wt.t[:, k, half * 512:(half + 1) * 512], start=(k == 0), stop=(k == 7)) for k in range(8)][-1], [nTc, Wt], [b])
            return b

        def front(q, nseg, seglen, nTc, tcol, B, hinc, hafter, segones):
            hs = slice(0, q)
            for half in range(2):
                cs = slice(half * 512, (half + 1) * 512)
                b = proj(q, nTc, tcol, Wf, half)
                kb.act(B["w"].t[hs, cs], b.t[hs, :], AF.Tanh, [b], [B["w"]], scale=0.5)
            kb.stt(B["w"].t[hs, :], B["w"].t[hs, :], 1.0, omlh.t[hs, :], ALU.add, ALU.mult, [B["w"], omlh], [B["w"]])
            kb.tt("dve", B["logf"].t[hs, :], B["w"].t[hs, :], lb.t[hs, :], ALU.add, [B["w"], lb], [B["logf"]])
            kb.act(B["logf"].t[hs, :], B["logf"].t[hs, :], AF.Ln, [B["logf"]], [B["logf"]])
            kb.tt("dve", B["kk"].t[hs, :], oml.t[hs, :], B["w"].t[hs, :], ALU.subtract, [B["w"], oml], [B["kk"]])
            for half in range(2):
                cs = slice(half * 512, (half + 1) * 512)
                b = kb.bank()
                kb.op("pe", lambda e, b=b, cs=cs: e.matmul(b.t[hs, :], lhsT=hinc, rhs=B["logf"].t[hs, cs], start=True, stop=True), [B["logf"], C.cst], [b])
                kb.act(B["E1"].t[hs, cs], b.t[hs, :], AF.Exp, [b], [B["E1"]])
                kb.act(B["E1n"].t[hs, cs], b.t[hs, :], AF.Exp, [b], [B["E1n"]], scale=-1.0)
                b2 = kb.bank()
                kb.op("pe", lambda e, b2=b2, cs=cs: e.matmul(b2.t[hs, :], lhsT=hafter, rhs=B["logf"].t[hs, cs], start=True, stop=True), [B["logf"], C.cst], [b2])
                kb.act(B["E2"].t[hs, cs], b2.t[hs, :], AF.Exp, [b2], [B["E2"]])
            b = kb.bank()
            kb.op("pe", lambda e, b=b: [e.matmul(b.t[:, h * nseg:(h + 1) * nseg], lhsT=B["logf"].t[hs, h * 128:(h + 1) * 128], rhs=segones, start=True, stop=True) for h in range(8)][-1], [B["logf"], C.cst], [b])
            kb.act(B["dS"].t[:, 0:8 * nseg], b.t[:, 0:8 * nseg], AF.Exp, [b], [B["dS"]])
            for half in range(2):
                cs = slice(half * 512, (half + 1) * 512)
                b = proj(q, nTc, tcol, Wq, half)
                kb.act(B["sq"].t[hs, cs], b.t[hs, :], AF.Tanh, [b], [B["sq"]], scale=0.5)
                kb.stt(B["sq"].t[hs, cs], B["sq"].t[hs, cs], 1.0, b.t[hs, :], ALU.add, ALU.mult, [B["sq"], b], [B["sq"]])
            kb.tt("dve", B["qg"].t[hs, :], B["sq"].t[hs, :], B["E1"].t[hs, :], ALU.mult, [B["sq"], B["E1"]], [B["qg"]])
            kb.tt("dve", B["kg"].t[hs, :], B["kk"].t[hs, :], B["E1n"].t[hs, :], ALU.mult, [B["kk"], B["E1n"]], [B["kg"]])
            kb.tt("dve", B["kdec"].t[hs, :], B["kk"].t[hs, :], B["E2"].t[hs, :], ALU.mult, [B["kk"], B["E2"]], [B["kdec"]])
            for half in range(2):
                cs = slice(half * 512, (half + 1) * 512)
                b = proj(q, nTc, tcol, Wi, half)
                kb.copy("act", B["v"].t[hs, cs], b.t[hs, :], [b], [B["v"]])
            for half in range(2):
                cs = slice(half * 512, (half + 1) * 512)
                b = proj(q, nTc, tcol, Wg, half)
                kb.act(B["sg"].t[hs, cs], b.t[hs, :], AF.Tanh, [b], [B["sg"]], scale=0.5)
                kb.stt(B["sg"].t[hs, cs], B["sg"].t[hs, cs], 1.0, b.t[hs, :], ALU.add, ALU.mult, [B["sg"], b], [B["sg"]])
            to_T(C, B["qg"], q, 8, None, B["qgT"].t[:, :, :q], B["qgT"])
            to_T(C, B["kg"], q, 8, None, B["kgT"].t[:, :, :q], B["kgT"])
            x = q + seglen
            kb.copy("pool", B["QM"].t[:, :, :].rearrange("p h (c x) -> p h c x", x=x)[:, :, :, 0:seglen],
                    B["qgT"].t[:, :, :q].rearrange("p h (c j) -> p h c j", j=seglen), [B["qgT"]], [B["QM"]])
            for hh in range(2):
                b = kb.bank()
                kb.op("pe", lambda e, b=b, hh=hh: [e.matmul(b.t[hs, h4 * q:(h4 + 1) * q], lhsT=B["kgT"].t[:, hh * 4 + h4, :q], rhs=B["qgT"].t[:, hh * 4 + h4, :q], start=True, stop=True) for h4 in range(4)][-1], [B["kgT"], B["qgT"]], [b])
                kb.tt("dve", B["att"].t[hs, hh * 4:(hh + 1) * 4, :q], b.t[hs, 0:4 * q].rearrange("p (h t) -> p h t", t=q), bc(hinc, [q, 4, q], 1), ALU.mult, [b, C.cst], [B["att"]])

        def back(q, bo, B, tok0, ci):
            hs = slice(0, q)
            for half in range(2):
                kb.copy("act", B["osb"].t[hs, half * 512:(half + 1) * 512], bo[half].t[hs, :], [bo[half]], [B["osb"]])
            o = B["osb"]
            kb.tt("dve", B["osq"].t[hs, :], o.t[hs, :], o.t[hs, :], ALU.mult, [o], [B["osq"]])
            hsm = B["hsm"]
            kb.op("dve", lambda e: e.tensor_reduce(out=hsm.t[hs, 0:8], in_=B["osq"].t[hs, :].rearrange("p (h v) -> p h v", v=128), axis=AX.X, op=ALU.add), [B["osq"]], [hsm])
            kb.ts("dve", hsm.t[hs, 8:16], hsm.t[hs, 0:8], 4.0 / 128, 16.0 * EPS, ALU.mult, ALU.add, [hsm], [hsm])
            kb.act(hsm.t[hs, 16:24], hsm.t[hs, 8:16], AF.Sqrt, [hsm], [hsm])
            kb.op("dve", lambda e: e.reciprocal(out=hsm.t[hs, 24:32], in_=hsm.t[hs, 16:24]), [hsm], [hsm])
            o3 = o.t[hs, :].rearrange("p (h v) -> p h v", v=128)
            kb.tt("dve", o3, o3, bc(hsm.t[hs, 24:32], [q, 8, 128], 2), ALU.mult, [o, hsm], [o])
            kb.tt("dve", B["on"].t[hs, :], o.t[hs, :], B["sg"].t[hs, :], ALU.mult, [o, B["sg"]], [B["on"]])
            to_T(C, B["on"], q, 8, C.colv.t[:, V_HNORM:V_HNORM + 8], B["onT"].t[:, :, :q], B["onT"])
            kb.dma("sp", C.ONT[:, :, tok0:tok0 + q], B["onT"].t[:, :, :q], reads=[B["onT"]], writes=[C.ONT_r[ci]])

        def bufs(q, nseg, seglen, st, nslots=1):
            shared = {}
            for n in ("w", "logf", "kk", "E1", "E1n", "E2", "sq", "osb", "osq"):
                shared[n] = kb.sb("hg_" + n, [q, 1024], F32, st)
            for n in ("qg", "kg", "on"):
                shared[n] = kb.sb("hg_" + n, [q, 1024], BF16, st)
            shared["qgT"] = kb.sb("hg_qgT", [128, 8, q], BF16, st)
            shared["kgT"] = kb.sb("hg_kgT", [128, 8, q], BF16, st)
            shared["onT"] = kb.sb("hg_onT", [128, 8, q], BF16, st)
            shared["hsm"] = kb.sb("hg_hsm", [q, 32], F32, st)
            out = []
            for i in range(nslots):
                B = dict(shared)
                B["sg"] = kb.sb(f"hg_sg{i}", [q, 1024], F32, st)
                for n in ("kdec", "v"):
                    B[n] = kb.sb(f"hg_{n}{i}", [q, 1024], BF16, st)
                B["QM"] = kb.sb(f"hg_QM{i}", [128, 8, nseg * (q + seglen)], BF16, st)
                B["att"] = kb.sb(f"hg_att{i}", [q, 8, q], BF16, st)
                B["dS"] = kb.sb(f"hg_dS{i}", [128, 8 * nseg], F32, st)
                kb.memset("pool", B["QM"].t[:], 0.0, [B["QM"]])
                out.append(B)
            return out

        with ExitStack() as st:
            q, nseg, seglen = 128, 4, 32
            Bs = bufs(q, nseg, seglen, st, 2)
            nTr = kb.rot("hnTc", [128, 8, 128], BF16, 2, st)
            S = kb.sb("hg_S", [128, 8, 128], F32, st)
            Sb = [kb.sb(f"hg_Sb{c}", [128, 8, 128], BF16, st) for c in range(4)]
            kb.memset("pool", S.t[:], 0.0, [S])
            hinc, hafter, segones = cst[:, C_HPI:C_HPI + 128], cst[:, C_HPA:C_HPA + 128], cst[:, C_SEGP:C_SEGP + 4]

            def F(c, nTc, B):
                kb.dma("sp", nTc.t[:, :, :], C.nTd[:, :, 3 + c * 128:3 + c * 128 + 128], reads=[C.nTd_r[c + 1]], writes=[nTc])
                front(q, nseg, seglen, nTc, 0, B, hinc, hafter, segones)

            def G(c, B):
                for sc in range(4):
                    kb.copy("act", Sb[sc].t[:, :, :], S.t[:, :, :], [S], [Sb[sc]])
                    bd = [kb.bank(), kb.bank()]
                    for hh in range(2):
                        kb.op("pe", lambda e, hh=hh, sc=sc, bd=bd: [e.matmul(bd[hh].t[:, h4 * 128:(h4 + 1) * 128], lhsT=B["kdec"].t[32 * sc:32 * sc + 32, (hh * 4 + h4) * 128:(hh * 4 + h4 + 1) * 128],
                                                                      rhs=B["v"].t[32 * sc:32 * sc + 32, (hh * 4 + h4) * 128:(hh * 4 + h4 + 1) * 128], start=True, stop=True, tile_position=(32 * sc, 0)) for h4 in range(4)][-1], [B["kdec"], B["v"]], [bd[hh]])
                    kb.tt("dve", S.t[:, :, :], S.t[:, :, :], bc(B["dS"].t[:, :].rearrange("p (h c) -> p h c", c=nseg)[:, :, sc], [128, 8, 128], 2), ALU.mult, [S, B["dS"]], [S])
                    for hh in range(2):
                        Sv = S.t[:, hh * 4:(hh + 1) * 4, :]
                        kb.tt("dve", Sv, Sv, bd[hh].t[:, :].rearrange("p (h v) -> p h v", v=128), ALU.add, [S, bd[hh]], [S])
                bo = [kb.bank(), kb.bank()]
                for hh in range(2):
                    def emit(e, hh=hh):
                        for h4 in range(4):
                            h = hh * 4 + h4
                            e.matmul(bo[hh].t[:, h4 * 128:(h4 + 1) * 128], lhsT=B["att"].t[:, h, :], rhs=B["v"].t[:, h * 128:(h + 1) * 128], start=True, stop=False)
                            for sc in range(4):
                                last = e.matmul(bo[hh].t[:, h4 * 128:(h4 + 1) * 128], lhsT=B["QM"].t[:, h, sc * 128:(sc + 1) * 128], rhs=Sb[sc].t[:, h, :], start=False, stop=(sc == 3))
                        return last
                    kb.op("pe", emit, [B["att"], B["v"], B["QM"]] + Sb, [bo[hh]])
                back(q, bo, B, c * 128, c)

            nts = [nTr.next(), nTr.next()]
            FP, GP = [0, 1, 2, 3], [4, 5, 6, 7]
            kb.begin(FP); F(0, nts[0], Bs[0]); kb.play(kb.end())
            for c in range(C.NCH):
                kb.begin(GP); G(c, Bs[c % 2]); gl = kb.end()
                fl = []
                if c + 1 < C.NCH:
                    kb.begin(FP); F(c + 1, nts[(c + 1) % 2], Bs[(c + 1) % 2]); fl = kb.end()
                kb.play(schedule(gl, fl))
            kb.set_pool(range(8))
            kb.dma("sp", C.o_hg_p.rearrange("(h k) v -> k h v", k=128), S.t[:, :, :], reads=[S])
            kb.barrier()
        with ExitStack() as st:
            q, nseg, seglen = 64, 16, 4
            B = bufs(q, nseg, seglen, st)[0]
            nTc = kb.sb("hnTs", [128, 8, 64], BF16, st)
            kb.dma("sp", nTc.t[:, :, :], C.nTd[:, :, 3 + C.TP:3 + C.TP + 64], reads=[C.nTd_r[C.NT]], writes=[nTc])
            hinc, hafter, segones = cst[:64, C_TSI:C_TSI + 64], cst[:64, C_TSA:C_TSA + 64], cst[:64, C_SEGS:C_SEGS + 16]
            front(q, nseg, seglen, nTc, 0, B, hinc, hafter, segones)
            bo, ids = kb.reserve(2)
            zt = kb.sb("hg_zero", [64, 64], BF16, st)
            kb.memset("pool", zt.t[:], 0.0, [zt])
            for hh in range(2):
                def emit(e, hh=hh):
                    e.matmul(bo[hh].t[:64, :], lhsT=zt.t[:, :], rhs=B["v"].t[:, hh * 512:(hh + 1) * 512], start=True, stop=False)
                    for h4 in range(4):
                        last = e.matmul(bo[hh].t[:64, h4 * 128:(h4 + 1) * 128], lhsT=B["att"].t[:, hh * 4 + h4, :], rhs=B["v"].t[:, (hh * 4 + h4) * 128:(hh * 4 + h4 + 1) * 128], start=False, stop=False)
                    return last
                kb.op("pe", emit, [B["att"], B["v"], zt], [bo[hh]])
            Sir = kb.rot("hg_Sin", [128, 8, 128], F32, 2, st)
            Sor = kb.rot("hg_Sout", [128, 8, 128], F32, 2, st)
            Sbr = kb.rot("hg_Sbb", [128, 8, 128], BF16, 2, st)
            k4r = kb.rot("hg_k4", [4, 1024], BF16, 2, st)
            v4r = kb.rot("hg_v4", [4, 1024], BF16, 2, st)
            sin = C.st_hg.rearrange("(b h k) v -> b k h v", h=8, k=128)
            sout = C.o_hg_s.rearrange("(b h k) v -> b k h v", h=8, k=128)
            for bb in range(16):
                Si = Sir.next()
                kb.dma("sp", Si.t[:, :, :], sin[bb], writes=[Si])
                k4 = k4r.next(); v4 = v4r.next()
                kb.dma("sp", k4.t[:, :], B["kdec"].t[4 * bb:4 * bb + 4, :], reads=[B["kdec"]], writes=[k4])
                kb.dma("sp", v4.t[:, :], B["v"].t[4 * bb:4 * bb + 4, :], reads=[B["v"]], writes=[v4])
                Sbb = Sbr.next()
                kb.copy("act", Sbb.t[:, :, :], Si.t[:, :, :], [Si], [Sbb])
                for hh in range(2):
                    kb.op("pe", lambda e, hh=hh, Sbb=Sbb, bb=bb: [e.matmul(bo[hh].t[:64, h4 * 128:(h4 + 1) * 128], lhsT=B["QM"].t[:, hh * 4 + h4, bb * 64:(bb + 1) * 64], rhs=Sbb.t[:, hh * 4 + h4, :], start=False, stop=(bb == 15 and h4 == 3)) for h4 in range(4)][-1], [B["QM"], Sbb], [bo[hh]])
                bd = [kb.bank(), kb.bank()]
                for hh in range(2):
                    kb.op("pe", lambda e, hh=hh, bd=bd, k4=k4, v4=v4: [e.matmul(bd[hh].t[:, h4 * 128:(h4 + 1) * 128], lhsT=k4.t[0:4, (hh * 4 + h4) * 128:(hh * 4 + h4 + 1) * 128], rhs=v4.t[0:4, (hh * 4 + h4) * 128:(hh * 4 + h4 + 1) * 128], start=True, stop=True) for h4 in range(4)][-1], [k4, v4], [bd[hh]])
                So = Sor.next()
                kb.tt("pool", So.t[:, :, :], Si.t[:, :, :], bc(B["dS"].t[:, :].rearrange("p (h c) -> p h c", c=nseg)[:, :, bb], [128, 8, 128], 2), ALU.mult, [Si, B["dS"]], [So])
                for hh in range(2):
                    Sv = So.t[:, hh * 4:(hh + 1) * 4, :]
                    kb.tt("dve", Sv, Sv, bd[hh].t[:, :].rearrange("p (h v) -> p h v", v=128), ALU.add, [So, bd[hh]], [So])
                kb.dma("sp", sout[bb], So.t[:, :, :], reads=[So])
            back(q, bo, B, C.TP, C.NCH)
            kb.release(ids)
            kb.barrier()


def chunk_info(C, ci):
    if ci < C.NCH:
        return 128, ci * 128
    return 64, C.TP


def merge_pass(C):
    kb = C.kb
    with ExitStack() as st:
        Wg1 = load_w(C, "Wg1", C.w_in[:, 9248:10272], 1024, stack=st)
        Wg2 = load_w(C, "Wg2", C.w_in[:, 10272:11296], 1024, stack=st)
        Wso = load_w(C, "Wso", C.w_ssd_out, 1024, nk=16, stack=st)
        Who = load_w(C, "Who", C.w_hgrn_out, 1024, stack=st)
        Wo = load_w(C, "Wo", C.w_out, 1024, stack=st)
        nTr = kb.rot("m_nT", [128, 8, 128], BF16, 2, st)
        yTr = kb.rot("m_yT", [128, 16, 128], BF16, 2, st)
        oTr = kb.rot("m_oT", [128, 8, 128], BF16, 2, st)
        xr = kb.rot("m_x", [128, D], F32, 2, st)
        s1 = kb.sb("m_s1", [128, D], F32, st); s2 = kb.sb("m_s2", [128, D], F32, st)
        m1 = kb.sb("m_m1", [128, D], F32, st); m2 = kb.sb("m_m2", [128, D], F32, st)
        mg = kb.sb("m_mg", [128, D], BF16, st); mT = kb.sb("m_mT", [128, 8, 128], BF16, st)
        x1r = kb.rot("m_x1", [128, D], F32, 2, st)

        def loads(ci):
            q, tok0 = chunk_info(C, ci)
            a, b_, c_, d_ = nTr.next(), yTr.next(), oTr.next(), xr.next()
            kb.dma("sp", a.t[:, :, :q], C.nTd[:, :, 3 + tok0:3 + tok0 + q], reads=[C.nTd_r[ci + 1]], writes=[a])
            kb.dma("sp", b_.t[:, :, :q], C.YNT[:, :, tok0:tok0 + q], reads=C.YNT_r[ci], writes=[b_])
            kb.dma("sp", c_.t[:, :, :q], C.ONT[:, :, tok0:tok0 + q], reads=[C.ONT_r[ci]], writes=[c_])
            src = C.xp[tok0:tok0 + q, :] if ci < C.NCH else C.xs[:, :]
            kb.dma("sp", d_.t[:q, :], src, writes=[d_])
            return a, b_, c_, d_
        nxt = loads(0)
        for ci in range(C.NT):
            q, tok0 = chunk_info(C, ci)
            nTc, ynT, onT, xt = nxt
            if ci + 1 < C.NT:
                nxt = loads(ci + 1)
            hs = slice(0, q)
            for half in range(2):
                cs = slice(half * 512, (half + 1) * 512)
                for (Wt, dst) in ((Wg1, s1), (Wg2, s2)):
                    b = kb.bank()
                    kb.op("pe", lambda e, b=b, Wt=Wt: [e.matmul(b.t[hs, :], lhsT=nTc.t[:, k, :q], rhs=Wt.t[:, k, cs], start=(k == 0), stop=(k == 7)) for k in range(8)][-1], [nTc, Wt], [b])
                    kb.act(dst.t[hs, cs], b.t[hs, :], AF.Tanh, [b], [dst], scale=0.5)
                b = kb.bank()
                kb.op("pe", lambda e, b=b: [e.matmul(b.t[hs, :], lhsT=ynT.t[:, k, :q], rhs=Wso.t[:, k, cs], start=(k == 0), stop=(k == 15)) for k in range(16)][-1], [ynT, Wso], [b])
                kb.stt(m1.t[hs, cs], s1.t[hs, cs], 1.0, b.t[hs, :], ALU.add, ALU.mult, [b, s1], [m1])
                b = kb.bank()
                kb.op("pe", lambda e, b=b: [e.matmul(b.t[hs, :], lhsT=onT.t[:, k, :q], rhs=Who.t[:, k, cs], start=(k == 0), stop=(k == 7)) for k in range(8)][-1], [onT, Who], [b])
                kb.stt(m2.t[hs, cs], s2.t[hs, cs], 1.0, b.t[hs, :], ALU.add, ALU.mult, [b, s2], [m2])
            kb.tt("dve", mg.t[hs, :], m1.t[hs, :], m2.t[hs, :], ALU.add, [m1, m2], [mg])
            to_T(C, mg, q, 8, None, mT.t[:, :, :q], mT)
            x1 = x1r.next()
            for half in range(2):
                cs = slice(half * 512, (half + 1) * 512)
                b = kb.bank()
                kb.op("pe", lambda e, b=b: [e.matmul(b.t[hs, :], lhsT=mT.t[:, k, :q], rhs=Wo.t[:, k, cs], start=(k == 0), stop=(k == 7)) for k in range(8)][-1], [mT, Wo], [b])
                kb.stt(x1.t[hs, cs], b.t[hs, :], 0.5, xt.t[hs, cs], ALU.mult, ALU.add, [b, xt], [x1])
            kb.dma("sp", C.X1[tok0:tok0 + q, :], x1.t[hs, :], reads=[x1], writes=[C.X1_r[ci]])
        kb.barrier()


def attn_pass(C):
    kb = C.kb
    with ExitStack() as st:
        Wcq = load_w(C, "Wcq", C.w_cq, 1024, stack=st)
        Wck = load_w(C, "Wck", C.w_ck, 1024, stack=st)
        Wcv = load_w(C, "Wcv", C.w_cv, 1024, stack=st)
        Wco = load_w(C, "Wco", C.w_co, 1024, stack=st)
        memT = kb.sb("a_memT", [128, 8, 256], BF16, st)
        KT = kb.sb("a_KT", [128, 8, 256], BF16, st)
        V = kb.sb("a_V", [128, 2, D], BF16, st)
        xr = kb.rot("a_x1", [128, D], F32, 2, st)
        kvo = kb.rot("a_kvo", [128, 512], F32, 2, st)
        for mt in range(2):
            xt = xr.next()
            kb.dma("sp", xt.t[:, :], C.mem[mt * 128:(mt + 1) * 128, :], writes=[xt])
            rms_to_T(C, xt, 128, C.colv.t[:, V_NMEM:V_NMEM + 8], memT.t[:, :, mt * 128:(mt + 1) * 128], memT)
        for mt in range(2):
            for half in range(2):
                cs = slice(half * 512, (half + 1) * 512)
                for (Wt, dst, isv) in ((Wck, C.o_mk, False), (Wcv, C.o_mv, True)):
                    b = kb.bank()
                    kb.op("pe", lambda e, b=b, Wt=Wt: [e.matmul(b.t[:, :], lhsT=memT.t[:, k, mt * 128:(mt + 1) * 128], rhs=Wt.t[:, k, cs], start=(k == 0), stop=(k == 7)) for k in range(8)][-1], [memT, Wt], [b])
                    o = kvo.next()
                    kb.copy("act", o.t[:, :], b.t[:, :], [b], [o])
                    kb.dma("sp", dst[mt * 128:(mt + 1) * 128, cs], o.t[:, :], reads=[o])
                    if isv:
                        kb.copy("pool", V.t[:, mt, cs], o.t[:, :], [o], [V])
        for c in range(8):
            b = kb.bank()
            kb.op("pe", lambda e, b=b, c=c: [e.matmul(b.t[:, 0:256], lhsT=Wck.t[:, k, c * 128:(c + 1) * 128], rhs=memT.t[:, k, :], start=(k == 0), stop=(k == 7)) for k in range(8)][-1], [memT, Wck], [b])
            kb.copy("act", KT.t[:, c, :], b.t[:, 0:256], [b], [KT])

        def mkset(i, st=st):
            return dict(hnT=kb.sb(f"a_hnT{i}", [128, 8, 128], BF16, st), Qs=kb.sb(f"a_Qs{i}", [128, D], BF16, st), QT=kb.sb(f"a_QT{i}", [128, 8, 128], BF16, st),
                        P=kb.sb(f"a_P{i}", [128, D], BF16, st), PT=kb.sb(f"a_PT{i}", [128, 8, 128], BF16, st), On=kb.sb(f"a_On{i}", [128, D], BF16, st),
                        OT=kb.sb(f"a_OT{i}", [128, 8, 128], BF16, st), sm=kb.sb(f"a_sm{i}", [128, 16], F32, st), x2=kb.sb(f"a_x2{i}", [128, D], F32, st),
                        x1=kb.sb(f"a_x1{i}", [128, D], F32, st), xn=kb.sb(f"a_xn{i}", [128, D], BF16, st), junk=kb.sb(f"a_junk{i}", [128, D], BF16, st),
                        ss=kb.sb(f"a_ss{i}", [128, 4], F32, st))
        sets = [mkset(0), mkset(1)]
        zt = kb.sb("a_zero", [128, 64], BF16, st)
        kb.memset("pool", zt.t[:], 0.0, [zt])

        def load_x1(ci, S):
            q, tok0 = chunk_info(C, ci)
            kb.dma("sp", S["x1"].t[:q, :], C.X1[tok0:tok0 + q, :], reads=[C.X1_r[ci]], writes=[S["x1"]])

        def q_front(q, S):
            hs = slice(0, q)
            hnT, Qs, QT = S["hnT"], S["Qs"], S["QT"]
            rms_to_T(C, S["x1"], q, C.colv.t[:, V_NCROSS:V_NCROSS + 8], hnT.t[:, :, :q], hnT, scratch=S)
            for half in range(2):
                cs = slice(half * 512, (half + 1) * 512)
                b = kb.bank()
                kb.op("pe", lambda e, b=b, cs=cs: [e.matmul(b.t[hs, :], lhsT=hnT.t[:, k, :q], rhs=Wcq.t[:, k, cs], start=(k == 0), stop=(k == 7)) for k in range(8)][-1], [hnT, Wcq], [b])
                kb.act(Qs.t[hs, cs], b.t[hs, :], AF.Copy, [b], [Qs], scale=1.0 / 16.0)
            to_T(C, Qs, q, 8, None, QT.t[:, :, :q], QT)

        def softmax(q, bs, S):
            hs = slice(0, q)
            sm, P, PT = S["sm"], S["P"], S["PT"]
            for hh in range(2):
                kb.op("dve", lambda e, hh=hh: e.tensor_reduce(out=sm.t[hs, hh * 2:hh * 2 + 2], in_=bs[hh].t[hs, :].rearrange("p (h m) -> p h m", m=256), axis=AX.X, op=ALU.max), [bs[hh]], [sm])
            kb.ts("dve", sm.t[hs, 4:8], sm.t[hs, 0:4], -1.0, None, ALU.mult, None, [sm], [sm])
            for h in range(4):
                kb.act(P.t[hs, h * 256:(h + 1) * 256], bs[h // 2].t[hs, (h % 2) * 256:(h % 2 + 1) * 256], AF.Exp, [bs[h // 2], sm], [P, sm], bias=sm.t[hs, 4 + h:5 + h], accum_out=sm.t[hs, 8 + h:9 + h])
            kb.op("dve", lambda e: e.reciprocal(out=sm.t[hs, 12:16], in_=sm.t[hs, 8:12]), [sm], [sm])
            to_T(C, P, q, 8, None, PT.t[:, :, :q], PT)

        def o_back(q, bo, S, ci, tok0):
            hs = slice(0, q)
            sm, On, OT, x2, xt = S["sm"], S["On"], S["OT"], S["x2"], S["x1"]
            for h in range(4):
                kb.act(On.t[hs, h * 256:(h + 1) * 256], bo[h // 2].t[hs, (h % 2) * 256:(h % 2 + 1) * 256], AF.Copy, [bo[h // 2], sm], [On], scale=sm.t[hs, 12 + h:13 + h])
            to_T(C, On, q, 8, None, OT.t[:, :, :q], OT)
            for half in range(2):
                cs = slice(half * 512, (half + 1) * 512)
                b = kb.bank()
                kb.op("pe", lambda e, b=b, cs=cs: [e.matmul(b.t[hs, :], lhsT=OT.t[:, k, :q], rhs=Wco.t[:, k, cs], start=(k == 0), stop=(k == 7)) for k in range(8)][-1], [OT, Wco], [b])
                kb.tt("dve", x2.t[hs, cs], b.t[hs, :], xt.t[hs, cs], ALU.add, [b, xt], [x2])
            kb.dma("sp", C.X2[tok0:tok0 + q, :], x2.t[hs, :], reads=[x2], writes=[C.X2_r[ci]])

        def prompt_chunk(ci, S):
            q, tok0 = 128, ci * 128
            load_x1(ci, S)
            q_front(q, S)
            QT, PT = S["QT"], S["PT"]
            bs = [kb.bank(), kb.bank()]
            for hh in range(2):
                def emit(e, hh=hh):
                    for h2 in range(2):
                        h = hh * 2 + h2
                        for dc in range(2):
                            last = e.matmul(bs[hh].t[:, h2 * 256:(h2 + 1) * 256], lhsT=QT.t[:, h * 2 + dc, :], rhs=KT.t[:, h * 2 + dc, :], start=(dc == 0), stop=(dc == 1))
                    return last
                kb.op("pe", emit, [QT, KT], [bs[hh]])
            softmax(q, bs, S)
            bo = [kb.bank(), kb.bank()]
            for hh in range(2):
                def emit(e, hh=hh):
                    for h2 in range(2):
                        h = hh * 2 + h2
                        for mt in range(2):
                            last = e.matmul(bo[hh].t[:, h2 * 256:(h2 + 1) * 256], lhsT=PT.t[:, h * 2 + mt, :], rhs=V.t[:, mt, h * 256:(h + 1) * 256], start=(mt == 0), stop=(mt == 1))
                    return last
                kb.op("pe", emit, [PT, V], [bo[hh]])
            o_back(q, bo, S, ci, tok0)

        with ExitStack() as st2:
            psets = sets + [mkset(2, st2), mkset(3, st2)]
            for c0 in range(0, C.NCH, 4):
                lists = []
                for i in range(4):
                    if c0 + i < C.NCH:
                        kb.begin([2 * i, 2 * i + 1]); prompt_chunk(c0 + i, psets[i]); lists.append(kb.end())
                kb.play(schedule(*lists))
            kb.set_pool(range(8))
            kb.barrier()
        q, tok0, ci = 64, C.TP, C.NCH
        S = sets[0]
        QT, PT = S["QT"], S["PT"]
        load_x1(ci, S)
        q_front(q, S)
        QTm = kb.sb("a_QTm", [128, 8, 16 * 68], BF16, st)
        PTm = kb.sb("a_PTm", [128, 8, 16 * 68], BF16, st)
        kb.memset("pool", QTm.t[:], 0.0, [QTm])
        kb.memset("pool", PTm.t[:], 0.0, [PTm])
        kb.copy("pool", QTm.t[:, :, :].rearrange("p h (c x) -> p h c x", x=68)[:, :, :, 0:4], QT.t[:, :, :64].rearrange("p h (c j) -> p h c j", j=4), [QT], [QTm])
        Kr = kb.rot("a_Kb", [128, 2, D], BF16, 2, st)
        KTr = kb.rot("a_KTb", [128, 8, 256], BF16, 2, st)
        bs, ids = kb.reserve(2)
        for hh in range(2):
            kb.op("pe", lambda e, hh=hh: e.matmul(bs[hh].t[:64, :], lhsT=zt.t[:64, :], rhs=Wcq.t[:64, 0, 0:512], start=True, stop=False), [zt, Wcq], [bs[hh]])
        ckv = C.ck.rearrange("(b mt p) c -> b p mt c", mt=2, p=128)
        cvv = C.cv.rearrange("(b mt p) c -> b p mt c", mt=2, p=128)
        for bb in range(16):
            Kb = Kr.next()
            kb.dma("pool", Kb.t[:, :, :], ckv[bb], writes=[Kb])
            KTb = KTr.next()
            for mt in range(2):
                b = kb.bank()
                pb = b.t.bitcast(BF16)
                kb.op("pe", lambda e, pb=pb, mt=mt, Kb=Kb: [e.transpose(pb[:, c * 128:(c + 1) * 128], Kb.t[:, mt, c * 128:(c + 1) * 128], C.identb.t[:, :]) for c in range(8)][-1], [Kb, C.identb], [b])
                kb.copy("act", KTb.t[:, :, mt * 128:(mt + 1) * 128], pb.rearrange("p (c m) -> p c m", m=128), [b], [KTb])
            for hh in range(2):
                def emit(e, hh=hh, KTb=KTb, bb=bb):
                    for h2 in range(2):
                        h = hh * 2 + h2
                        for dc in range(2):
                            last = e.matmul(bs[hh].t[:64, h2 * 256:(h2 + 1) * 256], lhsT=QTm.t[:, h * 2 + dc, bb * 64:(bb + 1) * 64], rhs=KTb.t[:, h * 2 + dc, :], start=False, stop=(bb == 15 and dc == 1 and h2 == 1))
                    return last
                kb.op("pe", emit, [QTm, KTb], [bs[hh]])
        softmax(q, bs, S)
        kb.copy("pool", PTm.t[:, :, :].rearrange("p h (c x) -> p h c x", x=68)[:, :, :, 0:4], PT.t[:, :, :64].rearrange("p h (c j) -> p h c j", j=4), [PT], [PTm])
        kb.release(ids)
        bo, ids = kb.reserve(2)
        for hh in range(2):
            kb.op("pe", lambda e, hh=hh: e.matmul(bo[hh].t[:64, :], lhsT=zt.t[:64, :], rhs=Wcq.t[:64, 0, 0:512], start=True, stop=False), [zt, Wcq], [bo[hh]])
        for bb in range(16):
            Vb = Kr.next()
            kb.dma("pool", Vb.t[:, :, :], cvv[bb], writes=[Vb])
            for hh in range(2):
                def emit(e, hh=hh, Vb=Vb, bb=bb):
                    for h2 in range(2):
                        h = hh * 2 + h2
                        for mt in range(2):
                            last = e.matmul(bo[hh].t[:64, h2 * 256:(h2 + 1) * 256], lhsT=PTm.t[:, h * 2 + mt, bb * 64:(bb + 1) * 64], rhs=Vb.t[:, mt, h * 256:(h + 1) * 256], start=False, stop=(bb == 15 and mt == 1 and h2 == 1))
                    return last
                kb.op("pe", emit, [PTm, Vb], [bo[hh]])
        o_back(q, bo, S, ci, tok0)
        kb.release(ids)
        kb.barrier()


def ffn_pass(C):
    kb = C.kb
    with ExitStack() as st:
        Wga = load_w(C, "Wga", C.w_gate, FFN, stack=st)
        Wup = load_w(C, "Wup", C.w_up, FFN, stack=st)
        Wdn = load_w(C, "Wdn", C.w_down, 1024, nk=22, stack=st)
        nfin = kb.sb("f_nfin", [128, D], F32, st)
        kb.dma("sp", nfin.t[:], C.rowvecs[:, RV_NFIN:RV_NFIN + D].partition_broadcast(128), writes=[nfin])
        x2r = kb.rot("f_x2", [128, 2, D], F32, 2, st)
        hnT = kb.sb("f_hnT", [128, 8, 256], BF16, st)
        hT = kb.sb("f_hT", [128, 22, 256], BF16, st)
        sgr = kb.rot("f_sg", [128, 256], F32, 2, st)
        yr = kb.rot("f_y", [128, D], F32, 2, st)
        ntile = C.NCH // 2 + 1

        def tinfo(ti):
            if ti < C.NCH // 2:
                return [(128, ti * 256, 2 * ti), (128, ti * 256 + 128, 2 * ti + 1)]
            return [(64, C.TP, C.NCH)]

        def loads(ti):
            t = x2r.next()
            for sub, (q, tok0, ci) in enumerate(tinfo(ti)):
                kb.dma("sp", t.t[:q, sub, :], C.X2[tok0:tok0 + q, :], reads=[C.X2_r[ci]], writes=[t])
            return t
        nxt = loads(0)
        for ti in range(ntile):
            xt = nxt
            if ti + 1 < ntile:
                nxt = loads(ti + 1)
            subs = tinfo(ti)
            Wd = sum(q for q, _, _ in subs)
            for sub, (q, tok0, ci) in enumerate(subs):
                xv = T(xt.t[:, sub, :], xt.r)
                rms_to_T(C, xv, q, C.colv.t[:, V_NFFN:V_NFFN + 8], hnT.t[:, :, sub * 128:sub * 128 + q], hnT)
            for f in range(22):
                b = kb.bank()

                def emit(e, b=b, f=f):
                    for k in range(8):
                        e.matmul(b.t[:, 0:Wd], lhsT=Wga.t[:, k, f * 128:(f + 1) * 128], rhs=hnT.t[:, k, 0:Wd], start=(k == 0), stop=(k == 7))
                    for k in range(8):
                        last = e.matmul(b.t[:, 256:256 + Wd], lhsT=Wup.t[:, k, f * 128:(f + 1) * 128], rhs=hnT.t[:, k, 0:Wd], start=(k == 0), stop=(k == 7))
                    return last
                kb.op("pe", emit, [Wga, Wup, hnT], [b])
                sg = sgr.next()
                kb.act(sg.t[:, 0:Wd], b.t[:, 0:Wd], AF.Tanh, [b], [sg], scale=0.5)
                kb.stt(sg.t[:, 0:Wd], sg.t[:, 0:Wd], 1.0, b.t[:, 0:Wd], ALU.add, ALU.mult, [sg, b], [sg])
                kb.tt("dve", hT.t[:, f, 0:Wd], sg.t[:, 0:Wd], b.t[:, 256:256 + Wd], ALU.mult, [sg, b], [hT])
            for sub, (q, tok0, ci) in enumerate(subs):
                hs = slice(0, q)
                for half in range(2):
                    cs = slice(half * 512, (half + 1) * 512)
                    b = kb.bank()
                    kb.op("pe", lambda e, b=b, sub=sub, q=q, cs=cs: [e.matmul(b.t[:q, :], lhsT=hT.t[:, f, sub * 128:sub * 128 + q], rhs=Wdn.t[:, f, cs], start=(f == 0), stop=(f == 21)) for f in range(22)][-1], [hT, Wdn], [b])
                    kb.stt(xt.t[hs, sub, cs], b.t[hs, :], 0.5, xt.t[hs, sub, cs], ALU.mult, ALU.add, [b, xt], [xt])
                xv = T(xt.t[:, sub, :], xt.r)
                ss = C.ss.next()
                rms_rstd(C, xv, q, ss)
                y = yr.next()
                kb.act(y.t[hs, :], xv.t[hs, :], AF.Copy, [xt, ss], [y], scale=ss.t[hs, 0:1])
                kb.tt("dve", y.t[hs, :], y.t[hs, :], nfin.t[hs, :], ALU.mult, [y, nfin], [y])
                dst = C.o_y_p[tok0:tok0 + q, :] if ci < C.NCH else C.o_y_s[:, :]
                kb.dma("sp", dst, y.t[hs, :], reads=[y])
        kb.barrier()

def build(NCH=16, debug=False, upto=99, skip_ssd=False):
    TP = NCH * 128
    TS = 64
    TT = TP + TS
    nc = bass.Bass("TRN2", target_bir_lowering=False)
    C = Ctx()
    C.nc, C.NCH, C.TP, C.TT, C.NT = nc, NCH, TP, TT, NCH + 1
    C.skip_ssd = skip_ssd

    def din(name, shape):
        return nc.dram_tensor(name, list(shape), F32, kind="ExternalInput").ap()

    def dout(name, shape):
        return nc.dram_tensor(name, list(shape), F32, kind="ExternalOutput").ap()

    def dscr(name, shape, dt):
        if debug:
            return nc.dram_tensor(name, list(shape), dt, kind="ExternalOutput").ap()
        return nc.dram_tensor(name, list(shape), dt).ap()

    C.xp = din("xp", [TP, D]); C.xs = din("xs", [TS, D]); C.mem = din("mem", [256, D])
    C.st_ssm = din("st_ssm", [16 * 2048, 128]); C.st_conv = din("st_conv", [48, 3072]); C.st_hg = din("st_hg", [16 * 1024, 128])
    C.ck = din("ck", [16 * 256, D]); C.cv = din("cv", [16 * 256, D])
    C.w_in = din("w_in", [D, IN_DIM]); C.w_ssd_out = din("w_ssd_out", [2048, D]); C.w_hgrn_out = din("w_hgrn_out", [D, D]); C.w_out = din("w_out", [D, D])
    C.w_cq = din("w_cq", [D, D]); C.w_ck = din("w_ck", [D, D]); C.w_cv = din("w_cv", [D, D]); C.w_co = din("w_co", [D, D])
    C.w_gate = din("w_gate", [D, FFN]); C.w_up = din("w_up", [D, FFN]); C.w_down = din("w_down", [FFN, D])
    C.consts = din("consts", [128, 1024]); C.colvecs = din("colvecs", [128, 176]); C.rowvecs = din("rowvecs", [1, RV_N])
    C.o_y_p = dout("o_y_p", [TP, D]); C.o_y_s = dout("o_y_s", [TS, D]); C.o_ssm_p = dout("o_ssm_p", [2048, 128]); C.o_conv_p = dout("o_conv_p", [3, 3072])
    C.o_hg_p = dout("o_hg_p", [1024, 128]); C.o_mk = dout("o_mk", [256, D]); C.o_mv = dout("o_mv", [256, D])
    C.o_ssm_s = dout("o_ssm_s", [16 * 2048, 128]); C.o_conv_s = dout("o_conv_s", [48, 3072]); C.o_hg_s = dout("o_hg_s", [16 * 1024, 128])
    C.nTd = dscr("nTd", [128, 8, 3 + TT], BF16); C.nTd_r = [Region(f"nTd{i}") for i in range(C.NT + 1)]
    C.YNT = dscr("YNT", [128, 16, TT], BF16); C.YNT_r = [[Region(f"YNT{i}_{g}") for g in range(4)] for i in range(C.NT)]
    C.ONT = dscr("ONT", [128, 8, TT], BF16); C.ONT_r = [Region(f"ONT{i}") for i in range(C.NT)]
    C.X1 = dscr("X1", [TT, D], F32); C.X1_r = [Region(f"X1{i}") for i in range(C.NT)]
    C.X2 = dscr("X2", [TT, D], F32); C.X2_r = [Region(f"X2{i}") for i in range(C.NT)]

    with ExitStack() as st:
        kb = KB(nc, st)
        C.kb = kb
        C.cst = kb.sb("cst", [128, 1024]); kb.dma("sp", C.cst.t[:], C.consts[:, :], writes=[C.cst])
        C.colv = kb.sb("colv", [128, 176]); kb.dma("sp", C.colv.t[:], C.colvecs[:, :], writes=[C.colv])
        C.rowb = kb.sb("rowb", [128, 96]); kb.dma("sp", C.rowb.t[:], C.rowvecs[:, 0:96].partition_broadcast(128), writes=[C.rowb])
        C.identb = kb.sb("identb", [128, 128], BF16); kb.copy("dve", C.identb.t[:], C.cst.t[:, C_ID:C_ID + 128], [C.cst], [C.identb])
        C.a_bc = kb.sb("a_bc", [128, 32])
        kb.act(C.a_bc.t[:], C.rowb.t[:, RV_ALOG:RV_ALOG + 32], AF.Exp, [C.rowb], [C.a_bc])
        kb.ts("dve", C.a_bc.t[:], C.a_bc.t[:], -1.0, None, ALU.mult, None, [C.a_bc], [C.a_bc])
        C.colvh = kb.sb("colvh", [128, 120]); kb.ts("dve", C.colvh.t[:], C.colv.t[:, V_CONVB:V_CONVB + 120], 0.5, None, ALU.mult, None, [C.colv], [C.colvh])
        C.junk = kb.rot("junk", [128, D], BF16, 1)
        C.ss = kb.rot("ss", [128, 4], F32, 4)
        C.xn = kb.rot("xn", [128, D], BF16, 1)
        if upto >= 0:
            pass0(C)
        if upto >= 1 and not C.skip_ssd:
            ssd_pass(C)
        if upto >= 2 and not C.skip_ssd:
            hg_pass(C)
        if upto >= 3:
            merge_pass(C)
        if upto >= 4:
            attn_pass(C)
        if upto >= 5:
            ffn_pass(C)
        kb.finish()
    return nc


def host_inputs(inp, NCH=16):
    f = lambda a: np.ascontiguousarray(np.asarray(a, dtype=np.float32))
    TP = NCH * 128
    cv = np.zeros((128, 176), np.float32)
    col8 = lambda v: f(v).reshape(-1, 128).T
    cv[:, V_NMIX:V_NMIX + 8] = col8(inp["norm_mix"][0]); cv[:, V_NCROSS:V_NCROSS + 8] = col8(inp["norm_cross"][0])
    cv[:, V_NMEM:V_NMEM + 8] = col8(inp["norm_mem"][0]); cv[:, V_NFFN:V_NFFN + 8] = col8(inp["norm_ffn"][0])
    cv[:, V_SNORM:V_SNORM + 16] = col8(inp["ssd_norm"][0]); cv[:, V_HNORM:V_HNORM + 8] = col8(inp["hgrn_norm"][0])
    cv[:, V_CONVB:V_CONVB + 24] = col8(inp["conv_b"][0])
    cw = f(inp["conv_w"][0])
    cv[:, V_CONVW:V_CONVW + 96] = cw.T.reshape(24, 128, 4).transpose(1, 0, 2).reshape(128, 96)
    rv = np.concatenate([f(inp["dt_bias"][0]), f(inp["a_log"][0]), f(inp["d_skip"][0]), f(inp["hgrn_lb"][0]), f(inp["hgrn_lb"][1]), f(inp["norm_final"])])[None, :]
    consts = make_consts()
    shared = dict(w_in=f(inp["w_in"][0]), w_ssd_out=f(inp["w_ssd_out"][0]), w_hgrn_out=f(inp["w_hgrn_out"][0]), w_out=f(inp["w_out"][0]),
                  w_cq=f(inp["w_cq"][0]), w_ck=f(inp["w_ck"][0]), w_cv=f(inp["w_cv"][0]), w_co=f(inp["w_co"][0]),
                  w_gate=f(inp["w_gate"][0]), w_up=f(inp["w_up"][0]), w_down=f(inp["w_down"][0]),
                  consts=consts, colvecs=cv, rowvecs=f(rv))
    maps = []
    for c in range(8):
        sl = slice(16 * c, 16 * c + 16)
        m = dict(shared)
        m["xp"] = f(inp["x_prompt"][c][:TP]); m["xs"] = f(inp["x_sample"][sl]).reshape(64, D); m["mem"] = f(inp["mem_prompt"][c])
        m["st_ssm"] = f(inp["state_ssm"][0, sl]).reshape(16 * 2048, 128); m["st_conv"] = f(inp["state_conv"][0, sl]).reshape(48, 3072)
        m["st_hg"] = f(inp["state_hgrn"][0, sl]).reshape(16 * 1024, 128)
        m["ck"] = f(inp["cache_mem_k"][0, sl]).reshape(16 * 256, D); m["cv"] = f(inp["cache_mem_v"][0, sl]).reshape(16 * 256, D)
        maps.append(m)
    return maps


def kernel(**inputs):
    NCH = 16
    nc = build(NCH)
    maps = host_inputs(inputs, NCH)
    res = run_bass_kernel_spmd(nc, maps, core_ids=list(range(8))).results
    g = lambda k: [np.asarray(r[k], dtype=np.float32) for r in res]
    y_p = np.stack(g("o_y_p")).reshape(8, 2048, D)
    y_s = np.stack(g("o_y_s")).reshape(128, 4, D)
    ssm_p = np.stack(g("o_ssm_p")).reshape(1, 8, 32, 64, 128)
    conv_p = np.stack(g("o_conv_p")).reshape(1, 8, 3, 3072)
    hg_p = np.stack(g("o_hg_p")).reshape(1, 8, 8, 128, 128)
    mk = np.stack(g("o_mk")).reshape(1, 8, 256, 4, 256)
    mv = np.stack(g("o_mv")).reshape(1, 8, 256, 4, 256)
    ssm_s = np.stack(g("o_ssm_s")).reshape(1, 128, 32, 64, 128)
    conv_s = np.stack(g("o_conv_s")).reshape(1, 128, 3, 3072)
    hg_s = np.stack(g("o_hg_s")).reshape(1, 128, 8, 128, 128)
    return (y_p, y_s, ssm_p, conv_p, hg_p, mk, mv, ssm_s, conv_s, hg_s)
```

```python
import numpy as np
from contextlib import ExitStack
import concourse.bass as bass
import concourse.mybir as mybir
from concourse.bass_utils import run_bass_kernel_spmd

F32 = mybir.dt.float32
BF16 = mybir.dt.bfloat16
ALU = mybir.AluOpType
AF = mybir.ActivationFunctionType
AX = mybir.AxisListType

ND = 8
EPS = 1e-6
D = 1024
IN_DIM = 11296
FFN = 2816


class Region:
    __slots__ = ("name", "w", "r")

    def __init__(self, name=""):
        self.name = name
        self.w = None
        self.r = {}


class T:
    __slots__ = ("t", "r")

    def __init__(self, t, r=None):
        self.t = t
        self.r = r if r is not None else Region()


class Rot:
    def __init__(self, items):
        self.items = items
        self.i = 0

    def next(self):
        x = self.items[self.i % len(self.items)]
        self.i += 1
        return x


def _reg(x):
    return x.r if isinstance(x, T) else x


class KB:
    def __init__(self, nc, stack):
        self.nc = nc
        self.stack = stack
        self.engs = {"pe": nc.tensor, "act": nc.scalar, "dve": nc.vector, "pool": nc.gpsimd, "sp": nc.sync}
        self.semh = {}
        self.cnt = {}
        self.known = {e: {} for e in self.engs}
        for e in ("pe", "act", "dve", "pool"):
            self.semh[e] = stack.enter_context(nc.semaphore("s_" + e))
            self.cnt[e] = 0
        self.dcount = {}
        for q in ("sp", "pool"):
            self.dcount[q] = 0
            for i in range(ND):
                self.semh[("d", q, i)] = stack.enter_context(nc.semaphore(f"d_{q}_{i}"))
        self.psall = stack.enter_context(nc.psum_tensor("psall", [128, 4096], F32))
        self.banks = [T(self.psall[:, i * 512:(i + 1) * 512], Region(f"bank{i}")) for i in range(8)]
        self.free = list(range(8))
        self.bi = 0
        self.uid = 0
        self.rec = None

    def sb(self, name, shape, dt=F32, stack=None):
        self.uid += 1
        st = stack if stack is not None else self.stack
        return T(st.enter_context(self.nc.sbuf_tensor(f"{name}_{self.uid}", list(shape), dt)), Region(name))

    def rot(self, name, shape, dt=F32, n=2, stack=None):
        return Rot([self.sb(f"{name}{i}", shape, dt, stack) for i in range(n)])

    def bank(self):
        b = self.free[self.bi % len(self.free)]
        self.bi += 1
        return self.banks[b]

    def reserve(self, n):
        out = [self.free.pop() for _ in range(n)]
        return [self.banks[b] for b in out], out

    def release(self, ids):
        self.free.extend(ids)
        self.free.sort()

    def _wait(self, e, deps):
        kn = self.known[e]
        best = {}
        for d in deps:
            if d is None:
                continue
            k, v = d
            if e == "pe" and k == "pe":
                continue
            if kn.get(k, 0) < v and best.get(k, 0) < v:
                best[k] = v
        for k, v in best.items():
            self.engs[e].wait_ge(self.semh[k], v)
            kn[k] = v

    def _deps(self, reads, writes):
        deps = []
        for r in reads:
            deps.append(_reg(r).w)
        for w in writes:
            w = _reg(w)
            deps.append(w.w)
            deps.extend(w.r.items())
        return deps

    def _mark(self, tok, reads, writes):
        k, v = tok
        for r in reads:
            r = _reg(r)
            if r.r.get(k, 0) < v:
                r.r[k] = v
        for w in writes:
            w = _reg(w)
            w.w = tok
            w.r = {}

    def set_pool(self, ids):
        self.free = list(ids)
        self.bi = 0

    def begin(self, pool):
        self.set_pool(pool)
        self.rec = []

    def end(self):
        r, self.rec = self.rec, None
        return r

    def play(self, lst):
        assert self.rec is None
        for it in lst:
            if it[0] == "op":
                self.op(*it[1:5])
            else:
                self.dma(it[1], it[2], it[3], it[4], it[5], **it[6])

    def op(self, e, emit, reads=(), writes=(), cost=None):
        if self.rec is not None:
            if cost is None:
                if e == "pe":
                    px = _PEProxy()
                    emit(px)
                    cost = px.cost
                else:
                    cost = 0.5
            self.rec.append(("op", e, emit, tuple(reads), tuple(writes), cost))
            return None
        self._wait(e, self._deps(reads, writes))
        inst = emit(self.engs[e])
        self.cnt[e] += 1
        inst.then_inc(self.semh[e], 1)
        self._mark((e, self.cnt[e]), reads, writes)
        return inst

    def dma(self, q, out, in_, reads=(), writes=(), **kw):
        if self.rec is not None:
            self.rec.append(("dma", q, out, in_, tuple(reads), tuple(writes), kw))
            return None
        i = self.dcount[q]
        self.dcount[q] += 1
        key = ("d", q, i % ND)
        prev = 16 * (i // ND)
        deps = self._deps(reads, writes)
        if prev:
            deps.append((key, prev))
        self._wait(q, deps)
        self.engs[q].dma_start(out=out, in_=in_, **kw).then_inc(self.semh[key], 16)
        self._mark((key, prev + 16), reads, writes)

    def _all_tokens(self):
        deps = []
        for q, n in self.dcount.items():
            for s in range(ND):
                cnt = (n - s + ND - 1) // ND if n > s else 0
                if cnt:
                    deps.append((("d", q, s), 16 * cnt))
        for e, c in self.cnt.items():
            if c:
                deps.append((e, c))
        return deps

    def barrier(self):
        deps = self._all_tokens()
        for e in self.engs:
            self._wait(e, deps)

    def finish(self):
        self._wait("sp", self._all_tokens())

    def act(self, out, in_, func, reads, writes, **kw):
        return self.op("act", lambda e: e.activation(out=out, in_=in_, func=func, **kw), reads, writes, cost=_est("act", out))

    def tt(self, eng, out, in0, in1, op, reads, writes):
        return self.op(eng, lambda e: e.tensor_tensor(out=out, in0=in0, in1=in1, op=op), reads, writes, cost=_est(eng, out))

    def ts(self, eng, out, in0, s1, s2, op0, op1, reads, writes):
        if s2 is None:
            return self.op(eng, lambda e: e.tensor_scalar(out=out, in0=in0, scalar1=s1, scalar2=None, op0=op0), reads, writes, cost=_est(eng, out))
        return self.op(eng, lambda e: e.tensor_scalar(out=out, in0=in0, scalar1=s1, scalar2=s2, op0=op0, op1=op1), reads, writes, cost=_est(eng, out))

    def stt(self, out, in0, scalar, in1, op0, op1, reads, writes):
        return self.op("dve", lambda e: e.scalar_tensor_tensor(out=out, in0=in0, scalar=scalar, in1=in1, op0=op0, op1=op1), reads, writes, cost=_est("dve", out))

    def copy(self, eng, out, in_, reads, writes):
        if eng == "act":
            return self.op("act", lambda e: e.copy(out=out, in_=in_), reads, writes, cost=_est("act", out))
        return self.op(eng, lambda e: e.tensor_copy(out=out, in_=in_), reads, writes, cost=_est(eng, out))

    def memset(self, eng, ap, val, writes):
        return self.op(eng, lambda e: e.memset(ap, val), (), writes)


def _free(ap):
    n = 1
    for d in ap.shape[1:]:
        n *= int(d)
    return n


def _est(eng, out):
    n = _free(out)
    if eng == "act":
        return 0.2 + n / 1200.0
    if eng == "dve":
        return 0.2 + n / 960.0
    return 0.25 + n / 450.0


class _PEProxy:
    def __init__(self):
        self.cost = 0.0

    def matmul(self, out, lhsT=None, rhs=None, **kw):
        p = 4 if lhsT.dtype == F32 else 1
        self.cost += 0.03 + max(_free(out), 32) * p / 2400.0
        return self

    def transpose(self, out, in_, ident, **kw):
        p = 4 if in_.dtype == F32 else 1
        self.cost += 0.03 + max(_free(out), 32) * p / 2400.0
        return self


def schedule(*streams):
    ops = [it for st_ in streams for it in st_]
    n = len(ops)
    engs, R, W, cost = [], [], [], []
    for it in ops:
        if it[0] == "op":
            engs.append(it[1]); R.append([_reg(x) for x in it[3]]); W.append([_reg(x) for x in it[4]]); cost.append(it[5])
        else:
            engs.append(it[1]); R.append([_reg(x) for x in it[4]]); W.append([_reg(x) for x in it[5]]); cost.append(2.0)
    preds = [set() for _ in range(n)]
    lastw, readers = {}, {}
    laste = {}
    k = 0
    for si, st_ in enumerate(streams):
        for _ in st_:
            i = k
            k += 1
            for r in R[i]:
                if id(r) in lastw:
                    preds[i].add(lastw[id(r)])
            for w in W[i]:
                if id(w) in lastw:
                    preds[i].add(lastw[id(w)])
                preds[i].update(readers.get(id(w), ()))
            for r in R[i]:
                readers.setdefault(id(r), []).append(i)
            for w in W[i]:
                lastw[id(w)] = i
                readers[id(w)] = []
            key = (si, engs[i])
            if key in laste:
                preds[i].add(laste[key])
            laste[key] = i
            preds[i].discard(i)
    succs = [[] for _ in range(n)]
    indeg = [len(p) for p in preds]
    for i, p in enumerate(preds):
        for j in p:
            succs[j].append(i)
    ready = [i for i in range(n) if indeg[i] == 0]
    efree = {}
    fin = [0.0] * n
    out = []
    LAT = 0.3
    while ready:
        best, bi = None, None
        for i in ready:
            e = engs[i]
            t = efree.get(e, 0.0)
            for j in preds[i]:
                tj = fin[j] + (0.0 if engs[j] == e else LAT)
                if tj > t:
                    t = tj
            if best is None or (t, i) < best:
                best, bi = (t, i), i
        ready.remove(bi)
        e = engs[bi]
        t0 = best[0]
        if ops[bi][0] == "dma":
            efree[e] = t0 + 0.1
            fin[bi] = t0 + 2.0
        else:
            fin[bi] = t0 + cost[bi]
            efree[e] = fin[bi]
        out.append(ops[bi])
        for j in succs[bi]:
            indeg[j] -= 1
            if indeg[j] == 0:
                ready.append(j)
    assert len(out) == n
    return out


def interleave(*lists):
    lists = [l for l in lists if l]
    idx = [0] * len(lists)
    out = []
    total = sum(len(l) for l in lists)
    while len(out) < total:
        best, bk = None, None
        for k, l in enumerate(lists):
            if idx[k] < len(l):
                key = (idx[k] + 0.5) / len(l)
                if best is None or key < best:
                    best, bk = key, k
        out.append(lists[bk][idx[bk]])
        idx[bk] += 1
    return out


def bc(ap, shape, axis):
    return ap.unsqueeze(axis).to_broadcast(list(shape))


C_ID, C_TPI, C_TPA, C_ONE, C_HPI, C_HPA, C_TSI, C_TSA, C_SSM, C_SEGP, C_SEGS = 0, 128, 256, 384, 512, 640, 768, 832, 896, 960, 964


def make_consts():
    c = np.zeros((128, 1024), np.float32)
    i = np.arange(128)
    c[:, C_ID:C_ID + 128] = np.eye(128)
    c[:, C_TPI:C_TPI + 128] = (i[:, None] <= i[None, :])
    c[:, C_TPA:C_TPA + 128] = (i[:, None] > i[None, :])
    c[:, C_ONE:C_ONE + 128] = 1.0
    s32 = i // 32
    same = s32[:, None] == s32[None, :]
    c[:, C_HPI:C_HPI + 128] = same & (i[:, None] <= i[None, :])
    c[:, C_HPA:C_HPA + 128] = same & (i[:, None] > i[None, :])
    j = np.arange(64)
    s4 = j // 4
    same4 = s4[:, None] == s4[None, :]
    c[:64, C_TSI:C_TSI + 64] = same4 & (j[:, None] <= j[None, :])
    c[:64, C_TSA:C_TSA + 64] = same4 & (j[:, None] > j[None, :])
    c[:64, C_SSM:C_SSM + 64] = same4
    c[:, C_SEGP:C_SEGP + 4] = (s32[:, None] == np.arange(4)[None, :])
    c[:64, C_SEGS:C_SEGS + 16] = (s4[:, None] == np.arange(16)[None, :])
    return c


V_NMIX, V_NCROSS, V_NMEM, V_NFFN, V_SNORM, V_HNORM, V_CONVB, V_CONVW = 0, 8, 16, 24, 32, 48, 56, 80
RV_DTB, RV_ALOG, RV_DSKIP, RV_LB0, RV_LB1, RV_NFIN = 0, 32, 64, 96, 1120, 2144
RV_N = 3168


class Ctx:
    pass


def load_w(C, name, src, ncols, nk=8, stack=None):
    kb = C.kb
    w = kb.sb(name, [128, nk, ncols], BF16, stack)
    for k in range(nk):
        kb.dma("pool", w.t[:, k, :], src[k * 128:(k + 1) * 128, :], writes=[w])
    return w


def rms_rstd(C, xt, q, ss, junk=None):
    kb = C.kb
    junk = junk if junk is not None else C.junk.next()
    kb.act(junk.t[:q], xt.t[:q], AF.Square, [xt], [junk, ss], accum_out=ss.t[:q, 1:2])
    kb.ts("dve", ss.t[:q, 2:3], ss.t[:q, 1:2], 1.0 / D, EPS, ALU.mult, ALU.add, [ss], [ss])
    kb.act(ss.t[:q, 1:2], ss.t[:q, 2:3], AF.Sqrt, [ss], [ss])
    kb.op("dve", lambda e: e.reciprocal(out=ss.t[:q, 0:1], in_=ss.t[:q, 1:2]), [ss], [ss])


def rms_to_T(C, xt, q, wcol, dst_ap, dst, scratch=None):
    kb = C.kb
    ss = scratch["ss"] if scratch else C.ss.next()
    rms_rstd(C, xt, q, ss, scratch["junk"] if scratch else None)
    xn = scratch["xn"] if scratch else C.xn.next()
    kb.act(xn.t[:q], xt.t[:q], AF.Copy, [xt, ss], [xn], scale=ss.t[:q, 0:1])
    to_T(C, xn, q, 8, wcol, dst_ap, dst)


def to_T(C, src, q, nk, wcol, dst_ap, dst, eng="dve"):
    kb = C.kb
    for k0 in range(0, nk, 8):
        n = min(8, nk - k0)
        b = kb.bank()
        pb = b.t.bitcast(BF16)

        def emit(e, k0=k0, n=n, pb=pb):
            for k in range(n):
                last = e.transpose(pb[:, k * 128:k * 128 + q], src.t[:q, (k0 + k) * 128:(k0 + k + 1) * 128], C.identb.t[:q, :q])
            return last
        kb.op("pe", emit, [src, C.identb], [b])
        pv = pb.rearrange("p (k t) -> p k t", t=128)[:, 0:n, 0:q]
        if wcol is None:
            kb.copy("act", dst_ap[:, k0:k0 + n, :], pv, [b], [dst])
        else:
            kb.tt(eng, dst_ap[:, k0:k0 + n, :], pv, bc(wcol[:, k0:k0 + n], [128, n, q], 2), ALU.mult, [b, C.colv], [dst])


def pass0(C):
    kb = C.kb
    with ExitStack() as st:
        z = kb.sb("zero3", [128, 8, 3], BF16, st)
        kb.memset("pool", z.t[:], 0.0, [z])
        kb.dma("sp", C.nTd[:, :, 0:3], z.t[:], reads=[z], writes=[C.nTd_r[0]])
        sets = [dict(xin=kb.rot(f"xin{i}", [128, D], F32, 2, st), nts=kb.rot(f"nts{i}", [128, 8, 128], BF16, 2, st),
                     sc=[dict(xn=kb.sb(f"p0xn{i}{j}", [128, D], BF16, st), junk=kb.sb(f"p0jk{i}{j}", [128, D], BF16, st), ss=kb.sb(f"p0ss{i}{j}", [128, 4], F32, st)) for j in range(2)]) for i in range(2)]

        def tile(ti, S, j):
            q = 128 if ti < C.NCH else 64
            src = C.xp[ti * 128:(ti + 1) * 128, :] if ti < C.NCH else C.xs[:, :]
            xt = S["xin"].next()
            kb.dma("sp", xt.t[:q], src, writes=[xt])
            nt = S["nts"].next()
            rms_to_T(C, xt, q, C.colv.t[:, V_NMIX:V_NMIX + 8], nt.t[:, :, :q], nt, scratch=S["sc"][j])
            kb.dma("sp", C.nTd[:, :, 3 + ti * 128:3 + ti * 128 + q], nt.t[:, :, :q], reads=[nt], writes=[C.nTd_r[ti + 1]])
        lists = []
        for i in range(2):
            kb.begin([0, 1, 2, 3] if i == 0 else [4, 5, 6, 7])
            for n_, ti in enumerate(range(i, C.NT, 2)):
                tile(ti, sets[i], n_ % 2)
            lists.append(kb.end())
        kb.play(schedule(*lists))
        kb.set_pool(range(8))
        kb.barrier()


def ssd_small(C, q, sm, nTc, tcol, Wdt, tri_inc, tri_after, same):
    kb = C.kb
    b = kb.bank()

    def emit(e):
        for k in range(8):
            last = e.matmul(b.t[:q, 0:32], lhsT=nTc.t[:, k, tcol:tcol + q], rhs=Wdt.t[:, k, :], start=(k == 0), stop=(k == 7))
        return last
    kb.op("pe", emit, [nTc, Wdt], [b])
    kb.tt("dve", sm.t[:q, 0:32], b.t[:q, 0:32], C.rowb.t[:q, RV_DTB:RV_DTB + 32], ALU.add, [b, C.rowb], [sm])
    kb.act(sm.t[:q, 32:64], sm.t[:q, 0:32], AF.Exp, [sm], [sm])
    kb.act(sm.t[:q, 64:96], sm.t[:q, 32:64], AF.Ln, [sm], [sm], bias=1.0)
    kb.tt("dve", sm.t[:q, 96:128], sm.t[:q, 64:96], C.a_bc.t[:q, :], ALU.mult, [sm, C.a_bc], [sm])
    b2 = kb.bank()

    def emit2(e):
        e.matmul(b2.t[:q, 0:32], lhsT=tri_inc, rhs=sm.t[:q, 96:128], start=True, stop=True)
        e.matmul(b2.t[:q, 32:64], lhsT=tri_after, rhs=sm.t[:q, 96:128], start=True, stop=True)
        return e.matmul(b2.t[:q, 64:96], lhsT=same, rhs=sm.t[:q, 96:128], start=True, stop=True)
    kb.op("pe", emit2, [sm, C.cst], [b2])
    kb.act(sm.t[:q, 128:224], b2.t[:q, 0:96], AF.Exp, [b2], [sm])
    kb.tt("dve", sm.t[:q, 224:256], sm.t[:q, 160:192], sm.t[:q, 64:96], ALU.mult, [sm], [sm])


def ssd_conv(C, q, pre_view, acc, BCT, eng_split=3):
    kb = C.kb
    cw = C.colvh.t
    WB, WW = 0, V_CONVW - V_CONVB
    for cc in range(24):
        o = acc["view"](cc)
        kb.act(o, pre_view(cc, 0), AF.Identity, [acc["pre"], C.colvh], [acc["T"]], scale=cw[:, WW + cc * 4:WW + cc * 4 + 1], bias=cw[:, WB + cc:WB + cc + 1])
    for cc in range(24):
        o = acc["view"](cc)
        for j in range(1, 4):
            kb.stt(o, pre_view(cc, j), cw[:, WW + cc * 4 + j:WW + cc * 4 + j + 1], o, ALU.mult, ALU.add, [acc["pre"], C.colvh, acc["T"]], [acc["T"]])
    a = acc["T"]
    th = acc["tanh"]
    kb.act(th, a.t[:, :, :q], AF.Tanh, [a], [acc["pre"]])
    kb.stt(a.t[:, :, :q], th, 1.0, a.t[:, :, :q], ALU.add, ALU.mult, [acc["pre"], a], [a])
    kb.copy(acc.get("cast_eng", "pool"), BCT.t[:, :, :q], a.t[:, 16:24, :q], [a], [BCT])


def ssd_group_front(C, q, g, a, sm, W):
    kb = C.kb
    bx = kb.bank()

    def emit(e):
        for j in range(4):
            last = e.transpose(bx.t[:q, j * 128:(j + 1) * 128], a.t[:, g * 4 + j, :q], C.cst.t[:, C_ID:C_ID + 128])
        return last
    kb.op("pe", emit, [a, C.cst], [bx])
    bx3 = bx.t[:q, :].rearrange("p (h d) -> p h d", d=64)
    v3 = lambda ap: ap.rearrange("p (h d) -> p h d", d=64)
    kb.tt("dve", v3(W["xdt"].t[:q, g * 512:(g + 1) * 512]), bx3, bc(sm.t[:q, 64 + g * 8:72 + g * 8], [q, 8, 64], 2), ALU.mult, [bx, sm], [W["xdt"]])
    kb.tt("dve", v3(W["xw"].t[:q, g * 512:(g + 1) * 512]), bx3, bc(sm.t[:q, 224 + g * 8:232 + g * 8], [q, 8, 64], 2), ALU.mult, [bx, sm], [W["xw"]])
    kb.tt("dve", v3(W["xsd"].t[:q, g * 512:(g + 1) * 512]), bx3, bc(C.rowb.t[:q, RV_DSKIP + g * 8:RV_DSKIP + g * 8 + 8], [q, 8, 64], 2), ALU.mult, [bx, C.rowb], [W["xsd"]])


def ssd_group_y(C, q, g, sm, W, BCT, CBm, tri_inc, tri_after, byi, nTc, tcol, Wz, tok0, ci):
    kb = C.kb
    v3 = lambda ap: ap.rearrange("p (h d) -> p h d", d=64)
    tmp = W["tmp"].next()
    kb.tt("dve", v3(tmp.t[:q, :]), v3(byi.t[:q, :]), bc(sm.t[:q, 128 + g * 8:136 + g * 8], [q, 8, 64], 2), ALU.mult, [byi, sm], [tmp])
    by = kb.bank()
    kb.op("pe", lambda e: e.matmul(by.t[:q, :], lhsT=C.identb.t[:q, :q], rhs=W["xsd"].t[:q, g * 512:(g + 1) * 512], start=True, stop=False), [C.identb, W["xsd"]], [by])
    for hh in range(2):
        Rt = W["R"].next()
        kb.tt("dve", Rt.t[:q, :, :q], bc(tri_inc, [q, 4, q], 1), bc(sm.t[:q, 96 + g * 8 + hh * 4:100 + g * 8 + hh * 4], [q, 4, q], 2), ALU.mult, [C.cst, sm], [Rt])
        bL = kb.bank()
        kb.op("pe", lambda e, Rt=Rt, bL=bL: [e.matmul(bL.t[:q, h * q:(h + 1) * q], lhsT=tri_after, rhs=Rt.t[:q, h, :q], start=True, stop=True) for h in range(4)][-1], [Rt, C.cst], [bL])
        Lt = W["L"].next()
        kb.act(Lt.t[:q, 0:4 * q], bL.t[:q, 0:4 * q], AF.Exp, [bL], [Lt])
        MT = W["MT"].next()
        kb.tt("dve", MT.t[:q, :, :q], Lt.t[:q, 0:4 * q].rearrange("p (h t) -> p h t", t=q), bc(CBm.t[:q, g, :q], [q, 4, q], 1), ALU.mult, [Lt, CBm], [MT])

        def emit(e, MT=MT, hh=hh):
            for h in range(4):
                c0 = (hh * 4 + h) * 64
                last = e.matmul(by.t[:q, c0:c0 + 64], lhsT=MT.t[:q, h, :q], rhs=W["xdt"].t[:q, g * 512 + c0:g * 512 + c0 + 64], start=False, stop=False)
            return last
        kb.op("pe", emit, [MT, W["xdt"]], [by])
    kb.op("pe", lambda e: e.matmul(by.t[:q, :], lhsT=C.identb.t[:q, :q], rhs=tmp.t[:q, :], start=False, stop=True), [C.identb, tmp], [by])
    bz = kb.bank()
    kb.op("pe", lambda e: [e.matmul(bz.t[:q, :], lhsT=nTc.t[:, k, tcol:tcol + q], rhs=Wz.t[:, k, g * 512:(g + 1) * 512], start=(k == 0), stop=(k == 7)) for k in range(8)][-1], [nTc, Wz], [bz])
    sz = W["sz"].next()
    kb.act(sz.t[:q, :], bz.t[:q, :], AF.Tanh, [bz], [sz], scale=0.5)
    kb.stt(sz.t[:q, :], sz.t[:q, :], 1.0, bz.t[:q, :], ALU.add, ALU.mult, [sz, bz], [sz])
    y = W["y"].next()
    kb.tt("dve", y.t[:q, :], by.t[:q, :], sz.t[:q, :], ALU.mult, [by, sz], [y])
    ms = C.ss.next()
    kb.act(sz.t[:q, :], y.t[:q, :], AF.Square, [y], [sz, ms], accum_out=ms.t[:q, 1:2])
    kb.ts("dve", ms.t[:q, 2:3], ms.t[:q, 1:2], 1.0 / 512, 4.0 * EPS, ALU.mult, ALU.add, [ms], [ms])
    kb.act(ms.t[:q, 1:2], ms.t[:q, 2:3], AF.Sqrt, [ms], [ms])
    kb.op("dve", lambda e: e.reciprocal(out=ms.t[:q, 0:1], in_=ms.t[:q, 1:2]), [ms], [ms])
    yn = W["yn"].next()
    kb.act(yn.t[:q, :], y.t[:q, :], AF.Copy, [y, ms], [yn], scale=ms.t[:q, 0:1])
    ynT = W["ynT"].next()
    to_T(C, yn, q, 4, C.colv.t[:, V_SNORM + g * 4:V_SNORM + g * 4 + 4], ynT.t[:, :, :q], ynT)
    kb.dma("sp", C.YNT[:, g * 4:(g + 1) * 4, tok0:tok0 + q], ynT.t[:, :, :q], reads=[ynT], writes=[C.YNT_r[ci][g]])


def ssd_pass(C):
    kb = C.kb
    NCH = C.NCH
    cst = C.cst.t
    with ExitStack() as stw:
        Wz = load_w(C, "Wz", C.w_in[:, 0:2048], 2048, stack=stw)
        Wx = load_w(C, "Wx", C.w_in[:, 2048:5120], 3072, stack=stw)
        Wdt = load_w(C, "Wdt", C.w_in[:, 5120:5152], 32, stack=stw)
        with ExitStack() as st:
            q = 128
            nTr = kb.rot("nTc", [128, 8, 131], BF16, 2, st)
            pre = kb.sb("pre", [128, 24, 131], F32, st)
            accr = kb.rot("acc", [128, 24, 128], F32, 1, st)
            BCTr = kb.rot("BCT", [128, 8, 128], BF16, 2, st)
            CBmr = kb.rot("CBm", [128, 4, 128], F32, 2, st)
            Btokr = kb.rot("Btok", [128, 512], BF16, 2, st)
            smr = kb.rot("sm", [128, 256], F32, 2, st)
            MTs = [[kb.sb(f"MT{i}_{g}", [128, 8, 128], BF16, st) for g in range(4)] for i in range(2)]
            szs = [[kb.sb(f"sz{i}_{g}", [128, 512], BF16, st) for g in range(4)] for i in range(2)]
            Rr = kb.rot("R", [128, 4, 128], F32, 2, st)
            Lr = kb.rot("L", [128, 512], F32, 2, st)
            xdt = [kb.sb(f"xdt{g}", [128, 512], BF16, st) for g in range(4)]
            xw = [kb.sb(f"xw{g}", [128, 512], BF16, st) for g in range(4)]
            xsd = [kb.sb(f"xsd{g}", [128, 512], BF16, st) for g in range(4)]
            hT = [kb.sb(f"hT{g}", [128, 512], F32, st) for g in range(4)]
            hTb = [kb.sb(f"hTb{g}", [128, 512], BF16, st) for g in range(4)]
            SW = [dict(y=kb.rot(f"y{i}", [128, 512], F32, 2, st), yn=kb.rot(f"yn{i}", [128, 512], BF16, 2, st), ynT=kb.rot(f"ynT{i}", [128, 4, 128], BF16, 2, st),
                       tmp=kb.rot(f"ytmp{i}", [128, 512], BF16, 2, st)) for i in range(2)]
            for g in range(4):
                kb.memset("pool", hT[g].t[:], 0.0, [hT[g]])
                kb.memset("pool", hTb[g].t[:], 0.0, [hTb[g]])
            tri_inc, tri_after, ones = cst[:, C_TPI:C_TPI + 128], cst[:, C_TPA:C_TPA + 128], cst[:, C_ONE:C_ONE + 128]
            v3 = lambda ap: ap.rearrange("p (h d) -> p h d", d=64)

            def Fa(c, nTc, sm, a, BCT, CBm, Btok, MTa, sza):
                kb.dma("sp", nTc.t[:, :, :], C.nTd[:, :, c * 128:c * 128 + 131], reads=[C.nTd_r[c], C.nTd_r[c + 1]], writes=[nTc])
                ssd_small(C, q, sm, nTc, 3, Wdt, tri_inc, tri_after, ones)
                for g in range(4):
                    for hh in range(2):
                        h0 = g * 8 + hh * 4
                        Rt = Rr.next()
                        kb.tt("dve", Rt.t[:, :, :], bc(tri_inc, [128, 4, 128], 1), bc(sm.t[:, 96 + h0:100 + h0], [128, 4, 128], 2), ALU.mult, [C.cst, sm], [Rt])
                        bL = kb.bank()
                        kb.op("pe", lambda e, Rt=Rt, bL=bL: [e.matmul(bL.t[:, h * 128:(h + 1) * 128], lhsT=tri_after, rhs=Rt.t[:, h, :], start=True, stop=True) for h in range(4)][-1], [Rt, C.cst], [bL])
                        kb.act(MTa[g].t[:, hh * 4:hh * 4 + 4, :], bL.t[:, :].rearrange("p (h t) -> p h t", t=128), AF.Exp, [bL], [MTa[g]])

            def Fc(c, nTc, sm, a, BCT, CBm, Btok, MTa, sza):
                for g in range(4):
                    bz = kb.bank()
                    kb.op("pe", lambda e, bz=bz, g=g: [e.matmul(bz.t[:, :], lhsT=nTc.t[:, k, 3:131], rhs=Wz.t[:, k, g * 512:(g + 1) * 512], start=(k == 0), stop=(k == 7)) for k in range(8)][-1], [nTc, Wz], [bz])
                    zt_ = Lr.next()
                    kb.act(zt_.t[:, :], bz.t[:, :], AF.Tanh, [bz], [zt_], scale=0.5)
                    kb.stt(sza[g].t[:, :], zt_.t[:, :], 1.0, bz.t[:, :], ALU.add, ALU.mult, [zt_, bz], [sza[g]])

            def Fb(c, nTc, sm, a, BCT, CBm, Btok, MTa, sza):
                for grp in range(8):
                    b = kb.bank()

                    def emit(e, grp=grp, b=b):
                        for j in range(3):
                            cc = grp * 3 + j
                            for k in range(8):
                                last = e.matmul(b.t[:, j * 131:(j + 1) * 131], lhsT=Wx.t[:, k, cc * 128:(cc + 1) * 128], rhs=nTc.t[:, k, 0:131], start=(k == 0), stop=(k == 7))
                        return last
                    kb.op("pe", emit, [Wx, nTc], [b])
                    kb.act(pre.t[:, grp * 3:grp * 3 + 3, :], b.t[:, 0:393].rearrange("p (j t) -> p j t", t=131), AF.Copy, [b], [pre])
                if c == NCH - 1:
                    for s_ in range(6):
                        b = kb.bank()
                        kb.op("pe", lambda e, b=b, s_=s_: [e.matmul(b.t[:3, :], lhsT=nTc.t[:, k, 128:131], rhs=Wx.t[:, k, s_ * 512:(s_ + 1) * 512], start=(k == 0), stop=(k == 7)) for k in range(8)][-1], [Wx, nTc], [b])
                        cv_ = Lr.next()
                        kb.copy("act", cv_.t[:3, :], b.t[:3, :], [b], [cv_])
                        kb.dma("sp", C.o_conv_p[:, s_ * 512:(s_ + 1) * 512], cv_.t[:3, :], reads=[cv_])
                accd = {"T": a, "pre": pre, "view": lambda cc: a.t[:, cc, :], "tanh": pre.t[:, :, 0:128], "cast_eng": "act"}
                ssd_conv(C, q, lambda cc, j: pre.t[:, cc, j:j + 128], accd, BCT)
                b = kb.bank()
                kb.op("pe", lambda e, b=b: [e.matmul(b.t[:, g * 128:(g + 1) * 128], lhsT=BCT.t[:, g, :], rhs=BCT.t[:, 4 + g, :], start=True, stop=True) for g in range(4)][-1], [BCT], [b])
                kb.tt("dve", CBm.t[:, :, :], b.t[:, :].rearrange("p (g t) -> p g t", t=128), bc(tri_inc, [128, 4, 128], 1), ALU.mult, [b, C.cst], [CBm])
                b2 = kb.bank()
                kb.op("pe", lambda e, b2=b2: [e.transpose(b2.t[:, g * 128:(g + 1) * 128], a.t[:, 16 + g, :], cst[:, C_ID:C_ID + 128]) for g in range(4)][-1], [a, C.cst], [b2])
                kb.copy("act", Btok.t[:, :], b2.t[:, :], [b2], [Btok])
                for g in range(4):
                    kb.tt("dve", MTa[g].t[:, :, :], MTa[g].t[:, :, :], bc(CBm.t[:, g, :], [128, 8, 128], 1), ALU.mult, [MTa[g], CBm], [MTa[g]])

            def Gfront(c, g, Wk, nTc, sm, a, BCT, CBm, Btok, MTa, sza):
                bx = kb.bank()
                kb.op("pe", lambda e: [e.transpose(bx.t[:, j * 128:(j + 1) * 128], a.t[:, g * 4 + j, :], cst[:, C_ID:C_ID + 128]) for j in range(4)][-1], [a, C.cst], [bx])
                bx3 = v3(bx.t[:, :])
                kb.tt("dve", v3(xdt[g].t[:, :]), bx3, bc(sm.t[:, 64 + g * 8:72 + g * 8], [128, 8, 64], 2), ALU.mult, [bx, sm], [xdt[g]])
                kb.tt("dve", v3(xw[g].t[:, :]), bx3, bc(sm.t[:, 224 + g * 8:232 + g * 8], [128, 8, 64], 2), ALU.mult, [bx, sm], [xw[g]])
                kb.tt("dve", v3(xsd[g].t[:, :]), bx3, bc(C.rowb.t[:, RV_DSKIP + g * 8:RV_DSKIP + g * 8 + 8], [128, 8, 64], 2), ALU.mult, [bx, C.rowb], [xsd[g]])

            def Ggroup(c, g, Wk, nTc, sm, a, BCT, CBm, Btok, MTa, sza):
                byi = kb.bank()
                kb.op("pe", lambda e: e.matmul(byi.t[:, :], lhsT=BCT.t[:, 4 + g, :], rhs=hTb[g].t[:, :], start=True, stop=True), [BCT, hTb[g]], [byi])
                bd = kb.bank()
                kb.op("pe", lambda e: e.matmul(bd.t[:, :], lhsT=Btok.t[:, g * 128:(g + 1) * 128], rhs=xw[g].t[:, :], start=True, stop=True), [Btok, xw[g]], [bd])
                tmp = Wk["tmp"].next()
                kb.tt("dve", v3(tmp.t[:, :]), v3(byi.t[:, :]), bc(sm.t[:, 128 + g * 8:136 + g * 8], [128, 8, 64], 2), ALU.mult, [byi, sm], [tmp])
                hv = v3(hT[g].t[:, :])
                kb.tt("dve", hv, hv, bc(sm.t[:, 192 + g * 8:200 + g * 8], [128, 8, 64], 2), ALU.mult, [hT[g], sm], [hT[g]])
                kb.tt("dve", hT[g].t[:, :], hT[g].t[:, :], bd.t[:, :], ALU.add, [hT[g], bd], [hT[g]])
                by = kb.bank()

                def emit(e):
                    e.matmul(by.t[:, :], lhsT=C.identb.t[:, :], rhs=xsd[g].t[:, :], start=True, stop=False)
                    for h in range(8):
                        e.matmul(by.t[:, h * 64:(h + 1) * 64], lhsT=MTa[g].t[:, h, :], rhs=xdt[g].t[:, h * 64:(h + 1) * 64], start=False, stop=False)
                    return e.matmul(by.t[:, :], lhsT=C.identb.t[:, :], rhs=tmp.t[:, :], start=False, stop=True)
                kb.op("pe", emit, [C.identb, xsd[g], MTa[g], xdt[g], tmp], [by])
                y = Wk["y"].next()
                kb.tt("dve", y.t[:, :], by.t[:, :], sza[g].t[:, :], ALU.mult, [by, sza[g]], [y])
                ms = C.ss.next()
                yn = Wk["yn"].next()
                kb.act(yn.t[:, :], y.t[:, :], AF.Square, [y], [yn, ms], accum_out=ms.t[:, 1:2])
                kb.ts("dve", ms.t[:, 2:3], ms.t[:, 1:2], 1.0 / 512, 4.0 * EPS, ALU.mult, ALU.add, [ms], [ms])
                kb.act(ms.t[:, 1:2], ms.t[:, 2:3], AF.Sqrt, [ms], [ms])
                kb.op("dve", lambda e: e.reciprocal(out=ms.t[:, 0:1], in_=ms.t[:, 1:2]), [ms], [ms])
                kb.act(yn.t[:, :], y.t[:, :], AF.Copy, [y, ms], [yn], scale=ms.t[:, 0:1])
                ynT = Wk["ynT"].next()
                to_T(C, yn, q, 4, C.colv.t[:, V_SNORM + g * 4:V_SNORM + g * 4 + 4], ynT.t[:, :, :], ynT)
                kb.dma("sp", C.YNT[:, g * 4:(g + 1) * 4, c * 128:c * 128 + 128], ynT.t[:, :, :], reads=[ynT], writes=[C.YNT_r[c][g]])
                kb.copy("act", hTb[g].t[:, :], hT[g].t[:, :], [hT[g]], [hTb[g]])

            sets = [(nTr.next(), smr.next(), accr.next(), BCTr.next(), CBmr.next(), Btokr.next(), MTs[i], szs[i]) for i in range(2)]
            GA, GB = [4, 5], [6, 7]

            def Flists(c, S_):
                kb.begin([0]); Fa(c, *S_); la = kb.end()
                kb.begin([3]); Fc(c, *S_); lc = kb.end()
                kb.begin([1, 2]); Fb(c, *S_); lb2 = kb.end()
                return [la, lc, lb2]
            kb.play(schedule(*Flists(0, sets[0])))
            for c in range(NCH):
                S_ = sets[c % 2]
                kb.begin(GA); Gfront(c, 0, SW[0], *S_); Gfront(c, 1, SW[0], *S_); Ggroup(c, 0, SW[0], *S_); Ggroup(c, 1, SW[0], *S_); ga = kb.end()
                kb.begin(GB); Gfront(c, 2, SW[1], *S_); Gfront(c, 3, SW[1], *S_); Ggroup(c, 2, SW[1], *S_); Ggroup(c, 3, SW[1], *S_); gb = kb.end()
                fl = Flists(c + 1, sets[(c + 1) % 2]) if c + 1 < NCH else []
                kb.play(schedule(ga, gb, *fl))
            kb.set_pool(range(8))
            so = T(pre.t[:].rearrange("p c t -> p (c t)")[:, 0:2048].rearrange("p (j n) -> p j n", n=128), pre.r)
            for j4 in range(4):
                b = kb.bank()
                kb.op("pe", lambda e, b=b, j4=j4: [e.transpose(b.t[:, jj * 128:(jj + 1) * 128], hT[j4].t[:, jj * 128:(jj + 1) * 128], cst[:, C_ID:C_ID + 128]) for jj in range(4)][-1], [hT[j4], C.cst], [b])
                kb.copy("act", so.t[:, j4 * 4:(j4 + 1) * 4, :], b.t[:, :].rearrange("p (j n) -> p j n", n=128), [b], [so])
            kb.dma("sp", C.o_ssm_p.rearrange("(j q) n -> q j n", q=128), so.t[:, :, :], reads=[so])
            kb.barrier()
        with ExitStack() as st:
            q = 64
            tri_inc, tri_after, same = cst[:64, C_TSI:C_TSI + 64], cst[:64, C_TSA:C_TSA + 64], cst[:64, C_SSM:C_SSM + 64]
            nTc = kb.sb("nTs", [128, 8, 64], BF16, st)
            kb.dma("sp", nTc.t[:, :, :], C.nTd[:, :, 3 + C.TP:3 + C.TP + 64], reads=[C.nTd_r[C.NT]], writes=[nTc])
            pre = kb.sb("pres", [128, 24, 16, 7], F32, st)
            a = kb.sb("accs", [128, 24, 64], F32, st)
            BCT = kb.sb("BCTs", [128, 8, 64], BF16, st)
            CBm = kb.sb("CBms", [64, 4, 64], F32, st)
            Btok = kb.sb("Btoks", [64, 512], BF16, st)
            sm = kb.sb("sms", [64, 256], F32, st)
            W = {
                "xdt": kb.sb("xdts", [64, 2048], BF16, st), "xw": kb.sb("xws", [64, 2048], BF16, st), "xsd": kb.sb("xsds", [64, 2048], BF16, st),
                "R": kb.rot("Rs", [64, 4, 64], F32, 2, st), "L": kb.rot("Ls", [64, 256], F32, 2, st), "MT": kb.rot("MTs", [64, 4, 64], BF16, 2, st),
                "y": kb.rot("ys", [64, 512], F32, 2, st), "sz": kb.rot("szs", [64, 512], F32, 2, st), "yn": kb.rot("yns", [64, 512], BF16, 2, st),
                "ynT": kb.rot("ynTs", [128, 4, 64], BF16, 2, st), "tmp": kb.rot("ytmps", [64, 512], BF16, 2, st),
            }
            cnvr = kb.rot("cnvs", [64, 512], F32, 2, st)
            ocs = C.o_conv_s.rearrange("(b j) c -> b j c", j=3)
            scr = kb.rot("stconv", [48, 512], F32, 2, st)
            for g6 in range(6):
                sc = scr.next()
                kb.dma("sp", sc.t[:, :], C.st_conv[:, g6 * 512:(g6 + 1) * 512], writes=[sc])
                b = kb.bank()
                kb.op("pe", lambda e, b=b, sc=sc: [e.transpose(b.t[:, jj * 48:(jj + 1) * 48], sc.t[:48, jj * 128:(jj + 1) * 128], cst[:48, C_ID:C_ID + 48]) for jj in range(4)][-1], [sc, C.cst], [b])
                kb.act(pre.t[:, g6 * 4:(g6 + 1) * 4, :, 0:3], b.t[:, 0:192].rearrange("p (c b j) -> p c b j", c=4, j=3), AF.Copy, [b], [pre])
            ssd_small(C, q, sm, nTc, 0, Wdt, tri_inc, tri_after, same)
            for grp in range(8):
                b = kb.bank()

                def emit(e, grp=grp, b=b):
                    for j in range(3):
                        cc = grp * 3 + j
                        for k in range(8):
                            last = e.matmul(b.t[:, j * 64:(j + 1) * 64], lhsT=Wx.t[:, k, cc * 128:(cc + 1) * 128], rhs=nTc.t[:, k, 0:64], start=(k == 0), stop=(k == 7))
                    return last
                kb.op("pe", emit, [Wx, nTc], [b])
                kb.act(pre.t[:, grp * 3:grp * 3 + 3, :, 3:7], b.t[:, 0:192].rearrange("p (c b j) -> p c b j", c=3, j=4), AF.Copy, [b], [pre])
            for s in range(6):
                b = kb.bank()
                kb.op("pe", lambda e, b=b, s=s: [e.matmul(b.t[:64, :], lhsT=nTc.t[:, k, 0:64], rhs=Wx.t[:, k, s * 512:(s + 1) * 512], start=(k == 0), stop=(k == 7)) for k in range(8)][-1], [Wx, nTc], [b])
                cv_ = cnvr.next()
                kb.copy("act", cv_.t[:64, :], b.t[:64, :], [b], [cv_])
                for t in range(1, 4):
                    kb.dma("sp", ocs[:, t - 1, s * 512:(s + 1) * 512], cv_.t[t:64:4, :], reads=[cv_])
            accd = {"T": a, "pre": pre, "view": lambda cc: a.t[:, cc, :].rearrange("p (b t) -> p b t", t=4), "tanh": pre.t[:].rearrange("p c b j -> p c (b j)")[:, :, 0:64]}
            ssd_conv(C, q, lambda cc, j: pre.t[:, cc, :, j:j + 4], accd, BCT)
            b = kb.bank()
            kb.op("pe", lambda e, b=b: [e.matmul(b.t[:64, g * 64:(g + 1) * 64], lhsT=BCT.t[:, g, :], rhs=BCT.t[:, 4 + g, :], start=True, stop=True) for g in range(4)][-1], [BCT], [b])
            kb.tt("dve", CBm.t[:, :, :], b.t[:64, 0:256].rearrange("p (g t) -> p g t", t=64), bc(tri_inc, [64, 4, 64], 1), ALU.mult, [b, C.cst], [CBm])
            b = kb.bank()
            kb.op("pe", lambda e, b=b: [e.transpose(b.t[:64, g * 128:(g + 1) * 128], a.t[:, 16 + g, :], cst[:, C_ID:C_ID + 128]) for g in range(4)][-1], [a, C.cst], [b])
            kb.copy("act", Btok.t[:, :], b.t[:64, :], [b], [Btok])
            CTm = kb.sb("CTm", [128, 4, 16 * 68], BF16, st)
            kb.memset("pool", CTm.t[:], 0.0, [CTm])
            for g in range(4):
                kb.copy("pool", CTm.t[:, g, :].rearrange("p (b x) -> p b x", x=68)[:, :, 0:4], BCT.t[:, 4 + g, :].rearrange("p (b t) -> p b t", t=4), [BCT], [CTm])
            for g in range(4):
                ssd_group_front(C, q, g, a, sm, W)
            decN = kb.sb("decN", [128, 256], F32, st)
            b = kb.bank()
            def emit_dec(e, b=b):
                for j in range(16):
                    for hl in range(2):
                        last = e.matmul(b.t[hl * 64:(hl + 1) * 64, j * 16:(j + 1) * 16], lhsT=sm.t[:64, 96 + 2 * j + hl:97 + 2 * j + hl].to_broadcast([64, 64]),
                                        rhs=cst[:64, C_SEGS:C_SEGS + 16], start=True, stop=True, tile_position=(0, hl * 64))
                return last
            kb.op("pe", emit_dec, [sm, C.cst], [b])
            kb.act(decN.t[:, :], b.t[:, 0:256], AF.Exp, [b], [decN])
            byis, ids = kb.reserve(4)
            h0r = kb.rot("h0nat", [128, 16, 128], F32, 2, st)
            nsr = kb.rot("newst", [128, 16, 128], F32, 1, st)
            hTbr = kb.rot("hTbb", [128, 2048], BF16, 2, st)
            xw4r = kb.rot("xw4", [4, 2048], BF16, 2, st)
            B4r = kb.rot("B4", [4, 512], BF16, 2, st)
            stin = C.st_ssm.rearrange("(b j q) n -> b q j n", j=16, q=128)
            stout = C.o_ssm_s.rearrange("(b j q) n -> b q j n", j=16, q=128)
            for bb in range(16):
                h0 = h0r.next()
                kb.dma("sp", h0.t[:, :, :], stin[bb], writes=[h0])
                xw4 = xw4r.next()
                B4 = B4r.next()
                kb.dma("sp", xw4.t[:, :], W["xw"].t[4 * bb:4 * bb + 4, :], reads=[W["xw"]], writes=[xw4])
                kb.dma("sp", B4.t[:, :], Btok.t[4 * bb:4 * bb + 4, :], reads=[Btok], writes=[B4])
                hb = hTbr.next()
                for j4 in range(4):
                    b = kb.bank()
                    kb.op("pe", lambda e, b=b, j4=j4, h0=h0: [e.transpose(b.t[:, jj * 128:(jj + 1) * 128], h0.t[:, j4 * 4 + jj, :], cst[:, C_ID:C_ID + 128]) for jj in range(4)][-1], [h0, C.cst], [b])
                    kb.copy("act", hb.t[:, j4 * 512:(j4 + 1) * 512], b.t[:, :], [b], [hb])
                for g in range(4):
                    kb.op("pe", lambda e, g=g, hb=hb, bb=bb: e.matmul(byis[g].t[:64, :], lhsT=CTm.t[:, g, bb * 64:(bb + 1) * 64], rhs=hb.t[:, g * 512:(g + 1) * 512], start=(bb == 0), stop=(bb == 15)), [CTm, hb], [byis[g]])
                ns = nsr.next()
                for j4 in range(4):
                    b = kb.bank()
                    kb.op("pe", lambda e, b=b, j4=j4, xw4=xw4, B4=B4: [e.matmul(b.t[:, jj * 128:(jj + 1) * 128], lhsT=xw4.t[0:4, (j4 * 4 + jj) * 128:(j4 * 4 + jj + 1) * 128], rhs=B4.t[0:4, j4 * 128:(j4 + 1) * 128], start=True, stop=True) for jj in range(4)][-1], [xw4, B4], [b])
                    for jj in range(4):
                        j = j4 * 4 + jj
                        kb.stt(ns.t[:, j, :], h0.t[:, j, :], decN.t[:, j * 16 + bb:j * 16 + bb + 1], b.t[:, jj * 128:(jj + 1) * 128], ALU.mult, ALU.add, [h0, decN, b], [ns])
                kb.dma("sp", stout[bb], ns.t[:, :, :], reads=[ns])
            for g in range(4):
                ssd_group_y(C, q, g, sm, W, BCT, CBm, tri_inc, tri_after, byis[g], nTc, 0, Wz, C.TP, C.NCH)
            kb.release(ids)
            kb.barrier()


def hg_pass(C):
    kb = C.kb
    cst = C.cst.t
    with ExitStack() as stw:
        Wq = load_w(C, "Wq", C.w_in[:, 5152:6176], 1024, stack=stw)
        Wf = load_w(C, "Wf", C.w_in[:, 6176:7200], 1024, stack=stw)
        Wi = load_w(C, "Wi", C.w_in[:, 7200:8224], 1024, stack=stw)
        Wg = load_w(C, "Wg", C.w_in[:, 8224:9248], 1024, stack=stw)
        lb = kb.sb("lb", [128, 1024], F32, stw)
        oml = kb.sb("oml", [128, 1024], F32, stw)
        kb.dma("sp", lb.t[:], C.rowvecs[:, RV_LB0:RV_LB0 + 1024].partition_broadcast(128), writes=[lb])
        kb.dma("sp", oml.t[:], C.rowvecs[:, RV_LB1:RV_LB1 + 1024].partition_broadcast(128), writes=[oml])
        kb.tt("dve", lb.t[:], lb.t[:], oml.t[:], ALU.subtract, [lb, oml], [lb])
        kb.act(lb.t[:], lb.t[:], AF.Sigmoid, [lb], [lb])
        kb.ts("dve", oml.t[:], lb.t[:], -1.0, 1.0, ALU.mult, ALU.add, [lb], [oml])
        omlh = kb.sb("omlh", [128, 1024], F32, stw)
        kb.ts("dve", omlh.t[:], oml.t[:], 0.5, None, ALU.mult, None, [oml], [omlh])

        def proj(q, nTc, tcol, Wt, half):
            b = kb.bank()
            kb.op("pe", lambda e: [e.matmul(b.t[:q, :], lhsT=nTc.t[:, k, tcol:tcol + q], rhs=Wt.t[:, k, half * 512:(half + 1) * 512], start=(k == 0), stop=(k == 7)) for k in range(8)][-1], [nTc, Wt], [b])
            return b

        def front(q, nseg, seglen, nTc, tcol, B, hinc, hafter, segones):
            hs = slice(0, q)
            for half in range(2):
                cs = slice(half * 512, (half + 1) * 512)
                b = proj(q, nTc, tcol, Wf, half)
                kb.act(B["w"].t[hs, cs], b.t[hs, :], AF.Tanh, [b], [B["w"]], scale=0.5)
            kb.stt(B["w"].t[hs, :], B["w"].t[hs, :], 1.0, omlh.t[hs, :], ALU.add, ALU.mult, [B["w"], omlh], [B["w"]])
            kb.tt("dve", B["logf"].t[hs, :], B["w"].t[hs, :], lb.t[hs, :], ALU.add, [B["w"], lb], [B["logf"]])
            kb.act(B["logf"].t[hs, :], B["logf"].t[hs, :], AF.Ln, [B["logf"]], [B["logf"]])
            kb.tt("dve", B["kk"].t[hs, :], oml.t[hs, :], B["w"].t[hs, :], ALU.subtract, [B["w"], oml], [B["kk"]])
            for half in range(2):
                cs = slice(half * 512, (half + 1) * 512)
                b = kb.bank()
                kb.op("pe", lambda e, b=b, cs=cs: e.matmul(b.t[hs, :], lhsT=hinc, rhs=B["logf"].t[hs, cs], start=True, stop=True), [B["logf"], C.cst], [b])
                kb.act(B["E1"].t[hs, cs], b.t[hs, :], AF.Exp, [b], [B["E1"]])
                kb.act(B["E1n"].t[hs, cs], b.t[hs, :], AF.Exp, [b], [B["E1n"]], scale=-1.0)
                b2 = kb.bank()
                kb.op("pe", lambda e, b2=b2, cs=cs: e.matmul(b2.t[hs, :], lhsT=hafter, rhs=B["logf"].t[hs, cs], start=True, stop=True), [B["logf"], C.cst], [b2])
                kb.act(B["E2"].t[hs, cs], b2.t[hs, :], AF.Exp, [b2], [B["E2"]])
            b = kb.bank()
            kb.op("pe", lambda e, b=b: [e.matmul(b.t[:, h * nseg:(h + 1) * nseg], lhsT=B["logf"].t[hs, h * 128:(h + 1) * 128], rhs=segones, start=True, stop=True) for h in range(8)][-1], [B["logf"], C.cst], [b])
            kb.act(B["dS"].t[:, 0:8 * nseg], b.t[:, 0:8 * nseg], AF.Exp, [b], [B["dS"]])
            for half in range(2):
                cs = slice(half * 512, (half + 1) * 512)
                b = proj(q, nTc, tcol, Wq, half)
                kb.act(B["sq"].t[hs, cs], b.t[hs, :], AF.Tanh, [b], [B["sq"]], scale=0.5)
                kb.stt(B["sq"].t[hs, cs], B["sq"].t[hs, cs], 1.0, b.t[hs, :], ALU.add, ALU.mult, [B["sq"], b], [B["sq"]])
            kb.tt("dve", B["qg"].t[hs, :], B["sq"].t[hs, :], B["E1"].t[hs, :], ALU.mult, [B["sq"], B["E1"]], [B["qg"]])
            kb.tt("dve", B["kg"].t[hs, :], B["kk"].t[hs, :], B["E1n"].t[hs, :], ALU.mult, [B["kk"], B["E1n"]], [B["kg"]])
            kb.tt("dve", B["kdec"].t[hs, :], B["kk"].t[hs, :], B["E2"].t[hs, :], ALU.mult, [B["kk"], B["E2"]], [B["kdec"]])
            for half in range(2):
                cs = slice(half * 512, (half + 1) * 512)
                b = proj(q, nTc, tcol, Wi, half)
                kb.copy("act", B["v"].t[hs, cs], b.t[hs, :], [b], [B["v"]])
            for half in range(2):
                cs = slice(half * 512, (half + 1) * 512)
                b = proj(q, nTc, tcol, Wg, half)
                kb.act(B["sg"].t[hs, cs], b.t[hs, :], AF.Tanh, [b], [B["sg"]], scale=0.5)
                kb.stt(B["sg"].t[hs, cs], B["sg"].t[hs, cs], 1.0, b.t[hs, :], ALU.add, ALU.mult, [B["sg"], b], [B["sg"]])
            to_T(C, B["qg"], q, 8, None, B["qgT"].t[:, :, :q], B["qgT"])
            to_T(C, B["kg"], q, 8, None, B["kgT"].t[:, :, :q], B["kgT"])
            x = q + seglen
            kb.copy("pool", B["QM"].t[:, :, :].rearrange("p h (c x) -> p h c x", x=x)[:, :, :, 0:seglen],
                    B["qgT"].t[:, :, :q].rearrange("p h (c j) -> p h c j", j=seglen), [B["qgT"]], [B["QM"]])
            for hh in range(2):
                b = kb.bank()
                kb.op("pe", lambda e, b=b, hh=hh: [e.matmul(b.t[hs, h4 * q:(h4 + 1) * q], lhsT=B["kgT"].t[:, hh * 4 + h4, :q], rhs=B["qgT"].t[:, hh * 4 + h4, :q], start=True, stop=True) for h4 in range(4)][-1], [B["kgT"], B["qgT"]], [b])
                kb.tt("dve", B["att"].t[hs, hh * 4:(hh + 1) * 4, :q], b.t[hs, 0:4 * q].rearrange("p (h t) -> p h t", t=q), bc(hinc, [q, 4, q], 1), ALU.mult, [b, C.cst], [B["att"]])

        def back(q, bo, B, tok0, ci):
            hs = slice(0, q)
            for half in range(2):
                kb.copy("act", B["osb"].t[hs, half * 512:(half + 1) * 512], bo[half].t[hs, :], [bo[half]], [B["osb"]])
            o = B["osb"]
            kb.tt("dve", B["osq"].t[hs, :], o.t[hs, :], o.t[hs, :], ALU.mult, [o], [B["osq"]])
            hsm = B["hsm"]
            kb.op("dve", lambda e: e.tensor_reduce(out=hsm.t[hs, 0:8], in_=B["osq"].t[hs, :].rearrange("p (h v) -> p h v", v=128), axis=AX.X, op=ALU.add), [B["osq"]], [hsm])
            kb.ts("dve", hsm.t[hs, 8:16], hsm.t[hs, 0:8], 4.0 / 128, 16.0 * EPS, ALU.mult, ALU.add, [hsm], [hsm])
            kb.act(hsm.t[hs, 16:24], hsm.t[hs, 8:16], AF.Sqrt, [hsm], [hsm])
            kb.op("dve", lambda e: e.reciprocal(out=hsm.t[hs, 24:32], in_=hsm.t[hs, 16:24]), [hsm], [hsm])
            o3 = o.t[hs, :].rearrange("p (h v) -> p h v", v=128)
            kb.tt("dve", o3, o3, bc(hsm.t[hs, 24:32], [q, 8, 128], 2), ALU.mult, [o, hsm], [o])
            kb.tt("dve", B["on"].t[hs, :], o.t[hs, :], B["sg"].t[hs, :], ALU.mult, [o, B["sg"]], [B["on"]])
            to_T(C, B["on"], q, 8, C.colv.t[:, V_HNORM:V_HNORM + 8], B["onT"].t[:, :, :q], B["onT"])
            kb.dma("sp", C.ONT[:, :, tok0:tok0 + q], B["onT"].t[:, :, :q], reads=[B["onT"]], writes=[C.ONT_r[ci]])

        def bufs(q, nseg, seglen, st, nslots=1):
            shared = {}
            for n in ("w", "logf", "kk", "E1", "E1n", "E2", "sq", "osb", "osq"):
                shared[n] = kb.sb("hg_" + n, [q, 1024], F32, st)
            for n in ("qg", "kg", "on"):
                shared[n] = kb.sb("hg_" + n, [q, 1024], BF16, st)
            shared["qgT"] = kb.sb("hg_qgT", [128, 8, q], BF16, st)
            shared["kgT"] = kb.sb("hg_kgT", [128, 8, q], BF16, st)
            shared["onT"] = kb.sb("hg_onT", [128, 8, q], BF16, st)
            shared["hsm"] = kb.sb("hg_hsm", [q, 32], F32, st)
            out = []
            for i in range(nslots):
                B = dict(shared)
                B["sg"] = kb.sb(f"hg_sg{i}", [q, 1024], F32, st)
                for n in ("kdec", "v"):
                    B[n] = kb.sb(f"hg_{n}{i}", [q, 1024], BF16, st)
                B["QM"] = kb.sb(f"hg_QM{i}", [128, 8, nseg * (q + seglen)], BF16, st)
                B["att"] = kb.sb(f"hg_att{i}", [q, 8, q], BF16, st)
                B["dS"] = kb.sb(f"hg_dS{i}", [128, 8 * nseg], F32, st)
                kb.memset("pool", B["QM"].t[:], 0.0, [B["QM"]])
                out.append(B)
            return out

        with ExitStack() as st:
            q, nseg, seglen = 128, 4, 32
            Bs = bufs(q, nseg, seglen, st, 2)
            nTr = kb.rot("hnTc", [128, 8, 128], BF16, 2, st)
            S = kb.sb("hg_S", [128, 8, 128], F32, st)
            Sb = [kb.sb(f"hg_Sb{c}", [128, 8, 128], BF16, st) for c in range(4)]
            kb.memset("pool", S.t[:], 0.0, [S])
            hinc, hafter, segones = cst[:, C_HPI:C_HPI + 128], cst[:, C_HPA:C_HPA + 128], cst[:, C_SEGP:C_SEGP + 4]

            def F(c, nTc, B):
                kb.dma("sp", nTc.t[:, :, :], C.nTd[:, :, 3 + c * 128:3 + c * 128 + 128], reads=[C.nTd_r[c + 1]], writes=[nTc])
                front(q, nseg, seglen, nTc, 0, B, hinc, hafter, segones)

            def G(c, B):
                for sc in range(4):
                    kb.copy("act", Sb[sc].t[:, :, :], S.t[:, :, :], [S], [Sb[sc]])
                    bd = [kb.bank(), kb.bank()]
                    for hh in range(2):
                        kb.op("pe", lambda e, hh=hh, sc=sc, bd=bd: [e.matmul(bd[hh].t[:, h4 * 128:(h4 + 1) * 128], lhsT=B["kdec"].t[32 * sc:32 * sc + 32, (hh * 4 + h4) * 128:(hh * 4 + h4 + 1) * 128],
                                                                      rhs=B["v"].t[32 * sc:32 * sc + 32, (hh * 4 + h4) * 128:(hh * 4 + h4 + 1) * 128], start=True, stop=True, tile_position=(32 * sc, 0)) for h4 in range(4)][-1], [B["kdec"], B["v"]], [bd[hh]])
                    kb.tt("dve", S.t[:, :, :], S.t[:, :, :], bc(B["dS"].t[:, :].rearrange("p (h c) -> p h c", c=nseg)[:, :, sc], [128, 8, 128], 2), ALU.mult, [S, B["dS"]], [S])
                    for hh in range(2):
                        Sv = S.t[:, hh * 4:(hh + 1) * 4, :]
                        kb.tt("dve", Sv, Sv, bd[hh].t[:, :].rearrange("p (h v) -> p h v", v=128), ALU.add, [S, bd[hh]], [S])
                bo = [kb.bank(), kb.bank()]
                for hh in range(2):
                    def emit(e, hh=hh):
                        for h4 in range(4):
                            h = hh * 4 + h4
                            e.matmul(bo[hh].t[:, h4 * 128:(h4 + 1) * 128], lhsT=B["att"].t[:, h, :], rhs=B["v"].t[:, h * 128:(h + 1) * 128], start=True, stop=False)
                            for sc in range(4):
                                last = e.matmul(bo[hh].t[:, h4 * 128:(h4 + 1) * 128], lhsT=B["QM"].t[:, h, sc * 128:(sc + 1) * 128], rhs=Sb[sc].t[:, h, :], start=False, stop=(sc == 3))
                        return last
                    kb.op("pe", emit, [B["att"], B["v"], B["QM"]] + Sb, [bo[hh]])
                back(q, bo, B, c * 128, c)

            nts = [nTr.next(), nTr.next()]
            FP, GP = [0, 1, 2, 3], [4, 5, 6, 7]
            kb.begin(FP); F(0, nts[0], Bs[0]); kb.play(kb.end())
            for c in range(C.NCH):
                kb.begin(GP); G(c, Bs[c % 2]); gl = kb.end()
                fl = []
                if c + 1 < C.NCH:
                    kb.begin(FP); F(c + 1, nts[(c + 1) % 2], Bs[(c + 1) % 2]); fl = kb.end()
                kb.play(schedule(gl, fl))
            kb.set_pool(range(8))
            kb.dma("sp", C.o_hg_p.rearrange("(h k) v -> k h v", k=128), S.t[:, :, :], reads=[S])
            kb.barrier()
        with ExitStack() as st:
            q, nseg, seglen = 64, 16, 4
            B = bufs(q, nseg, seglen, st)[0]
            nTc = kb.sb("hnTs", [128, 8, 64], BF16, st)
            kb.dma("sp", nTc.t[:, :, :], C.nTd[:, :, 3 + C.TP:3 + C.TP + 64], reads=[C.nTd_r[C.NT]], writes=[nTc])
            hinc, hafter, segones = cst[:64, C_TSI:C_TSI + 64], cst[:64, C_TSA:C_TSA + 64], cst[:64, C_SEGS:C_SEGS + 16]
            front(q, nseg, seglen, nTc, 0, B, hinc, hafter, segones)
            bo, ids = kb.reserve(2)
            zt = kb.sb("hg_zero", [64, 64], BF16, st)
            kb.memset("pool", zt.t[:], 0.0, [zt])
            for hh in range(2):
                def emit(e, hh=hh):
                    e.matmul(bo[hh].t[:64, :], lhsT=zt.t[:, :], rhs=B["v"].t[:, hh * 512:(hh + 1) * 512], start=True, stop=False)
                    for h4 in range(4):
                        last = e.matmul(bo[hh].t[:64, h4 * 128:(h4 + 1) * 128], lhsT=B["att"].t[:, hh * 4 + h4, :], rhs=B["v"].t[:, (hh * 4 + h4) * 128:(hh * 4 + h4 + 1) * 128], start=False, stop=False)
                    return last
                kb.op("pe", emit, [B["att"], B["v"], zt], [bo[hh]])
            Sir = kb.rot("hg_Sin", [128, 8, 128], F32, 2, st)
            Sor = kb.rot("hg_Sout", [128, 8, 128], F32, 2, st)
            Sbr = kb.rot("hg_Sbb", [128, 8, 128], BF16, 2, st)
            k4r = kb.rot("hg_k4", [4, 1024], BF16, 2, st)
            v4r = kb.rot("hg_v4", [4, 1024], BF16, 2, st)
            sin = C.st_hg.rearrange("(b h k) v -> b k h v", h=8, k=128)
            sout = C.o_hg_s.rearrange("(b h k) v -> b k h v", h=8, k=128)
            for bb in range(16):
                Si = Sir.next()
                kb.dma("sp", Si.t[:, :, :], sin[bb], writes=[Si])
                k4 = k4r.next(); v4 = v4r.next()
                kb.dma("sp", k4.t[:, :], B["kdec"].t[4 * bb:4 * bb + 4, :], reads=[B["kdec"]], writes=[k4])
                kb.dma("sp", v4.t[:, :], B["v"].t[4 * bb:4 * bb + 4, :], reads=[B["v"]], writes=[v4])
                Sbb = Sbr.next()
                kb.copy("act", Sbb.t[:, :, :], Si.t[:, :, :], [Si], [Sbb])
                for hh in range(2):
                    kb.op("pe", lambda e, hh=hh, Sbb=Sbb, bb=bb: [e.matmul(bo[hh].t[:64, h4 * 128:(h4 + 1) * 128], lhsT=B["QM"].t[:, hh * 4 + h4, bb * 64:(bb + 1) * 64], rhs=Sbb.t[:, hh * 4 + h4, :], start=False, stop=(bb == 15 and h4 == 3)) for h4 in range(4)][-1], [B["QM"], Sbb], [bo[hh]])
                bd = [kb.bank(), kb.bank()]
                for hh in range(2):
                    kb.op("pe", lambda e, hh=hh, bd=bd, k4=k4, v4=v4: [e.matmul(bd[hh].t[:, h4 * 128:(h4 + 1) * 128], lhsT=k4.t[0:4, (hh * 4 + h4) * 128:(hh * 4 + h4 + 1) * 128], rhs=v4.t[0:4, (hh * 4 + h4) * 128:(hh * 4 + h4 + 1) * 128], start=True, stop=True) for h4 in range(4)][-1], [k4, v4], [bd[hh]])
                So = Sor.next()
                kb.tt("pool", So.t[:, :, :], Si.t[:, :, :], bc(B["dS"].t[:, :].rearrange("p (h c) -> p h c", c=nseg)[:, :, bb], [128, 8, 128], 2), ALU.mult, [Si, B["dS"]], [So])
                for hh in range(2):
                    Sv = So.t[:, hh * 4:(hh + 1) * 4, :]
                    kb.tt("dve", Sv, Sv, bd[hh].t[:, :].rearrange("p (h v) -> p h v", v=128), ALU.add, [So, bd[hh]], [So])
                kb.dma("sp", sout[bb], So.t[:, :, :], reads=[So])
            back(q, bo, B, C.TP, C.NCH)
            kb.release(ids)
            kb.barrier()


def chunk_info(C, ci):
    if ci < C.NCH:
        return 128, ci * 128
    return 64, C.TP


def merge_pass(C):
    kb = C.kb
    with ExitStack() as st:
        Wg1 = load_w(C, "Wg1", C.w_in[:, 9248:10272], 1024, stack=st)
        Wg2 = load_w(C, "Wg2", C.w_in[:, 10272:11296], 1024, stack=st)
        Wso = load_w(C, "Wso", C.w_ssd_out, 1024, nk=16, stack=st)
        Who = load_w(C, "Who", C.w_hgrn_out, 1024, stack=st)
        Wo = load_w(C, "Wo", C.w_out, 1024, stack=st)
        def mset(i):
            return dict(s1=kb.sb(f"m_s1{i}", [128, D], F32, st), s2=kb.sb(f"m_s2{i}", [128, D], F32, st), m1=kb.sb(f"m_m1{i}", [128, D], F32, st),
                        m2=kb.sb(f"m_m2{i}", [128, D], F32, st), mg=kb.sb(f"m_mg{i}", [128, D], BF16, st), mT=kb.sb(f"m_mT{i}", [128, 8, 128], BF16, st),
                        x1=kb.sb(f"m_x1{i}", [128, D], F32, st), nT=kb.sb(f"m_nT{i}", [128, 8, 128], BF16, st), yT=kb.sb(f"m_yT{i}", [128, 16, 128], BF16, st),
                        oT=kb.sb(f"m_oT{i}", [128, 8, 128], BF16, st), x=kb.sb(f"m_x{i}", [128, D], F32, st))
        msets = [mset(0), mset(1)]

        def chunk(ci, S):
            q, tok0 = chunk_info(C, ci)
            nTc, ynT, onT, xt = S["nT"], S["yT"], S["oT"], S["x"]
            s1, s2, m1, m2, mg, mT, x1 = S["s1"], S["s2"], S["m1"], S["m2"], S["mg"], S["mT"], S["x1"]
            kb.dma("sp", nTc.t[:, :, :q], C.nTd[:, :, 3 + tok0:3 + tok0 + q], reads=[C.nTd_r[ci + 1]], writes=[nTc])
            kb.dma("sp", ynT.t[:, :, :q], C.YNT[:, :, tok0:tok0 + q], reads=C.YNT_r[ci], writes=[ynT])
            kb.dma("sp", onT.t[:, :, :q], C.ONT[:, :, tok0:tok0 + q], reads=[C.ONT_r[ci]], writes=[onT])
            src = C.xp[tok0:tok0 + q, :] if ci < C.NCH else C.xs[:, :]
            kb.dma("sp", xt.t[:q, :], src, writes=[xt])
            hs = slice(0, q)
            for half in range(2):
                cs = slice(half * 512, (half + 1) * 512)
                for (Wt, dst) in ((Wg1, s1), (Wg2, s2)):
                    b = kb.bank()
                    kb.op("pe", lambda e, b=b, Wt=Wt, cs=cs: [e.matmul(b.t[hs, :], lhsT=nTc.t[:, k, :q], rhs=Wt.t[:, k, cs], start=(k == 0), stop=(k == 7)) for k in range(8)][-1], [nTc, Wt], [b])
                    kb.act(dst.t[hs, cs], b.t[hs, :], AF.Tanh, [b], [dst], scale=0.5)
                b = kb.bank()
                kb.op("pe", lambda e, b=b, cs=cs: [e.matmul(b.t[hs, :], lhsT=ynT.t[:, k, :q], rhs=Wso.t[:, k, cs], start=(k == 0), stop=(k == 15)) for k in range(16)][-1], [ynT, Wso], [b])
                kb.stt(m1.t[hs, cs], s1.t[hs, cs], 1.0, b.t[hs, :], ALU.add, ALU.mult, [b, s1], [m1])
                b = kb.bank()
                kb.op("pe", lambda e, b=b, cs=cs: [e.matmul(b.t[hs, :], lhsT=onT.t[:, k, :q], rhs=Who.t[:, k, cs], start=(k == 0), stop=(k == 7)) for k in range(8)][-1], [onT, Who], [b])
                kb.stt(m2.t[hs, cs], s2.t[hs, cs], 1.0, b.t[hs, :], ALU.add, ALU.mult, [b, s2], [m2])
            kb.tt("dve", mg.t[hs, :], m1.t[hs, :], m2.t[hs, :], ALU.add, [m1, m2], [mg])
            to_T(C, mg, q, 8, None, mT.t[:, :, :q], mT)
            for half in range(2):
                cs = slice(half * 512, (half + 1) * 512)
                b = kb.bank()
                kb.op("pe", lambda e, b=b, cs=cs: [e.matmul(b.t[hs, :], lhsT=mT.t[:, k, :q], rhs=Wo.t[:, k, cs], start=(k == 0), stop=(k == 7)) for k in range(8)][-1], [mT, Wo], [b])
                kb.stt(x1.t[hs, cs], b.t[hs, :], 0.5, xt.t[hs, cs], ALU.mult, ALU.add, [b, xt], [x1])
            kb.dma("sp", C.X1[tok0:tok0 + q, :], x1.t[hs, :], reads=[x1], writes=[C.X1_r[ci]])

        for c0 in range(0, C.NT, 2):
            lists = []
            for i in range(2):
                if c0 + i < C.NT:
                    kb.begin([4 * i, 4 * i + 1, 4 * i + 2, 4 * i + 3]); chunk(c0 + i, msets[i]); lists.append(kb.end())
            kb.play(schedule(*lists))
        kb.set_pool(range(8))
        kb.barrier()


def attn_pass(C):
    kb = C.kb
    with ExitStack() as st:
        Wcq = load_w(C, "Wcq", C.w_cq, 1024, stack=st)
        Wck = load_w(C, "Wck", C.w_ck, 1024, stack=st)
        Wcv = load_w(C, "Wcv", C.w_cv, 1024, stack=st)
        Wco = load_w(C, "Wco", C.w_co, 1024, stack=st)
        memT = kb.sb("a_memT", [128, 8, 256], BF16, st)
        KT = kb.sb("a_KT", [128, 8, 256], BF16, st)
        V = kb.sb("a_V", [128, 2, D], BF16, st)
        xr = kb.rot("a_x1", [128, D], F32, 2, st)
        kvo = kb.rot("a_kvo", [128, 512], F32, 2, st)
        for mt in range(2):
            xt = xr.next()
            kb.dma("sp", xt.t[:, :], C.mem[mt * 128:(mt + 1) * 128, :], writes=[xt])
            rms_to_T(C, xt, 128, C.colv.t[:, V_NMEM:V_NMEM + 8], memT.t[:, :, mt * 128:(mt + 1) * 128], memT)
        for mt in range(2):
            for half in range(2):
                cs = slice(half * 512, (half + 1) * 512)
                for (Wt, dst, isv) in ((Wck, C.o_mk, False), (Wcv, C.o_mv, True)):
                    b = kb.bank()
                    kb.op("pe", lambda e, b=b, Wt=Wt: [e.matmul(b.t[:, :], lhsT=memT.t[:, k, mt * 128:(mt + 1) * 128], rhs=Wt.t[:, k, cs], start=(k == 0), stop=(k == 7)) for k in range(8)][-1], [memT, Wt], [b])
                    o = kvo.next()
                    kb.copy("act", o.t[:, :], b.t[:, :], [b], [o])
                    kb.dma("sp", dst[mt * 128:(mt + 1) * 128, cs], o.t[:, :], reads=[o])
                    if isv:
                        kb.copy("pool", V.t[:, mt, cs], o.t[:, :], [o], [V])
        for c in range(8):
            b = kb.bank()
            kb.op("pe", lambda e, b=b, c=c: [e.matmul(b.t[:, 0:256], lhsT=Wck.t[:, k, c * 128:(c + 1) * 128], rhs=memT.t[:, k, :], start=(k == 0), stop=(k == 7)) for k in range(8)][-1], [memT, Wck], [b])
            kb.copy("act", KT.t[:, c, :], b.t[:, 0:256], [b], [KT])

        def mkset(i, st=st):
            return dict(hnT=kb.sb(f"a_hnT{i}", [128, 8, 128], BF16, st), Qs=kb.sb(f"a_Qs{i}", [128, D], BF16, st), QT=kb.sb(f"a_QT{i}", [128, 8, 128], BF16, st),
                        P=kb.sb(f"a_P{i}", [128, D], BF16, st), PT=kb.sb(f"a_PT{i}", [128, 8, 128], BF16, st), On=kb.sb(f"a_On{i}", [128, D], BF16, st),
                        OT=kb.sb(f"a_OT{i}", [128, 8, 128], BF16, st), sm=kb.sb(f"a_sm{i}", [128, 16], F32, st), x2=kb.sb(f"a_x2{i}", [128, D], F32, st),
                        x1=kb.sb(f"a_x1{i}", [128, D], F32, st), xn=kb.sb(f"a_xn{i}", [128, D], BF16, st), junk=kb.sb(f"a_junk{i}", [128, D], BF16, st),
                        ss=kb.sb(f"a_ss{i}", [128, 4], F32, st))
        sets = [mkset(0), mkset(1)]
        zt = kb.sb("a_zero", [128, 64], BF16, st)
        kb.memset("pool", zt.t[:], 0.0, [zt])

        def load_x1(ci, S):
            q, tok0 = chunk_info(C, ci)
            kb.dma("sp", S["x1"].t[:q, :], C.X1[tok0:tok0 + q, :], reads=[C.X1_r[ci]], writes=[S["x1"]])

        def q_front(q, S):
            hs = slice(0, q)
            hnT, Qs, QT = S["hnT"], S["Qs"], S["QT"]
            rms_to_T(C, S["x1"], q, C.colv.t[:, V_NCROSS:V_NCROSS + 8], hnT.t[:, :, :q], hnT, scratch=S)
            for half in range(2):
                cs = slice(half * 512, (half + 1) * 512)
                b = kb.bank()
                kb.op("pe", lambda e, b=b, cs=cs: [e.matmul(b.t[hs, :], lhsT=hnT.t[:, k, :q], rhs=Wcq.t[:, k, cs], start=(k == 0), stop=(k == 7)) for k in range(8)][-1], [hnT, Wcq], [b])
                kb.act(Qs.t[hs, cs], b.t[hs, :], AF.Copy, [b], [Qs], scale=1.0 / 16.0)
            to_T(C, Qs, q, 8, None, QT.t[:, :, :q], QT)

        def softmax(q, bs, S):
            hs = slice(0, q)
            sm, P, PT = S["sm"], S["P"], S["PT"]
            for hh in range(2):
                kb.op("dve", lambda e, hh=hh: e.tensor_reduce(out=sm.t[hs, hh * 2:hh * 2 + 2], in_=bs[hh].t[hs, :].rearrange("p (h m) -> p h m", m=256), axis=AX.X, op=ALU.max), [bs[hh]], [sm])
            kb.ts("dve", sm.t[hs, 4:8], sm.t[hs, 0:4], -1.0, None, ALU.mult, None, [sm], [sm])
            for h in range(4):
                kb.act(P.t[hs, h * 256:(h + 1) * 256], bs[h // 2].t[hs, (h % 2) * 256:(h % 2 + 1) * 256], AF.Exp, [bs[h // 2], sm], [P, sm], bias=sm.t[hs, 4 + h:5 + h], accum_out=sm.t[hs, 8 + h:9 + h])
            kb.op("dve", lambda e: e.reciprocal(out=sm.t[hs, 12:16], in_=sm.t[hs, 8:12]), [sm], [sm])
            to_T(C, P, q, 8, None, PT.t[:, :, :q], PT)

        def o_back(q, bo, S, ci, tok0):
            hs = slice(0, q)
            sm, On, OT, x2, xt = S["sm"], S["On"], S["OT"], S["x2"], S["x1"]
            for h in range(4):
                kb.act(On.t[hs, h * 256:(h + 1) * 256], bo[h // 2].t[hs, (h % 2) * 256:(h % 2 + 1) * 256], AF.Copy, [bo[h // 2], sm], [On], scale=sm.t[hs, 12 + h:13 + h])
            to_T(C, On, q, 8, None, OT.t[:, :, :q], OT)
            for half in range(2):
                cs = slice(half * 512, (half + 1) * 512)
                b = kb.bank()
                kb.op("pe", lambda e, b=b, cs=cs: [e.matmul(b.t[hs, :], lhsT=OT.t[:, k, :q], rhs=Wco.t[:, k, cs], start=(k == 0), stop=(k == 7)) for k in range(8)][-1], [OT, Wco], [b])
                kb.tt("dve", x2.t[hs, cs], b.t[hs, :], xt.t[hs, cs], ALU.add, [b, xt], [x2])
            kb.dma("sp", C.X2[tok0:tok0 + q, :], x2.t[hs, :], reads=[x2], writes=[C.X2_r[ci]])

        def prompt_chunk(ci, S):
            q, tok0 = 128, ci * 128
            load_x1(ci, S)
            q_front(q, S)
            QT, PT = S["QT"], S["PT"]
            bs = [kb.bank(), kb.bank()]
            for hh in range(2):
                def emit(e, hh=hh):
                    for h2 in range(2):
                        h = hh * 2 + h2
                        for dc in range(2):
                            last = e.matmul(bs[hh].t[:, h2 * 256:(h2 + 1) * 256], lhsT=QT.t[:, h * 2 + dc, :], rhs=KT.t[:, h * 2 + dc, :], start=(dc == 0), stop=(dc == 1))
                    return last
                kb.op("pe", emit, [QT, KT], [bs[hh]])
            softmax(q, bs, S)
            bo = [kb.bank(), kb.bank()]
            for hh in range(2):
                def emit(e, hh=hh):
                    for h2 in range(2):
                        h = hh * 2 + h2
                        for mt in range(2):
                            last = e.matmul(bo[hh].t[:, h2 * 256:(h2 + 1) * 256], lhsT=PT.t[:, h * 2 + mt, :], rhs=V.t[:, mt, h * 256:(h + 1) * 256], start=(mt == 0), stop=(mt == 1))
                    return last
                kb.op("pe", emit, [PT, V], [bo[hh]])
            o_back(q, bo, S, ci, tok0)

        with ExitStack() as st2:
            psets = sets + [mkset(2, st2), mkset(3, st2)]
            for c0 in range(0, C.NCH, 4):
                lists = []
                for i in range(4):
                    if c0 + i < C.NCH:
                        kb.begin([2 * i, 2 * i + 1]); prompt_chunk(c0 + i, psets[i]); lists.append(kb.end())
                kb.play(schedule(*lists))
            kb.set_pool(range(8))
            kb.barrier()
        q, tok0, ci = 64, C.TP, C.NCH
        S = sets[0]
        QT, PT = S["QT"], S["PT"]
        load_x1(ci, S)
        q_front(q, S)
        QTm = kb.sb("a_QTm", [128, 8, 16 * 68], BF16, st)
        PTm = kb.sb("a_PTm", [128, 8, 16 * 68], BF16, st)
        kb.memset("pool", QTm.t[:], 0.0, [QTm])
        kb.memset("pool", PTm.t[:], 0.0, [PTm])
        kb.copy("pool", QTm.t[:, :, :].rearrange("p h (c x) -> p h c x", x=68)[:, :, :, 0:4], QT.t[:, :, :64].rearrange("p h (c j) -> p h c j", j=4), [QT], [QTm])
        Kr = kb.rot("a_Kb", [128, 2, D], BF16, 2, st)
        KTr = kb.rot("a_KTb", [128, 8, 256], BF16, 2, st)
        bs, ids = kb.reserve(2)
        for hh in range(2):
            kb.op("pe", lambda e, hh=hh: e.matmul(bs[hh].t[:64, :], lhsT=zt.t[:64, :], rhs=Wcq.t[:64, 0, 0:512], start=True, stop=False), [zt, Wcq], [bs[hh]])
        ckv = C.ck.rearrange("(b mt p) c -> b p mt c", mt=2, p=128)
        cvv = C.cv.rearrange("(b mt p) c -> b p mt c", mt=2, p=128)
        for bb in range(16):
            Kb = Kr.next()
            kb.dma("pool", Kb.t[:, :, :], ckv[bb], writes=[Kb])
            KTb = KTr.next()
            for mt in range(2):
                b = kb.bank()
                pb = b.t.bitcast(BF16)
                kb.op("pe", lambda e, pb=pb, mt=mt, Kb=Kb: [e.transpose(pb[:, c * 128:(c + 1) * 128], Kb.t[:, mt, c * 128:(c + 1) * 128], C.identb.t[:, :]) for c in range(8)][-1], [Kb, C.identb], [b])
                kb.copy("act", KTb.t[:, :, mt * 128:(mt + 1) * 128], pb.rearrange("p (c m) -> p c m", m=128), [b], [KTb])
            for hh in range(2):
                def emit(e, hh=hh, KTb=KTb, bb=bb):
                    for h2 in range(2):
                        h = hh * 2 + h2
                        for dc in range(2):
                            last = e.matmul(bs[hh].t[:64, h2 * 256:(h2 + 1) * 256], lhsT=QTm.t[:, h * 2 + dc, bb * 64:(bb + 1) * 64], rhs=KTb.t[:, h * 2 + dc, :], start=False, stop=(bb == 15 and dc == 1 and h2 == 1))
                    return last
                kb.op("pe", emit, [QTm, KTb], [bs[hh]])
        softmax(q, bs, S)
        kb.copy("pool", PTm.t[:, :, :].rearrange("p h (c x) -> p h c x", x=68)[:, :, :, 0:4], PT.t[:, :, :64].rearrange("p h (c j) -> p h c j", j=4), [PT], [PTm])
        kb.release(ids)
        bo, ids = kb.reserve(2)
        for hh in range(2):
            kb.op("pe", lambda e, hh=hh: e.matmul(bo[hh].t[:64, :], lhsT=zt.t[:64, :], rhs=Wcq.t[:64, 0, 0:512], start=True, stop=False), [zt, Wcq], [bo[hh]])
        for bb in range(16):
            Vb = Kr.next()
            kb.dma("pool", Vb.t[:, :, :], cvv[bb], writes=[Vb])
            for hh in range(2):
                def emit(e, hh=hh, Vb=Vb, bb=bb):
                    for h2 in range(2):
                        h = hh * 2 + h2
                        for mt in range(2):
                            last = e.matmul(bo[hh].t[:64, h2 * 256:(h2 + 1) * 256], lhsT=PTm.t[:, h * 2 + mt, bb * 64:(bb + 1) * 64], rhs=Vb.t[:, mt, h * 256:(h + 1) * 256], start=False, stop=(bb == 15 and mt == 1 and h2 == 1))
                    return last
                kb.op("pe", emit, [PTm, Vb], [bo[hh]])
        o_back(q, bo, S, ci, tok0)
        kb.release(ids)
        kb.barrier()


def ffn_pass(C):
    kb = C.kb
    with ExitStack() as st:
        Wga = load_w(C, "Wga", C.w_gate, FFN, stack=st)
        Wup = load_w(C, "Wup", C.w_up, FFN, stack=st)
        Wdn = load_w(C, "Wdn", C.w_down, 1024, nk=22, stack=st)
        nfin = kb.sb("f_nfin", [128, D], F32, st)
        kb.dma("sp", nfin.t[:], C.rowvecs[:, RV_NFIN:RV_NFIN + D].partition_broadcast(128), writes=[nfin])
        x2r = kb.rot("f_x2", [128, 2, D], F32, 2, st)
        hnT = kb.sb("f_hnT", [128, 8, 256], BF16, st)
        hT = kb.sb("f_hT", [128, 22, 256], BF16, st)
        sgr = kb.rot("f_sg", [128, 256], F32, 2, st)
        yr = kb.rot("f_y", [128, D], F32, 2, st)
        ntile = C.NCH // 2 + 1

        def tinfo(ti):
            if ti < C.NCH // 2:
                return [(128, ti * 256, 2 * ti), (128, ti * 256 + 128, 2 * ti + 1)]
            return [(64, C.TP, C.NCH)]

        def loads(ti):
            t = x2r.next()
            for sub, (q, tok0, ci) in enumerate(tinfo(ti)):
                kb.dma("sp", t.t[:q, sub, :], C.X2[tok0:tok0 + q, :], reads=[C.X2_r[ci]], writes=[t])
            return t
        nxt = loads(0)
        for ti in range(ntile):
            xt = nxt
            if ti + 1 < ntile:
                nxt = loads(ti + 1)
            subs = tinfo(ti)
            Wd = sum(q for q, _, _ in subs)
            for sub, (q, tok0, ci) in enumerate(subs):
                xv = T(xt.t[:, sub, :], xt.r)
                rms_to_T(C, xv, q, C.colv.t[:, V_NFFN:V_NFFN + 8], hnT.t[:, :, sub * 128:sub * 128 + q], hnT)
            for f in range(22):
                b = kb.bank()

                def emit(e, b=b, f=f):
                    for k in range(8):
                        e.matmul(b.t[:, 0:Wd], lhsT=Wga.t[:, k, f * 128:(f + 1) * 128], rhs=hnT.t[:, k, 0:Wd], start=(k == 0), stop=(k == 7))
                    for k in range(8):
                        last = e.matmul(b.t[:, 256:256 + Wd], lhsT=Wup.t[:, k, f * 128:(f + 1) * 128], rhs=hnT.t[:, k, 0:Wd], start=(k == 0), stop=(k == 7))
                    return last
                kb.op("pe", emit, [Wga, Wup, hnT], [b])
                sg = sgr.next()
                kb.act(sg.t[:, 0:Wd], b.t[:, 0:Wd], AF.Tanh, [b], [sg], scale=0.5)
                kb.stt(sg.t[:, 0:Wd], sg.t[:, 0:Wd], 1.0, b.t[:, 0:Wd], ALU.add, ALU.mult, [sg, b], [sg])
                kb.tt("dve", hT.t[:, f, 0:Wd], sg.t[:, 0:Wd], b.t[:, 256:256 + Wd], ALU.mult, [sg, b], [hT])
            for sub, (q, tok0, ci) in enumerate(subs):
                hs = slice(0, q)
                for half in range(2):
                    cs = slice(half * 512, (half + 1) * 512)
                    b = kb.bank()
                    kb.op("pe", lambda e, b=b, sub=sub, q=q, cs=cs: [e.matmul(b.t[:q, :], lhsT=hT.t[:, f, sub * 128:sub * 128 + q], rhs=Wdn.t[:, f, cs], start=(f == 0), stop=(f == 21)) for f in range(22)][-1], [hT, Wdn], [b])
                    kb.stt(xt.t[hs, sub, cs], b.t[hs, :], 0.5, xt.t[hs, sub, cs], ALU.mult, ALU.add, [b, xt], [xt])
                xv = T(xt.t[:, sub, :], xt.r)
                ss = C.ss.next()
                rms_rstd(C, xv, q, ss)
                y = yr.next()
                kb.act(y.t[hs, :], xv.t[hs, :], AF.Copy, [xt, ss], [y], scale=ss.t[hs, 0:1])
                kb.tt("dve", y.t[hs, :], y.t[hs, :], nfin.t[hs, :], ALU.mult, [y, nfin], [y])
                dst = C.o_y_p[tok0:tok0 + q, :] if ci < C.NCH else C.o_y_s[:, :]
                kb.dma("sp", dst, y.t[hs, :], reads=[y])
        kb.barrier()

def build(NCH=16, debug=False, upto=99, skip_ssd=False):
    TP = NCH * 128
    TS = 64
    TT = TP + TS
    nc = bass.Bass("TRN2", target_bir_lowering=False)
    C = Ctx()
    C.nc, C.NCH, C.TP, C.TT, C.NT = nc, NCH, TP, TT, NCH + 1
    C.skip_ssd = skip_ssd

    def din(name, shape):
        return nc.dram_tensor(name, list(shape), F32, kind="ExternalInput").ap()

    def dout(name, shape):
        return nc.dram_tensor(name, list(shape), F32, kind="ExternalOutput").ap()

    def dscr(name, shape, dt):
        if debug:
            return nc.dram_tensor(name, list(shape), dt, kind="ExternalOutput").ap()
        return nc.dram_tensor(name, list(shape), dt).ap()

    C.xp = din("xp", [TP, D]); C.xs = din("xs", [TS, D]); C.mem = din("mem", [256, D])
    C.st_ssm = din("st_ssm", [16 * 2048, 128]); C.st_conv = din("st_conv", [48, 3072]); C.st_hg = din("st_hg", [16 * 1024, 128])
    C.ck = din("ck", [16 * 256, D]); C.cv = din("cv", [16 * 256, D])
    C.w_in = din("w_in", [D, IN_DIM]); C.w_ssd_out = din("w_ssd_out", [2048, D]); C.w_hgrn_out = din("w_hgrn_out", [D, D]); C.w_out = din("w_out", [D, D])
    C.w_cq = din("w_cq", [D, D]); C.w_ck = din("w_ck", [D, D]); C.w_cv = din("w_cv", [D, D]); C.w_co = din("w_co", [D, D])
    C.w_gate = din("w_gate", [D, FFN]); C.w_up = din("w_up", [D, FFN]); C.w_down = din("w_down", [FFN, D])
    C.consts = din("consts", [128, 1024]); C.colvecs = din("colvecs", [128, 176]); C.rowvecs = din("rowvecs", [1, RV_N])
    C.o_y_p = dout("o_y_p", [TP, D]); C.o_y_s = dout("o_y_s", [TS, D]); C.o_ssm_p = dout("o_ssm_p", [2048, 128]); C.o_conv_p = dout("o_conv_p", [3, 3072])
    C.o_hg_p = dout("o_hg_p", [1024, 128]); C.o_mk = dout("o_mk", [256, D]); C.o_mv = dout("o_mv", [256, D])
    C.o_ssm_s = dout("o_ssm_s", [16 * 2048, 128]); C.o_conv_s = dout("o_conv_s", [48, 3072]); C.o_hg_s = dout("o_hg_s", [16 * 1024, 128])
    C.nTd = dscr("nTd", [128, 8, 3 + TT], BF16); C.nTd_r = [Region(f"nTd{i}") for i in range(C.NT + 1)]
    C.YNT = dscr("YNT", [128, 16, TT], BF16); C.YNT_r = [[Region(f"YNT{i}_{g}") for g in range(4)] for i in range(C.NT)]
    C.ONT = dscr("ONT", [128, 8, TT], BF16); C.ONT_r = [Region(f"ONT{i}") for i in range(C.NT)]
    C.X1 = dscr("X1", [TT, D], F32); C.X1_r = [Region(f"X1{i}") for i in range(C.NT)]
    C.X2 = dscr("X2", [TT, D], F32); C.X2_r = [Region(f"X2{i}") for i in range(C.NT)]

    with ExitStack() as st:
        kb = KB(nc, st)
        C.kb = kb
        C.cst = kb.sb("cst", [128, 1024]); kb.dma("sp", C.cst.t[:], C.consts[:, :], writes=[C.cst])
        C.colv = kb.sb("colv", [128, 176]); kb.dma("sp", C.colv.t[:], C.colvecs[:, :], writes=[C.colv])
        C.rowb = kb.sb("rowb", [128, 96]); kb.dma("sp", C.rowb.t[:], C.rowvecs[:, 0:96].partition_broadcast(128), writes=[C.rowb])
        C.identb = kb.sb("identb", [128, 128], BF16); kb.copy("dve", C.identb.t[:], C.cst.t[:, C_ID:C_ID + 128], [C.cst], [C.identb])
        C.a_bc = kb.sb("a_bc", [128, 32])
        kb.act(C.a_bc.t[:], C.rowb.t[:, RV_ALOG:RV_ALOG + 32], AF.Exp, [C.rowb], [C.a_bc])
        kb.ts("dve", C.a_bc.t[:], C.a_bc.t[:], -1.0, None, ALU.mult, None, [C.a_bc], [C.a_bc])
        C.colvh = kb.sb("colvh", [128, 120]); kb.ts("dve", C.colvh.t[:], C.colv.t[:, V_CONVB:V_CONVB + 120], 0.5, None, ALU.mult, None, [C.colv], [C.colvh])
        C.junk = kb.rot("junk", [128, D], BF16, 1)
        C.ss = kb.rot("ss", [128, 4], F32, 4)
        C.xn = kb.rot("xn", [128, D], BF16, 1)
        if upto >= 0:
            pass0(C)
        if upto >= 1 and not C.skip_ssd:
            ssd_pass(C)
        if upto >= 2 and not C.skip_ssd:
            hg_pass(C)
        if upto >= 3:
            merge_pass(C)
        if upto >= 4:
            attn_pass(C)
        if upto >= 5:
            ffn_pass(C)
        kb.finish()
    return nc


def host_inputs(inp, NCH=16):
    f = lambda a: np.ascontiguousarray(np.asarray(a, dtype=np.float32))
    TP = NCH * 128
    cv = np.zeros((128, 176), np.float32)
    col8 = lambda v: f(v).reshape(-1, 128).T
    cv[:, V_NMIX:V_NMIX + 8] = col8(inp["norm_mix"][0]); cv[:, V_NCROSS:V_NCROSS + 8] = col8(inp["norm_cross"][0])
    cv[:, V_NMEM:V_NMEM + 8] = col8(inp["norm_mem"][0]); cv[:, V_NFFN:V_NFFN + 8] = col8(inp["norm_ffn"][0])
    cv[:, V_SNORM:V_SNORM + 16] = col8(inp["ssd_norm"][0]); cv[:, V_HNORM:V_HNORM + 8] = col8(inp["hgrn_norm"][0])
    cv[:, V_CONVB:V_CONVB + 24] = col8(inp["conv_b"][0])
    cw = f(inp["conv_w"][0])
    cv[:, V_CONVW:V_CONVW + 96] = cw.T.reshape(24, 128, 4).transpose(1, 0, 2).reshape(128, 96)
    rv = np.concatenate([f(inp["dt_bias"][0]), f(inp["a_log"][0]), f(inp["d_skip"][0]), f(inp["hgrn_lb"][0]), f(inp["hgrn_lb"][1]), f(inp["norm_final"])])[None, :]
    consts = make_consts()
    shared = dict(w_in=f(inp["w_in"][0]), w_ssd_out=f(inp["w_ssd_out"][0]), w_hgrn_out=f(inp["w_hgrn_out"][0]), w_out=f(inp["w_out"][0]),
                  w_cq=f(inp["w_cq"][0]), w_ck=f(inp["w_ck"][0]), w_cv=f(inp["w_cv"][0]), w_co=f(inp["w_co"][0]),
                  w_gate=f(inp["w_gate"][0]), w_up=f(inp["w_up"][0]), w_down=f(inp["w_down"][0]),
                  consts=consts, colvecs=cv, rowvecs=f(rv))
    maps = []
    for c in range(8):
        sl = slice(16 * c, 16 * c + 16)
        m = dict(shared)
        m["xp"] = f(inp["x_prompt"][c][:TP]); m["xs"] = f(inp["x_sample"][sl]).reshape(64, D); m["mem"] = f(inp["mem_prompt"][c])
        m["st_ssm"] = f(inp["state_ssm"][0, sl]).reshape(16 * 2048, 128); m["st_conv"] = f(inp["state_conv"][0, sl]).reshape(48, 3072)
        m["st_hg"] = f(inp["state_hgrn"][0, sl]).reshape(16 * 1024, 128)
        m["ck"] = f(inp["cache_mem_k"][0, sl]).reshape(16 * 256, D); m["cv"] = f(inp["cache_mem_v"][0, sl]).reshape(16 * 256, D)
        maps.append(m)
    return maps


def kernel(**inputs):
    NCH = 16
    nc = build(NCH)
    maps = host_inputs(inputs, NCH)
    res = run_bass_kernel_spmd(nc, maps, core_ids=list(range(8))).results
    g = lambda k: [np.asarray(r[k], dtype=np.float32) for r in res]
    y_p = np.stack(g("o_y_p")).reshape(8, 2048, D)
    y_s = np.stack(g("o_y_s")).reshape(128, 4, D)
    ssm_p = np.stack(g("o_ssm_p")).reshape(1, 8, 32, 64, 128)
    conv_p = np.stack(g("o_conv_p")).reshape(1, 8, 3, 3072)
    hg_p = np.stack(g("o_hg_p")).reshape(1, 8, 8, 128, 128)
    mk = np.stack(g("o_mk")).reshape(1, 8, 256, 4, 256)
    mv = np.stack(g("o_mv")).reshape(1, 8, 256, 4, 256)
    ssm_s = np.stack(g("o_ssm_s")).reshape(1, 128, 32, 64, 128)
    conv_s = np.stack(g("o_conv_s")).reshape(1, 128, 3, 3072)
    hg_s = np.stack(g("o_hg_s")).reshape(1, 128, 8, 128, 128)
    return (y_p, y_s, ssm_p, conv_p, hg_p, mk, mv, ssm_s, conv_s, hg_s)
```
